# Optimizing a Trainium2 kernel written in Bass

```python
import math
import numpy as np
import jax
import jax.numpy as jnp
from jax import lax

D_MODEL = 1024
BATCH = 4
SEQ = 8192
DEPTH = 4

GRID_W = 64
CTX_LEN = 256
N_DIRS = 2
EPS = 1e-6
S5_WIDTH = D_MODEL // 2
S5_GROUP = 16
S5_GROUPS = S5_WIDTH // S5_GROUP
S5_STATE = 64
DT_MIN = 1e-3
DT_MAX = 1e-1
GLA_HEADS = 4
GLA_DV = D_MODEL // 2 // GLA_HEADS
GLA_DK = GLA_DV // 2
GLA_WIDTH = GLA_HEADS * GLA_DV
GLA_KEY = GLA_HEADS * GLA_DK
GLA_RANK = 16
GLA_NORMALIZER = 16.0
GLA_CHUNK = 64
MIX_WIDTH = S5_WIDTH + GLA_WIDTH
IN_WIDTH = 2 * S5_WIDTH + 2 * GLA_KEY + 2 * GLA_WIDTH + N_DIRS * GLA_RANK

kernel_name = 'hybrid_s5_gla_prefix_dit'


def _split_points():
    sizes = (S5_WIDTH, S5_WIDTH, GLA_KEY, GLA_KEY, GLA_WIDTH, GLA_WIDTH, N_DIRS * GLA_RANK)
    return [int(v) for v in np.cumsum(sizes)[:-1]]


def rms_norm(x, gain):
    xf = x.astype(jnp.float32)
    y = xf * lax.rsqrt(jnp.mean(xf * xf, axis=-1, keepdims=True) + EPS)
    return (y * gain.astype(jnp.float32)).astype(x.dtype)


def modulation(cond, w_mod, b_mod):
    m = jax.nn.silu(cond) @ w_mod + b_mod
    return jnp.split(m, 3, axis=-1)


def _flip(t, rev):
    return t[:, ::-1] if rev else t


def to_colmajor(t, rows):
    b, l = t.shape[:2]
    return t.reshape(b, rows, GRID_W, *t.shape[2:]).swapaxes(1, 2).reshape(b, l, *t.shape[2:])


def from_colmajor(t, rows):
    b, l = t.shape[:2]
    return t.reshape(b, GRID_W, rows, *t.shape[2:]).swapaxes(1, 2).reshape(b, l, *t.shape[2:])


def _lin_combine(e1, e2):
    a1, b1 = e1
    a2, b2 = e2
    return a2 * a1, a2 * b1 + b2


def s5_scan(lam_bar, bu, h0):
    a = jnp.broadcast_to(lam_bar, bu.shape)
    a_cum, h = lax.associative_scan(_lin_combine, (a, bu), axis=1)
    if h0 is None:
        return h
    return h + a_cum * h0[:, None]


def s5_discretize(lam_re, lam_im, log_dt, b_cplx):
    lam = lax.complex(lam_re.astype(jnp.float32), lam_im.astype(jnp.float32))
    dt = jnp.exp(log_dt.astype(jnp.float32))[:, None]
    lam_bar = jnp.exp(lam * dt)
    b_bar = ((lam_bar - 1.0) / lam)[..., None] * b_cplx
    return lam_bar, b_bar


def s5_branch(u_c, u_l, lam_re, lam_im, log_dt, b_re, b_im, c_re, c_im, d_skip, w_glu, b_glu, need_ctx):
    dtype = u_l.dtype
    grp = lambda u: u.astype(jnp.float32).reshape(*u.shape[:2], S5_GROUPS, S5_GROUP)
    uc, ul = grp(u_c), grp(u_l)
    b_cplx = lax.complex(b_re.astype(jnp.float32), b_im.astype(jnp.float32))
    h_c = 0.0
    h_l = 0.0
    for d in range(N_DIRS):
        rev = d == 1
        lam_bar, b_bar = s5_discretize(lam_re[d], lam_im[d], log_dt[d], b_cplx)
        hc = s5_scan(lam_bar, jnp.einsum('gnp,blgp->blgn', b_bar, _flip(uc, rev)), None)
        hl = s5_scan(lam_bar, jnp.einsum('gnp,blgp->blgn', b_bar, _flip(ul, rev)), hc[:, -1])
        h_c = h_c + _flip(hc, rev)
        h_l = h_l + _flip(hl, rev)
    c_cplx = lax.complex(c_re.astype(jnp.float32), c_im.astype(jnp.float32))
    d_g = d_skip.astype(jnp.float32).reshape(S5_GROUPS, S5_GROUP)

    def readout(h, u):
        y = jnp.einsum('gpn,blgn->blgp', c_cplx, h).real + d_g * u
        y = jax.nn.gelu(y.reshape(*u.shape[:2], S5_WIDTH)).astype(dtype)
        return y * jax.nn.sigmoid(y @ w_glu + b_glu)

    y_c = readout(h_c, uc) if need_ctx else None
    return y_c, readout(h_l, ul)


def gla_chunked(q, k, v, g, s0):
    bsz, l, h, dk = q.shape
    dv = v.shape[-1]
    n = l // GLA_CHUNK
    chunks = lambda t: t.astype(jnp.float32).reshape(bsz, n, GLA_CHUNK, h, t.shape[-1])
    q, k, v, g = chunks(q), chunks(k), chunks(v), chunks(g)
    gc = jnp.cumsum(g, axis=2)
    g_last = gc[:, :, -1]
    q_t = q * jnp.exp(gc)
    k_t = k * jnp.exp(-gc)
    scores = jnp.einsum('bnihd,bnjhd->bnhij', q_t, k_t)
    mask = jnp.tril(jnp.ones((GLA_CHUNK, GLA_CHUNK), dtype=bool))
    scores = jnp.where(mask, scores, 0.0)
    o = jnp.einsum('bnhij,bnjhv->bnihv', scores, v)
    ds = jnp.einsum('bnjhd,bnjhv->bnhdv', k * jnp.exp(g_last[:, :, None] - gc), v)
    decay = jnp.exp(g_last)
    if s0 is None:
        s0 = jnp.zeros((bsz, h, dk, dv), jnp.float32)

    def step(s, inp):
        dec, d_s = inp
        return dec[..., None] * s + d_s, s

    s_final, s_in = lax.scan(step, s0, (jnp.moveaxis(decay, 1, 0), jnp.moveaxis(ds, 1, 0)))
    s_in = jnp.moveaxis(s_in, 0, 1)
    o = o + jnp.einsum('bnihd,bnhdv->bnihv', q_t, s_in)
    return o.reshape(bsz, l, h, dv), s_final


def gla_log_decay(lr, w_gate, b_gate, d):
    z = lr[..., d * GLA_RANK:(d + 1) * GLA_RANK] @ w_gate[d] + b_gate[d]
    g = jax.nn.log_sigmoid(z.astype(jnp.float32)) / GLA_NORMALIZER
    return g.reshape(*g.shape[:2], GLA_HEADS, GLA_DK)


def gla_branch(q_c, k_c, v_c, lr_c, q_l, k_l, v_l, lr_l, w_gate, b_gate, norm_g, need_ctx):
    dtype = q_l.dtype
    heads = lambda t, dd: t.reshape(*t.shape[:2], GLA_HEADS, dd)
    q_c, q_l = heads(q_c, GLA_DK) * GLA_DK ** -0.5, heads(q_l, GLA_DK) * GLA_DK ** -0.5
    k_c, k_l = heads(k_c, GLA_DK), heads(k_l, GLA_DK)
    v_c, v_l = heads(v_c, GLA_DV), heads(v_l, GLA_DV)
    o_c = 0.0
    o_l = 0.0
    for d in range(N_DIRS):
        rev = d == 1
        g_c = gla_log_decay(lr_c, w_gate, b_gate, d)
        g_l = gla_log_decay(lr_l, w_gate, b_gate, d)
        oc, s_c = gla_chunked(_flip(q_c, rev), _flip(k_c, rev), _flip(v_c, rev), _flip(g_c, rev), None)
        ol, _ = gla_chunked(_flip(q_l, rev), _flip(k_l, rev), _flip(v_l, rev), _flip(g_l, rev), s_c)
        o_c = o_c + _flip(oc, rev)
        o_l = o_l + _flip(ol, rev)

    def finish(o):
        return rms_norm(o, norm_g).reshape(*o.shape[:2], GLA_WIDTH).astype(dtype)

    y_c = finish(o_c) if need_ctx else None
    return y_c, finish(o_l)


def hybrid_layer(x_lat, x_ctx, c, c_ctx, norm_g, w_mod, b_mod, w_in, lam_re, lam_im, log_dt, b_re, b_im,
                 c_re, c_im, d_skip, w_glu, b_glu, gla_w_gate, gla_b_gate, gla_norm_g, w_out, need_ctx_out):
    rows = x_lat.shape[1] // GRID_W
    sh_l, sc_l, gt_l = modulation(c, w_mod, b_mod)
    sh_c, sc_c, gt_c = modulation(c_ctx, w_mod, b_mod)
    h_l = rms_norm(x_lat, norm_g) * (1.0 + sc_l[:, None]) + sh_l[:, None]
    h_c = rms_norm(x_ctx, norm_g) * (1.0 + sc_c) + sh_c
    sp = _split_points()
    u_l, zs_l, q_l, k_l, v_l, zg_l, lr_l = jnp.split(h_l @ w_in, sp, axis=-1)
    u_c, zs_c, q_c, k_c, v_c, zg_c, lr_c = jnp.split(h_c @ w_in, sp, axis=-1)

    ys_c, ys_l = s5_branch(u_c, u_l, lam_re, lam_im, log_dt, b_re, b_im, c_re, c_im, d_skip, w_glu, b_glu,
                           need_ctx_out)
    cm = lambda t: to_colmajor(t, rows)
    yg_c, yg_l = gla_branch(q_c, k_c, v_c, lr_c, cm(q_l), cm(k_l), cm(v_l), cm(lr_l),
                            gla_w_gate, gla_b_gate, gla_norm_g, need_ctx_out)
    yg_l = from_colmajor(yg_l, rows)

    y_l = jnp.concatenate([ys_l * jax.nn.silu(zs_l), yg_l * jax.nn.silu(zg_l)], axis=-1) @ w_out
    x_lat = x_lat + gt_l[:, None] * y_l
    if need_ctx_out:
        y_c = jnp.concatenate([ys_c * jax.nn.silu(zs_c), yg_c * jax.nn.silu(zg_c)], axis=-1) @ w_out
        x_ctx = x_ctx + gt_c * y_c
    return x_lat, x_ctx


def setup_inputs(seed: int = 0) -> dict:
    key = jax.random.key(seed)
    ks = jax.random.split(key, 24)
    f32 = jnp.float32
    nrm = lambda k, shape, s: s * jax.random.normal(k, shape, f32)
    return {
        'x': nrm(ks[0], (BATCH, SEQ, D_MODEL), 1.0),
        'c': nrm(ks[1], (BATCH, D_MODEL), 1.0),
        'ctx': nrm(ks[2], (BATCH, CTX_LEN, D_MODEL), 1.0),
        'c_ctx': nrm(ks[3], (D_MODEL,), 1.0),
        'norm_g': 1.0 + nrm(ks[4], (DEPTH, D_MODEL), 0.02),
        'w_mod': nrm(ks[5], (DEPTH, D_MODEL, 3 * D_MODEL), 0.5 * D_MODEL ** -0.5),
        'b_mod': nrm(ks[6], (DEPTH, 3 * D_MODEL), 0.02),
        'w_in': nrm(ks[7], (DEPTH, D_MODEL, IN_WIDTH), D_MODEL ** -0.5),
        's5_lam_re': jnp.full((DEPTH, N_DIRS, S5_GROUPS, S5_STATE), -0.5, f32),
        's5_lam_im': jnp.broadcast_to(math.pi * jnp.arange(S5_STATE, dtype=f32),
                                      (DEPTH, N_DIRS, S5_GROUPS, S5_STATE)),
        's5_log_dt': jax.random.uniform(ks[8], (DEPTH, N_DIRS, S5_GROUPS), f32,
                                        math.log(DT_MIN), math.log(DT_MAX)),
        's5_b_re': nrm(ks[9], (DEPTH, S5_GROUPS, S5_STATE, S5_GROUP), (2 * S5_GROUP) ** -0.5),
        's5_b_im': nrm(ks[10], (DEPTH, S5_GROUPS, S5_STATE, S5_GROUP), (2 * S5_GROUP) ** -0.5),
        's5_c_re': nrm(ks[11], (DEPTH, S5_GROUPS, S5_GROUP, S5_STATE), S5_STATE ** -0.5),
        's5_c_im': nrm(ks[12], (DEPTH, S5_GROUPS, S5_GROUP, S5_STATE), S5_STATE ** -0.5),
        's5_d': nrm(ks[13], (DEPTH, S5_WIDTH), 1.0),
        's5_w_glu': nrm(ks[14], (DEPTH, S5_WIDTH, S5_WIDTH), S5_WIDTH ** -0.5),
        's5_b_glu': nrm(ks[15], (DEPTH, S5_WIDTH), 0.02),
        'gla_w_gate': nrm(ks[16], (DEPTH, N_DIRS, GLA_RANK, GLA_KEY), GLA_RANK ** -0.5),
        'gla_b_gate': nrm(ks[17], (DEPTH, N_DIRS, GLA_KEY), 0.1),
        'gla_norm_g': 1.0 + nrm(ks[18], (DEPTH, GLA_DV), 0.02),
        'w_out': nrm(ks[19], (DEPTH, MIX_WIDTH, D_MODEL), MIX_WIDTH ** -0.5),
        'final_norm': 1.0 + nrm(ks[20], (D_MODEL,), 0.02),
    }


def reference(x, c, ctx, c_ctx, norm_g, w_mod, b_mod, w_in, s5_lam_re, s5_lam_im, s5_log_dt, s5_b_re, s5_b_im,
              s5_c_re, s5_c_im, s5_d, s5_w_glu, s5_b_glu, gla_w_gate, gla_b_gate, gla_norm_g, w_out, final_norm):
    x_lat, x_ctx = x, ctx
    for i in range(DEPTH):
        x_lat, x_ctx = hybrid_layer(
            x_lat, x_ctx, c, c_ctx, norm_g[i], w_mod[i], b_mod[i], w_in[i],
            s5_lam_re[i], s5_lam_im[i], s5_log_dt[i], s5_b_re[i], s5_b_im[i], s5_c_re[i], s5_c_im[i],
            s5_d[i], s5_w_glu[i], s5_b_glu[i], gla_w_gate[i], gla_b_gate[i], gla_norm_g[i], w_out[i],
            need_ctx_out=(i < DEPTH - 1))
    return rms_norm(x_lat, final_norm)
```

```python
from contextlib import ExitStack
import numpy as np
import concourse.bass as bass
import concourse.mybir as mybir
from concourse.bass_utils import run_bass_kernel_spmd

F32 = mybir.dt.float32
BF16 = mybir.dt.bfloat16
I32 = mybir.dt.int32
AF = mybir.ActivationFunctionType
ALU = mybir.AluOpType
AX = mybir.AxisListType

D = 1024
L = 8192
CT = 256
NT = L + CT
NTILE = NT // 128
INW = 2592
DEPTH = 4
EPS = 1e-6


class Res:
    __slots__ = ("name", "w", "r", "dsem", "dcount")

    def __init__(self, name):
        self.name = name
        self.w = None
        self.r = []
        self.dsem = None
        self.dcount = 0


class Eng:
    def __init__(self, name, sem):
        self.name = name
        self.sem = sem
        self.count = 0
        self.waited = {}
        self.prog = []


class FW:
    SEM_LIMIT = 24000

    def __init__(self, nc, stack):
        self.nc = nc
        self.stack = stack
        self.engs = {}
        self.nsem = 0
        for n in ("tensor", "vector", "scalar", "gpsimd", "sync"):
            self.engs[n] = Eng(n, self.new_sem("prog_" + n))
        self.selfwait = {"vector": True, "scalar": True, "gpsimd": True, "tensor": False, "sync": False}

    def new_sem(self, name):
        self.nsem += 1
        return self.stack.enter_context(self.nc.semaphore("%s_%d" % (name, self.nsem)))

    def sb(self, name, shape, dt):
        return self.stack.enter_context(self.nc.sbuf_tensor(name, list(shape), dt))

    def ps(self, name, shape, dt):
        return self.stack.enter_context(self.nc.psum_tensor(name, list(shape), dt))

    @staticmethod
    def _deps(reads, writes):
        deps = []
        for r in reads:
            if r.w is not None:
                deps.append(r.w)
        for w in writes:
            if w.w is not None:
                deps.append(w.w)
            deps.extend(w.r)
        return deps

    def _emit_waits(self, eng, deps):
        best = {}
        for (sem, val, en) in deps:
            if en == eng.name and not self.selfwait[eng.name]:
                continue
            k = id(sem)
            if eng.waited.get(k, 0) >= val:
                continue
            if k not in best or best[k][1] < val:
                best[k] = (sem, val)
        for k, (sem, val) in best.items():
            eng.waited[k] = val
            eng.prog.append(lambda e, sem=sem, val=val: e.wait_ge(sem, val))

    def op(self, engname, fn, reads=(), writes=()):
        eng = self.engs[engname]
        if eng.count >= self.SEM_LIMIT:
            eng.sem = self.new_sem("prog_" + engname)
            eng.count = 0
        self._emit_waits(eng, self._deps(reads, writes))
        eng.count += 1
        sem = eng.sem
        tok = (sem, eng.count, engname)
        eng.prog.append(lambda e, fn=fn, sem=sem: fn(e).then_inc(sem, 1))
        for r in reads:
            r.r.append(tok)
        for w in writes:
            w.w = tok
            w.r = []
        return tok

    def dma(self, qname, pairs, reads, writes, sres=None):
        eng = self.engs[qname]
        sres = sres or writes[0]
        if sres.dsem is None or sres.dcount >= 16 * 3000:
            sres.dsem = self.new_sem("d_" + sres.name)
            sres.dcount = 0
        self._emit_waits(eng, self._deps(reads, writes))
        for (o, i) in pairs:
            sres.dcount += 16
            eng.prog.append(lambda e, o=o, i=i, s=sres.dsem: e.dma_start(out=o, in_=i).then_inc(s, 16))
        tok = (sres.dsem, sres.dcount, "dma")
        for r in reads:
            r.r.append(tok)
        for w in writes:
            w.w = tok
            w.r = []
        return tok

    def barrier(self, all_res):
        toks = []
        for n, e in self.engs.items():
            if e.count > 0:
                toks.append((e.sem, e.count, n))
        for r in all_res:
            if r.dsem is not None and r.dcount > 0:
                toks.append((r.dsem, r.dcount, "dma"))
        for n, e in self.engs.items():
            self._emit_waits(e, [t for t in toks if t[2] != n])

    def finish(self, final_res):
        eng = self.engs["sync"]
        deps = []
        for r in final_res:
            if r.w is not None:
                deps.append(r.w)
        self._emit_waits(eng, deps)
        nc = self.nc
        engs = self.engs
        with nc.allow_non_contiguous_dma(reason="small param layouts"), nc.Block() as block:
            @block.tensor
            def _(e):
                for f in engs["tensor"].prog:
                    f(e)

            @block.vector
            def _(e):
                for f in engs["vector"].prog:
                    f(e)

            @block.scalar
            def _(e):
                for f in engs["scalar"].prog:
                    f(e)

            @block.gpsimd
            def _(e):
                for f in engs["gpsimd"].prog:
                    f(e)

            @block.sync
            def _(e):
                for f in engs["sync"].prog:
                    f(e)


class Prog:
    def __init__(self, depth=DEPTH, stub_s5=False, stub_gla=False):
        self.depth = depth
        self.stub_s5 = stub_s5
        self.stub_gla = stub_gla
        nc = self.nc = bass.Bass("TRN2", target_bir_lowering=False)
        di = lambda n, s, dt=F32: nc.dram_tensor(n, list(s), dt, kind="ExternalInput").ap()
        ds = lambda n, s, dt=F32: nc.dram_tensor(n, list(s), dt, kind="Internal").ap()
        self.x = di("x", [L, D])
        self.ctx = di("ctx", [CT, D])
        self.cc = di("cc", [2, D])
        self.norm_g = di("norm_g", [DEPTH, D])
        self.w_mod = di("w_mod", [DEPTH, D, 3 * D])
        self.b_mod = di("b_mod", [DEPTH, 3 * D])
        self.w_in = di("w_in", [DEPTH, D, INW])
        self.lam_re = di("s5_lam_re", [DEPTH, 2, 32, 64])
        self.lam_im = di("s5_lam_im", [DEPTH, 2, 32, 64])
        self.log_dt = di("s5_log_dt", [DEPTH, 2, 32])
        self.b_re = di("s5_b_re", [DEPTH, 32, 64, 16])
        self.b_im = di("s5_b_im", [DEPTH, 32, 64, 16])
        self.c_re = di("s5_c_re", [DEPTH, 32, 16, 64])
        self.c_im = di("s5_c_im", [DEPTH, 32, 16, 64])
        self.s5_d = di("s5_d", [DEPTH, 512])
        self.w_glu = di("s5_w_glu", [DEPTH, 512, 512])
        self.b_glu = di("s5_b_glu", [DEPTH, 512])
        self.w_gate = di("gla_w_gate", [DEPTH, 2, 16, 256])
        self.b_gate = di("gla_b_gate", [DEPTH, 2, 256])
        self.gnorm = di("gla_norm_g", [DEPTH, 128])
        self.w_out = di("w_out", [DEPTH, D, D])
        self.final_norm = di("final_norm", [1, D])
        self.out = nc.dram_tensor("out", [L, D], F32, kind="ExternalOutput").ap()
        self.xs = [ds("xs0", [NT, D]), ds("xs1", [NT, D])]
        self.P = ds("P", [NT, INW], BF16)
        import os
        if os.environ.get("DEBUG_GY"):
            self.gy = nc.dram_tensor("gy", [NT, 512], BF16, kind="ExternalOutput").ap()
        else:
            self.gy = ds("gy", [NT, 512], BF16)
        self.yg = ds("yg", [NT, 512], BF16)
        self.R = {}
        with ExitStack() as st:
            self.fw = FW(nc, st)
            self.alloc()
            self.consts()
            for l in range(depth):
                self.weights_in(l)
                self.modulation(l)
                self.phase1(l)
                if stub_s5:
                    self.s5_stub(l)
                else:
                    self.fw.barrier(self.R.values())
                    self.s5(l)
                    self.fw.barrier(self.R.values())
                if stub_gla:
                    self.gla_stub(l)
                else:
                    self.fw.barrier(self.R.values())
                    self.gla(l)
                    self.fw.barrier(self.R.values())
                self.weights_out(l)
                self.phase3(l)
            self.fw.finish([self.res("out")])

    def res(self, name):
        if name not in self.R:
            self.R[name] = Res(name)
        return self.R[name]

    def view(self, shape, dt):
        n = 1
        for d_ in shape[1:]:
            n *= d_
        words = n if dt in (F32, I32) else (n + 1) // 2
        ap = self.arena[:, self.aoff:self.aoff + words]
        self.aoff += words
        assert self.aoff <= self.NW, (self.aoff, self.NW)
        if dt != F32:
            ap = ap.bitcast(dt)
        if len(shape) == 3:
            ap = ap.rearrange("p (a b) -> p a b", a=shape[1])
        elif len(shape) == 4:
            ap = ap.rearrange("p (a b c) -> p a b c", a=shape[1], b=shape[2])
        return ap

    def alloc(self):
        fw = self.fw
        self.NW = 52600
        self.arena = fw.sb("arena", [128, self.NW], F32)
        self.aoff = 0
        self.identF = fw.sb("identF", [128, 128], F32)
        self.identB = fw.sb("identB", [128, 128], BF16)
        self.ccT = fw.sb("ccT", [128, 8, 2], F32)
        self.st1 = [fw.sb("st1_%d" % i, [128, 4], F32) for i in range(2)]
        self.pb = [fw.ps("pb%d" % i, [128, 512], F32) for i in range(8)]
        v = self.view
        self.scB = v([128, 8, 2, 128], F32)
        self.modb = v([128, 2, 3 * D], F32)
        self.bglub = v([128, 512], F32)
        self.fnb = v([128, D], F32)
        self.phase_base = self.aoff
        self.winb = v([128, 8, INW], BF16)
        self.woutb = v([128, 8, D], BF16)
        self.wglub = v([128, 4, 512], BF16)
        self.wst = v([128, 6144], F32)
        self.xt = [v([128, D], F32) for i in range(2)]
        self.xn = [v([128, D], F32) for i in range(2)]
        self.yo = [v([128, D], F32) for i in range(2)]
        self.hb = [v([128, D], BF16) for i in range(2)]
        self.mix = self.hb
        self.hT = [v([128, 8, 128], BF16) for i in range(2)]
        self.mixT = self.hT
        self.pj = [v([128, INW], BF16) for i in range(2)]
        self.g3 = [v([128, 2048], BF16) for i in range(2)]
        self.gyT = [v([128, 4, 128], BF16) for i in range(2)]
        self.t3 = [v([128, 512], F32) for i in range(2)]
        self.dense_end = self.aoff
        print("arena dense end", self.dense_end, "phase_base", self.phase_base)

    def consts(self):
        fw = self.fw
        identF, identB = self.identF, self.identB
        rI = self.res("ident")
        fw.op("gpsimd", lambda e: e.memset(identF[:], 0.0), [], [rI])
        fw.op("gpsimd", lambda e: e.affine_select(out=identF[:], in_=identF[:], compare_op=ALU.not_equal, fill=1.0,
                                                 base=0, pattern=[[-1, 128]], channel_multiplier=1), [rI], [rI])
        fw.op("gpsimd", lambda e: e.tensor_copy(out=identB[:], in_=identF[:]), [rI], [rI])
        rc = self.res("cc")
        ccT, scB = self.ccT, self.scB
        fw.dma("sync", [(ccT[:, :, j], self.cc[j, :].rearrange("(k p) -> p k", p=128)) for j in range(2)], [], [rc])
        fw.op("scalar", lambda e: e.activation(out=ccT[:], in_=ccT[:], func=AF.Silu), [rc], [rc])
        for k in range(8):
            for j in range(2):
                fw.op("vector", lambda e, k=k, j=j: e.tensor_copy(out=scB[:, k, j, :],
                                                                   in_=ccT[:, k, j:j + 1].to_broadcast([128, 128])),
                      [rc], [self.res("scB")])
        fw.dma("sync", [(self.fnb, self.final_norm[0:1, :].partition_broadcast(128)[:, 0, :])], [], [self.res("fnb")])

    def _wload(self, src_rows, ncols, dst, rdst, n):
        fw = self.fw
        s_ = n % 2
        rs = self.res("wst_s%d" % s_)
        stg = self.wst[:, s_ * 3072:s_ * 3072 + ncols]
        fw.dma("sync" if n % 2 == 0 else "gpsimd", [(stg, src_rows)], [], [rs])
        fw.op("gpsimd" if n % 2 == 0 else "vector", lambda e: e.tensor_copy(out=dst, in_=stg), [rs], [rdst])

    def weights_in(self, l):
        for k in range(8):
            self._wload(self.w_in[l, k * 128:(k + 1) * 128, :], INW, self.winb[:, k, :], self.res("winb"), k)

    def weights_out(self, l):
        fw = self.fw
        for k in range(8):
            self._wload(self.w_out[l, k * 128:(k + 1) * 128, :], D, self.woutb[:, k, :], self.res("woutb"), k)
        for k in range(4):
            self._wload(self.w_glu[l, k * 128:(k + 1) * 128, :], 512, self.wglub[:, k, :], self.res("wglub"), k)
        fw.dma("sync", [(self.bglub, self.b_glu[l:l + 1, :].partition_broadcast(128)[:, 0, :])], [], [self.res("bglub")])

    def modulation(self, l):
        fw = self.fw
        modb = self.modb
        rs0, rs1 = self.res("wst_s0"), self.res("wst_s1")
        rmod = self.res("modb")
        wv = self.wst.rearrange("p (k n) -> p k n", k=8)
        bt = self.t3[0]
        rbt = self.res("t3_0")
        for q in range(4):
            c0 = q * 768
            fw.dma("sync", [(wv[:, k, :], self.w_mod[l, k * 128:(k + 1) * 128, c0:c0 + 768]) for k in range(4)],
                   [], [rs0, rs1])
            fw.dma("gpsimd", [(wv[:, k, :], self.w_mod[l, k * 128:(k + 1) * 128, c0:c0 + 768]) for k in range(4, 8)],
                   [], [rs0, rs1], sres=self.res("wst_g"))
            for n in range(2):
                col = c0 + n * 384
                fw.dma("sync", [(bt[:, 0:384], self.b_mod[l:l + 1, col:col + 384].partition_broadcast(128)[:, 0, :])], [], [rbt])
                for j in range(2):
                    bi = (n * 2 + j) % 4
                    pb = self.pb[bi]
                    rp = self.res("pb%d" % bi)
                    for k in range(8):
                        fw.op("tensor", lambda e, k=k, j=j, n=n, pb=pb: e.matmul(
                            pb[:, 0:384], lhsT=self.scB[:, k, j, :], rhs=wv[:, k, n * 384:(n + 1) * 384],
                            start=(k == 0), stop=(k == 7)), [rs0, rs1, self.res("scB")], [rp])
                    fw.op("vector", lambda e, j=j, col=col, pb=pb: e.tensor_tensor(
                        out=modb[:, j, col:col + 384], in0=pb[:, 0:384], in1=bt[:, 0:384], op=ALU.add),
                        [rp, rbt], [rmod])
        ngb = self.xn[0]
        rng = self.res("xn0")
        fw.dma("sync", [(ngb, self.norm_g[l:l + 1, :].partition_broadcast(128)[:, 0, :])], [], [rng])
        for j in range(2):
            fw.op("vector", lambda e, j=j: e.scalar_tensor_tensor(
                out=modb[:, j, D:2 * D], in0=modb[:, j, D:2 * D], scalar=1.0, in1=ngb,
                op0=ALU.add, op1=ALU.mult), [rmod, rng], [rmod])

    def xsrc(self, l, i):
        if l == 0:
            if i < 2:
                return self.ctx[i * 128:(i + 1) * 128, :], None
            return self.x[(i - 2) * 128:(i - 1) * 128, :], None
        return self.xs[l % 2][i * 128:(i + 1) * 128, :], self.res("xs%d" % (l % 2))

    def phase1(self, l):
        fw = self.fw
        rmod = self.res("modb")
        rP = self.res("P")
        for i in range(NTILE):
            s = i % 2
            j = 1 if i < 2 else 0
            xt, xn, hb, hT, pj, st1 = self.xt[s], self.xn[s], self.hb[s], self.hT[s], self.pj[s], self.st1[s]
            rxt, rxn, rhb, rhT, rpj, rst = (self.res("%s%d" % (n, s)) for n in ("xt", "xn", "hb", "hT", "pj", "st1"))
            src, rsrc = self.xsrc(l, i)
            fw.dma("sync", [(xt[:], src)], [rsrc] if rsrc else [], [rxt])
            fw.op("scalar", lambda e, xt=xt, xn=xn, st1=st1: e.activation(out=xn[:], in_=xt[:], func=AF.Square,
                                                                        accum_out=st1[:, 0:1]), [rxt], [rxn, rst])
            fw.op("vector", lambda e, st1=st1: e.tensor_scalar(out=st1[:, 1:2], in0=st1[:, 0:1], scalar1=1.0 / D, scalar2=EPS,
                                                              op0=ALU.mult, op1=ALU.add), [rst], [rst])
            fw.op("scalar", lambda e, st1=st1: e.activation(out=st1[:, 2:3], in_=st1[:, 1:2], func=AF.Sqrt), [rst], [rst])
            fw.op("vector", lambda e, st1=st1: e.reciprocal(out=st1[:, 3:4], in_=st1[:, 2:3]), [rst], [rst])
            fw.op("vector", lambda e, xt=xt, xn=xn, st1=st1, j=j: e.scalar_tensor_tensor(
                out=xn[:], in0=xt[:], scalar=st1[:, 3:4], in1=self.modb[:, j, D:2 * D], op0=ALU.mult, op1=ALU.mult),
                [rxt, rst, rmod], [rxn])
            fw.op("gpsimd", lambda e, xn=xn, hb=hb, j=j: e.tensor_tensor(out=hb[:], in0=xn[:], in1=self.modb[:, j, 0:D], op=ALU.add),
                  [rxn, rmod], [rhb])
            ptb = self.pb[0][:].bitcast(BF16)
            rp0 = self.res("pb0")
            for k in range(8):
                fw.op("tensor", lambda e, k=k, hb=hb, ptb=ptb: e.transpose(ptb[:, k * 128:(k + 1) * 128], hb[:, k * 128:(k + 1) * 128],
                                                                         self.identB[:]), [rhb, self.res("ident")], [rp0])
            fw.op("scalar", lambda e, hT=hT, ptb=ptb: e.activation(out=hT[:].rearrange("p k t -> p (k t)"), in_=ptb, func=AF.Copy),
                  [rp0], [rhT])
            chunks = [(0, 512, "copy"), (512, 512, "silu"), (1024, 256, "q"), (1280, 256, "copy"), (1536, 512, "copy"),
                      (2048, 512, "silu"), (2560, 32, "copy")]
            groups = [(0, 512), (512, 512), (1024, 512), (1536, 512), (2048, 512), (2560, 32)]
            for gi, (c0, w) in enumerate(groups):
                b = 1 + (gi % 4)
                pb = self.pb[b]
                rp = self.res("pb%d" % b)
                for k in range(8):
                    fw.op("tensor", lambda e, k=k, c0=c0, w=w, pb=pb, hT=hT: e.matmul(
                        pb[:, 0:w], lhsT=hT[:, k, :], rhs=self.winb[:, k, c0:c0 + w], start=(k == 0), stop=(k == 7)),
                        [rhT, self.res("winb")], [rp])
                for (a0, aw, kind) in chunks:
                    if a0 < c0 or a0 >= c0 + w:
                        continue
                    o = pj[:, a0:a0 + aw]
                    src_ = pb[:, a0 - c0:a0 - c0 + aw]
                    if kind == "silu":
                        fw.op("scalar", lambda e, o=o, src_=src_: e.activation(out=o, in_=src_, func=AF.Silu), [rp], [rpj])
                    elif kind == "q":
                        fw.op("vector", lambda e, o=o, src_=src_: e.tensor_scalar(out=o, in0=src_, scalar1=0.125, scalar2=None,
                                                                                op0=ALU.mult), [rp], [rpj])
                    else:
                        fw.op("vector", lambda e, o=o, src_=src_: e.tensor_copy(out=o, in_=src_), [rp], [rpj])
            fw.dma("gpsimd", [(self.P[i * 128:(i + 1) * 128, :], pj[:])], [rpj], [rP])

    def gelu(self, eng_a, out, in_, tmp, reads, writes, rtmp):
        fw = self.fw
        fw.op("scalar", lambda e: e.activation(out=tmp, in_=in_, func=AF.Square), reads, [rtmp])
        fw.op(eng_a, lambda e: e.tensor_scalar(out=tmp, in0=tmp, scalar1=0.044715, scalar2=1.0, op0=ALU.mult, op1=ALU.add),
              [rtmp], [rtmp])
        fw.op(eng_a, lambda e: e.tensor_tensor(out=tmp, in0=tmp, in1=in_, op=ALU.mult), [rtmp] + list(reads), [rtmp])
        fw.op("scalar", lambda e: e.activation(out=tmp, in_=tmp, func=AF.Sigmoid, scale=1.5957691216), [rtmp], [rtmp])
        fw.op(eng_a, lambda e: e.tensor_tensor(out=out, in0=tmp, in1=in_, op=ALU.mult), [rtmp] + list(reads), writes)

    def s5_stub(self, l):
        fw = self.fw
        for i in range(NTILE):
            s = i % 2
            t = self.g3[s]
            rt = self.res("g3_%d" % s)
            fw.dma("sync", [(t[:, 0:512], self.P[i * 128:(i + 1) * 128, 0:512])], [self.res("P")], [rt])
            self.gelu("vector", t[:, 512:1024], t[:, 0:512], self.t3[s][:], [rt], [rt], self.res("t3_%d" % s))
            fw.dma("sync", [(self.gy[i * 128:(i + 1) * 128, :], t[:, 512:1024])], [rt], [self.res("gy")])

    def gla_stub(self, l):
        fw = self.fw
        for i in range(NTILE):
            s = i % 2
            t = self.g3[s]
            rt = self.res("g3_%d" % s)
            fw.dma("sync", [(t[:, 0:512], self.P[i * 128:(i + 1) * 128, 1536:2048])], [self.res("P")], [rt])
            fw.dma("sync", [(self.yg[i * 128:(i + 1) * 128, :], t[:, 0:512])], [rt], [self.res("yg")])

    def phase3(self, l):
        fw = self.fw
        last = (l == self.depth - 1)
        rmod = self.res("modb")
        rout = self.res("out")
        rxd = self.res("xs%d" % ((l + 1) % 2))
        for i in range(NTILE):
            if last and i < 2:
                continue
            s = i % 2
            j = 1 if i < 2 else 0
            g3, gyT, t3, mix, mixT, yo, xt, st1 = (self.g3[s], self.gyT[s], self.t3[s], self.mix[s], self.mixT[s], self.yo[s],
                                                   self.xt[s], self.st1[s])
            rg3, rgyT, rt3, rmix, rmixT, ryo, rxt, rst = (self.res("%s%d" % (n, s)) for n in
                                                          ("g3_", "gyT", "t3_", "mix", "mixT", "yo", "xt", "st1"))
            rows = slice(i * 128, (i + 1) * 128)
            fw.dma("sync", [(g3[:, 0:512], self.gy[rows, :])], [self.res("gy")], [rg3])
            fw.dma("sync", [(g3[:, 512:1024], self.yg[rows, :])], [self.res("yg")], [rg3])
            fw.dma("sync", [(g3[:, 1024:1536], self.P[rows, 512:1024]), (g3[:, 1536:2048], self.P[rows, 2048:2560])],
                   [self.res("P")], [rg3])
            src, rsrc = self.xsrc(l, i)
            fw.dma("gpsimd", [(xt[:], src)], [rsrc] if rsrc else [], [rxt])
            ptb = self.pb[5][:].bitcast(BF16)
            rp5 = self.res("pb5")
            for k in range(4):
                fw.op("tensor", lambda e, k=k, g3=g3, ptb=ptb: e.transpose(ptb[:, k * 128:(k + 1) * 128], g3[:, k * 128:(k + 1) * 128],
                                                                         self.identB[:]), [rg3, self.res("ident")], [rp5])
            fw.op("scalar", lambda e, gyT=gyT, ptb=ptb: e.activation(out=gyT[:].rearrange("p k t -> p (k t)"), in_=ptb[:, 0:512],
                                                                   func=AF.Copy), [rp5], [rgyT])
            pg = self.pb[6]
            rp6 = self.res("pb6")
            for k in range(4):
                fw.op("tensor", lambda e, k=k, gyT=gyT, pg=pg: e.matmul(pg[:], lhsT=gyT[:, k, :], rhs=self.wglub[:, k, :],
                                                                      start=(k == 0), stop=(k == 3)), [rgyT, self.res("wglub")], [rp6])
            fw.op("vector", lambda e, t3=t3, pg=pg: e.tensor_tensor(out=t3[:], in0=pg[:], in1=self.bglub, op=ALU.add),
                  [rp6, self.res("bglub")], [rt3])
            fw.op("scalar", lambda e, t3=t3: e.activation(out=t3[:], in_=t3[:], func=AF.Sigmoid), [rt3], [rt3])
            fw.op("vector", lambda e, t3=t3, g3=g3: e.tensor_tensor(out=t3[:], in0=t3[:], in1=g3[:, 0:512], op=ALU.mult),
                  [rt3, rg3], [rt3])
            fw.op("vector", lambda e, t3=t3, g3=g3, mix=mix: e.tensor_tensor(out=mix[:, 0:512], in0=t3[:], in1=g3[:, 1024:1536],
                                                                            op=ALU.mult), [rt3, rg3], [rmix])
            fw.op("gpsimd", lambda e, g3=g3, mix=mix: e.tensor_tensor(out=mix[:, 512:1024], in0=g3[:, 512:1024], in1=g3[:, 1536:2048],
                                                                     op=ALU.mult), [rg3], [rmix])
            ptm = self.pb[7][:].bitcast(BF16)
            rp7 = self.res("pb7")
            for k in range(8):
                fw.op("tensor", lambda e, k=k, mix=mix, ptm=ptm: e.transpose(ptm[:, k * 128:(k + 1) * 128], mix[:, k * 128:(k + 1) * 128],
                                                                           self.identB[:]), [rmix, self.res("ident")], [rp7])
            fw.op("scalar", lambda e, mixT=mixT, ptm=ptm: e.activation(out=mixT[:].rearrange("p k t -> p (k t)"), in_=ptm, func=AF.Copy),
                  [rp7], [rmixT])
            for n in range(2):
                b = 1 + n
                pb = self.pb[b]
                rp = self.res("pb%d" % b)
                for k in range(8):
                    fw.op("tensor", lambda e, k=k, n=n, pb=pb, mixT=mixT: e.matmul(
                        pb[:], lhsT=mixT[:, k, :], rhs=self.woutb[:, k, n * 512:(n + 1) * 512], start=(k == 0), stop=(k == 7)),
                        [rmixT, self.res("woutb")], [rp])
                cs = slice(n * 512, (n + 1) * 512)
                fw.op("vector", lambda e, pb=pb, yo=yo, cs=cs, j=j, n=n: e.tensor_tensor(
                    out=yo[:, cs], in0=pb[:], in1=self.modb[:, j, 2 * D + n * 512:2 * D + (n + 1) * 512], op=ALU.mult),
                    [rp, rmod], [ryo])
            fw.op("gpsimd", lambda e, yo=yo, xt=xt: e.tensor_tensor(out=yo[:], in0=yo[:], in1=xt[:], op=ALU.add), [ryo, rxt], [ryo])
            if not last:
                fw.dma("sync", [(self.xs[(l + 1) % 2][rows, :], yo[:])], [ryo], [rxd])
            else:
                xn = self.xn[s]
                rxn = self.res("xn%d" % s)
                fw.op("scalar", lambda e, yo=yo, xn=xn, st1=st1: e.activation(out=xn[:], in_=yo[:], func=AF.Square,
                                                                            accum_out=st1[:, 0:1]), [ryo], [rxn, rst])
                fw.op("vector", lambda e, st1=st1: e.tensor_scalar(out=st1[:, 1:2], in0=st1[:, 0:1], scalar1=1.0 / D, scalar2=EPS,
                                                                  op0=ALU.mult, op1=ALU.add), [rst], [rst])
                fw.op("scalar", lambda e, st1=st1: e.activation(out=st1[:, 2:3], in_=st1[:, 1:2], func=AF.Sqrt), [rst], [rst])
                fw.op("vector", lambda e, st1=st1: e.reciprocal(out=st1[:, 3:4], in_=st1[:, 2:3]), [rst], [rst])
                fw.op("vector", lambda e, yo=yo, xn=xn, st1=st1: e.scalar_tensor_tensor(
                    out=xn[:], in0=yo[:], scalar=st1[:, 3:4], in1=self.fnb, op0=ALU.mult, op1=ALU.mult),
                    [ryo, rst, self.res("fnb")], [rxn])
                fw.dma("sync", [(self.out[(i - 2) * 128:(i - 1) * 128, :], xn[:])], [rxn], [rout])

    def s5(self, l):
        raise NotImplementedError

    def gla(self, l):
        raise NotImplementedError


_CACHE = {}


def make_in_maps(inputs):
    maps = []
    for core in range(8):
        b = core % 4
        m = {
            "x": np.ascontiguousarray(inputs["x"][b]),
            "ctx": np.ascontiguousarray(inputs["ctx"][b]),
            "cc": np.ascontiguousarray(np.stack([inputs["c"][b], inputs["c_ctx"]], axis=0)),
            "final_norm": np.ascontiguousarray(inputs["final_norm"][None, :]),
        }
        for k in ("norm_g", "w_mod", "b_mod", "w_in", "s5_lam_re", "s5_lam_im", "s5_log_dt", "s5_b_re", "s5_b_im", "s5_c_re",
                  "s5_c_im", "s5_d", "s5_w_glu", "s5_b_glu", "gla_w_gate", "gla_b_gate", "gla_norm_g", "w_out"):
            m[k] = np.ascontiguousarray(inputs[k])
        maps.append(m)
    return maps


def kernel(**inputs):
    inputs = {k: np.asarray(v) for k, v in inputs.items()}
    if "prog" not in _CACHE:
        _CACHE["prog"] = Prog()
    prog = _CACHE["prog"]
    res = run_bass_kernel_spmd(prog.nc, make_in_maps(inputs), core_ids=list(range(8)))
    return np.stack([np.asarray(res.results[b]["out"]) for b in range(4)], axis=0).astype(np.float32)


def _gla(self, l):
    fw = self.fw
    v = self.view
    self.aoff = self.phase_base
    T = [v([128, 1056], BF16) for _ in range(2)]
    lrT = v([128, 128], BF16)
    wg32 = v([128, 2, 256], F32)
    wgp = v([128, 2, 256], BF16)
    nbg = v([128, 2, 2], F32)
    sp = v([128, 2, 2, 128], F32)
    cs = v([128, 2, 2, 128], F32)
    eq = v([128, 2, 2, 128], F32)
    ek = v([128, 2, 2, 128], F32)
    ekd = v([128, 2, 2, 128], F32)
    tot = v([128, 2, 2, 4], F32)
    qtT = v([128, 2, 2, 128], BF16)
    ktT = v([128, 2, 2, 128], BF16)
    kdT = v([128, 2, 2, 128], BF16)
    kdt = v([128, 2, 2, 128], BF16)
    sT = v([128, 2, 4, 128], BF16)
    S32 = v([128, 2, 2, 128], F32)
    Sbf = v([128, 2, 128], BF16)
    Sst = v([128, NTILE, 2, 128], BF16)
    Mf = v([128, 128], F32)
    Mb = v([128, 128], F32)
    ones = v([128, 128], F32)
    gnb = v([128, 128], F32)
    ygt = [v([128, 512], BF16) for _ in range(2)]
    sq = v([128, 128], F32)
    rs = v([128, 8], F32)
    R = lambda n: self.res("gla_" + n)
    rI = self.res("ident")
    fw.op("gpsimd", lambda e: e.memset(Mf, 1.0), [], [R("Mf")])
    fw.op("gpsimd", lambda e: e.affine_select(out=Mf, in_=Mf, compare_op=ALU.is_ge, fill=0.0, base=0,
                                             pattern=[[1, 128]], channel_multiplier=-1), [R("Mf")], [R("Mf")])
    fw.op("gpsimd", lambda e: e.memset(Mb, 1.0), [], [R("Mb")])
    fw.op("gpsimd", lambda e: e.affine_select(out=Mb, in_=Mb, compare_op=ALU.is_ge, fill=0.0, base=0,
                                             pattern=[[-1, 128]], channel_multiplier=1), [R("Mb")], [R("Mb")])
    fw.op("gpsimd", lambda e: e.memset(ones, 1.0), [], [R("ones")])
    fw.op("vector", lambda e: e.memset(wg32, 0.0), [], [R("wg32")])
    fw.dma("sync", [(wg32[0:16, 0, :], self.w_gate[l, 0, :, :]), (wg32[16:32, 1, :], self.w_gate[l, 1, :, :])], [], [R("wg32")])
    fw.op("vector", lambda e: e.tensor_copy(out=wgp[0:32], in_=wg32[0:32]), [R("wg32")], [R("wgp")])
    fw.dma("sync", [(nbg[:, d, hp:hp + 1], self.b_gate[l, d, hp * 128:(hp + 1) * 128].rearrange("(p o) -> p o", o=1))
                    for d in range(2) for hp in range(2)], [], [R("nbg")])
    fw.op("vector", lambda e: e.tensor_scalar(out=nbg, in0=nbg, scalar1=-1.0, scalar2=None, op0=ALU.mult), [R("nbg")], [R("nbg")])
    fw.dma("sync", [(gnb, self.gnorm[l:l + 1, :].partition_broadcast(128)[:, 0, :])], [], [R("gnb")])
    fw.op("vector", lambda e: e.memset(S32, 0.0), [], [R("S32")])
    fw.op("vector", lambda e: e.memset(Sbf, 0.0), [], [R("Sbf")])

    Plat = self.P[CT:, :].rearrange("(r c) w -> c r w", c=64)
    yglat = self.yg[CT:, :].rearrange("(r c) w -> c r w", c=64)

    def rows(ap, lat, ci, c0, c1):
        if ci < 2:
            return ap[ci * 128:(ci + 1) * 128, c0:c1]
        return lat[ci - 2, :, c0:c1]

    pz, pqk, plr, psc0, psc1, pkd, pdS, po = self.pb
    rpb = [self.res("pb%d" % i) for i in range(8)]

    def load(ci, s):
        fw.dma("sync", [(T[s][:, 0:1024], rows(self.P, Plat, ci, 1024, 2048))], [self.res("P")], [R("T%d" % s)])
        fw.dma("gpsimd", [(T[s][:, 1024:1056], rows(self.P, Plat, ci, 2560, 2592))], [self.res("P")], [R("T%d" % s)],
               sres=R("T%db" % s))

    def gates(s, dirs, need_qk):
        Tt = T[s]
        rT = [R("T%d" % s), R("T%db" % s)]
        plrb = plr[:].bitcast(BF16)
        fw.op("tensor", lambda e: e.transpose(plrb[0:32, 0:128], Tt[:, 1024:1056], self.identB[:]), rT + [rI], [rpb[2]])
        fw.op("scalar", lambda e: e.activation(out=lrT[0:32, :], in_=plrb[0:32, 0:128], func=AF.Copy), [rpb[2]], [R("lrT")])
        pqkb = pqk[:].bitcast(BF16)
        for t4 in range(4):
            fw.op("tensor", lambda e, t4=t4: e.transpose(pqkb[:, t4 * 128:(t4 + 1) * 128], Tt[:, t4 * 128:(t4 + 1) * 128],
                                                        self.identB[:]), rT + [rI], [rpb[1]])
        for d in dirs:
            for hp in range(2):
                fw.op("tensor", lambda e, d=d, hp=hp: e.matmul(pz[:, (d * 2 + hp) * 128:(d * 2 + hp + 1) * 128],
                                                              lhsT=wgp[0:32, d, hp * 128:(hp + 1) * 128], rhs=lrT[0:32, :],
                                                              start=True, stop=True), [R("wgp"), R("lrT")], [rpb[0]])
                fw.op("scalar", lambda e, d=d, hp=hp: e.activation(out=sp[:, d, hp, :], in_=pz[:, (d * 2 + hp) * 128:(d * 2 + hp + 1) * 128],
                                                                  func=AF.Exp, scale=-1.0, bias=nbg[:, d, hp:hp + 1]),
                      [rpb[0], R("nbg")], [R("sp")])
            fw.op("scalar", lambda e, d=d: e.activation(out=sp[:, d], in_=sp[:, d], func=AF.Ln, bias=1.0), [R("sp")], [R("sp")])
            for hp in range(2):
                fw.op("vector", lambda e, d=d, hp=hp: e.tensor_tensor_scan(out=cs[:, d, hp, :], data0=ones, data1=sp[:, d, hp, :],
                                                                          initial=0.0, op0=ALU.mult, op1=ALU.add),
                      [R("sp"), R("ones")], [R("cs")])
                fw.op("vector", lambda e, d=d, hp=hp: e.tensor_copy(out=tot[:, d, hp, 0:1], in_=cs[:, d, hp, 127:128]),
                      [R("cs")], [R("tot")])
                if d == 1:
                    fw.op("vector", lambda e, d=d, hp=hp: e.scalar_tensor_tensor(out=cs[:, d, hp, :], in0=sp[:, d, hp, :],
                                                                                scalar=tot[:, d, hp, 0:1], in1=cs[:, d, hp, :],
                                                                                op0=ALU.add, op1=ALU.subtract),
                          [R("sp"), R("tot"), R("cs")], [R("cs")])
            fw.op("vector", lambda e, d=d: e.tensor_scalar(out=tot[:, d, :, 1:2], in0=tot[:, d, :, 0:1], scalar1=-1.0 / 16, scalar2=None,
                                                          op0=ALU.mult), [R("tot")], [R("tot")])
            fw.op("scalar", lambda e, d=d: e.activation(out=tot[:, d, :, 2:3], in_=tot[:, d, :, 0:1], func=AF.Exp, scale=-1.0 / 16),
                  [R("tot")], [R("tot")])
            for hp in range(2):
                fw.op("scalar", lambda e, d=d, hp=hp: e.activation(out=ekd[:, d, hp, :], in_=cs[:, d, hp, :], func=AF.Exp,
                                                                  scale=1.0 / 16, bias=tot[:, d, hp, 1:2]), [R("cs"), R("tot")], [R("ekd")])
                fw.op("vector", lambda e, d=d, hp=hp: e.tensor_tensor(out=kdT[:, d, hp, :], in0=pqkb[:, (2 + hp) * 128:(3 + hp) * 128],
                                                                     in1=ekd[:, d, hp, :], op=ALU.mult), [rpb[1], R("ekd")], [R("kdT")])
            if need_qk:
                fw.op("scalar", lambda e, d=d: e.activation(out=eq[:, d], in_=cs[:, d], func=AF.Exp, scale=-1.0 / 16), [R("cs")], [R("eq")])
                fw.op("scalar", lambda e, d=d: e.activation(out=ek[:, d], in_=cs[:, d], func=AF.Exp, scale=1.0 / 16), [R("cs")], [R("ek")])
                fw.op("vector", lambda e, d=d: e.tensor_tensor(out=qtT[:, d], in0=pqkb[:, 0:256].rearrange("p (a b) -> p a b", a=2),
                                                              in1=eq[:, d], op=ALU.mult), [rpb[1], R("eq")], [R("qtT")])
                fw.op("gpsimd", lambda e, d=d: e.tensor_copy(out=ktT[:, d], in_=ek[:, d]), [R("ek")], [R("ktT")])
                fw.op("vector", lambda e, d=d: e.tensor_tensor(out=ktT[:, d], in0=pqkb[:, 256:512].rearrange("p (a b) -> p a b", a=2),
                                                              in1=ek[:, d], op=ALU.mult), [rpb[1], R("ek"), R("ktT")], [R("ktT")])
            pkdb = pkd[:].bitcast(BF16)
            for hp in range(2):
                fw.op("tensor", lambda e, d=d, hp=hp: e.transpose(pkdb[:, (d * 2 + hp) * 128:(d * 2 + hp + 1) * 128], kdT[:, d, hp, :],
                                                                 self.identB[:]), [R("kdT"), rI], [rpb[5]])
            fw.op("scalar", lambda e, d=d: e.activation(out=kdt[:, d].rearrange("p a b -> p (a b)"), in_=pkdb[:, d * 256:(d + 1) * 256],
                                                       func=AF.Copy), [rpb[5]], [R("kdt")])

    def dstate(s, d):
        Tt = T[s]
        rT = [R("T%d" % s)]
        for hp in range(2):
            for h2 in range(2):
                h = hp * 2 + h2
                fw.op("tensor", lambda e, hp=hp, h2=h2, h=h: e.matmul(
                    pdS[h2 * 64:(h2 + 1) * 64, (d * 2 + hp) * 128:(d * 2 + hp + 1) * 128],
                    lhsT=kdt[:, d, hp, h2 * 64:(h2 + 1) * 64], rhs=Tt[:, 512 + h * 128:512 + (h + 1) * 128],
                    start=True, stop=True), [R("kdt")] + rT, [rpb[6]])

    def supdate(d):
        for hp in range(2):
            fw.op("vector", lambda e, hp=hp: e.scalar_tensor_tensor(out=S32[:, d, hp, :], in0=S32[:, d, hp, :], scalar=tot[:, d, hp, 2:3],
                                                                   in1=pdS[:, (d * 2 + hp) * 128:(d * 2 + hp + 1) * 128],
                                                                   op0=ALU.mult, op1=ALU.add), [R("S32"), R("tot"), rpb[6]], [R("S32")])

    order_b = [1, 0] + list(range(NTILE - 1, 1, -1))
    for n, ci in enumerate(order_b):
        s = n % 2
        load(ci, s)
        gates(s, [1], False)
        fw.op("gpsimd", lambda e, ci=ci: e.tensor_copy(out=Sst[:, ci], in_=S32[:, 1]), [R("S32")], [R("Sst")])
        dstate(s, 1)
        supdate(1)
    for ci in range(NTILE):
        s = ci % 2
        load(ci, s)
        gates(s, [0, 1], True)
        Tt = T[s]
        rT = [R("T%d" % s)]
        for d in range(2):
            M = Mf if d == 0 else Mb
            rM = R("Mf") if d == 0 else R("Mb")
            for h in range(4):
                hp, h2 = h // 2, h % 2
                psc = psc0 if d == 0 else psc1
                rps = rpb[3] if d == 0 else rpb[4]
                fw.op("tensor", lambda e, d=d, hp=hp, h2=h2, h=h, psc=psc: e.matmul(
                    psc[:, h * 128:(h + 1) * 128], lhsT=ktT[h2 * 64:(h2 + 1) * 64, d, hp, :], rhs=qtT[h2 * 64:(h2 + 1) * 64, d, hp, :],
                    start=True, stop=True), [R("ktT"), R("qtT")], [rps])
                fw.op("vector" if h % 2 == 0 else "gpsimd" if False else "vector", lambda e, d=d, h=h, psc=psc, M=M: e.tensor_tensor(
                    out=sT[:, d, h, :], in0=psc[:, h * 128:(h + 1) * 128], in1=M, op=ALU.mult), [rps, rM], [R("sT")])
        for h in range(4):
            hp, h2 = h // 2, h % 2
            ops = []
            for d in range(2):
                ops.append((sT[:, d, h, :], Tt[:, 512 + h * 128:512 + (h + 1) * 128], [R("sT")] + rT))
                if d == 0:
                    ops.append((qtT[h2 * 64:(h2 + 1) * 64, 0, hp, :], Sbf[h2 * 64:(h2 + 1) * 64, hp, :], [R("qtT"), R("Sbf")]))
                else:
                    ops.append((qtT[h2 * 64:(h2 + 1) * 64, 1, hp, :], Sst[h2 * 64:(h2 + 1) * 64, ci, hp, :], [R("qtT"), R("Sst")]))
            for n_, (lt, rh, rd) in enumerate(ops):
                fw.op("tensor", lambda e, lt=lt, rh=rh, n_=n_, h=h: e.matmul(po[:, h * 128:(h + 1) * 128], lhsT=lt, rhs=rh,
                                                                            start=(n_ == 0), stop=(n_ == 3)), rd, [rpb[7]])
        dstate(s, 0)
        supdate(0)
        fw.op("gpsimd", lambda e: e.tensor_copy(out=Sbf, in_=S32[:, 0]), [R("S32")], [R("Sbf")])
        yt = ygt[s]
        ry = R("yg%d" % s)
        for h in range(4):
            fw.op("scalar", lambda e, h=h: e.activation(out=sq, in_=po[:, h * 128:(h + 1) * 128], func=AF.Square,
                                                       accum_out=rs[:, h:h + 1]), [rpb[7]], [R("sq"), R("rs")])
        fw.op("vector", lambda e: e.tensor_scalar(out=rs[:, 4:8], in0=rs[:, 0:4], scalar1=1.0 / 128, scalar2=EPS, op0=ALU.mult,
                                                 op1=ALU.add), [R("rs")], [R("rs")])
        fw.op("scalar", lambda e: e.activation(out=rs[:, 4:8], in_=rs[:, 4:8], func=AF.Sqrt), [R("rs")], [R("rs")])
        fw.op("vector", lambda e: e.reciprocal(out=rs[:, 4:8], in_=rs[:, 4:8]), [R("rs")], [R("rs")])
        for h in range(4):
            fw.op("vector", lambda e, h=h, yt=yt: e.scalar_tensor_tensor(out=yt[:, h * 128:(h + 1) * 128], in0=po[:, h * 128:(h + 1) * 128],
                                                                        scalar=rs[:, 4 + h:5 + h], in1=gnb, op0=ALU.mult, op1=ALU.mult),
                  [rpb[7], R("rs"), R("gnb")], [ry])
        fw.dma("gpsimd", [(rows(self.yg, yglat, ci, 0, 512), yt)], [ry], [self.res("yg")])


Prog.gla = _gla


def _s5(self, l):
    fw = self.fw
    v = self.view
    self.aoff = self.phase_base
    TWO_PI = 6.283185307179586
    NG = 8
    X8 = v([128, 9, 8, NG * 16], BF16)
    Ytok = v([128, 9, 8, NG * 16], BF16)
    U8 = v([128, 1056], BF16)
    Xg = v([128, 9, 128], BF16)
    gy8 = v([128, 1056], BF16)
    gsc = v([128, 1056], F32)
    Ere = v([128, 2, NG, 65], F32)
    Eim = v([128, 2, NG, 65], F32)
    ErD = v([128, 2, NG, 65], F32)
    EiD = v([128, 2, NG, 65], F32)
    kvr = v([128, 65], F32)
    kv = v([128, 65], F32)
    kvi = v([128, 65], I32)
    sm = v([128, 24, 2, NG], F32)
    AKr = v([128, 8, 2, NG], F32)
    AKs = v([128, 8, 2, NG], F32)
    sgn = v([128, 2], F32)
    ba = v([128, NG, 16], F32)
    bb = v([128, NG, 16], F32)
    Ca = v([128, NG, 16], F32)
    Cb = v([128, NG, 16], F32)
    Bw = v([128, 2, 4, 16, 16], F32) if False else None
    BA = [[v([128, NG, 16], F32) for _ in range(4)] for _ in range(2)]
    CA = [[v([128, NG, 16], F32) for _ in range(2)] for _ in range(2)]
    Dcol = v([128, NG], F32)
    swapM = v([128, 128], F32)
    maskF = v([128, 8, 16], F32)
    maskB = v([128, 8, 16], F32)
    scr = v([128, 4096], F32)
    M2T = [scr[:, i * 1024:(i + 1) * 1024].rearrange("p (a b) -> p a b", a=64) for i in range(2)]
    M3f = [scr[:, (2 + i) * 1024:(3 + i) * 1024].rearrange("p (a b) -> p a b", a=64) for i in range(2)]
    tA = scr[:, 0:NG * 65].rearrange("p (a b) -> p a b", a=NG)
    tB = scr[:, 1040:1040 + NG * 65].rearrange("p (a b) -> p a b", a=NG)
    tI = scr[:, 2080:2080 + NG * 65].bitcast(I32).rearrange("p (a b) -> p a b", a=NG)
    M2Tp = [v([128, 8, 16], F32) for _ in range(2)]
    M2b = [v([128, 8, 128], BF16) for _ in range(2)]
    M3b = [v([128, 8, 128], BF16) for _ in range(2)]
    M1b = [v([128, 8, 128], BF16) for _ in range(2)]
    Ak = [v([128, 8, 128], F32) for _ in range(2)]
    Pst = [v([128, 132], F32) for _ in range(2)]
    HHb = [v([128, 132], BF16) for _ in range(2)]
    R = lambda n: self.res("s5_" + n)
    rI = self.res("ident")
    pb = self.pb
    rpb = [self.res("pb%d" % i) for i in range(8)]
    V, G_ = "vector", "gpsimd"

    def tt(eng, out, a, b, op, reads, writes):
        fw.op(eng, lambda e: e.tensor_tensor(out=out, in0=a, in1=b, op=op), reads, writes)

    def bc(ap, shape):
        return ap.to_broadcast(shape)

    fw.op(G_, lambda e: e.iota(kvi, pattern=[[1, 65]], base=0, channel_multiplier=0), [], [R("kv")])
    fw.op(V, lambda e: e.tensor_copy(out=kv, in_=kvi), [R("kv")], [R("kv")])
    fw.op(V, lambda e: e.tensor_scalar(out=kvr, in0=kv, scalar1=-1.0, scalar2=64.0, op0=ALU.mult, op1=ALU.add), [R("kv")], [R("kv")])
    fw.op(V, lambda e: e.memset(sgn[0:64, 0:1], -1.0), [], [R("sgn")])
    fw.op(V, lambda e: e.memset(sgn[64:128, 0:1], 1.0), [], [R("sgn")])
    fw.op(V, lambda e: e.memset(sgn[0:64, 1:2], 1.0), [], [R("sgn")])
    fw.op(V, lambda e: e.memset(sgn[64:128, 1:2], -1.0), [], [R("sgn")])
    fw.op(V, lambda e: e.tensor_copy(out=swapM[:, 0:64], in_=self.identF[:, 64:128]), [rI], [R("swapM")])
    fw.op(V, lambda e: e.tensor_copy(out=swapM[:, 64:128], in_=self.identF[:, 0:64]), [rI], [R("swapM")])
    fw.op(G_, lambda e: e.memset(maskF, 1.0), [], [R("mask")])
    fw.op(G_, lambda e: e.affine_select(out=maskF, in_=maskF, compare_op=ALU.is_ge, fill=0.0, base=15,
                                       pattern=[[16, 8], [0, 16]], channel_multiplier=-1), [R("mask")], [R("mask")])
    fw.op(G_, lambda e: e.memset(maskB, 1.0), [], [R("mask")])
    fw.op(G_, lambda e: e.affine_select(out=maskB, in_=maskB, compare_op=ALU.is_ge, fill=0.0, base=0,
                                       pattern=[[-16, 8], [0, 16]], channel_multiplier=1), [R("mask")], [R("mask")])

    for gh in range(32 // NG):
        g0 = gh * NG
        fw.barrier(self.R.values())
        fw.dma("sync", [(X8[:, ct], self.P[ct * 1024:(ct + 1) * 1024, g0 * 16:g0 * 16 + NG * 16].rearrange("(c s) w -> c s w", s=8))
                        for ct in range(8)], [self.res("P")], [R("X8")])
        fw.dma("sync", [(X8[0:32, 8], self.P[8192:8448, g0 * 16:g0 * 16 + NG * 16].rearrange("(c s) w -> c s w", s=8))],
               [self.res("P")], [R("X8")])
        rsm = R("sm")
        pairs = []
        for d in range(2):
            for half in range(2):
                ps_ = slice(half * 64, half * 64 + 64)
                pairs.append((sm[ps_, 0, d, :], self.lam_re[l, d, g0:g0 + NG, :].rearrange("g n -> n g")))
                pairs.append((sm[ps_, 1, d, :], self.lam_im[l, d, g0:g0 + NG, :].rearrange("g n -> n g")))
            pairs.append((sm[:, 2, d, :], self.log_dt[l, d:d + 1, g0:g0 + NG].partition_broadcast(128)[:, 0, :]))
        fw.dma("gpsimd", pairs, [], [rsm])
        rb = R("bc")
        fw.dma("sync", [(ba[0:64], self.b_re[l, g0:g0 + NG].rearrange("g n p -> n g p")),
                        (bb[64:128], self.b_re[l, g0:g0 + NG].rearrange("g n p -> n g p")),
                        (ba[64:128], self.b_im[l, g0:g0 + NG].rearrange("g n p -> n g p")),
                        (bb[0:64], self.b_im[l, g0:g0 + NG].rearrange("g n p -> n g p"))], [], [rb])
        for gi in range(NG):
            g = g0 + gi
            fw.dma("sync" if gi % 2 == 0 else "gpsimd",
                   [(Ca[0:64, gi, :], self.c_re[l, g].rearrange("p n -> n p")), (Cb[64:128, gi, :], self.c_re[l, g].rearrange("p n -> n p")),
                    (Ca[64:128, gi, :], self.c_im[l, g].rearrange("p n -> n p")), (Cb[0:64, gi, :], self.c_im[l, g].rearrange("p n -> n p"))],
                   [], [rb], sres=R("bc%d" % (gi % 2)))
        fw.dma("sync", [(Dcol[s_ * 16:(s_ + 1) * 16, :], self.s5_d[l, g0 * 16:g0 * 16 + NG * 16].rearrange("(g p) -> p g", p=16))
                        for s_ in range(8)], [], [R("Dcol")])
        S_ = lambda i: sm[:, i]
        fw.op("scalar", lambda e: e.activation(out=S_(2), in_=S_(2), func=AF.Exp), [rsm], [rsm])
        tt(V, S_(3), S_(0), S_(2), ALU.mult, [rsm], [rsm])
        tt(V, S_(4), S_(1), S_(2), ALU.mult, [rsm], [rsm])
        fw.op(V, lambda e: e.tensor_scalar(out=S_(4), in0=S_(4), scalar1=1.0 / TWO_PI, scalar2=None, op0=ALU.mult), [rsm], [rsm])
        rE = R("E")
        rt = R("tab")
        for d in range(2):
          for (kvx, TRe, TIm) in ((kv, Ere, Eim), (kvr, ErD, EiD)):
            tt(V, tA, bc(sm[:, 3, d, :].unsqueeze(2), [128, NG, 65]), bc(kvx.unsqueeze(1), [128, NG, 65]), ALU.mult, [rsm, R("kv")], [rt])
            fw.op("scalar", lambda e: e.activation(out=tA, in_=tA, func=AF.Exp), [rt], [rt])
            for which in range(2):
                tt(V, tB, bc(sm[:, 4, d, :].unsqueeze(2), [128, NG, 65]), bc(kvx.unsqueeze(1), [128, NG, 65]), ALU.mult, [rsm, R("kv")], [rt])
                if which == 1:
                    fw.op(V, lambda e: e.tensor_scalar(out=tB, in0=tB, scalar1=0.25, scalar2=None, op0=ALU.add), [rt], [rt])
                fw.op(V, lambda e: e.tensor_copy(out=tI, in_=tB), [rt], [rt])
                tt(V, tB, tB, tI, ALU.subtract, [rt], [rt])
                dst = TIm[:, d] if which == 0 else TRe[:, d]
                fw.op(V, lambda e, dst=dst: e.tensor_single_scalar(out=dst, in_=tB, scalar=0.5, op=ALU.is_gt), [rt], [rE])
                tt(V, tB, tB, dst, ALU.subtract, [rt, rE], [rt])
                fw.op(V, lambda e, dst=dst: e.tensor_single_scalar(out=dst, in_=tB, scalar=-0.5, op=ALU.is_lt), [rt], [rE])
                tt(V, tB, tB, dst, ALU.add, [rt, rE], [rt])
                fw.op("scalar", lambda e: e.activation(out=tB, in_=tB, func=AF.Sin, scale=6.283185), [rt], [rt])
                tt(V, dst, tB, tA, ALU.mult, [rt], [rE])
        def coef_(d):
            s = lambda i: sm[:, i, d, :]
            e1r, e1i = Ere[:, d, :, 1], Eim[:, d, :, 1]
            e64r, e64i = Ere[:, d, :, 64], Eim[:, d, :, 64]
            stt = lambda o, i0, c, i1: fw.op(V, lambda e: e.scalar_tensor_tensor(out=o, in0=i0, scalar=c, in1=i1, op0=ALU.add, op1=ALU.mult),
                                             [rsm], [rsm])
            ti_ = tI[:, :, 0]
            fw.op(V, lambda e: e.tensor_copy(out=ti_, in_=s(4)), [rsm], [rt])
            tt(V, s(20), s(4), ti_, ALU.subtract, [rsm, rt], [rsm])
            fw.op(V, lambda e: e.tensor_single_scalar(out=s(21), in_=s(20), scalar=0.5, op=ALU.is_gt), [rsm], [rsm])
            tt(V, s(20), s(20), s(21), ALU.subtract, [rsm], [rsm])
            fw.op(V, lambda e: e.tensor_single_scalar(out=s(21), in_=s(20), scalar=-0.5, op=ALU.is_lt), [rsm], [rsm])
            tt(V, s(20), s(20), s(21), ALU.add, [rsm], [rsm])
            fw.op(V, lambda e: e.tensor_scalar(out=s(20), in0=s(20), scalar1=3.14159265358979, scalar2=None, op0=ALU.mult), [rsm], [rsm])
            tt(V, s(21), s(20), s(20), ALU.mult, [rsm], [rsm])
            fw.op(V, lambda e: e.tensor_scalar(out=s(22), in0=s(21), scalar1=-1.0 / 39916800, scalar2=None, op0=ALU.mult), [rsm], [rsm])
            for c_ in (1.0 / 362880, -1.0 / 5040, 1.0 / 120, -1.0 / 6):
                stt(s(22), s(22), c_, s(21))
            stt(s(22), s(22), 1.0, s(20))
            fw.op(V, lambda e: e.tensor_scalar(out=s(23), in0=s(21), scalar1=1.0 / 479001600, scalar2=None, op0=ALU.mult), [rsm], [rsm])
            for c_ in (-1.0 / 3628800, 1.0 / 40320, -1.0 / 720, 1.0 / 24, -0.5):
                stt(s(23), s(23), c_, s(21))
            fw.op(V, lambda e: e.tensor_scalar(out=s(23), in0=s(23), scalar1=1.0, scalar2=None, op0=ALU.add), [rsm], [rsm])
            fw.op(V, lambda e: e.tensor_scalar(out=s(15), in0=s(3), scalar1=1.0 / 120, scalar2=None, op0=ALU.mult), [rsm], [rsm])
            for c_ in (1.0 / 24, 1.0 / 6, 0.5, 1.0):
                stt(s(15), s(15), c_, s(3))
            tt(V, s(16), s(22), s(23), ALU.mult, [rsm], [rsm])
            fw.op(V, lambda e: e.tensor_scalar(out=s(16), in0=s(16), scalar1=2.0, scalar2=None, op0=ALU.mult), [rsm], [rsm])
            tt(V, s(21), s(22), s(22), ALU.mult, [rsm], [rsm])
            fw.op(V, lambda e: e.tensor_scalar(out=s(21), in0=s(21), scalar1=2.0, scalar2=None, op0=ALU.mult), [rsm], [rsm])
            fw.op(V, lambda e: e.tensor_scalar(out=s(20), in0=s(21), scalar1=-1.0, scalar2=1.0, op0=ALU.mult, op1=ALU.add), [rsm], [rsm])
            tt(V, s(5), s(15), s(20), ALU.mult, [rsm], [rsm])
            tt(V, s(5), s(5), s(21), ALU.subtract, [rsm], [rsm])
            fw.op(V, lambda e: e.tensor_scalar(out=s(15), in0=s(15), scalar1=1.0, scalar2=None, op0=ALU.add), [rsm], [rsm])
            tt(V, s(6), s(15), s(16), ALU.mult, [rsm], [rsm])
            tt(V, s(15), s(0), s(0), ALU.mult, [rsm], [rsm])
            tt(V, s(16), s(1), s(1), ALU.mult, [rsm], [rsm])
            tt(V, s(15), s(15), s(16), ALU.add, [rsm], [rsm])
            fw.op(V, lambda e: e.reciprocal(out=s(7), in_=s(15)), [rsm], [rsm])
            tt(V, s(15), s(5), s(0), ALU.mult, [rsm], [rsm])
            tt(V, s(16), s(6), s(1), ALU.mult, [rsm], [rsm])
            tt(V, s(15), s(15), s(16), ALU.add, [rsm], [rsm])
            tt(V, s(8), s(15), s(7), ALU.mult, [rsm], [rsm])
            tt(V, s(15), s(6), s(0), ALU.mult, [rsm], [rsm])
            tt(V, s(16), s(5), s(1), ALU.mult, [rsm], [rsm])
            tt(V, s(15), s(15), s(16), ALU.subtract, [rsm], [rsm])
            tt(V, s(9), s(15), s(7), ALU.mult, [rsm], [rsm])
            fw.op(V, lambda e: e.tensor_scalar(out=s(10), in0=s(9), scalar1=sgn[:, 0:1], scalar2=None, op0=ALU.mult), [rsm, R("sgn")], [rsm])
            fw.op(V, lambda e: e.tensor_scalar(out=s(11), in0=s(9), scalar1=sgn[:, 1:2], scalar2=None, op0=ALU.mult), [rsm, R("sgn")], [rsm])
            tt(V, s(15), e64r, e64r, ALU.mult, [rE], [rsm])
            tt(V, s(16), e64i, e64i, ALU.mult, [rE], [rsm])
            tt(V, s(15), s(15), s(16), ALU.add, [rsm], [rsm])
            fw.op(V, lambda e: e.reciprocal(out=s(15), in_=s(15)), [rsm], [rsm])
            tt(V, s(12), e64r, s(15), ALU.mult, [rE, rsm], [rsm])
            tt(V, s(16), e64i, s(15), ALU.mult, [rE, rsm], [rsm])
            fw.op(V, lambda e: e.tensor_scalar(out=s(13), in0=s(16), scalar1=sgn[:, 1:2], scalar2=None, op0=ALU.mult), [rsm, R("sgn")], [rsm])
            fw.op(V, lambda e: e.tensor_scalar(out=s(14), in0=s(16), scalar1=sgn[:, 0:1], scalar2=None, op0=ALU.mult), [rsm, R("sgn")], [rsm])
            fw.op(V, lambda e: e.tensor_copy(out=s(17), in_=e1r), [rE], [rsm])
            fw.op(V, lambda e: e.tensor_scalar(out=s(18), in0=e1i, scalar1=sgn[:, 0:1], scalar2=None, op0=ALU.mult), [rE, R("sgn")], [rsm])
            fw.op(V, lambda e: e.tensor_scalar(out=s(19), in0=e1i, scalar1=sgn[:, 1:2], scalar2=None, op0=ALU.mult), [rE, R("sgn")], [rsm])
            B = lambda i: bc(sm[:, i, d, :].unsqueeze(2), [128, NG, 16])
            rB = R("BA")
            Ba, Bbs, Bpa, Bpbs = BA[d]
            tt(V, Ba, ba, B(8), ALU.mult, [rb, rsm], [rB])
            tt(V, Bpa, bb, B(10), ALU.mult, [rb, rsm], [rB])
            tt(V, Ba, Ba, Bpa, ALU.add, [rB], [rB])
            tt(V, Bbs, bb, B(8), ALU.mult, [rb, rsm], [rB])
            tt(V, Bpa, ba, B(11), ALU.mult, [rb, rsm], [rB])
            tt(V, Bbs, Bbs, Bpa, ALU.add, [rB], [rB])
            tt(V, Bpa, Ba, B(12), ALU.mult, [rB, rsm], [rB])
            tt(V, Bpbs, Bbs, B(13), ALU.mult, [rB, rsm], [rB])
            tt(V, Bpa, Bpa, Bpbs, ALU.add, [rB], [rB])
            tt(V, Bpbs, Bbs, B(12), ALU.mult, [rB, rsm], [rB])
            tt(V, gsc[:, 0:NG * 16].rearrange("p (a b) -> p a b", a=NG), Ba, B(14), ALU.mult, [rB, rsm], [R("gsc")])
            tt(V, Bpbs, Bpbs, gsc[:, 0:NG * 16].rearrange("p (a b) -> p a b", a=NG), ALU.add, [rB, R("gsc")], [rB])
            fw.op(V, lambda e: e.tensor_scalar(out=Bbs, in0=Bbs, scalar1=sgn[:, 0:1], scalar2=None, op0=ALU.mult), [rB, R("sgn")], [rB])
            fw.op(V, lambda e: e.tensor_scalar(out=Bpbs, in0=Bpbs, scalar1=sgn[:, 0:1], scalar2=None, op0=ALU.mult), [rB, R("sgn")], [rB])
            Cas, Cbn = CA[d]
            rC = R("CA")
            tmpc = gsc[:, 256:256 + NG * 16].rearrange("p (a b) -> p a b", a=NG)
            tt(V, Cas, Ca, B(17), ALU.mult, [rb, rsm], [rC])
            tt(V, tmpc, Cb, B(18), ALU.mult, [rb, rsm], [R("gsc")])
            tt(V, Cas, Cas, tmpc, ALU.add, [rC, R("gsc")], [rC])
            tt(V, Cbn, Cb, B(17), ALU.mult, [rb, rsm], [rC])
            tt(V, tmpc, Ca, B(19), ALU.mult, [rb, rsm], [R("gsc")])
            tt(V, Cbn, Cbn, tmpc, ALU.add, [rC, R("gsc")], [rC])
            fw.op(V, lambda e: e.tensor_scalar(out=Cas, in0=Cas, scalar1=sgn[:, 1:2], scalar2=None, op0=ALU.mult), [rC, R("sgn")], [rC])
            fw.op(V, lambda e: e.tensor_scalar(out=Cbn, in0=Cbn, scalar1=-1.0, scalar2=None, op0=ALU.mult), [rC], [rC])
        for d_ in range(2):
            coef_(d_)
        rAK = R("AK")
        fw.op(V, lambda e: e.tensor_copy(out=AKr[:, 0], in_=Ere[:, :, :, 64]), [rE], [rAK])
        fw.op(V, lambda e: e.tensor_copy(out=AKs[:, 0], in_=Eim[:, :, :, 64]), [rE], [rAK])
        for k in range(1, 8):
            t15, t16 = sm[:, 15], sm[:, 16]
            tt(V, t15, AKr[:, k - 1], AKr[:, k - 1], ALU.mult, [rAK], [rsm])
            tt(V, t16, AKs[:, k - 1], AKs[:, k - 1], ALU.mult, [rAK], [rsm])
            tt(V, AKr[:, k], t15, t16, ALU.subtract, [rsm], [rAK])
            tt(V, t15, AKr[:, k - 1], AKs[:, k - 1], ALU.mult, [rAK], [rsm])
            fw.op(V, lambda e, k=k: e.tensor_scalar(out=AKs[:, k], in0=t15, scalar1=2.0, scalar2=None, op0=ALU.mult), [rsm], [rAK])
        fw.op(V, lambda e: e.tensor_scalar(out=AKs, in0=AKs, scalar1=sgn[:, 1:2], scalar2=None, op0=ALU.mult), [rAK, R("sgn")], [rAK])

        fw.barrier(self.R.values())
        def grp_(gi):
            pub = [pb[0][:].bitcast(BF16), pb[1][:].bitcast(BF16)]
            for ct in range(9):
                fw.op(G_ if ct % 2 == 0 else "scalar",
                      (lambda e, ct=ct: e.tensor_copy(out=Xg[:, ct, :].rearrange("p (a b) -> p a b", a=8), in_=X8[:, ct, :, gi * 16:(gi + 1) * 16]))
                      if ct % 2 == 0 else
                      (lambda e, ct=ct: e.activation(out=Xg[:, ct, :].rearrange("p (a b) -> p a b", a=8), in_=X8[:, ct, :, gi * 16:(gi + 1) * 16],
                                                     func=AF.Copy)), [R("X8")], [R("Xg")])
            for ct in range(9):
                npart = 128 if ct < 8 else 32
                bank, off = (0, ct * 128) if ct < 8 else (1, 0)
                fw.op("tensor", lambda e, ct=ct, npart=npart, bank=bank, off=off: e.transpose(
                    pub[bank][:, off:off + npart], Xg[0:npart, ct, :], self.identB[0:npart, 0:npart]),
                    [R("Xg"), rI], [rpb[bank]])
            fw.op("scalar", lambda e: e.activation(out=U8[:, 0:1024], in_=pub[0][:, 0:1024], func=AF.Copy), [rpb[0]], [R("U8")])
            fw.op("scalar", lambda e: e.activation(out=U8[:, 1024:1056], in_=pub[1][:, 0:32], func=AF.Copy), [rpb[1]], [R("U8")])
            U8v = U8.rearrange("p (c j) -> p c j", j=8)
            for d in range(2):
                Ba, Bbs, Bpa, Bpbs = BA[d]
                Cas, Cbn = CA[d]
                rM = R("M%d" % d)
                if d == 0:
                    eM2r, eM2i = ErD[:, d, gi, 1:65], EiD[:, d, gi, 1:65]
                    eM3r, eM3i = Ere[:, d, gi, 0:64], Eim[:, d, gi, 0:64]
                    ePr, ePi = ErD[:, d, gi, 1:9], EiD[:, d, gi, 1:9]
                else:
                    eM2r, eM2i = Ere[:, d, gi, 0:64], Eim[:, d, gi, 0:64]
                    eM3r, eM3i = ErD[:, d, gi, 1:65], EiD[:, d, gi, 1:65]
                    ePr, ePi = Ere[:, d, gi, 56:64], Eim[:, d, gi, 56:64]
                b64 = lambda ap: bc(ap.unsqueeze(2), [128, 64, 16])
                w64 = lambda ap: bc(ap.unsqueeze(1), [128, 64, 16])
                tmp = gsc[:, 0:1024].rearrange("p (a b) -> p a b", a=64)
                rg = R("gsc")
                tt(V, M2T[d], b64(eM2r), w64(Ba[:, gi, :]), ALU.mult, [rE, R("BA")], [rM])
                tt(G_, tmp, b64(eM2i), w64(Bbs[:, gi, :]), ALU.mult, [rE, R("BA")], [rg])
                tt(V, M2T[d], M2T[d], tmp, ALU.add, [rM, rg], [rM])
                tt(V, M3f[d], b64(eM3r), w64(Cas[:, gi, :]), ALU.mult, [rE, R("CA")], [rM])
                tt(G_, tmp, b64(eM3i), w64(Cbn[:, gi, :]), ALU.mult, [rE, R("CA")], [rg])
                tt(V, M3f[d], M3f[d], tmp, ALU.add, [rM, rg], [rM])
                tmp8 = gsc[:, 0:128].rearrange("p (a b) -> p a b", a=8)
                tt(V, M2Tp[d], bc(ePr.unsqueeze(2), [128, 8, 16]), bc(Bpa[:, gi, :].unsqueeze(1), [128, 8, 16]), ALU.mult, [rE, R("BA")], [rM])
                tt(V, tmp8, bc(ePi.unsqueeze(2), [128, 8, 16]), bc(Bpbs[:, gi, :].unsqueeze(1), [128, 8, 16]), ALU.mult, [rE, R("BA")], [rg])
                tt(V, M2Tp[d], M2Tp[d], tmp8, ALU.add, [rM, rg], [rM])
                fw.op("scalar", lambda e, d=d: e.activation(out=M3b[d].rearrange("p a b -> p (a b)"), in_=M3f[d].rearrange("p a b -> p (a b)"),
                                                           func=AF.Copy), [rM], [R("M3b%d" % d)])
                for j in range(8):
                    bank = 2 + j // 4
                    fw.op("tensor", lambda e, d=d, j=j, bank=bank: e.transpose(
                        pb[bank][:, (j % 4) * 128:(j % 4 + 1) * 128], M2T[d][:, j * 8:(j + 1) * 8, :].rearrange("p a b -> p (a b)"),
                        self.identF[:]), [rM, rI], [rpb[bank]])
                for hb_ in range(2):
                    fw.op("scalar" if hb_ == 0 else V, (lambda e, d=d, hb_=hb_: e.activation(
                        out=M2b[d][:, hb_ * 4:(hb_ + 1) * 4, :].rearrange("p a b -> p (a b)"), in_=pb[2 + hb_][:], func=AF.Copy))
                        if hb_ == 0 else (lambda e, d=d, hb_=hb_: e.tensor_copy(
                            out=M2b[d][:, hb_ * 4:(hb_ + 1) * 4, :].rearrange("p a b -> p (a b)"), in_=pb[2 + hb_][:])),
                        [rpb[2 + hb_]], [R("M2b%d" % d)])
                for hb_ in range(2):
                    fw.op("tensor", lambda e, d=d, hb_=hb_: e.matmul(
                        pb[2 + hb_][:], lhsT=M2Tp[d].rearrange("p a b -> p (a b)"),
                        rhs=M3f[d][:, hb_ * 32:(hb_ + 1) * 32, :].rearrange("p a b -> p (a b)"), start=True, stop=True),
                        [rM], [rpb[2 + hb_]])
                if d == 0:
                    blk = pb[2][:, 0:128].rearrange("p (a b) -> p a b", a=8)
                    t8 = gsc[:, 0:128].rearrange("p (a b) -> p a b", a=8)
                    tt(V, t8, blk, maskF, ALU.mult, [rpb[2], R("mask")], [rg])
                    fw.op(V, lambda e: e.scalar_tensor_tensor(out=gsc[:, 0:128], in0=self.identF[:], scalar=Dcol[:, gi:gi + 1],
                                                              in1=gsc[:, 0:128], op0=ALU.mult, op1=ALU.add), [rg, rI, R("Dcol")], [rg])
                    fw.op(V, lambda e, d=d: e.tensor_copy(out=M1b[d][:, 0, :], in_=gsc[:, 0:128]), [rg], [R("M1b%d" % d)])
                    fw.op("scalar", lambda e, d=d: e.activation(out=M1b[d][:, 1:4, :].rearrange("p a b -> p (a b)"), in_=pb[2][:, 128:512],
                                                               func=AF.Copy), [rpb[2]], [R("M1b%d" % d)])
                    fw.op("scalar", lambda e, d=d: e.activation(out=M1b[d][:, 4:8, :].rearrange("p a b -> p (a b)"), in_=pb[3][:],
                                                               func=AF.Copy), [rpb[3]], [R("M1b%d" % d)])
                else:
                    blk = pb[3][:, 384:512].rearrange("p (a b) -> p a b", a=8)
                    tt(V, M1b[d][:, 7, :].rearrange("p (a b) -> p a b", a=8), blk, maskB, ALU.mult, [rpb[3], R("mask")], [R("M1b%d" % d)])
                    fw.op("scalar", lambda e, d=d: e.activation(out=M1b[d][:, 0:4, :].rearrange("p a b -> p (a b)"), in_=pb[2][:],
                                                               func=AF.Copy), [rpb[2]], [R("M1b%d" % d)])
                    fw.op("scalar", lambda e, d=d: e.activation(out=M1b[d][:, 4:7, :].rearrange("p a b -> p (a b)"), in_=pb[3][:, 0:384],
                                                               func=AF.Copy), [rpb[3]], [R("M1b%d" % d)])
                rA = R("Ak%d" % d)
                for k in range(8):
                    fw.op(G_, lambda e, d=d, k=k: e.tensor_scalar(out=Ak[d][:, k, :], in0=self.identF[:], scalar1=AKr[:, k, d, gi:gi + 1],
                                                                 scalar2=None, op0=ALU.mult), [rI, rAK], [rA])
                    fw.op(V, lambda e, d=d, k=k: e.scalar_tensor_tensor(out=Ak[d][:, k, :], in0=swapM, scalar=AKs[:, k, d, gi:gi + 1],
                                                                       in1=Ak[d][:, k, :], op0=ALU.mult, op1=ALU.add),
                          [R("swapM"), rAK, rA], [rA])
                ps = pb[4]
                if d == 0:
                    for j in range(8):
                        fw.op("tensor", lambda e, d=d, j=j: e.matmul(ps[:, 0:132], lhsT=M2b[d][:, j, :], rhs=U8v[:, :, j],
                                                                    start=(j == 0), stop=(j == 7)), [R("M2b%d" % d), R("U8")], [rpb[4]])
                else:
                    for j in range(8):
                        fw.op("tensor", lambda e, d=d, j=j: e.matmul(ps[:, 0:128], lhsT=M2b[d][:, j, :], rhs=U8v[:, 4:132, j],
                                                                    start=(j == 0), stop=(j == 7)), [R("M2b%d" % d), R("U8")], [rpb[4]])
                    for j in range(8):
                        fw.op("tensor", lambda e, d=d, j=j: e.matmul(ps[:, 128:132], lhsT=M2b[d][:, j, :], rhs=U8v[:, 0:4, j],
                                                                    start=False, stop=(j == 7), skip_group_check=True),
                              [R("M2b%d" % d), R("U8")], [rpb[4]])
                rP = R("P%d" % d)
                fw.op(V, lambda e, d=d: e.tensor_copy(out=Pst[d], in_=ps[:, 0:132]), [rpb[4]], [rP])
                for k in range(8):
                    sft = 1 << k
                    if d == 0:
                        o_sl, i_sl = slice(sft, 132), slice(0, 132 - sft)
                    else:
                        o_sl, i_sl = slice(0, 132 - sft), slice(sft, 132)
                    fw.op("tensor", lambda e, d=d, k=k, o_sl=o_sl, i_sl=i_sl: e.matmul(ps[:, o_sl], lhsT=Ak[d][:, k, :], rhs=Pst[d][:, i_sl],
                                                                                     start=True, stop=True), [rA, rP], [rpb[4]])
                    fw.op(V, lambda e, d=d, o_sl=o_sl: e.tensor_tensor(out=Pst[d][:, o_sl], in0=Pst[d][:, o_sl], in1=ps[:, o_sl], op=ALU.add),
                          [rP, rpb[4]], [rP])
                rH = R("HH%d" % d)
                fw.op(G_, lambda e, d=d: e.memset(HHb[d], 0.0), [], [rH])
                if d == 0:
                    fw.op(V, lambda e, d=d: e.tensor_copy(out=HHb[d][:, 1:132], in_=Pst[d][:, 0:131]), [rP, rH], [rH])
                else:
                    fw.op(V, lambda e, d=d: e.tensor_copy(out=HHb[d][:, 0:131], in_=Pst[d][:, 1:132]), [rP, rH], [rH])
            for jt in range(8):
                bank = 5 + jt // 3
                yo_ = pb[bank][:, (jt % 3) * 132:(jt % 3 + 1) * 132]
                ops = []
                for js in range(0, jt + 1):
                    ops.append((yo_, M1b[0][:, jt - js, :], U8v[:, :, js], [R("M1b0"), R("U8")]))
                for js in range(jt, 8):
                    ops.append((yo_, M1b[1][:, 7 - (js - jt), :], U8v[:, :, js], [R("M1b1"), R("U8")]))
                ops.append((yo_, M3b[0][:, jt, :], HHb[0][:, 0:132], [R("M3b0"), R("HH0")]))
                ops.append((yo_[:, 4:132], M3b[1][:, jt, :], HHb[1][:, 0:128], [R("M3b1"), R("HH1")]))
                ops.append((yo_[:, 0:4], M3b[1][:, jt, :], HHb[1][:, 128:132], [R("M3b1"), R("HH1")]))
                for n_, (o_, lt, rh, rd) in enumerate(ops):
                    fw.op("tensor", lambda e, o_=o_, lt=lt, rh=rh, n_=n_, last=(n_ == len(ops) - 1): e.matmul(
                        o_, lhsT=lt, rhs=rh, start=(n_ == 0), stop=last, skip_group_check=True), rd, [rpb[bank]])
            gy8v = gy8.rearrange("p (c j) -> p j c", j=8)
            gscv = gsc[:, 0:1056].rearrange("p (j c) -> p j c", j=8)
            rg = R("gsc")
            for b3 in range(3):
                njt = 3 if b3 < 2 else 2
                src_ = pb[5 + b3][:, 0:njt * 132].rearrange("p (j c) -> p j c", j=njt)
                tmp_ = gscv[:, b3 * 3:b3 * 3 + njt, :]
                dst_ = gy8v[:, b3 * 3:b3 * 3 + njt, :]
                rp_ = rpb[5 + b3]
                fw.op("scalar", lambda e, src_=src_, tmp_=tmp_: e.activation(out=tmp_, in_=src_, func=AF.Square), [rp_], [rg])
                fw.op(V, lambda e, tmp_=tmp_: e.tensor_scalar(out=tmp_, in0=tmp_, scalar1=0.044715, scalar2=1.0, op0=ALU.mult, op1=ALU.add),
                      [rg], [rg])
                tt(V, tmp_, tmp_, src_, ALU.mult, [rg, rp_], [rg])
                fw.op("scalar", lambda e, tmp_=tmp_: e.activation(out=tmp_, in_=tmp_, func=AF.Sigmoid, scale=1.5957691216), [rg], [rg])
                tt(V, dst_, tmp_, src_, ALU.mult, [rg, rp_], [R("gy8")])
            for ct in range(9):
                npart = 128 if ct < 8 else 32
                bank, off = (0, ct * 128) if ct < 8 else (1, 0)
                fw.op("tensor", lambda e, ct=ct, npart=npart, bank=bank, off=off: e.transpose(
                    pub[bank][0:npart, off:off + 128], gy8[:, ct * 128:ct * 128 + npart], self.identB[:]),
                    [R("gy8"), rI], [rpb[bank]])
                fw.op("scalar" if ct % 2 == 0 else V,
                      (lambda e, ct=ct, npart=npart, bank=bank, off=off: e.activation(
                          out=Ytok[0:npart, ct, :, gi * 16:(gi + 1) * 16], in_=pub[bank][0:npart, off:off + 128].rearrange("p (a b) -> p a b", a=8),
                          func=AF.Copy)) if ct % 2 == 0 else
                      (lambda e, ct=ct, npart=npart, bank=bank, off=off: e.tensor_copy(
                          out=Ytok[0:npart, ct, :, gi * 16:(gi + 1) * 16], in_=pub[bank][0:npart, off:off + 128].rearrange("p (a b) -> p a b", a=8))),
                      [rpb[bank]], [R("Ytok")])
        for gi_ in range(NG):
            grp_(gi_)
        fw.dma("sync", [(self.gy[ct * 1024:(ct + 1) * 1024, g0 * 16:g0 * 16 + NG * 16].rearrange("(c s) w -> c s w", s=8), Ytok[:, ct])
                        for ct in range(8)], [R("Ytok")], [self.res("gy")])
        fw.dma("sync", [(self.gy[8192:8448, g0 * 16:g0 * 16 + NG * 16].rearrange("(c s) w -> c s w", s=8), Ytok[0:32, 8])],
               [R("Ytok")], [self.res("gy")])
    print("s5 arena end", self.aoff)


Prog.s5 = _s5
```

```python
from contextlib import ExitStack
import numpy as np
import concourse.bass as bass
import concourse.mybir as mybir
from concourse.bass_utils import run_bass_kernel_spmd

F32 = mybir.dt.float32
BF16 = mybir.dt.bfloat16
I32 = mybir.dt.int32
AF = mybir.ActivationFunctionType
ALU = mybir.AluOpType
AX = mybir.AxisListType

D = 1024
L = 8192
CT = 256
NT = L + CT
NTILE = NT // 128
INW = 2592
DEPTH = 4
EPS = 1e-6


import heapq


class Res:
    __slots__ = ("name", "w", "r", "dsem", "dcount", "last_dma")

    def __init__(self, name):
        self.name = name
        self.w = None
        self.r = []
        self.dsem = None
        self.dcount = 0
        self.last_dma = None


class _ProbeInst:
    def then_inc(self, *a, **k):
        return self


class _Probe:
    def __init__(self):
        self.name = None
        self.args = None
        self.kw = None

    def __getattr__(self, name):
        def f(*args, **kw):
            self.name, self.args, self.kw = name, args, kw
            return _ProbeInst()
        return f


def _fsize(ap):
    n = 1
    for d_ in ap.shape[1:]:
        n *= d_
    return n


class FW:
    SEM_LIMIT = 24000
    HOP = 1.2

    def __init__(self, nc, stack, schedule=True):
        self.nc = nc
        self.stack = stack
        self.schedule = schedule
        self.nsem = 0
        self.nodes = []
        self.bar = None
        self.bar_start = 0
        self.engnames = ("tensor", "vector", "scalar", "gpsimd", "sync")

    def new_sem(self, name):
        self.nsem += 1
        return self.stack.enter_context(self.nc.semaphore("%s_%d" % (name, self.nsem)))

    def sb(self, name, shape, dt):
        return self.stack.enter_context(self.nc.sbuf_tensor(name, list(shape), dt))

    def ps(self, name, shape, dt):
        return self.stack.enter_context(self.nc.psum_tensor(name, list(shape), dt))

    def _deps(self, reads, writes):
        deps = set()
        for r in reads:
            if r.w is not None:
                deps.add(r.w)
        for w in writes:
            if w.w is not None:
                deps.add(w.w)
            deps.update(w.r)
        if self.bar is not None:
            deps.add(self.bar)
        return deps

    def _cost(self, engname, fn):
        p = _Probe()
        try:
            fn(p)
            nm, kw, args = p.name, p.kw, p.args
            if nm == "matmul":
                rhs = kw["rhs"]
                n = _fsize(rhs)
                passes = 4 if rhs.dtype == F32 else 1
                return max(64, n) * passes / 2400.0 + 0.03
            if nm == "transpose" and engname == "tensor":
                return 128 / 2400.0 + 0.05
            ap = kw.get("out", None)
            if ap is None:
                ap = args[0]
            n = _fsize(ap)
            if engname == "vector":
                return n * 1.3 / 960.0 + 0.12
            if engname == "scalar":
                return n / 1200.0 + 0.25
            return n * 2.0 / 1200.0 + 0.3
        except Exception:
            return 0.5

    def op(self, engname, fn, reads=(), writes=()):
        nid = len(self.nodes)
        deps = self._deps(reads, writes)
        self.nodes.append(dict(id=nid, eng=engname, kind="op", fn=fn, deps=deps, cost=self._cost(engname, fn)))
        for r in reads:
            r.r.append(nid)
        for w in writes:
            w.w = nid
            w.r = []
        return nid

    def dma(self, qname, pairs, reads, writes, sres=None):
        sres = sres or writes[0]
        qt = "sw" if qname == "gpsimd" else "hw"
        if not isinstance(sres.dsem, dict):
            sres.dsem = {}
        st_ = sres.dsem.get(qt)
        if st_ is None or st_[1] >= 16 * 3000:
            st_ = [self.new_sem("d%s_%s" % (qt, sres.name)), 0, None]
            sres.dsem[qt] = st_
        nid = len(self.nodes)
        deps = self._deps(reads, writes)
        if st_[2] is not None:
            deps.add(st_[2])
        nbytes = 0
        for (o, i) in pairs:
            n = 1
            for d_ in o.shape:
                n *= d_
            nbytes += n * (4 if o.dtype in (F32, I32) else 2)
        st_[1] += 16 * len(pairs)
        self.nodes.append(dict(id=nid, eng=qname, kind="dma", pairs=pairs, deps=deps, cost=0.08 * len(pairs), nbytes=nbytes,
                               dsem=st_[0], dval=st_[1]))
        st_[2] = nid
        for r in reads:
            r.r.append(nid)
        for w in writes:
            w.w = nid
            w.r = []
        return nid

    def barrier(self, _unused=None):
        nid = len(self.nodes)
        deps = set(range(self.bar_start, nid))
        self.nodes.append(dict(id=nid, eng=None, kind="bar", deps=deps, cost=0.0))
        self.bar = nid
        self.bar_start = nid

    def _simulate(self):
        nodes = self.nodes
        n = len(nodes)
        fin = [0.0] * n
        start = [0.0] * n
        if not self.schedule:
            order = {e: [] for e in self.engnames}
            for nd in nodes:
                if nd["eng"] is not None:
                    order[nd["eng"]].append(nd["id"])
            return order
        children = [[] for _ in range(n)]
        rem = [0] * n
        for nd in nodes:
            rem[nd["id"]] = len(nd["deps"])
            for d_ in nd["deps"]:
                children[d_].append(nd["id"])
        ready = [0.0] * n
        heap = []
        for nd in nodes:
            if rem[nd["id"]] == 0:
                heapq.heappush(heap, (0.0, nd["id"]))
        efree = {e: 0.0 for e in self.engnames}
        dma_free = 0.0
        order = {e: [] for e in self.engnames}
        done = 0
        while heap:
            rt, nid = heapq.heappop(heap)
            nd = nodes[nid]
            e = nd["eng"]
            if e is None:
                st = rt
                f = rt
            else:
                st = max(rt, efree[e])
                efree[e] = st + nd["cost"]
                order[e].append((st, nid))
                if nd["kind"] == "dma":
                    t0 = max(st + nd["cost"], dma_free)
                    dur = nd["nbytes"] / 150e3
                    dma_free = t0 + dur
                    f = t0 + dur + 2.0
                else:
                    f = st + nd["cost"]
            start[nid] = st
            fin[nid] = f
            done += 1
            for c in children[nid]:
                hop = 0.0 if (nodes[c]["eng"] == e and e == "tensor") else self.HOP
                if nodes[c]["kind"] == "bar" or nd["kind"] == "bar":
                    hop = 0.0
                ready[c] = max(ready[c], f + hop)
                rem[c] -= 1
                if rem[c] == 0:
                    heapq.heappush(heap, (ready[c], c))
        assert done == n, (done, n)
        self.sim_time = max(fin) if fin else 0.0
        out = {}
        for e in self.engnames:
            lst = sorted(order[e])
            out[e] = [nid for (_, nid) in lst]
        return out

    def finish(self, final_res):
        nc = self.nc
        nodes = self.nodes
        fdeps = set(r.w for r in final_res if r.w is not None)
        nid = len(nodes)
        nodes.append(dict(id=nid, eng="sync", kind="waitonly", deps=fdeps | set(range(self.bar_start, nid)), cost=0.0))
        order = self._simulate()
        tok = {}
        for e in self.engnames:
            sem = self.new_sem("prog_" + e)
            cnt = 0
            for nid_ in order[e]:
                nd = nodes[nid_]
                if nd["kind"] == "op":
                    if cnt >= self.SEM_LIMIT:
                        sem = self.new_sem("prog_" + e)
                        cnt = 0
                    cnt += 1
                    tok[nid_] = (sem, cnt, e)
                    nd["sem"] = sem
                elif nd["kind"] == "dma":
                    tok[nid_] = (nd["dsem"], nd["dval"], "dma")
        bartok = {}
        for nd in nodes:
            if nd["kind"] == "bar":
                best = {}
                for d_ in nd["deps"]:
                    if nodes[d_]["kind"] == "bar":
                        for k, v in bartok[d_].items():
                            if k not in best or best[k][1] < v[1]:
                                best[k] = v
                    elif d_ in tok:
                        s_, v_, en = tok[d_]
                        k = id(s_)
                        if k not in best or best[k][1] < v_:
                            best[k] = (s_, v_, "bar")
                bartok[nd["id"]] = best
        selfwait = {"vector": True, "scalar": True, "gpsimd": True, "tensor": False, "sync": False}
        progs = {}
        for e in self.engnames:
            waited = {}
            prog = []
            for nid_ in order[e]:
                nd = nodes[nid_]
                best = {}
                for d_ in nd["deps"]:
                    if nodes[d_]["kind"] == "bar":
                        items = bartok[d_].values()
                    elif d_ in tok:
                        items = [tok[d_]]
                    else:
                        items = []
                    for (s_, v_, en) in items:
                        if en == e and not selfwait[e]:
                            continue
                        k = id(s_)
                        if waited.get(k, 0) >= v_:
                            continue
                        if k not in best or best[k][1] < v_:
                            best[k] = (s_, v_)
                for k, (s_, v_) in best.items():
                    waited[k] = v_
                    prog.append(("wait", s_, v_))
                if nd["kind"] == "op":
                    prog.append(("op", nd["fn"], nd["sem"]))
                elif nd["kind"] == "dma":
                    for (o, i) in nd["pairs"]:
                        prog.append(("dma", o, i, nd["dsem"]))
            progs[e] = prog
        self.progs = progs

        def run(e, prog):
            for it in prog:
                if it[0] == "wait":
                    e.wait_ge(it[1], it[2])
                elif it[0] == "op":
                    it[1](e).then_inc(it[2], 1)
                else:
                    e.dma_start(out=it[1], in_=it[2]).then_inc(it[3], 16)

        with nc.allow_non_contiguous_dma(reason="small param layouts"), nc.Block() as block:
            @block.tensor
            def _(e):
                run(e, progs["tensor"])

            @block.vector
            def _(e):
                run(e, progs["vector"])

            @block.scalar
            def _(e):
                run(e, progs["scalar"])

            @block.gpsimd
            def _(e):
                run(e, progs["gpsimd"])

            @block.sync
            def _(e):
                run(e, progs["sync"])


class Prog:
    def __init__(self, depth=DEPTH, stub_s5=False, stub_gla=False, schedule=True):
        self.depth = depth
        self.schedule = schedule
        self.stub_s5 = stub_s5
        self.stub_gla = stub_gla
        nc = self.nc = bass.Bass("TRN2", target_bir_lowering=False)
        di = lambda n, s, dt=F32: nc.dram_tensor(n, list(s), dt, kind="ExternalInput").ap()
        ds = lambda n, s, dt=F32: nc.dram_tensor(n, list(s), dt, kind="Internal").ap()
        self.x = di("x", [L, D])
        self.ctx = di("ctx", [CT, D])
        self.cc = di("cc", [2, D])
        self.norm_g = di("norm_g", [DEPTH, D])
        self.w_mod = di("w_mod", [DEPTH, D, 3 * D])
        self.b_mod = di("b_mod", [DEPTH, 3 * D])
        self.w_in = di("w_in", [DEPTH, D, INW])
        self.lam_re = di("s5_lam_re", [DEPTH, 2, 32, 64])
        self.lam_im = di("s5_lam_im", [DEPTH, 2, 32, 64])
        self.log_dt = di("s5_log_dt", [DEPTH, 2, 32])
        self.b_re = di("s5_b_re", [DEPTH, 32, 64, 16])
        self.b_im = di("s5_b_im", [DEPTH, 32, 64, 16])
        self.c_re = di("s5_c_re", [DEPTH, 32, 16, 64])
        self.c_im = di("s5_c_im", [DEPTH, 32, 16, 64])
        self.s5_d = di("s5_d", [DEPTH, 512])
        self.w_glu = di("s5_w_glu", [DEPTH, 512, 512])
        self.b_glu = di("s5_b_glu", [DEPTH, 512])
        self.w_gate = di("gla_w_gate", [DEPTH, 2, 16, 256])
        self.b_gate = di("gla_b_gate", [DEPTH, 2, 256])
        self.gnorm = di("gla_norm_g", [DEPTH, 128])
        self.w_out = di("w_out", [DEPTH, D, D])
        self.final_norm = di("final_norm", [1, D])
        self.out = nc.dram_tensor("out", [L, D], F32, kind="ExternalOutput").ap()
        self.xs = [ds("xs0", [NT, D]), ds("xs1", [NT, D])]
        self.P = ds("P", [NT, INW], BF16)
        import os
        if os.environ.get("DEBUG_GY"):
            self.gy = nc.dram_tensor("gy", [NT, 512], BF16, kind="ExternalOutput").ap()
        else:
            self.gy = ds("gy", [NT, 512], BF16)
        self.yg = ds("yg", [NT, 512], BF16)
        self.R = {}
        with ExitStack() as st:
            self.fw = FW(nc, st, schedule=self.schedule)
            self.alloc()
            self.consts()
            for l in range(depth):
                self.weights_in(l)
                self.modulation(l)
                self.phase1(l)
                if stub_s5:
                    self.s5_stub(l)
                else:
                    self.fw.barrier(self.R.values())
                    self.s5(l)
                    self.fw.barrier(self.R.values())
                if stub_gla:
                    self.gla_stub(l)
                else:
                    self.fw.barrier(self.R.values())
                    self.gla(l)
                    self.fw.barrier(self.R.values())
                self.weights_out(l)
                self.phase3(l)
            self.fw.finish([self.res("out")])

    def res(self, name):
        if name not in self.R:
            self.R[name] = Res(name)
        return self.R[name]

    def view(self, shape, dt):
        n = 1
        for d_ in shape[1:]:
            n *= d_
        words = n if dt in (F32, I32) else (n + 1) // 2
        ap = self.arena[:, self.aoff:self.aoff + words]
        self.aoff += words
        assert self.aoff <= self.NW, (self.aoff, self.NW)
        if dt != F32:
            ap = ap.bitcast(dt)
        if len(shape) == 3:
            ap = ap.rearrange("p (a b) -> p a b", a=shape[1])
        elif len(shape) == 4:
            ap = ap.rearrange("p (a b c) -> p a b c", a=shape[1], b=shape[2])
        return ap

    def alloc(self):
        fw = self.fw
        self.NW = 52600
        self.arena = fw.sb("arena", [128, self.NW], F32)
        self.aoff = 0
        self.identF = fw.sb("identF", [128, 128], F32)
        self.identB = fw.sb("identB", [128, 128], BF16)
        self.ccT = fw.sb("ccT", [128, 8, 2], F32)
        self.st1 = [fw.sb("st1_%d" % i, [128, 4], F32) for i in range(2)]
        self.pb = [fw.ps("pb%d" % i, [128, 512], F32) for i in range(8)]
        v = self.view
        self.scB = v([128, 8, 2, 128], F32)
        self.modb = v([128, 2, 3 * D], F32)
        self.bglub = v([128, 512], F32)
        self.fnb = v([128, D], F32)
        self.phase_base = self.aoff
        self.winb = v([128, 8, INW], BF16)
        self.woutb = v([128, 8, D], BF16)
        self.wglub = v([128, 4, 512], BF16)
        self.wst = v([128, 6144], F32)
        self.xt = [v([128, D], F32) for i in range(2)]
        self.xn = [v([128, D], F32) for i in range(2)]
        self.yo = [v([128, D], F32) for i in range(2)]
        self.hb = [v([128, D], BF16) for i in range(2)]
        self.mix = self.hb
        self.hT = [v([128, 8, 128], BF16) for i in range(2)]
        self.mixT = self.hT
        self.pj = [v([128, INW], BF16) for i in range(2)]
        self.g3 = [v([128, 2048], BF16) for i in range(2)]
        self.gyT = [v([128, 4, 128], BF16) for i in range(2)]
        self.t3 = [v([128, 512], F32) for i in range(2)]
        self.dense_end = self.aoff
        print("arena dense end", self.dense_end, "phase_base", self.phase_base)

    def consts(self):
        fw = self.fw
        identF, identB = self.identF, self.identB
        rI = self.res("ident")
        fw.op("gpsimd", lambda e: e.memset(identF[:], 0.0), [], [rI])
        fw.op("gpsimd", lambda e: e.affine_select(out=identF[:], in_=identF[:], compare_op=ALU.not_equal, fill=1.0,
                                                 base=0, pattern=[[-1, 128]], channel_multiplier=1), [rI], [rI])
        fw.op("gpsimd", lambda e: e.tensor_copy(out=identB[:], in_=identF[:]), [rI], [rI])
        rc = self.res("cc")
        ccT, scB = self.ccT, self.scB
        fw.dma("sync", [(ccT[:, :, j], self.cc[j, :].rearrange("(k p) -> p k", p=128)) for j in range(2)], [], [rc])
        fw.op("scalar", lambda e: e.activation(out=ccT[:], in_=ccT[:], func=AF.Silu), [rc], [rc])
        for k in range(8):
            for j in range(2):
                fw.op("vector", lambda e, k=k, j=j: e.tensor_copy(out=scB[:, k, j, :],
                                                                   in_=ccT[:, k, j:j + 1].to_broadcast([128, 128])),
                      [rc], [self.res("scB")])
        fw.dma("sync", [(self.fnb, self.final_norm[0:1, :].partition_broadcast(128)[:, 0, :])], [], [self.res("fnb")])

    def _wload(self, src_rows, ncols, dst, rdst, n):
        fw = self.fw
        s_ = n % 2
        rs = self.res("wst_s%d" % s_)
        stg = self.wst[:, s_ * 3072:s_ * 3072 + ncols]
        fw.dma("sync" if n % 2 == 0 else "gpsimd", [(stg, src_rows)], [], [rs])
        fw.op("gpsimd" if n % 2 == 0 else "vector", lambda e: e.tensor_copy(out=dst, in_=stg), [rs], [rdst])

    def weights_in(self, l):
        for k in range(8):
            self._wload(self.w_in[l, k * 128:(k + 1) * 128, :], INW, self.winb[:, k, :], self.res("winb"), k)

    def weights_out(self, l):
        fw = self.fw
        for k in range(8):
            self._wload(self.w_out[l, k * 128:(k + 1) * 128, :], D, self.woutb[:, k, :], self.res("woutb"), k)
        for k in range(4):
            self._wload(self.w_glu[l, k * 128:(k + 1) * 128, :], 512, self.wglub[:, k, :], self.res("wglub"), k)
        fw.dma("sync", [(self.bglub, self.b_glu[l:l + 1, :].partition_broadcast(128)[:, 0, :])], [], [self.res("bglub")])

    def modulation(self, l):
        fw = self.fw
        modb = self.modb
        rs0, rs1 = self.res("wst_s0"), self.res("wst_s1")
        rmod = self.res("modb")
        wv = self.wst.rearrange("p (k n) -> p k n", k=8)
        bt = self.t3[0]
        rbt = self.res("t3_0")
        for q in range(4):
            c0 = q * 768
            fw.dma("sync", [(wv[:, k, :], self.w_mod[l, k * 128:(k + 1) * 128, c0:c0 + 768]) for k in range(4)],
                   [], [rs0, rs1])
            fw.dma("gpsimd", [(wv[:, k, :], self.w_mod[l, k * 128:(k + 1) * 128, c0:c0 + 768]) for k in range(4, 8)],
                   [], [rs0, rs1], sres=self.res("wst_g"))
            for n in range(2):
                col = c0 + n * 384
                fw.dma("sync", [(bt[:, 0:384], self.b_mod[l:l + 1, col:col + 384].partition_broadcast(128)[:, 0, :])], [], [rbt])
                for j in range(2):
                    bi = (n * 2 + j) % 4
                    pb = self.pb[bi]
                    rp = self.res("pb%d" % bi)
                    for k in range(8):
                        fw.op("tensor", lambda e, k=k, j=j, n=n, pb=pb: e.matmul(
                            pb[:, 0:384], lhsT=self.scB[:, k, j, :], rhs=wv[:, k, n * 384:(n + 1) * 384],
                            start=(k == 0), stop=(k == 7)), [rs0, rs1, self.res("scB")], [rp])
                    fw.op("vector", lambda e, j=j, col=col, pb=pb: e.tensor_tensor(
                        out=modb[:, j, col:col + 384], in0=pb[:, 0:384], in1=bt[:, 0:384], op=ALU.add),
                        [rp, rbt], [rmod])
        ngb = self.xn[0]
        rng = self.res("xn0")
        fw.dma("sync", [(ngb, self.norm_g[l:l + 1, :].partition_broadcast(128)[:, 0, :])], [], [rng])
        for j in range(2):
            fw.op("vector", lambda e, j=j: e.scalar_tensor_tensor(
                out=modb[:, j, D:2 * D], in0=modb[:, j, D:2 * D], scalar=1.0, in1=ngb,
                op0=ALU.add, op1=ALU.mult), [rmod, rng], [rmod])

    def xsrc(self, l, i):
        if l == 0:
            if i < 2:
                return self.ctx[i * 128:(i + 1) * 128, :], None
            return self.x[(i - 2) * 128:(i - 1) * 128, :], None
        return self.xs[l % 2][i * 128:(i + 1) * 128, :], self.res("xs%d" % (l % 2))

    def phase1(self, l):
        fw = self.fw
        rmod = self.res("modb")
        rP = self.res("P")
        for i in range(NTILE):
            s = i % 2
            j = 1 if i < 2 else 0
            xt, xn, hb, hT, pj, st1 = self.xt[s], self.xn[s], self.hb[s], self.hT[s], self.pj[s], self.st1[s]
            rxt, rxn, rhb, rhT, rpj, rst = (self.res("%s%d" % (n, s)) for n in ("xt", "xn", "hb", "hT", "pj", "st1"))
            src, rsrc = self.xsrc(l, i)
            fw.dma("sync", [(xt[:], src)], [rsrc] if rsrc else [], [rxt])
            fw.op("scalar", lambda e, xt=xt, xn=xn, st1=st1: e.activation(out=xn[:], in_=xt[:], func=AF.Square,
                                                                        accum_out=st1[:, 0:1]), [rxt], [rxn, rst])
            fw.op("vector", lambda e, st1=st1: e.tensor_scalar(out=st1[:, 1:2], in0=st1[:, 0:1], scalar1=1.0 / D, scalar2=EPS,
                                                              op0=ALU.mult, op1=ALU.add), [rst], [rst])
            fw.op("scalar", lambda e, st1=st1: e.activation(out=st1[:, 2:3], in_=st1[:, 1:2], func=AF.Sqrt), [rst], [rst])
            fw.op("vector", lambda e, st1=st1: e.reciprocal(out=st1[:, 3:4], in_=st1[:, 2:3]), [rst], [rst])
            fw.op("vector", lambda e, xt=xt, xn=xn, st1=st1, j=j: e.scalar_tensor_tensor(
                out=xn[:], in0=xt[:], scalar=st1[:, 3:4], in1=self.modb[:, j, D:2 * D], op0=ALU.mult, op1=ALU.mult),
                [rxt, rst, rmod], [rxn])
            fw.op("gpsimd", lambda e, xn=xn, hb=hb, j=j: e.tensor_tensor(out=hb[:], in0=xn[:], in1=self.modb[:, j, 0:D], op=ALU.add),
                  [rxn, rmod], [rhb])
            ptb = self.pb[0][:].bitcast(BF16)
            rp0 = self.res("pb0")
            for k in range(8):
                fw.op("tensor", lambda e, k=k, hb=hb, ptb=ptb: e.transpose(ptb[:, k * 128:(k + 1) * 128], hb[:, k * 128:(k + 1) * 128],
                                                                         self.identB[:]), [rhb, self.res("ident")], [rp0])
            fw.op("scalar", lambda e, hT=hT, ptb=ptb: e.activation(out=hT[:].rearrange("p k t -> p (k t)"), in_=ptb, func=AF.Copy),
                  [rp0], [rhT])
            chunks = [(0, 512, "copy"), (512, 512, "silu"), (1024, 256, "q"), (1280, 256, "copy"), (1536, 512, "copy"),
                      (2048, 512, "silu"), (2560, 32, "copy")]
            groups = [(0, 512), (512, 512), (1024, 512), (1536, 512), (2048, 512), (2560, 32)]
            for gi, (c0, w) in enumerate(groups):
                b = 1 + (gi % 4)
                pb = self.pb[b]
                rp = self.res("pb%d" % b)
                for k in range(8):
                    fw.op("tensor", lambda e, k=k, c0=c0, w=w, pb=pb, hT=hT: e.matmul(
                        pb[:, 0:w], lhsT=hT[:, k, :], rhs=self.winb[:, k, c0:c0 + w], start=(k == 0), stop=(k == 7)),
                        [rhT, self.res("winb")], [rp])
                for (a0, aw, kind) in chunks:
                    if a0 < c0 or a0 >= c0 + w:
                        continue
                    o = pj[:, a0:a0 + aw]
                    src_ = pb[:, a0 - c0:a0 - c0 + aw]
                    if kind == "silu":
                        fw.op("scalar", lambda e, o=o, src_=src_: e.activation(out=o, in_=src_, func=AF.Silu), [rp], [rpj])
                    elif kind == "q":
                        fw.op("vector", lambda e, o=o, src_=src_: e.tensor_scalar(out=o, in0=src_, scalar1=0.125, scalar2=None,
                                                                                op0=ALU.mult), [rp], [rpj])
                    else:
                        fw.op("vector", lambda e, o=o, src_=src_: e.tensor_copy(out=o, in_=src_), [rp], [rpj])
            fw.dma("gpsimd", [(self.P[i * 128:(i + 1) * 128, :], pj[:])], [rpj], [rP])

    def gelu(self, eng_a, out, in_, tmp, reads, writes, rtmp):
        fw = self.fw
        fw.op("scalar", lambda e: e.activation(out=tmp, in_=in_, func=AF.Square), reads, [rtmp])
        fw.op(eng_a, lambda e: e.tensor_scalar(out=tmp, in0=tmp, scalar1=0.044715, scalar2=1.0, op0=ALU.mult, op1=ALU.add),
              [rtmp], [rtmp])
        fw.op(eng_a, lambda e: e.tensor_tensor(out=tmp, in0=tmp, in1=in_, op=ALU.mult), [rtmp] + list(reads), [rtmp])
        fw.op("scalar", lambda e: e.activation(out=tmp, in_=tmp, func=AF.Sigmoid, scale=1.5957691216), [rtmp], [rtmp])
        fw.op(eng_a, lambda e: e.tensor_tensor(out=out, in0=tmp, in1=in_, op=ALU.mult), [rtmp] + list(reads), writes)

    def s5_stub(self, l):
        fw = self.fw
        for i in range(NTILE):
            s = i % 2
            t = self.g3[s]
            rt = self.res("g3_%d" % s)
            fw.dma("sync", [(t[:, 0:512], self.P[i * 128:(i + 1) * 128, 0:512])], [self.res("P")], [rt])
            self.gelu("vector", t[:, 512:1024], t[:, 0:512], self.t3[s][:], [rt], [rt], self.res("t3_%d" % s))
            fw.dma("sync", [(self.gy[i * 128:(i + 1) * 128, :], t[:, 512:1024])], [rt], [self.res("gy")])

    def gla_stub(self, l):
        fw = self.fw
        for i in range(NTILE):
            s = i % 2
            t = self.g3[s]
            rt = self.res("g3_%d" % s)
            fw.dma("sync", [(t[:, 0:512], self.P[i * 128:(i + 1) * 128, 1536:2048])], [self.res("P")], [rt])
            fw.dma("sync", [(self.yg[i * 128:(i + 1) * 128, :], t[:, 0:512])], [rt], [self.res("yg")])

    def phase3(self, l):
        fw = self.fw
        last = (l == self.depth - 1)
        rmod = self.res("modb")
        rout = self.res("out")
        rxd = self.res("xs%d" % ((l + 1) % 2))
        for i in range(NTILE):
            if last and i < 2:
                continue
            s = i % 2
            j = 1 if i < 2 else 0
            g3, gyT, t3, mix, mixT, yo, xt, st1 = (self.g3[s], self.gyT[s], self.t3[s], self.mix[s], self.mixT[s], self.yo[s],
                                                   self.xt[s], self.st1[s])
            rg3, rgyT, rt3, rmix, rmixT, ryo, rxt, rst = (self.res("%s%d" % (n, s)) for n in
                                                          ("g3_", "gyT", "t3_", "mix", "mixT", "yo", "xt", "st1"))
            rows = slice(i * 128, (i + 1) * 128)
            fw.dma("sync", [(g3[:, 0:512], self.gy[rows, :])], [self.res("gy")], [rg3])
            fw.dma("sync", [(g3[:, 512:1024], self.yg[rows, :])], [self.res("yg")], [rg3])
            fw.dma("sync", [(g3[:, 1024:1536], self.P[rows, 512:1024]), (g3[:, 1536:2048], self.P[rows, 2048:2560])],
                   [self.res("P")], [rg3])
            src, rsrc = self.xsrc(l, i)
            fw.dma("gpsimd", [(xt[:], src)], [rsrc] if rsrc else [], [rxt])
            ptb = self.pb[5][:].bitcast(BF16)
            rp5 = self.res("pb5")
            for k in range(4):
                fw.op("tensor", lambda e, k=k, g3=g3, ptb=ptb: e.transpose(ptb[:, k * 128:(k + 1) * 128], g3[:, k * 128:(k + 1) * 128],
                                                                         self.identB[:]), [rg3, self.res("ident")], [rp5])
            fw.op("scalar", lambda e, gyT=gyT, ptb=ptb: e.activation(out=gyT[:].rearrange("p k t -> p (k t)"), in_=ptb[:, 0:512],
                                                                   func=AF.Copy), [rp5], [rgyT])
            pg = self.pb[6]
            rp6 = self.res("pb6")
            for k in range(4):
                fw.op("tensor", lambda e, k=k, gyT=gyT, pg=pg: e.matmul(pg[:], lhsT=gyT[:, k, :], rhs=self.wglub[:, k, :],
                                                                      start=(k == 0), stop=(k == 3)), [rgyT, self.res("wglub")], [rp6])
            fw.op("vector", lambda e, t3=t3, pg=pg: e.tensor_tensor(out=t3[:], in0=pg[:], in1=self.bglub, op=ALU.add),
                  [rp6, self.res("bglub")], [rt3])
            fw.op("scalar", lambda e, t3=t3: e.activation(out=t3[:], in_=t3[:], func=AF.Sigmoid), [rt3], [rt3])
            fw.op("vector", lambda e, t3=t3, g3=g3: e.tensor_tensor(out=t3[:], in0=t3[:], in1=g3[:, 0:512], op=ALU.mult),
                  [rt3, rg3], [rt3])
            fw.op("vector", lambda e, t3=t3, g3=g3, mix=mix: e.tensor_tensor(out=mix[:, 0:512], in0=t3[:], in1=g3[:, 1024:1536],
                                                                            op=ALU.mult), [rt3, rg3], [rmix])
            fw.op("gpsimd", lambda e, g3=g3, mix=mix: e.tensor_tensor(out=mix[:, 512:1024], in0=g3[:, 512:1024], in1=g3[:, 1536:2048],
                                                                     op=ALU.mult), [rg3], [rmix])
            ptm = self.pb[7][:].bitcast(BF16)
            rp7 = self.res("pb7")
            for k in range(8):
                fw.op("tensor", lambda e, k=k, mix=mix, ptm=ptm: e.transpose(ptm[:, k * 128:(k + 1) * 128], mix[:, k * 128:(k + 1) * 128],
                                                                           self.identB[:]), [rmix, self.res("ident")], [rp7])
            fw.op("scalar", lambda e, mixT=mixT, ptm=ptm: e.activation(out=mixT[:].rearrange("p k t -> p (k t)"), in_=ptm, func=AF.Copy),
                  [rp7], [rmixT])
            for n in range(2):
                b = 1 + n
                pb = self.pb[b]
                rp = self.res("pb%d" % b)
                for k in range(8):
                    fw.op("tensor", lambda e, k=k, n=n, pb=pb, mixT=mixT: e.matmul(
                        pb[:], lhsT=mixT[:, k, :], rhs=self.woutb[:, k, n * 512:(n + 1) * 512], start=(k == 0), stop=(k == 7)),
                        [rmixT, self.res("woutb")], [rp])
                cs = slice(n * 512, (n + 1) * 512)
                fw.op("vector", lambda e, pb=pb, yo=yo, cs=cs, j=j, n=n: e.tensor_tensor(
                    out=yo[:, cs], in0=pb[:], in1=self.modb[:, j, 2 * D + n * 512:2 * D + (n + 1) * 512], op=ALU.mult),
                    [rp, rmod], [ryo])
            fw.op("gpsimd", lambda e, yo=yo, xt=xt: e.tensor_tensor(out=yo[:], in0=yo[:], in1=xt[:], op=ALU.add), [ryo, rxt], [ryo])
            if not last:
                fw.dma("sync", [(self.xs[(l + 1) % 2][rows, :], yo[:])], [ryo], [rxd])
            else:
                xn = self.xn[s]
                rxn = self.res("xn%d" % s)
                fw.op("scalar", lambda e, yo=yo, xn=xn, st1=st1: e.activation(out=xn[:], in_=yo[:], func=AF.Square,
                                                                            accum_out=st1[:, 0:1]), [ryo], [rxn, rst])
                fw.op("vector", lambda e, st1=st1: e.tensor_scalar(out=st1[:, 1:2], in0=st1[:, 0:1], scalar1=1.0 / D, scalar2=EPS,
                                                                  op0=ALU.mult, op1=ALU.add), [rst], [rst])
                fw.op("scalar", lambda e, st1=st1: e.activation(out=st1[:, 2:3], in_=st1[:, 1:2], func=AF.Sqrt), [rst], [rst])
                fw.op("vector", lambda e, st1=st1: e.reciprocal(out=st1[:, 3:4], in_=st1[:, 2:3]), [rst], [rst])
                fw.op("vector", lambda e, yo=yo, xn=xn, st1=st1: e.scalar_tensor_tensor(
                    out=xn[:], in0=yo[:], scalar=st1[:, 3:4], in1=self.fnb, op0=ALU.mult, op1=ALU.mult),
                    [ryo, rst, self.res("fnb")], [rxn])
                fw.dma("sync", [(self.out[(i - 2) * 128:(i - 1) * 128, :], xn[:])], [rxn], [rout])

    def s5(self, l):
        raise NotImplementedError

    def gla(self, l):
        raise NotImplementedError


_CACHE = {}


def make_in_maps(inputs):
    maps = []
    for core in range(8):
        b = core % 4
        m = {
            "x": np.ascontiguousarray(inputs["x"][b]),
            "ctx": np.ascontiguousarray(inputs["ctx"][b]),
            "cc": np.ascontiguousarray(np.stack([inputs["c"][b], inputs["c_ctx"]], axis=0)),
            "final_norm": np.ascontiguousarray(inputs["final_norm"][None, :]),
        }
        for k in ("norm_g", "w_mod", "b_mod", "w_in", "s5_lam_re", "s5_lam_im", "s5_log_dt", "s5_b_re", "s5_b_im", "s5_c_re",
                  "s5_c_im", "s5_d", "s5_w_glu", "s5_b_glu", "gla_w_gate", "gla_b_gate", "gla_norm_g", "w_out"):
            m[k] = np.ascontiguousarray(inputs[k])
        maps.append(m)
    return maps


def kernel(**inputs):
    inputs = {k: np.asarray(v) for k, v in inputs.items()}
    if "prog" not in _CACHE:
        _CACHE["prog"] = Prog()
    prog = _CACHE["prog"]
    res = run_bass_kernel_spmd(prog.nc, make_in_maps(inputs), core_ids=list(range(8)))
    return np.stack([np.asarray(res.results[b]["out"]) for b in range(4)], axis=0).astype(np.float32)


def _gla(self, l):
    fw = self.fw
    v = self.view
    self.aoff = self.phase_base
    T = [v([128, 1056], BF16) for _ in range(2)]
    lrT = v([128, 128], BF16)
    wg32 = v([128, 2, 256], F32)
    wgp = v([128, 2, 256], BF16)
    nbg = v([128, 2, 2], F32)
    sp = v([128, 2, 2, 128], F32)
    cs = v([128, 2, 2, 128], F32)
    eq = v([128, 2, 2, 128], F32)
    ek = v([128, 2, 2, 128], F32)
    ekd = v([128, 2, 2, 128], F32)
    tot = v([128, 2, 2, 4], F32)
    qtT = v([128, 2, 2, 128], BF16)
    ktT = v([128, 2, 2, 128], BF16)
    kdT = v([128, 2, 2, 128], BF16)
    kdt = v([128, 2, 2, 128], BF16)
    sT = v([128, 2, 4, 128], BF16)
    S32 = v([128, 2, 2, 128], F32)
    Sbf = v([128, 2, 128], BF16)
    Sst = v([128, NTILE, 2, 128], BF16)
    Mf = v([128, 128], F32)
    Mb = v([128, 128], F32)
    ones = v([128, 128], F32)
    gnb = v([128, 128], F32)
    ygt = [v([128, 512], BF16) for _ in range(2)]
    sq = v([128, 128], F32)
    rs = v([128, 8], F32)
    R = lambda n: self.res("gla_" + n)
    rI = self.res("ident")
    fw.op("gpsimd", lambda e: e.memset(Mf, 1.0), [], [R("Mf")])
    fw.op("gpsimd", lambda e: e.affine_select(out=Mf, in_=Mf, compare_op=ALU.is_ge, fill=0.0, base=0,
                                             pattern=[[1, 128]], channel_multiplier=-1), [R("Mf")], [R("Mf")])
    fw.op("gpsimd", lambda e: e.memset(Mb, 1.0), [], [R("Mb")])
    fw.op("gpsimd", lambda e: e.affine_select(out=Mb, in_=Mb, compare_op=ALU.is_ge, fill=0.0, base=0,
                                             pattern=[[-1, 128]], channel_multiplier=1), [R("Mb")], [R("Mb")])
    fw.op("gpsimd", lambda e: e.memset(ones, 1.0), [], [R("ones")])
    fw.op("vector", lambda e: e.memset(wg32, 0.0), [], [R("wg32")])
    fw.dma("sync", [(wg32[0:16, 0, :], self.w_gate[l, 0, :, :]), (wg32[16:32, 1, :], self.w_gate[l, 1, :, :])], [], [R("wg32")])
    fw.op("vector", lambda e: e.tensor_copy(out=wgp[0:32], in_=wg32[0:32]), [R("wg32")], [R("wgp")])
    fw.dma("sync", [(nbg[:, d, hp:hp + 1], self.b_gate[l, d, hp * 128:(hp + 1) * 128].rearrange("(p o) -> p o", o=1))
                    for d in range(2) for hp in range(2)], [], [R("nbg")])
    fw.op("vector", lambda e: e.tensor_scalar(out=nbg, in0=nbg, scalar1=-1.0, scalar2=None, op0=ALU.mult), [R("nbg")], [R("nbg")])
    fw.dma("sync", [(gnb, self.gnorm[l:l + 1, :].partition_broadcast(128)[:, 0, :])], [], [R("gnb")])
    fw.op("vector", lambda e: e.memset(S32, 0.0), [], [R("S32")])
    fw.op("vector", lambda e: e.memset(Sbf, 0.0), [], [R("Sbf")])

    Plat = self.P[CT:, :].rearrange("(r c) w -> c r w", c=64)
    yglat = self.yg[CT:, :].rearrange("(r c) w -> c r w", c=64)

    def rows(ap, lat, ci, c0, c1):
        if ci < 2:
            return ap[ci * 128:(ci + 1) * 128, c0:c1]
        return lat[ci - 2, :, c0:c1]

    pz, pqk, plr, psc0, psc1, pkd, pdS, po = self.pb
    rpb = [self.res("pb%d" % i) for i in range(8)]

    def load(ci, s):
        fw.dma("sync", [(T[s][:, 0:1024], rows(self.P, Plat, ci, 1024, 2048))], [self.res("P")], [R("T%d" % s)])
        fw.dma("gpsimd", [(T[s][:, 1024:1056], rows(self.P, Plat, ci, 2560, 2592))], [self.res("P")], [R("T%d" % s)],
               sres=R("T%db" % s))

    def gates(s, dirs, need_qk):
        Tt = T[s]
        rT = [R("T%d" % s), R("T%db" % s)]
        plrb = plr[:].bitcast(BF16)
        fw.op("tensor", lambda e: e.transpose(plrb[0:32, 0:128], Tt[:, 1024:1056], self.identB[:]), rT + [rI], [rpb[2]])
        fw.op("scalar", lambda e: e.activation(out=lrT[0:32, :], in_=plrb[0:32, 0:128], func=AF.Copy), [rpb[2]], [R("lrT")])
        pqkb = pqk[:].bitcast(BF16)
        for t4 in range(4):
            fw.op("tensor", lambda e, t4=t4: e.transpose(pqkb[:, t4 * 128:(t4 + 1) * 128], Tt[:, t4 * 128:(t4 + 1) * 128],
                                                        self.identB[:]), rT + [rI], [rpb[1]])
        for d in dirs:
            for hp in range(2):
                fw.op("tensor", lambda e, d=d, hp=hp: e.matmul(pz[:, (d * 2 + hp) * 128:(d * 2 + hp + 1) * 128],
                                                              lhsT=wgp[0:32, d, hp * 128:(hp + 1) * 128], rhs=lrT[0:32, :],
                                                              start=True, stop=True), [R("wgp"), R("lrT")], [rpb[0]])
                fw.op("scalar", lambda e, d=d, hp=hp: e.activation(out=sp[:, d, hp, :], in_=pz[:, (d * 2 + hp) * 128:(d * 2 + hp + 1) * 128],
                                                                  func=AF.Exp, scale=-1.0, bias=nbg[:, d, hp:hp + 1]),
                      [rpb[0], R("nbg")], [R("sp")])
            fw.op("scalar", lambda e, d=d: e.activation(out=sp[:, d], in_=sp[:, d], func=AF.Ln, bias=1.0), [R("sp")], [R("sp")])
            for hp in range(2):
                fw.op("vector", lambda e, d=d, hp=hp: e.tensor_tensor_scan(out=cs[:, d, hp, :], data0=ones, data1=sp[:, d, hp, :],
                                                                          initial=0.0, op0=ALU.mult, op1=ALU.add),
                      [R("sp"), R("ones")], [R("cs")])
                fw.op("vector", lambda e, d=d, hp=hp: e.tensor_copy(out=tot[:, d, hp, 0:1], in_=cs[:, d, hp, 127:128]),
                      [R("cs")], [R("tot")])
                if d == 1:
                    fw.op("vector", lambda e, d=d, hp=hp: e.scalar_tensor_tensor(out=cs[:, d, hp, :], in0=sp[:, d, hp, :],
                                                                                scalar=tot[:, d, hp, 0:1], in1=cs[:, d, hp, :],
                                                                                op0=ALU.add, op1=ALU.subtract),
                          [R("sp"), R("tot"), R("cs")], [R("cs")])
            fw.op("vector", lambda e, d=d: e.tensor_scalar(out=tot[:, d, :, 1:2], in0=tot[:, d, :, 0:1], scalar1=-1.0 / 16, scalar2=None,
                                                          op0=ALU.mult), [R("tot")], [R("tot")])
            fw.op("scalar", lambda e, d=d: e.activation(out=tot[:, d, :, 2:3], in_=tot[:, d, :, 0:1], func=AF.Exp, scale=-1.0 / 16),
                  [R("tot")], [R("tot")])
            for hp in range(2):
                fw.op("scalar", lambda e, d=d, hp=hp: e.activation(out=ekd[:, d, hp, :], in_=cs[:, d, hp, :], func=AF.Exp,
                                                                  scale=1.0 / 16, bias=tot[:, d, hp, 1:2]), [R("cs"), R("tot")], [R("ekd")])
                fw.op("vector", lambda e, d=d, hp=hp: e.tensor_tensor(out=kdT[:, d, hp, :], in0=pqkb[:, (2 + hp) * 128:(3 + hp) * 128],
                                                                     in1=ekd[:, d, hp, :], op=ALU.mult), [rpb[1], R("ekd")], [R("kdT")])
            if need_qk:
                fw.op("scalar", lambda e, d=d: e.activation(out=eq[:, d], in_=cs[:, d], func=AF.Exp, scale=-1.0 / 16), [R("cs")], [R("eq")])
                fw.op("scalar", lambda e, d=d: e.activation(out=ek[:, d], in_=cs[:, d], func=AF.Exp, scale=1.0 / 16), [R("cs")], [R("ek")])
                fw.op("vector", lambda e, d=d: e.tensor_tensor(out=qtT[:, d], in0=pqkb[:, 0:256].rearrange("p (a b) -> p a b", a=2),
                                                              in1=eq[:, d], op=ALU.mult), [rpb[1], R("eq")], [R("qtT")])
                fw.op("gpsimd", lambda e, d=d: e.tensor_copy(out=ktT[:, d], in_=ek[:, d]), [R("ek")], [R("ktT")])
                fw.op("vector", lambda e, d=d: e.tensor_tensor(out=ktT[:, d], in0=pqkb[:, 256:512].rearrange("p (a b) -> p a b", a=2),
                                                              in1=ek[:, d], op=ALU.mult), [rpb[1], R("ek"), R("ktT")], [R("ktT")])
            pkdb = pkd[:].bitcast(BF16)
            for hp in range(2):
                fw.op("tensor", lambda e, d=d, hp=hp: e.transpose(pkdb[:, (d * 2 + hp) * 128:(d * 2 + hp + 1) * 128], kdT[:, d, hp, :],
                                                                 self.identB[:]), [R("kdT"), rI], [rpb[5]])
            fw.op("scalar", lambda e, d=d: e.activation(out=kdt[:, d].rearrange("p a b -> p (a b)"), in_=pkdb[:, d * 256:(d + 1) * 256],
                                                       func=AF.Copy), [rpb[5]], [R("kdt")])

    def dstate(s, d):
        Tt = T[s]
        rT = [R("T%d" % s)]
        for hp in range(2):
            for h2 in range(2):
                h = hp * 2 + h2
                fw.op("tensor", lambda e, hp=hp, h2=h2, h=h: e.matmul(
                    pdS[h2 * 64:(h2 + 1) * 64, (d * 2 + hp) * 128:(d * 2 + hp + 1) * 128],
                    lhsT=kdt[:, d, hp, h2 * 64:(h2 + 1) * 64], rhs=Tt[:, 512 + h * 128:512 + (h + 1) * 128],
                    start=True, stop=True), [R("kdt")] + rT, [rpb[6]])

    def supdate(d):
        for hp in range(2):
            fw.op("vector", lambda e, hp=hp: e.scalar_tensor_tensor(out=S32[:, d, hp, :], in0=S32[:, d, hp, :], scalar=tot[:, d, hp, 2:3],
                                                                   in1=pdS[:, (d * 2 + hp) * 128:(d * 2 + hp + 1) * 128],
                                                                   op0=ALU.mult, op1=ALU.add), [R("S32"), R("tot"), rpb[6]], [R("S32")])

    order_b = [1, 0] + list(range(NTILE - 1, 1, -1))
    for n, ci in enumerate(order_b):
        s = n % 2
        load(ci, s)
        gates(s, [1], False)
        fw.op("gpsimd", lambda e, ci=ci: e.tensor_copy(out=Sst[:, ci], in_=S32[:, 1]), [R("S32")], [R("Sst")])
        dstate(s, 1)
        supdate(1)
    for ci in range(NTILE):
        s = ci % 2
        load(ci, s)
        gates(s, [0, 1], True)
        Tt = T[s]
        rT = [R("T%d" % s)]
        for d in range(2):
            M = Mf if d == 0 else Mb
            rM = R("Mf") if d == 0 else R("Mb")
            for h in range(4):
                hp, h2 = h // 2, h % 2
                psc = psc0 if d == 0 else psc1
                rps = rpb[3] if d == 0 else rpb[4]
                fw.op("tensor", lambda e, d=d, hp=hp, h2=h2, h=h, psc=psc: e.matmul(
                    psc[:, h * 128:(h + 1) * 128], lhsT=ktT[h2 * 64:(h2 + 1) * 64, d, hp, :], rhs=qtT[h2 * 64:(h2 + 1) * 64, d, hp, :],
                    start=True, stop=True), [R("ktT"), R("qtT")], [rps])
                fw.op("vector" if h % 2 == 0 else "gpsimd" if False else "vector", lambda e, d=d, h=h, psc=psc, M=M: e.tensor_tensor(
                    out=sT[:, d, h, :], in0=psc[:, h * 128:(h + 1) * 128], in1=M, op=ALU.mult), [rps, rM], [R("sT")])
        for h in range(4):
            hp, h2 = h // 2, h % 2
            ops = []
            for d in range(2):
                ops.append((sT[:, d, h, :], Tt[:, 512 + h * 128:512 + (h + 1) * 128], [R("sT")] + rT))
                if d == 0:
                    ops.append((qtT[h2 * 64:(h2 + 1) * 64, 0, hp, :], Sbf[h2 * 64:(h2 + 1) * 64, hp, :], [R("qtT"), R("Sbf")]))
                else:
                    ops.append((qtT[h2 * 64:(h2 + 1) * 64, 1, hp, :], Sst[h2 * 64:(h2 + 1) * 64, ci, hp, :], [R("qtT"), R("Sst")]))
            for n_, (lt, rh, rd) in enumerate(ops):
                fw.op("tensor", lambda e, lt=lt, rh=rh, n_=n_, h=h: e.matmul(po[:, h * 128:(h + 1) * 128], lhsT=lt, rhs=rh,
                                                                            start=(n_ == 0), stop=(n_ == 3)), rd, [rpb[7]])
        dstate(s, 0)
        supdate(0)
        fw.op("gpsimd", lambda e: e.tensor_copy(out=Sbf, in_=S32[:, 0]), [R("S32")], [R("Sbf")])
        yt = ygt[s]
        ry = R("yg%d" % s)
        for h in range(4):
            fw.op("scalar", lambda e, h=h: e.activation(out=sq, in_=po[:, h * 128:(h + 1) * 128], func=AF.Square,
                                                       accum_out=rs[:, h:h + 1]), [rpb[7]], [R("sq"), R("rs")])
        fw.op("vector", lambda e: e.tensor_scalar(out=rs[:, 4:8], in0=rs[:, 0:4], scalar1=1.0 / 128, scalar2=EPS, op0=ALU.mult,
                                                 op1=ALU.add), [R("rs")], [R("rs")])
        fw.op("scalar", lambda e: e.activation(out=rs[:, 4:8], in_=rs[:, 4:8], func=AF.Sqrt), [R("rs")], [R("rs")])
        fw.op("vector", lambda e: e.reciprocal(out=rs[:, 4:8], in_=rs[:, 4:8]), [R("rs")], [R("rs")])
        for h in range(4):
            fw.op("vector", lambda e, h=h, yt=yt: e.scalar_tensor_tensor(out=yt[:, h * 128:(h + 1) * 128], in0=po[:, h * 128:(h + 1) * 128],
                                                                        scalar=rs[:, 4 + h:5 + h], in1=gnb, op0=ALU.mult, op1=ALU.mult),
                  [rpb[7], R("rs"), R("gnb")], [ry])
        fw.dma("gpsimd", [(rows(self.yg, yglat, ci, 0, 512), yt)], [ry], [self.res("yg")])


Prog.gla = _gla


def _s5(self, l):
    fw = self.fw
    v = self.view
    self.aoff = self.phase_base
    TWO_PI = 6.283185307179586
    NG = 8
    X8 = v([128, 9, 8, NG * 16], BF16)
    Ytok = v([128, 9, 8, NG * 16], BF16)
    U8 = v([128, 1056], BF16)
    Xg = v([128, 9, 128], BF16)
    gy8 = v([128, 1056], BF16)
    gsc = v([128, 1056], F32)
    Ere = v([128, 2, NG, 65], F32)
    Eim = v([128, 2, NG, 65], F32)
    ErD = v([128, 2, NG, 65], F32)
    EiD = v([128, 2, NG, 65], F32)
    kvr = v([128, 65], F32)
    kv = v([128, 65], F32)
    kvi = v([128, 65], I32)
    sm = v([128, 24, 2, NG], F32)
    AKr = v([128, 8, 2, NG], F32)
    AKs = v([128, 8, 2, NG], F32)
    sgn = v([128, 2], F32)
    ba = v([128, NG, 16], F32)
    bb = v([128, NG, 16], F32)
    Ca = v([128, NG, 16], F32)
    Cb = v([128, NG, 16], F32)
    Bw = v([128, 2, 4, 16, 16], F32) if False else None
    BA = [[v([128, NG, 16], F32) for _ in range(4)] for _ in range(2)]
    CA = [[v([128, NG, 16], F32) for _ in range(2)] for _ in range(2)]
    Dcol = v([128, NG], F32)
    swapM = v([128, 128], F32)
    maskF = v([128, 8, 16], F32)
    maskB = v([128, 8, 16], F32)
    scr = v([128, 4096], F32)
    M2T = [scr[:, i * 1024:(i + 1) * 1024].rearrange("p (a b) -> p a b", a=64) for i in range(2)]
    M3f = [scr[:, (2 + i) * 1024:(3 + i) * 1024].rearrange("p (a b) -> p a b", a=64) for i in range(2)]
    tA = scr[:, 0:NG * 65].rearrange("p (a b) -> p a b", a=NG)
    tB = scr[:, 1040:1040 + NG * 65].rearrange("p (a b) -> p a b", a=NG)
    tI = scr[:, 2080:2080 + NG * 65].bitcast(I32).rearrange("p (a b) -> p a b", a=NG)
    M2Tp = [v([128, 8, 16], F32) for _ in range(2)]
    M2b = [v([128, 8, 128], BF16) for _ in range(2)]
    M3b = [v([128, 8, 128], BF16) for _ in range(2)]
    M1b = [v([128, 8, 128], BF16) for _ in range(2)]
    Ak = [v([128, 8, 128], F32) for _ in range(2)]
    Pst = [v([128, 132], F32) for _ in range(2)]
    HHb = [v([128, 132], BF16) for _ in range(2)]
    R = lambda n: self.res("s5_" + n)
    rI = self.res("ident")
    pb = self.pb
    rpb = [self.res("pb%d" % i) for i in range(8)]
    V, G_ = "vector", "gpsimd"

    def tt(eng, out, a, b, op, reads, writes):
        fw.op(eng, lambda e: e.tensor_tensor(out=out, in0=a, in1=b, op=op), reads, writes)

    def bc(ap, shape):
        return ap.to_broadcast(shape)

    fw.op(G_, lambda e: e.iota(kvi, pattern=[[1, 65]], base=0, channel_multiplier=0), [], [R("kv")])
    fw.op(V, lambda e: e.tensor_copy(out=kv, in_=kvi), [R("kv")], [R("kv")])
    fw.op(V, lambda e: e.tensor_scalar(out=kvr, in0=kv, scalar1=-1.0, scalar2=64.0, op0=ALU.mult, op1=ALU.add), [R("kv")], [R("kv")])
    fw.op(V, lambda e: e.memset(sgn[0:64, 0:1], -1.0), [], [R("sgn")])
    fw.op(V, lambda e: e.memset(sgn[64:128, 0:1], 1.0), [], [R("sgn")])
    fw.op(V, lambda e: e.memset(sgn[0:64, 1:2], 1.0), [], [R("sgn")])
    fw.op(V, lambda e: e.memset(sgn[64:128, 1:2], -1.0), [], [R("sgn")])
    fw.op(V, lambda e: e.tensor_copy(out=swapM[:, 0:64], in_=self.identF[:, 64:128]), [rI], [R("swapM")])
    fw.op(V, lambda e: e.tensor_copy(out=swapM[:, 64:128], in_=self.identF[:, 0:64]), [rI], [R("swapM")])
    fw.op(G_, lambda e: e.memset(maskF, 1.0), [], [R("mask")])
    fw.op(G_, lambda e: e.affine_select(out=maskF, in_=maskF, compare_op=ALU.is_ge, fill=0.0, base=15,
                                       pattern=[[16, 8], [0, 16]], channel_multiplier=-1), [R("mask")], [R("mask")])
    fw.op(G_, lambda e: e.memset(maskB, 1.0), [], [R("mask")])
    fw.op(G_, lambda e: e.affine_select(out=maskB, in_=maskB, compare_op=ALU.is_ge, fill=0.0, base=0,
                                       pattern=[[-16, 8], [0, 16]], channel_multiplier=1), [R("mask")], [R("mask")])

    for gh in range(32 // NG):
        g0 = gh * NG
        fw.barrier(self.R.values())
        fw.dma("sync", [(X8[:, ct], self.P[ct * 1024:(ct + 1) * 1024, g0 * 16:g0 * 16 + NG * 16].rearrange("(c s) w -> c s w", s=8))
                        for ct in range(8)], [self.res("P")], [R("X8")])
        fw.dma("sync", [(X8[0:32, 8], self.P[8192:8448, g0 * 16:g0 * 16 + NG * 16].rearrange("(c s) w -> c s w", s=8))],
               [self.res("P")], [R("X8")])
        rsm = R("sm")
        pairs = []
        for d in range(2):
            for half in range(2):
                ps_ = slice(half * 64, half * 64 + 64)
                pairs.append((sm[ps_, 0, d, :], self.lam_re[l, d, g0:g0 + NG, :].rearrange("g n -> n g")))
                pairs.append((sm[ps_, 1, d, :], self.lam_im[l, d, g0:g0 + NG, :].rearrange("g n -> n g")))
            pairs.append((sm[:, 2, d, :], self.log_dt[l, d:d + 1, g0:g0 + NG].partition_broadcast(128)[:, 0, :]))
        fw.dma("gpsimd", pairs, [], [rsm])
        rb = R("bc")
        fw.dma("sync", [(ba[0:64], self.b_re[l, g0:g0 + NG].rearrange("g n p -> n g p")),
                        (bb[64:128], self.b_re[l, g0:g0 + NG].rearrange("g n p -> n g p")),
                        (ba[64:128], self.b_im[l, g0:g0 + NG].rearrange("g n p -> n g p")),
                        (bb[0:64], self.b_im[l, g0:g0 + NG].rearrange("g n p -> n g p"))], [], [rb])
        for gi in range(NG):
            g = g0 + gi
            fw.dma("sync" if gi % 2 == 0 else "gpsimd",
                   [(Ca[0:64, gi, :], self.c_re[l, g].rearrange("p n -> n p")), (Cb[64:128, gi, :], self.c_re[l, g].rearrange("p n -> n p")),
                    (Ca[64:128, gi, :], self.c_im[l, g].rearrange("p n -> n p")), (Cb[0:64, gi, :], self.c_im[l, g].rearrange("p n -> n p"))],
                   [], [rb], sres=R("bc%d" % (gi % 2)))
        fw.dma("sync", [(Dcol[s_ * 16:(s_ + 1) * 16, :], self.s5_d[l, g0 * 16:g0 * 16 + NG * 16].rearrange("(g p) -> p g", p=16))
                        for s_ in range(8)], [], [R("Dcol")])
        S_ = lambda i: sm[:, i]
        fw.op("scalar", lambda e: e.activation(out=S_(2), in_=S_(2), func=AF.Exp), [rsm], [rsm])
        tt(V, S_(3), S_(0), S_(2), ALU.mult, [rsm], [rsm])
        tt(V, S_(4), S_(1), S_(2), ALU.mult, [rsm], [rsm])
        fw.op(V, lambda e: e.tensor_scalar(out=S_(4), in0=S_(4), scalar1=1.0 / TWO_PI, scalar2=None, op0=ALU.mult), [rsm], [rsm])
        rE = R("E")
        rt = R("tab")
        for d in range(2):
          for (kvx, TRe, TIm) in ((kv, Ere, Eim), (kvr, ErD, EiD)):
            tt(V, tA, bc(sm[:, 3, d, :].unsqueeze(2), [128, NG, 65]), bc(kvx.unsqueeze(1), [128, NG, 65]), ALU.mult, [rsm, R("kv")], [rt])
            fw.op("scalar", lambda e: e.activation(out=tA, in_=tA, func=AF.Exp), [rt], [rt])
            for which in range(2):
                tt(V, tB, bc(sm[:, 4, d, :].unsqueeze(2), [128, NG, 65]), bc(kvx.unsqueeze(1), [128, NG, 65]), ALU.mult, [rsm, R("kv")], [rt])
                if which == 1:
                    fw.op(V, lambda e: e.tensor_scalar(out=tB, in0=tB, scalar1=0.25, scalar2=None, op0=ALU.add), [rt], [rt])
                fw.op(V, lambda e: e.tensor_copy(out=tI, in_=tB), [rt], [rt])
                tt(V, tB, tB, tI, ALU.subtract, [rt], [rt])
                dst = TIm[:, d] if which == 0 else TRe[:, d]
                fw.op(V, lambda e, dst=dst: e.tensor_single_scalar(out=dst, in_=tB, scalar=0.5, op=ALU.is_gt), [rt], [rE])
                tt(V, tB, tB, dst, ALU.subtract, [rt, rE], [rt])
                fw.op(V, lambda e, dst=dst: e.tensor_single_scalar(out=dst, in_=tB, scalar=-0.5, op=ALU.is_lt), [rt], [rE])
                tt(V, tB, tB, dst, ALU.add, [rt, rE], [rt])
                fw.op("scalar", lambda e: e.activation(out=tB, in_=tB, func=AF.Sin, scale=6.283185), [rt], [rt])
                tt(V, dst, tB, tA, ALU.mult, [rt], [rE])
        def coef_(d):
            s = lambda i: sm[:, i, d, :]
            e1r, e1i = Ere[:, d, :, 1], Eim[:, d, :, 1]
            e64r, e64i = Ere[:, d, :, 64], Eim[:, d, :, 64]
            stt = lambda o, i0, c, i1: fw.op(V, lambda e: e.scalar_tensor_tensor(out=o, in0=i0, scalar=c, in1=i1, op0=ALU.add, op1=ALU.mult),
                                             [rsm], [rsm])
            ti_ = tI[:, :, 0]
            fw.op(V, lambda e: e.tensor_copy(out=ti_, in_=s(4)), [rsm], [rt])
            tt(V, s(20), s(4), ti_, ALU.subtract, [rsm, rt], [rsm])
            fw.op(V, lambda e: e.tensor_single_scalar(out=s(21), in_=s(20), scalar=0.5, op=ALU.is_gt), [rsm], [rsm])
            tt(V, s(20), s(20), s(21), ALU.subtract, [rsm], [rsm])
            fw.op(V, lambda e: e.tensor_single_scalar(out=s(21), in_=s(20), scalar=-0.5, op=ALU.is_lt), [rsm], [rsm])
            tt(V, s(20), s(20), s(21), ALU.add, [rsm], [rsm])
            fw.op(V, lambda e: e.tensor_scalar(out=s(20), in0=s(20), scalar1=3.14159265358979, scalar2=None, op0=ALU.mult), [rsm], [rsm])
            tt(V, s(21), s(20), s(20), ALU.mult, [rsm], [rsm])
            fw.op(V, lambda e: e.tensor_scalar(out=s(22), in0=s(21), scalar1=-1.0 / 39916800, scalar2=None, op0=ALU.mult), [rsm], [rsm])
            for c_ in (1.0 / 362880, -1.0 / 5040, 1.0 / 120, -1.0 / 6):
                stt(s(22), s(22), c_, s(21))
            stt(s(22), s(22), 1.0, s(20))
            fw.op(V, lambda e: e.tensor_scalar(out=s(23), in0=s(21), scalar1=1.0 / 479001600, scalar2=None, op0=ALU.mult), [rsm], [rsm])
            for c_ in (-1.0 / 3628800, 1.0 / 40320, -1.0 / 720, 1.0 / 24, -0.5):
                stt(s(23), s(23), c_, s(21))
            fw.op(V, lambda e: e.tensor_scalar(out=s(23), in0=s(23), scalar1=1.0, scalar2=None, op0=ALU.add), [rsm], [rsm])
            fw.op(V, lambda e: e.tensor_scalar(out=s(15), in0=s(3), scalar1=1.0 / 120, scalar2=None, op0=ALU.mult), [rsm], [rsm])
            for c_ in (1.0 / 24, 1.0 / 6, 0.5, 1.0):
                stt(s(15), s(15), c_, s(3))
            tt(V, s(16), s(22), s(23), ALU.mult, [rsm], [rsm])
            fw.op(V, lambda e: e.tensor_scalar(out=s(16), in0=s(16), scalar1=2.0, scalar2=None, op0=ALU.mult), [rsm], [rsm])
            tt(V, s(21), s(22), s(22), ALU.mult, [rsm], [rsm])
            fw.op(V, lambda e: e.tensor_scalar(out=s(21), in0=s(21), scalar1=2.0, scalar2=None, op0=ALU.mult), [rsm], [rsm])
            fw.op(V, lambda e: e.tensor_scalar(out=s(20), in0=s(21), scalar1=-1.0, scalar2=1.0, op0=ALU.mult, op1=ALU.add), [rsm], [rsm])
            tt(V, s(5), s(15), s(20), ALU.mult, [rsm], [rsm])
            tt(V, s(5), s(5), s(21), ALU.subtract, [rsm], [rsm])
            fw.op(V, lambda e: e.tensor_scalar(out=s(15), in0=s(15), scalar1=1.0, scalar2=None, op0=ALU.add), [rsm], [rsm])
            tt(V, s(6), s(15), s(16), ALU.mult, [rsm], [rsm])
            tt(V, s(15), s(0), s(0), ALU.mult, [rsm], [rsm])
            tt(V, s(16), s(1), s(1), ALU.mult, [rsm], [rsm])
            tt(V, s(15), s(15), s(16), ALU.add, [rsm], [rsm])
            fw.op(V, lambda e: e.reciprocal(out=s(7), in_=s(15)), [rsm], [rsm])
            tt(V, s(15), s(5), s(0), ALU.mult, [rsm], [rsm])
            tt(V, s(16), s(6), s(1), ALU.mult, [rsm], [rsm])
            tt(V, s(15), s(15), s(16), ALU.add, [rsm], [rsm])
            tt(V, s(8), s(15), s(7), ALU.mult, [rsm], [rsm])
            tt(V, s(15), s(6), s(0), ALU.mult, [rsm], [rsm])
            tt(V, s(16), s(5), s(1), ALU.mult, [rsm], [rsm])
            tt(V, s(15), s(15), s(16), ALU.subtract, [rsm], [rsm])
            tt(V, s(9), s(15), s(7), ALU.mult, [rsm], [rsm])
            fw.op(V, lambda e: e.tensor_scalar(out=s(10), in0=s(9), scalar1=sgn[:, 0:1], scalar2=None, op0=ALU.mult), [rsm, R("sgn")], [rsm])
            fw.op(V, lambda e: e.tensor_scalar(out=s(11), in0=s(9), scalar1=sgn[:, 1:2], scalar2=None, op0=ALU.mult), [rsm, R("sgn")], [rsm])
            tt(V, s(15), e64r, e64r, ALU.mult, [rE], [rsm])
            tt(V, s(16), e64i, e64i, ALU.mult, [rE], [rsm])
            tt(V, s(15), s(15), s(16), ALU.add, [rsm], [rsm])
            fw.op(V, lambda e: e.reciprocal(out=s(15), in_=s(15)), [rsm], [rsm])
            tt(V, s(12), e64r, s(15), ALU.mult, [rE, rsm], [rsm])
            tt(V, s(16), e64i, s(15), ALU.mult, [rE, rsm], [rsm])
            fw.op(V, lambda e: e.tensor_scalar(out=s(13), in0=s(16), scalar1=sgn[:, 1:2], scalar2=None, op0=ALU.mult), [rsm, R("sgn")], [rsm])
            fw.op(V, lambda e: e.tensor_scalar(out=s(14), in0=s(16), scalar1=sgn[:, 0:1], scalar2=None, op0=ALU.mult), [rsm, R("sgn")], [rsm])
            fw.op(V, lambda e: e.tensor_copy(out=s(17), in_=e1r), [rE], [rsm])
            fw.op(V, lambda e: e.tensor_scalar(out=s(18), in0=e1i, scalar1=sgn[:, 0:1], scalar2=None, op0=ALU.mult), [rE, R("sgn")], [rsm])
            fw.op(V, lambda e: e.tensor_scalar(out=s(19), in0=e1i, scalar1=sgn[:, 1:2], scalar2=None, op0=ALU.mult), [rE, R("sgn")], [rsm])
            B = lambda i: bc(sm[:, i, d, :].unsqueeze(2), [128, NG, 16])
            rB = R("BA")
            Ba, Bbs, Bpa, Bpbs = BA[d]
            tt(V, Ba, ba, B(8), ALU.mult, [rb, rsm], [rB])
            tt(V, Bpa, bb, B(10), ALU.mult, [rb, rsm], [rB])
            tt(V, Ba, Ba, Bpa, ALU.add, [rB], [rB])
            tt(V, Bbs, bb, B(8), ALU.mult, [rb, rsm], [rB])
            tt(V, Bpa, ba, B(11), ALU.mult, [rb, rsm], [rB])
            tt(V, Bbs, Bbs, Bpa, ALU.add, [rB], [rB])
            tt(V, Bpa, Ba, B(12), ALU.mult, [rB, rsm], [rB])
            tt(V, Bpbs, Bbs, B(13), ALU.mult, [rB, rsm], [rB])
            tt(V, Bpa, Bpa, Bpbs, ALU.add, [rB], [rB])
            tt(V, Bpbs, Bbs, B(12), ALU.mult, [rB, rsm], [rB])
            tt(V, gsc[:, 0:NG * 16].rearrange("p (a b) -> p a b", a=NG), Ba, B(14), ALU.mult, [rB, rsm], [R("gsc")])
            tt(V, Bpbs, Bpbs, gsc[:, 0:NG * 16].rearrange("p (a b) -> p a b", a=NG), ALU.add, [rB, R("gsc")], [rB])
            fw.op(V, lambda e: e.tensor_scalar(out=Bbs, in0=Bbs, scalar1=sgn[:, 0:1], scalar2=None, op0=ALU.mult), [rB, R("sgn")], [rB])
            fw.op(V, lambda e: e.tensor_scalar(out=Bpbs, in0=Bpbs, scalar1=sgn[:, 0:1], scalar2=None, op0=ALU.mult), [rB, R("sgn")], [rB])
            Cas, Cbn = CA[d]
            rC = R("CA")
            tmpc = gsc[:, 256:256 + NG * 16].rearrange("p (a b) -> p a b", a=NG)
            tt(V, Cas, Ca, B(17), ALU.mult, [rb, rsm], [rC])
            tt(V, tmpc, Cb, B(18), ALU.mult, [rb, rsm], [R("gsc")])
            tt(V, Cas, Cas, tmpc, ALU.add, [rC, R("gsc")], [rC])
            tt(V, Cbn, Cb, B(17), ALU.mult, [rb, rsm], [rC])
            tt(V, tmpc, Ca, B(19), ALU.mult, [rb, rsm], [R("gsc")])
            tt(V, Cbn, Cbn, tmpc, ALU.add, [rC, R("gsc")], [rC])
            fw.op(V, lambda e: e.tensor_scalar(out=Cas, in0=Cas, scalar1=sgn[:, 1:2], scalar2=None, op0=ALU.mult), [rC, R("sgn")], [rC])
            fw.op(V, lambda e: e.tensor_scalar(out=Cbn, in0=Cbn, scalar1=-1.0, scalar2=None, op0=ALU.mult), [rC], [rC])
        for d_ in range(2):
            coef_(d_)
        rAK = R("AK")
        fw.op(V, lambda e: e.tensor_copy(out=AKr[:, 0], in_=Ere[:, :, :, 64]), [rE], [rAK])
        fw.op(V, lambda e: e.tensor_copy(out=AKs[:, 0], in_=Eim[:, :, :, 64]), [rE], [rAK])
        for k in range(1, 8):
            t15, t16 = sm[:, 15], sm[:, 16]
            tt(V, t15, AKr[:, k - 1], AKr[:, k - 1], ALU.mult, [rAK], [rsm])
            tt(V, t16, AKs[:, k - 1], AKs[:, k - 1], ALU.mult, [rAK], [rsm])
            tt(V, AKr[:, k], t15, t16, ALU.subtract, [rsm], [rAK])
            tt(V, t15, AKr[:, k - 1], AKs[:, k - 1], ALU.mult, [rAK], [rsm])
            fw.op(V, lambda e, k=k: e.tensor_scalar(out=AKs[:, k], in0=t15, scalar1=2.0, scalar2=None, op0=ALU.mult), [rsm], [rAK])
        fw.op(V, lambda e: e.tensor_scalar(out=AKs, in0=AKs, scalar1=sgn[:, 1:2], scalar2=None, op0=ALU.mult), [rAK, R("sgn")], [rAK])

        fw.barrier(self.R.values())
        def grp_(gi):
            pub = [pb[0][:].bitcast(BF16), pb[1][:].bitcast(BF16)]
            for ct in range(9):
                fw.op(G_ if ct % 2 == 0 else "scalar",
                      (lambda e, ct=ct: e.tensor_copy(out=Xg[:, ct, :].rearrange("p (a b) -> p a b", a=8), in_=X8[:, ct, :, gi * 16:(gi + 1) * 16]))
                      if ct % 2 == 0 else
                      (lambda e, ct=ct: e.activation(out=Xg[:, ct, :].rearrange("p (a b) -> p a b", a=8), in_=X8[:, ct, :, gi * 16:(gi + 1) * 16],
                                                     func=AF.Copy)), [R("X8")], [R("Xg")])
            for ct in range(9):
                npart = 128 if ct < 8 else 32
                bank, off = (0, ct * 128) if ct < 8 else (1, 0)
                fw.op("tensor", lambda e, ct=ct, npart=npart, bank=bank, off=off: e.transpose(
                    pub[bank][:, off:off + npart], Xg[0:npart, ct, :], self.identB[0:npart, 0:npart]),
                    [R("Xg"), rI], [rpb[bank]])
            fw.op("scalar", lambda e: e.activation(out=U8[:, 0:1024], in_=pub[0][:, 0:1024], func=AF.Copy), [rpb[0]], [R("U8")])
            fw.op("scalar", lambda e: e.activation(out=U8[:, 1024:1056], in_=pub[1][:, 0:32], func=AF.Copy), [rpb[1]], [R("U8")])
            U8v = U8.rearrange("p (c j) -> p c j", j=8)
            for d in range(2):
                Ba, Bbs, Bpa, Bpbs = BA[d]
                Cas, Cbn = CA[d]
                rM = R("M%d" % d)
                if d == 0:
                    eM2r, eM2i = ErD[:, d, gi, 1:65], EiD[:, d, gi, 1:65]
                    eM3r, eM3i = Ere[:, d, gi, 0:64], Eim[:, d, gi, 0:64]
                    ePr, ePi = ErD[:, d, gi, 1:9], EiD[:, d, gi, 1:9]
                else:
                    eM2r, eM2i = Ere[:, d, gi, 0:64], Eim[:, d, gi, 0:64]
                    eM3r, eM3i = ErD[:, d, gi, 1:65], EiD[:, d, gi, 1:65]
                    ePr, ePi = Ere[:, d, gi, 56:64], Eim[:, d, gi, 56:64]
                b64 = lambda ap: bc(ap.unsqueeze(2), [128, 64, 16])
                w64 = lambda ap: bc(ap.unsqueeze(1), [128, 64, 16])
                tmp = gsc[:, 0:1024].rearrange("p (a b) -> p a b", a=64)
                rg = R("gsc")
                tt(V, M2T[d], b64(eM2r), w64(Ba[:, gi, :]), ALU.mult, [rE, R("BA")], [rM])
                tt(G_, tmp, b64(eM2i), w64(Bbs[:, gi, :]), ALU.mult, [rE, R("BA")], [rg])
                tt(V, M2T[d], M2T[d], tmp, ALU.add, [rM, rg], [rM])
                tt(V, M3f[d], b64(eM3r), w64(Cas[:, gi, :]), ALU.mult, [rE, R("CA")], [rM])
                tt(G_, tmp, b64(eM3i), w64(Cbn[:, gi, :]), ALU.mult, [rE, R("CA")], [rg])
                tt(V, M3f[d], M3f[d], tmp, ALU.add, [rM, rg], [rM])
                tmp8 = gsc[:, 0:128].rearrange("p (a b) -> p a b", a=8)
                tt(V, M2Tp[d], bc(ePr.unsqueeze(2), [128, 8, 16]), bc(Bpa[:, gi, :].unsqueeze(1), [128, 8, 16]), ALU.mult, [rE, R("BA")], [rM])
                tt(V, tmp8, bc(ePi.unsqueeze(2), [128, 8, 16]), bc(Bpbs[:, gi, :].unsqueeze(1), [128, 8, 16]), ALU.mult, [rE, R("BA")], [rg])
                tt(V, M2Tp[d], M2Tp[d], tmp8, ALU.add, [rM, rg], [rM])
                fw.op("scalar", lambda e, d=d: e.activation(out=M3b[d].rearrange("p a b -> p (a b)"), in_=M3f[d].rearrange("p a b -> p (a b)"),
                                                           func=AF.Copy), [rM], [R("M3b%d" % d)])
                for j in range(8):
                    bank = 2 + j // 4
                    fw.op("tensor", lambda e, d=d, j=j, bank=bank: e.transpose(
                        pb[bank][:, (j % 4) * 128:(j % 4 + 1) * 128], M2T[d][:, j * 8:(j + 1) * 8, :].rearrange("p a b -> p (a b)"),
                        self.identF[:]), [rM, rI], [rpb[bank]])
                for hb_ in range(2):
                    fw.op("scalar" if hb_ == 0 else V, (lambda e, d=d, hb_=hb_: e.activation(
                        out=M2b[d][:, hb_ * 4:(hb_ + 1) * 4, :].rearrange("p a b -> p (a b)"), in_=pb[2 + hb_][:], func=AF.Copy))
                        if hb_ == 0 else (lambda e, d=d, hb_=hb_: e.tensor_copy(
                            out=M2b[d][:, hb_ * 4:(hb_ + 1) * 4, :].rearrange("p a b -> p (a b)"), in_=pb[2 + hb_][:])),
                        [rpb[2 + hb_]], [R("M2b%d" % d)])
                for hb_ in range(2):
                    fw.op("tensor", lambda e, d=d, hb_=hb_: e.matmul(
                        pb[2 + hb_][:], lhsT=M2Tp[d].rearrange("p a b -> p (a b)"),
                        rhs=M3f[d][:, hb_ * 32:(hb_ + 1) * 32, :].rearrange("p a b -> p (a b)"), start=True, stop=True),
                        [rM], [rpb[2 + hb_]])
                if d == 0:
                    blk = pb[2][:, 0:128].rearrange("p (a b) -> p a b", a=8)
                    t8 = gsc[:, 0:128].rearrange("p (a b) -> p a b", a=8)
                    tt(V, t8, blk, maskF, ALU.mult, [rpb[2], R("mask")], [rg])
                    fw.op(V, lambda e: e.scalar_tensor_tensor(out=gsc[:, 0:128], in0=self.identF[:], scalar=Dcol[:, gi:gi + 1],
                                                              in1=gsc[:, 0:128], op0=ALU.mult, op1=ALU.add), [rg, rI, R("Dcol")], [rg])
                    fw.op(V, lambda e, d=d: e.tensor_copy(out=M1b[d][:, 0, :], in_=gsc[:, 0:128]), [rg], [R("M1b%d" % d)])
                    fw.op("scalar", lambda e, d=d: e.activation(out=M1b[d][:, 1:4, :].rearrange("p a b -> p (a b)"), in_=pb[2][:, 128:512],
                                                               func=AF.Copy), [rpb[2]], [R("M1b%d" % d)])
                    fw.op("scalar", lambda e, d=d: e.activation(out=M1b[d][:, 4:8, :].rearrange("p a b -> p (a b)"), in_=pb[3][:],
                                                               func=AF.Copy), [rpb[3]], [R("M1b%d" % d)])
                else:
                    blk = pb[3][:, 384:512].rearrange("p (a b) -> p a b", a=8)
                    tt(V, M1b[d][:, 7, :].rearrange("p (a b) -> p a b", a=8), blk, maskB, ALU.mult, [rpb[3], R("mask")], [R("M1b%d" % d)])
                    fw.op("scalar", lambda e, d=d: e.activation(out=M1b[d][:, 0:4, :].rearrange("p a b -> p (a b)"), in_=pb[2][:],
                                                               func=AF.Copy), [rpb[2]], [R("M1b%d" % d)])
                    fw.op("scalar", lambda e, d=d: e.activation(out=M1b[d][:, 4:7, :].rearrange("p a b -> p (a b)"), in_=pb[3][:, 0:384],
                                                               func=AF.Copy), [rpb[3]], [R("M1b%d" % d)])
                rA = R("Ak%d" % d)
                for k in range(8):
                    fw.op(G_, lambda e, d=d, k=k: e.tensor_scalar(out=Ak[d][:, k, :], in0=self.identF[:], scalar1=AKr[:, k, d, gi:gi + 1],
                                                                 scalar2=None, op0=ALU.mult), [rI, rAK], [rA])
                    fw.op(V, lambda e, d=d, k=k: e.scalar_tensor_tensor(out=Ak[d][:, k, :], in0=swapM, scalar=AKs[:, k, d, gi:gi + 1],
                                                                       in1=Ak[d][:, k, :], op0=ALU.mult, op1=ALU.add),
                          [R("swapM"), rAK, rA], [rA])
                ps = pb[4]
                if d == 0:
                    for j in range(8):
                        fw.op("tensor", lambda e, d=d, j=j: e.matmul(ps[:, 0:132], lhsT=M2b[d][:, j, :], rhs=U8v[:, :, j],
                                                                    start=(j == 0), stop=(j == 7)), [R("M2b%d" % d), R("U8")], [rpb[4]])
                else:
                    for j in range(8):
                        fw.op("tensor", lambda e, d=d, j=j: e.matmul(ps[:, 0:128], lhsT=M2b[d][:, j, :], rhs=U8v[:, 4:132, j],
                                                                    start=(j == 0), stop=(j == 7)), [R("M2b%d" % d), R("U8")], [rpb[4]])
                    for j in range(8):
                        fw.op("tensor", lambda e, d=d, j=j: e.matmul(ps[:, 128:132], lhsT=M2b[d][:, j, :], rhs=U8v[:, 0:4, j],
                                                                    start=False, stop=(j == 7), skip_group_check=True),
                              [R("M2b%d" % d), R("U8")], [rpb[4]])
                rP = R("P%d" % d)
                fw.op(V, lambda e, d=d: e.tensor_copy(out=Pst[d], in_=ps[:, 0:132]), [rpb[4]], [rP])
                for k in range(8):
                    sft = 1 << k
                    if d == 0:
                        o_sl, i_sl = slice(sft, 132), slice(0, 132 - sft)
                    else:
                        o_sl, i_sl = slice(0, 132 - sft), slice(sft, 132)
                    fw.op("tensor", lambda e, d=d, k=k, o_sl=o_sl, i_sl=i_sl: e.matmul(ps[:, o_sl], lhsT=Ak[d][:, k, :], rhs=Pst[d][:, i_sl],
                                                                                     start=True, stop=True), [rA, rP], [rpb[4]])
                    fw.op(V, lambda e, d=d, o_sl=o_sl: e.tensor_tensor(out=Pst[d][:, o_sl], in0=Pst[d][:, o_sl], in1=ps[:, o_sl], op=ALU.add),
                          [rP, rpb[4]], [rP])
                rH = R("HH%d" % d)
                fw.op(G_, lambda e, d=d: e.memset(HHb[d], 0.0), [], [rH])
                if d == 0:
                    fw.op(V, lambda e, d=d: e.tensor_copy(out=HHb[d][:, 1:132], in_=Pst[d][:, 0:131]), [rP, rH], [rH])
                else:
                    fw.op(V, lambda e, d=d: e.tensor_copy(out=HHb[d][:, 0:131], in_=Pst[d][:, 1:132]), [rP, rH], [rH])
            for jt in range(8):
                bank = 5 + jt // 3
                yo_ = pb[bank][:, (jt % 3) * 132:(jt % 3 + 1) * 132]
                ops = []
                for js in range(0, jt + 1):
                    ops.append((yo_, M1b[0][:, jt - js, :], U8v[:, :, js], [R("M1b0"), R("U8")]))
                for js in range(jt, 8):
                    ops.append((yo_, M1b[1][:, 7 - (js - jt), :], U8v[:, :, js], [R("M1b1"), R("U8")]))
                ops.append((yo_, M3b[0][:, jt, :], HHb[0][:, 0:132], [R("M3b0"), R("HH0")]))
                ops.append((yo_[:, 4:132], M3b[1][:, jt, :], HHb[1][:, 0:128], [R("M3b1"), R("HH1")]))
                ops.append((yo_[:, 0:4], M3b[1][:, jt, :], HHb[1][:, 128:132], [R("M3b1"), R("HH1")]))
                for n_, (o_, lt, rh, rd) in enumerate(ops):
                    fw.op("tensor", lambda e, o_=o_, lt=lt, rh=rh, n_=n_, last=(n_ == len(ops) - 1): e.matmul(
                        o_, lhsT=lt, rhs=rh, start=(n_ == 0), stop=last, skip_group_check=True), rd, [rpb[bank]])
            gy8v = gy8.rearrange("p (c j) -> p j c", j=8)
            gscv = gsc[:, 0:1056].rearrange("p (j c) -> p j c", j=8)
            rg = R("gsc")
            for b3 in range(3):
                njt = 3 if b3 < 2 else 2
                src_ = pb[5 + b3][:, 0:njt * 132].rearrange("p (j c) -> p j c", j=njt)
                tmp_ = gscv[:, b3 * 3:b3 * 3 + njt, :]
                dst_ = gy8v[:, b3 * 3:b3 * 3 + njt, :]
                rp_ = rpb[5 + b3]
                fw.op("scalar", lambda e, src_=src_, tmp_=tmp_: e.activation(out=tmp_, in_=src_, func=AF.Square), [rp_], [rg])
                fw.op(V, lambda e, tmp_=tmp_: e.tensor_scalar(out=tmp_, in0=tmp_, scalar1=0.044715, scalar2=1.0, op0=ALU.mult, op1=ALU.add),
                      [rg], [rg])
                tt(V, tmp_, tmp_, src_, ALU.mult, [rg, rp_], [rg])
                fw.op("scalar", lambda e, tmp_=tmp_: e.activation(out=tmp_, in_=tmp_, func=AF.Sigmoid, scale=1.5957691216), [rg], [rg])
                tt(V, dst_, tmp_, src_, ALU.mult, [rg, rp_], [R("gy8")])
            for ct in range(9):
                npart = 128 if ct < 8 else 32
                bank, off = (0, ct * 128) if ct < 8 else (1, 0)
                fw.op("tensor", lambda e, ct=ct, npart=npart, bank=bank, off=off: e.transpose(
                    pub[bank][0:npart, off:off + 128], gy8[:, ct * 128:ct * 128 + npart], self.identB[:]),
                    [R("gy8"), rI], [rpb[bank]])
                fw.op("scalar" if ct % 2 == 0 else V,
                      (lambda e, ct=ct, npart=npart, bank=bank, off=off: e.activation(
                          out=Ytok[0:npart, ct, :, gi * 16:(gi + 1) * 16], in_=pub[bank][0:npart, off:off + 128].rearrange("p (a b) -> p a b", a=8),
                          func=AF.Copy)) if ct % 2 == 0 else
                      (lambda e, ct=ct, npart=npart, bank=bank, off=off: e.tensor_copy(
                          out=Ytok[0:npart, ct, :, gi * 16:(gi + 1) * 16], in_=pub[bank][0:npart, off:off + 128].rearrange("p (a b) -> p a b", a=8))),
                      [rpb[bank]], [R("Ytok")])
        for gi_ in range(NG):
            grp_(gi_)
        fw.dma("sync", [(self.gy[ct * 1024:(ct + 1) * 1024, g0 * 16:g0 * 16 + NG * 16].rearrange("(c s) w -> c s w", s=8), Ytok[:, ct])
                        for ct in range(8)], [R("Ytok")], [self.res("gy")])
        fw.dma("sync", [(self.gy[8192:8448, g0 * 16:g0 * 16 + NG * 16].rearrange("(c s) w -> c s w", s=8), Ytok[0:32, 8])],
               [R("Ytok")], [self.res("gy")])
    print("s5 arena end", self.aoff)


Prog.s5 = _s5
```

```python
from contextlib import ExitStack
import numpy as np
import concourse.bass as bass
import concourse.mybir as mybir
from concourse.bass_utils import run_bass_kernel_spmd

F32 = mybir.dt.float32
BF16 = mybir.dt.bfloat16
I32 = mybir.dt.int32
AF = mybir.ActivationFunctionType
ALU = mybir.AluOpType
AX = mybir.AxisListType

D = 1024
L = 8192
CT = 256
NT = L + CT
NTILE = NT // 128
INW = 2592
DEPTH = 4
EPS = 1e-6


import heapq


class Res:
    __slots__ = ("name", "w", "r", "dsem", "dcount", "last_dma")

    def __init__(self, name):
        self.name = name
        self.w = None
        self.r = []
        self.dsem = None
        self.dcount = 0
        self.last_dma = None


class _ProbeInst:
    def then_inc(self, *a, **k):
        return self


class _Probe:
    def __init__(self):
        self.name = None
        self.args = None
        self.kw = None

    def __getattr__(self, name):
        def f(*args, **kw):
            self.name, self.args, self.kw = name, args, kw
            return _ProbeInst()
        return f


def _fsize(ap):
    n = 1
    for d_ in ap.shape[1:]:
        n *= d_
    return n


class FW:
    SEM_LIMIT = 24000
    HOP = 1.2

    def __init__(self, nc, stack, schedule=True):
        self.nc = nc
        self.stack = stack
        self.schedule = schedule
        self.nsem = 0
        self.nodes = []
        self.bar = None
        self.bar_start = 0
        self.engnames = ("tensor", "vector", "scalar", "gpsimd", "sync")

    def new_sem(self, name):
        self.nsem += 1
        return self.stack.enter_context(self.nc.semaphore("%s_%d" % (name, self.nsem)))

    def sb(self, name, shape, dt):
        return self.stack.enter_context(self.nc.sbuf_tensor(name, list(shape), dt))

    def ps(self, name, shape, dt):
        return self.stack.enter_context(self.nc.psum_tensor(name, list(shape), dt))

    def _deps(self, reads, writes):
        deps = set()
        for r in reads:
            if r.w is not None:
                deps.add(r.w)
        for w in writes:
            if w.w is not None:
                deps.add(w.w)
            deps.update(w.r)
        if self.bar is not None:
            deps.add(self.bar)
        return deps

    def _cost(self, engname, fn):
        p = _Probe()
        try:
            fn(p)
            nm, kw, args = p.name, p.kw, p.args
            if nm == "matmul":
                rhs = kw["rhs"]
                n = _fsize(rhs)
                passes = 4 if rhs.dtype == F32 else 1
                return max(64, n) * passes / 2400.0 + 0.03
            if nm == "transpose" and engname == "tensor":
                return 128 / 2400.0 + 0.05
            ap = kw.get("out", None)
            if ap is None:
                ap = args[0]
            n = _fsize(ap)
            if engname == "vector":
                return n * 1.3 / 960.0 + 0.12
            if engname == "scalar":
                return n / 1200.0 + 0.25
            return n * 2.0 / 1200.0 + 0.3
        except Exception:
            return 0.5

    def op(self, engname, fn, reads=(), writes=()):
        nid = len(self.nodes)
        deps = self._deps(reads, writes)
        self.nodes.append(dict(id=nid, eng=engname, kind="op", fn=fn, deps=deps, cost=self._cost(engname, fn)))
        for r in reads:
            r.r.append(nid)
        for w in writes:
            w.w = nid
            w.r = []
        return nid

    def dma(self, qname, pairs, reads, writes, sres=None):
        sres = sres or writes[0]
        qt = "sw" if qname == "gpsimd" else "hw"
        if not isinstance(sres.dsem, dict):
            sres.dsem = {}
        st_ = sres.dsem.get(qt)
        if st_ is None or st_[1] >= 16 * 3000:
            st_ = [self.new_sem("d%s_%s" % (qt, sres.name)), 0, None]
            sres.dsem[qt] = st_
        nid = len(self.nodes)
        deps = self._deps(reads, writes)
        if st_[2] is not None:
            deps.add(st_[2])
        nbytes = 0
        for pr in pairs:
            if callable(pr):
                nbytes += 8 << 20
                continue
            (o, i) = pr
            n = 1
            for d_ in o.shape:
                n *= d_
            nbytes += n * (4 if o.dtype in (F32, I32) else 2)
        st_[1] += 16 * len(pairs)
        self.nodes.append(dict(id=nid, eng=qname, kind="dma", pairs=pairs, deps=deps, cost=0.08 * len(pairs), nbytes=nbytes,
                               dsem=st_[0], dval=st_[1]))
        st_[2] = nid
        for r in reads:
            r.r.append(nid)
        for w in writes:
            w.w = nid
            w.r = []
        return nid

    def barrier(self, _unused=None):
        nid = len(self.nodes)
        deps = set(range(self.bar_start, nid))
        self.nodes.append(dict(id=nid, eng=None, kind="bar", deps=deps, cost=0.0))
        self.bar = nid
        self.bar_start = nid

    def _simulate(self):
        nodes = self.nodes
        n = len(nodes)
        fin = [0.0] * n
        start = [0.0] * n
        if not self.schedule:
            order = {e: [] for e in self.engnames}
            for nd in nodes:
                if nd["eng"] is not None:
                    order[nd["eng"]].append(nd["id"])
            return order
        children = [[] for _ in range(n)]
        rem = [0] * n
        for nd in nodes:
            rem[nd["id"]] = len(nd["deps"])
            for d_ in nd["deps"]:
                children[d_].append(nd["id"])
        ready = [0.0] * n
        heap = []
        for nd in nodes:
            if rem[nd["id"]] == 0:
                heapq.heappush(heap, (0.0, nd["id"]))
        efree = {e: 0.0 for e in self.engnames}
        dma_free = 0.0
        order = {e: [] for e in self.engnames}
        done = 0
        while heap:
            rt, nid = heapq.heappop(heap)
            nd = nodes[nid]
            e = nd["eng"]
            if e is None:
                st = rt
                f = rt
            else:
                st = max(rt, efree[e])
                efree[e] = st + nd["cost"]
                order[e].append((st, nid))
                if nd["kind"] == "dma":
                    t0 = max(st + nd["cost"], dma_free)
                    dur = nd["nbytes"] / 150e3
                    dma_free = t0 + dur
                    f = t0 + dur + 2.0
                else:
                    f = st + nd["cost"]
            start[nid] = st
            fin[nid] = f
            done += 1
            for c in children[nid]:
                hop = 0.0 if (nodes[c]["eng"] == e and e == "tensor") else self.HOP
                if nodes[c]["kind"] == "bar" or nd["kind"] == "bar":
                    hop = 0.0
                ready[c] = max(ready[c], f + hop)
                rem[c] -= 1
                if rem[c] == 0:
                    heapq.heappush(heap, (ready[c], c))
        assert done == n, (done, n)
        self.sim_time = max(fin) if fin else 0.0
        out = {}
        for e in self.engnames:
            lst = sorted(order[e])
            out[e] = [nid for (_, nid) in lst]
        return out

    def finish(self, final_res):
        nc = self.nc
        nodes = self.nodes
        fdeps = set(r.w for r in final_res if r.w is not None)
        nid = len(nodes)
        nodes.append(dict(id=nid, eng="sync", kind="waitonly", deps=fdeps | set(range(self.bar_start, nid)), cost=0.0))
        order = self._simulate()
        tok = {}
        for e in self.engnames:
            sem = self.new_sem("prog_" + e)
            cnt = 0
            for nid_ in order[e]:
                nd = nodes[nid_]
                if nd["kind"] == "op":
                    if cnt >= self.SEM_LIMIT:
                        sem = self.new_sem("prog_" + e)
                        cnt = 0
                    cnt += 1
                    tok[nid_] = (sem, cnt, e)
                    nd["sem"] = sem
                elif nd["kind"] == "dma":
                    tok[nid_] = (nd["dsem"], nd["dval"], "dma")
        bartok = {}
        for nd in nodes:
            if nd["kind"] == "bar":
                best = {}
                for d_ in nd["deps"]:
                    if nodes[d_]["kind"] == "bar":
                        for k, v in bartok[d_].items():
                            if k not in best or best[k][1] < v[1]:
                                best[k] = v
                    elif d_ in tok:
                        s_, v_, en = tok[d_]
                        k = id(s_)
                        if k not in best or best[k][1] < v_:
                            best[k] = (s_, v_, "bar")
                bartok[nd["id"]] = best
        selfwait = {"vector": True, "scalar": True, "gpsimd": True, "tensor": False, "sync": False}
        progs = {}
        for e in self.engnames:
            waited = {}
            prog = []
            for nid_ in order[e]:
                nd = nodes[nid_]
                best = {}
                for d_ in nd["deps"]:
                    if nodes[d_]["kind"] == "bar":
                        items = bartok[d_].values()
                    elif d_ in tok:
                        items = [tok[d_]]
                    else:
                        items = []
                    for (s_, v_, en) in items:
                        if en == e and not selfwait[e]:
                            continue
                        k = id(s_)
                        if waited.get(k, 0) >= v_:
                            continue
                        if k not in best or best[k][1] < v_:
                            best[k] = (s_, v_)
                for k, (s_, v_) in best.items():
                    waited[k] = v_
                    prog.append(("wait", s_, v_))
                if nd["kind"] == "op":
                    prog.append(("op", nd["fn"], nd["sem"]))
                elif nd["kind"] == "dma":
                    for pr in nd["pairs"]:
                        if callable(pr):
                            prog.append(("cdma", pr, None, nd["dsem"]))
                        else:
                            prog.append(("dma", pr[0], pr[1], nd["dsem"]))
            progs[e] = prog
        self.progs = progs

        def run(e, prog):
            for it in prog:
                if it[0] == "wait":
                    e.wait_ge(it[1], it[2])
                elif it[0] == "op":
                    it[1](e).then_inc(it[2], 1)
                elif it[0] == "cdma":
                    it[1](e).then_inc(it[3], 16)
                else:
                    e.dma_start(out=it[1], in_=it[2]).then_inc(it[3], 16)

        with nc.allow_non_contiguous_dma(reason="small param layouts"), nc.Block() as block:
            @block.tensor
            def _(e):
                run(e, progs["tensor"])

            @block.vector
            def _(e):
                run(e, progs["vector"])

            @block.scalar
            def _(e):
                run(e, progs["scalar"])

            @block.gpsimd
            def _(e):
                run(e, progs["gpsimd"])

            @block.sync
            def _(e):
                run(e, progs["sync"])


class Prog:
    def __init__(self, depth=DEPTH, stub_s5=False, stub_gla=False, schedule=True):
        self.depth = depth
        self.schedule = schedule
        self.stub_s5 = stub_s5
        self.stub_gla = stub_gla
        nc = self.nc = bass.Bass("TRN2", target_bir_lowering=False)
        di = lambda n, s, dt=F32: nc.dram_tensor(n, list(s), dt, kind="ExternalInput").ap()
        ds = lambda n, s, dt=F32: nc.dram_tensor(n, list(s), dt, kind="Internal").ap()
        self.x = di("x", [L, D])
        self.ctx = di("ctx", [CT, D])
        self.cc = di("cc", [2, D])
        self.norm_g = di("norm_g", [DEPTH, D])
        self.w_mod = di("w_mod", [DEPTH, D, 3 * D])
        self.b_mod = di("b_mod", [DEPTH, 3 * D])
        self.w_in = di("w_in", [DEPTH, D, INW])
        self.lam_re = di("s5_lam_re", [DEPTH, 2, 32, 64])
        self.lam_im = di("s5_lam_im", [DEPTH, 2, 32, 64])
        self.log_dt = di("s5_log_dt", [DEPTH, 2, 32])
        self.b_re = di("s5_b_re", [DEPTH, 32, 64, 16])
        self.b_im = di("s5_b_im", [DEPTH, 32, 64, 16])
        self.c_re = di("s5_c_re", [DEPTH, 32, 16, 64])
        self.c_im = di("s5_c_im", [DEPTH, 32, 16, 64])
        self.s5_d = di("s5_d", [DEPTH, 512])
        self.w_glu = di("s5_w_glu", [DEPTH, 512, 512])
        self.b_glu = di("s5_b_glu", [DEPTH, 512])
        self.w_gate = di("gla_w_gate", [DEPTH, 2, 16, 256])
        self.b_gate = di("gla_b_gate", [DEPTH, 2, 256])
        self.gnorm = di("gla_norm_g", [DEPTH, 128])
        self.w_out = di("w_out", [DEPTH, D, D])
        self.final_norm = di("final_norm", [1, D])
        self.out = nc.dram_tensor("out", [L, D], F32, kind="ExternalOutput").ap()
        self.xs = [ds("xs0", [NT, D]), ds("xs1", [NT, D])]
        self.P = ds("P", [NT, INW], BF16)
        import os
        if os.environ.get("DEBUG_GY"):
            self.gy = nc.dram_tensor("gy", [NT, 512], BF16, kind="ExternalOutput").ap()
        else:
            self.gy = ds("gy", [NT, 512], BF16)
        self.yg = ds("yg", [NT, 512], BF16)
        self.R = {}
        with ExitStack() as st:
            self.fw = FW(nc, st, schedule=self.schedule)
            self.alloc()
            self.consts()
            for l in range(depth):
                self.weights_in(l)
                self.modulation(l)
                self.phase1(l)
                if stub_s5:
                    self.s5_stub(l)
                else:
                    self.fw.barrier(self.R.values())
                    self.s5(l)
                    self.fw.barrier(self.R.values())
                if stub_gla:
                    self.gla_stub(l)
                else:
                    self.fw.barrier(self.R.values())
                    self.gla(l)
                    self.fw.barrier(self.R.values())
                self.weights_out(l)
                self.phase3(l)
            self.fw.finish([self.res("out")])

    def res(self, name):
        if name not in self.R:
            self.R[name] = Res(name)
        return self.R[name]

    def view(self, shape, dt):
        n = 1
        for d_ in shape[1:]:
            n *= d_
        words = n if dt in (F32, I32) else (n + 1) // 2
        ap = self.arena[:, self.aoff:self.aoff + words]
        self.aoff += words
        assert self.aoff <= self.NW, (self.aoff, self.NW)
        if dt != F32:
            ap = ap.bitcast(dt)
        if len(shape) == 3:
            ap = ap.rearrange("p (a b) -> p a b", a=shape[1])
        elif len(shape) == 4:
            ap = ap.rearrange("p (a b c) -> p a b c", a=shape[1], b=shape[2])
        return ap

    def alloc(self):
        fw = self.fw
        self.NW = 52600
        self.arena = fw.sb("arena", [128, self.NW], F32)
        self.aoff = 0
        self.identF = fw.sb("identF", [128, 128], F32)
        self.identB = fw.sb("identB", [128, 128], BF16)
        self.ccT = fw.sb("ccT", [128, 8, 2], F32)
        self.st1 = [fw.sb("st1_%d" % i, [128, 4], F32) for i in range(2)]
        self.pb = [fw.ps("pb%d" % i, [128, 512], F32) for i in range(8)]
        v = self.view
        self.scB = v([128, 8, 2, 128], F32)
        self.modb = v([128, 2, 3 * D], F32)
        self.bglub = v([128, 512], F32)
        self.fnb = v([128, D], F32)
        self.phase_base = self.aoff
        self.winb = v([128, 8, INW], BF16)
        self.woutb = v([128, 8, D], BF16)
        self.wglub = v([128, 4, 512], BF16)
        self.wst = v([128, 6144], F32)
        self.xt = [v([128, D], F32) for i in range(2)]
        self.xn = [v([128, D], F32) for i in range(2)]
        self.yo = [v([128, D], F32) for i in range(2)]
        self.hb = [v([128, D], BF16) for i in range(2)]
        self.mix = self.hb
        self.hT = [v([128, 8, 128], BF16) for i in range(2)]
        self.mixT = self.hT
        self.pj = [v([128, INW], BF16) for i in range(2)]
        self.g3 = [v([128, 2048], BF16) for i in range(2)]
        self.gyT = [v([128, 4, 128], BF16) for i in range(2)]
        self.t3 = [v([128, 512], F32) for i in range(2)]
        self.dense_end = self.aoff
        print("arena dense end", self.dense_end, "phase_base", self.phase_base)

    def consts(self):
        fw = self.fw
        identF, identB = self.identF, self.identB
        rI = self.res("ident")
        fw.op("gpsimd", lambda e: e.memset(identF[:], 0.0), [], [rI])
        fw.op("gpsimd", lambda e: e.affine_select(out=identF[:], in_=identF[:], compare_op=ALU.not_equal, fill=1.0,
                                                 base=0, pattern=[[-1, 128]], channel_multiplier=1), [rI], [rI])
        fw.op("gpsimd", lambda e: e.tensor_copy(out=identB[:], in_=identF[:]), [rI], [rI])
        rc = self.res("cc")
        ccT, scB = self.ccT, self.scB
        fw.dma("sync", [(ccT[:, :, j], self.cc[j, :].rearrange("(k p) -> p k", p=128)) for j in range(2)], [], [rc])
        fw.op("scalar", lambda e: e.activation(out=ccT[:], in_=ccT[:], func=AF.Silu), [rc], [rc])
        for k in range(8):
            for j in range(2):
                fw.op("vector", lambda e, k=k, j=j: e.tensor_copy(out=scB[:, k, j, :],
                                                                   in_=ccT[:, k, j:j + 1].to_broadcast([128, 128])),
                      [rc], [self.res("scB")])
        fw.dma("sync", [(self.fnb, self.final_norm[0:1, :].partition_broadcast(128)[:, 0, :])], [], [self.res("fnb")])

    def _wload(self, src_rows, ncols, dst, rdst, n):
        fw = self.fw
        s_ = n % 2
        rs = self.res("wst_s%d" % s_)
        stg = self.wst[:, s_ * 3072:s_ * 3072 + ncols]
        fw.dma("sync" if n % 2 == 0 else "gpsimd", [(stg, src_rows)], [], [rs])
        fw.op("gpsimd" if n % 2 == 0 else "vector", lambda e: e.tensor_copy(out=dst, in_=stg), [rs], [rdst])

    def weights_in(self, l):
        for k in range(8):
            self._wload(self.w_in[l, k * 128:(k + 1) * 128, :], INW, self.winb[:, k, :], self.res("winb"), k)

    def weights_out(self, l):
        fw = self.fw
        for k in range(8):
            self._wload(self.w_out[l, k * 128:(k + 1) * 128, :], D, self.woutb[:, k, :], self.res("woutb"), k)
        for k in range(4):
            self._wload(self.w_glu[l, k * 128:(k + 1) * 128, :], 512, self.wglub[:, k, :], self.res("wglub"), k)
        fw.dma("sync", [(self.bglub, self.b_glu[l:l + 1, :].partition_broadcast(128)[:, 0, :])], [], [self.res("bglub")])

    def modulation(self, l):
        fw = self.fw
        modb = self.modb
        rs0, rs1 = self.res("wst_s0"), self.res("wst_s1")
        rmod = self.res("modb")
        wv = self.wst.rearrange("p (k n) -> p k n", k=8)
        bt = self.t3[0]
        rbt = self.res("t3_0")
        for q in range(4):
            c0 = q * 768
            fw.dma("sync", [(wv[:, k, :], self.w_mod[l, k * 128:(k + 1) * 128, c0:c0 + 768]) for k in range(4)],
                   [], [rs0, rs1])
            fw.dma("gpsimd", [(wv[:, k, :], self.w_mod[l, k * 128:(k + 1) * 128, c0:c0 + 768]) for k in range(4, 8)],
                   [], [rs0, rs1], sres=self.res("wst_g"))
            for n in range(2):
                col = c0 + n * 384
                fw.dma("sync", [(bt[:, 0:384], self.b_mod[l:l + 1, col:col + 384].partition_broadcast(128)[:, 0, :])], [], [rbt])
                for j in range(2):
                    bi = (n * 2 + j) % 4
                    pb = self.pb[bi]
                    rp = self.res("pb%d" % bi)
                    for k in range(8):
                        fw.op("tensor", lambda e, k=k, j=j, n=n, pb=pb: e.matmul(
                            pb[:, 0:384], lhsT=self.scB[:, k, j, :], rhs=wv[:, k, n * 384:(n + 1) * 384],
                            start=(k == 0), stop=(k == 7)), [rs0, rs1, self.res("scB")], [rp])
                    fw.op("vector", lambda e, j=j, col=col, pb=pb: e.tensor_tensor(
                        out=modb[:, j, col:col + 384], in0=pb[:, 0:384], in1=bt[:, 0:384], op=ALU.add),
                        [rp, rbt], [rmod])
        ngb = self.xn[0]
        rng = self.res("xn0")
        fw.dma("sync", [(ngb, self.norm_g[l:l + 1, :].partition_broadcast(128)[:, 0, :])], [], [rng])
        for j in range(2):
            fw.op("vector", lambda e, j=j: e.scalar_tensor_tensor(
                out=modb[:, j, D:2 * D], in0=modb[:, j, D:2 * D], scalar=1.0, in1=ngb,
                op0=ALU.add, op1=ALU.mult), [rmod, rng], [rmod])

    def xsrc(self, l, i):
        if l == 0:
            if i < 2:
                return self.ctx[i * 128:(i + 1) * 128, :], None
            return self.x[(i - 2) * 128:(i - 1) * 128, :], None
        return self.xs[l % 2][i * 128:(i + 1) * 128, :], self.res("xs%d" % (l % 2))

    def phase1(self, l):
        fw = self.fw
        rmod = self.res("modb")
        rP = self.res("P")
        for i in range(NTILE):
            s = i % 2
            j = 1 if i < 2 else 0
            xt, xn, hb, hT, pj, st1 = self.xt[s], self.xn[s], self.hb[s], self.hT[s], self.pj[s], self.st1[s]
            rxt, rxn, rhb, rhT, rpj, rst = (self.res("%s%d" % (n, s)) for n in ("xt", "xn", "hb", "hT", "pj", "st1"))
            src, rsrc = self.xsrc(l, i)
            fw.dma("sync", [(xt[:], src)], [rsrc] if rsrc else [], [rxt])
            fw.op("scalar", lambda e, xt=xt, xn=xn, st1=st1: e.activation(out=xn[:], in_=xt[:], func=AF.Square,
                                                                        accum_out=st1[:, 0:1]), [rxt], [rxn, rst])
            fw.op("vector", lambda e, st1=st1: e.tensor_scalar(out=st1[:, 1:2], in0=st1[:, 0:1], scalar1=1.0 / D, scalar2=EPS,
                                                              op0=ALU.mult, op1=ALU.add), [rst], [rst])
            fw.op("scalar", lambda e, st1=st1: e.activation(out=st1[:, 2:3], in_=st1[:, 1:2], func=AF.Sqrt), [rst], [rst])
            fw.op("vector", lambda e, st1=st1: e.reciprocal(out=st1[:, 3:4], in_=st1[:, 2:3]), [rst], [rst])
            fw.op("vector", lambda e, xt=xt, xn=xn, st1=st1, j=j: e.scalar_tensor_tensor(
                out=xn[:], in0=xt[:], scalar=st1[:, 3:4], in1=self.modb[:, j, D:2 * D], op0=ALU.mult, op1=ALU.mult),
                [rxt, rst, rmod], [rxn])
            fw.op("gpsimd", lambda e, xn=xn, hb=hb, j=j: e.tensor_tensor(out=hb[:], in0=xn[:], in1=self.modb[:, j, 0:D], op=ALU.add),
                  [rxn, rmod], [rhb])
            ptb = self.pb[0][:].bitcast(BF16)
            rp0 = self.res("pb0")
            for k in range(8):
                fw.op("tensor", lambda e, k=k, hb=hb, ptb=ptb: e.transpose(ptb[:, k * 128:(k + 1) * 128], hb[:, k * 128:(k + 1) * 128],
                                                                         self.identB[:]), [rhb, self.res("ident")], [rp0])
            fw.op("scalar", lambda e, hT=hT, ptb=ptb: e.activation(out=hT[:].rearrange("p k t -> p (k t)"), in_=ptb, func=AF.Copy),
                  [rp0], [rhT])
            chunks = [(0, 512, "copy"), (512, 512, "silu"), (1024, 256, "q"), (1280, 256, "copy"), (1536, 512, "copy"),
                      (2048, 512, "silu"), (2560, 32, "copy")]
            groups = [(0, 512), (512, 512), (1024, 512), (1536, 512), (2048, 512), (2560, 32)]
            for gi, (c0, w) in enumerate(groups):
                b = 1 + (gi % 4)
                pb = self.pb[b]
                rp = self.res("pb%d" % b)
                for k in range(8):
                    fw.op("tensor", lambda e, k=k, c0=c0, w=w, pb=pb, hT=hT: e.matmul(
                        pb[:, 0:w], lhsT=hT[:, k, :], rhs=self.winb[:, k, c0:c0 + w], start=(k == 0), stop=(k == 7)),
                        [rhT, self.res("winb")], [rp])
                for (a0, aw, kind) in chunks:
                    if a0 < c0 or a0 >= c0 + w:
                        continue
                    o = pj[:, a0:a0 + aw]
                    src_ = pb[:, a0 - c0:a0 - c0 + aw]
                    if kind == "silu":
                        fw.op("scalar", lambda e, o=o, src_=src_: e.activation(out=o, in_=src_, func=AF.Silu), [rp], [rpj])
                    elif kind == "q":
                        fw.op("vector", lambda e, o=o, src_=src_: e.tensor_scalar(out=o, in0=src_, scalar1=0.125, scalar2=None,
                                                                                op0=ALU.mult), [rp], [rpj])
                    else:
                        fw.op("vector", lambda e, o=o, src_=src_: e.tensor_copy(out=o, in_=src_), [rp], [rpj])
            fw.dma("gpsimd", [(self.P[i * 128:(i + 1) * 128, :], pj[:])], [rpj], [rP])

    def gelu(self, eng_a, out, in_, tmp, reads, writes, rtmp):
        fw = self.fw
        fw.op("scalar", lambda e: e.activation(out=tmp, in_=in_, func=AF.Square), reads, [rtmp])
        fw.op(eng_a, lambda e: e.tensor_scalar(out=tmp, in0=tmp, scalar1=0.044715, scalar2=1.0, op0=ALU.mult, op1=ALU.add),
              [rtmp], [rtmp])
        fw.op(eng_a, lambda e: e.tensor_tensor(out=tmp, in0=tmp, in1=in_, op=ALU.mult), [rtmp] + list(reads), [rtmp])
        fw.op("scalar", lambda e: e.activation(out=tmp, in_=tmp, func=AF.Sigmoid, scale=1.5957691216), [rtmp], [rtmp])
        fw.op(eng_a, lambda e: e.tensor_tensor(out=out, in0=tmp, in1=in_, op=ALU.mult), [rtmp] + list(reads), writes)

    def s5_stub(self, l):
        fw = self.fw
        for i in range(NTILE):
            s = i % 2
            t = self.g3[s]
            rt = self.res("g3_%d" % s)
            fw.dma("sync", [(t[:, 0:512], self.P[i * 128:(i + 1) * 128, 0:512])], [self.res("P")], [rt])
            self.gelu("vector", t[:, 512:1024], t[:, 0:512], self.t3[s][:], [rt], [rt], self.res("t3_%d" % s))
            fw.dma("sync", [(self.gy[i * 128:(i + 1) * 128, :], t[:, 512:1024])], [rt], [self.res("gy")])

    def gla_stub(self, l):
        fw = self.fw
        for i in range(NTILE):
            s = i % 2
            t = self.g3[s]
            rt = self.res("g3_%d" % s)
            fw.dma("sync", [(t[:, 0:512], self.P[i * 128:(i + 1) * 128, 1536:2048])], [self.res("P")], [rt])
            fw.dma("sync", [(self.yg[i * 128:(i + 1) * 128, :], t[:, 0:512])], [rt], [self.res("yg")])

    def phase3(self, l):
        fw = self.fw
        last = (l == self.depth - 1)
        rmod = self.res("modb")
        rout = self.res("out")
        rxd = self.res("xs%d" % ((l + 1) % 2))
        for i in range(NTILE):
            if last and i < 2:
                continue
            s = i % 2
            j = 1 if i < 2 else 0
            g3, gyT, t3, mix, mixT, yo, xt, st1 = (self.g3[s], self.gyT[s], self.t3[s], self.mix[s], self.mixT[s], self.yo[s],
                                                   self.xt[s], self.st1[s])
            rg3, rgyT, rt3, rmix, rmixT, ryo, rxt, rst = (self.res("%s%d" % (n, s)) for n in
                                                          ("g3_", "gyT", "t3_", "mix", "mixT", "yo", "xt", "st1"))
            rows = slice(i * 128, (i + 1) * 128)
            fw.dma("sync", [(g3[:, 0:512], self.gy[rows, :])], [self.res("gy")], [rg3])
            fw.dma("sync", [(g3[:, 512:1024], self.yg[rows, :])], [self.res("yg")], [rg3])
            fw.dma("sync", [(g3[:, 1024:1536], self.P[rows, 512:1024]), (g3[:, 1536:2048], self.P[rows, 2048:2560])],
                   [self.res("P")], [rg3])
            src, rsrc = self.xsrc(l, i)
            fw.dma("gpsimd", [(xt[:], src)], [rsrc] if rsrc else [], [rxt])
            ptb = self.pb[5][:].bitcast(BF16)
            rp5 = self.res("pb5")
            for k in range(4):
                fw.op("tensor", lambda e, k=k, g3=g3, ptb=ptb: e.transpose(ptb[:, k * 128:(k + 1) * 128], g3[:, k * 128:(k + 1) * 128],
                                                                         self.identB[:]), [rg3, self.res("ident")], [rp5])
            fw.op("scalar", lambda e, gyT=gyT, ptb=ptb: e.activation(out=gyT[:].rearrange("p k t -> p (k t)"), in_=ptb[:, 0:512],
                                                                   func=AF.Copy), [rp5], [rgyT])
            pg = self.pb[6]
            rp6 = self.res("pb6")
            for k in range(4):
                fw.op("tensor", lambda e, k=k, gyT=gyT, pg=pg: e.matmul(pg[:], lhsT=gyT[:, k, :], rhs=self.wglub[:, k, :],
                                                                      start=(k == 0), stop=(k == 3)), [rgyT, self.res("wglub")], [rp6])
            fw.op("vector", lambda e, t3=t3, pg=pg: e.tensor_tensor(out=t3[:], in0=pg[:], in1=self.bglub, op=ALU.add),
                  [rp6, self.res("bglub")], [rt3])
            fw.op("scalar", lambda e, t3=t3: e.activation(out=t3[:], in_=t3[:], func=AF.Sigmoid), [rt3], [rt3])
            fw.op("vector", lambda e, t3=t3, g3=g3: e.tensor_tensor(out=t3[:], in0=t3[:], in1=g3[:, 0:512], op=ALU.mult),
                  [rt3, rg3], [rt3])
            fw.op("vector", lambda e, t3=t3, g3=g3, mix=mix: e.tensor_tensor(out=mix[:, 0:512], in0=t3[:], in1=g3[:, 1024:1536],
                                                                            op=ALU.mult), [rt3, rg3], [rmix])
            fw.op("gpsimd", lambda e, g3=g3, mix=mix: e.tensor_tensor(out=mix[:, 512:1024], in0=g3[:, 512:1024], in1=g3[:, 1536:2048],
                                                                     op=ALU.mult), [rg3], [rmix])
            ptm = self.pb[7][:].bitcast(BF16)
            rp7 = self.res("pb7")
            for k in range(8):
                fw.op("tensor", lambda e, k=k, mix=mix, ptm=ptm: e.transpose(ptm[:, k * 128:(k + 1) * 128], mix[:, k * 128:(k + 1) * 128],
                                                                           self.identB[:]), [rmix, self.res("ident")], [rp7])
            fw.op("scalar", lambda e, mixT=mixT, ptm=ptm: e.activation(out=mixT[:].rearrange("p k t -> p (k t)"), in_=ptm, func=AF.Copy),
                  [rp7], [rmixT])
            for n in range(2):
                b = 1 + n
                pb = self.pb[b]
                rp = self.res("pb%d" % b)
                for k in range(8):
                    fw.op("tensor", lambda e, k=k, n=n, pb=pb, mixT=mixT: e.matmul(
                        pb[:], lhsT=mixT[:, k, :], rhs=self.woutb[:, k, n * 512:(n + 1) * 512], start=(k == 0), stop=(k == 7)),
                        [rmixT, self.res("woutb")], [rp])
                cs = slice(n * 512, (n + 1) * 512)
                fw.op("vector", lambda e, pb=pb, yo=yo, cs=cs, j=j, n=n: e.tensor_tensor(
                    out=yo[:, cs], in0=pb[:], in1=self.modb[:, j, 2 * D + n * 512:2 * D + (n + 1) * 512], op=ALU.mult),
                    [rp, rmod], [ryo])
            fw.op("gpsimd", lambda e, yo=yo, xt=xt: e.tensor_tensor(out=yo[:], in0=yo[:], in1=xt[:], op=ALU.add), [ryo, rxt], [ryo])
            if not last:
                fw.dma("sync", [(self.xs[(l + 1) % 2][rows, :], yo[:])], [ryo], [rxd])
            else:
                xn = self.xn[s]
                rxn = self.res("xn%d" % s)
                fw.op("scalar", lambda e, yo=yo, xn=xn, st1=st1: e.activation(out=xn[:], in_=yo[:], func=AF.Square,
                                                                            accum_out=st1[:, 0:1]), [ryo], [rxn, rst])
                fw.op("vector", lambda e, st1=st1: e.tensor_scalar(out=st1[:, 1:2], in0=st1[:, 0:1], scalar1=1.0 / D, scalar2=EPS,
                                                                  op0=ALU.mult, op1=ALU.add), [rst], [rst])
                fw.op("scalar", lambda e, st1=st1: e.activation(out=st1[:, 2:3], in_=st1[:, 1:2], func=AF.Sqrt), [rst], [rst])
                fw.op("vector", lambda e, st1=st1: e.reciprocal(out=st1[:, 3:4], in_=st1[:, 2:3]), [rst], [rst])
                fw.op("vector", lambda e, yo=yo, xn=xn, st1=st1: e.scalar_tensor_tensor(
                    out=xn[:], in0=yo[:], scalar=st1[:, 3:4], in1=self.fnb, op0=ALU.mult, op1=ALU.mult),
                    [ryo, rst, self.res("fnb")], [rxn])
                fw.dma("sync", [(self.out[(i - 2) * 128:(i - 1) * 128, :], xn[:])], [rxn], [rout])

    def s5(self, l):
        raise NotImplementedError

    def gla(self, l):
        raise NotImplementedError


_CACHE = {}


def make_in_maps(inputs):
    maps = []
    for core in range(8):
        b = core % 4
        m = {
            "x": np.ascontiguousarray(inputs["x"][b]),
            "ctx": np.ascontiguousarray(inputs["ctx"][b]),
            "cc": np.ascontiguousarray(np.stack([inputs["c"][b], inputs["c_ctx"]], axis=0)),
            "final_norm": np.ascontiguousarray(inputs["final_norm"][None, :]),
        }
        for k in ("norm_g", "w_mod", "b_mod", "w_in", "s5_lam_re", "s5_lam_im", "s5_log_dt", "s5_b_re", "s5_b_im", "s5_c_re",
                  "s5_c_im", "s5_d", "s5_w_glu", "s5_b_glu", "gla_w_gate", "gla_b_gate", "gla_norm_g", "w_out"):
            m[k] = np.ascontiguousarray(inputs[k])
        maps.append(m)
    return maps


def kernel(**inputs):
    inputs = {k: np.asarray(v) for k, v in inputs.items()}
    if "prog" not in _CACHE:
        _CACHE["prog"] = Prog()
    prog = _CACHE["prog"]
    res = run_bass_kernel_spmd(prog.nc, make_in_maps(inputs), core_ids=list(range(8)))
    return np.stack([np.asarray(res.results[b]["out"]) for b in range(4)], axis=0).astype(np.float32)


def _gla(self, l):
    fw = self.fw
    v = self.view
    self.aoff = self.phase_base
    T = [v([128, 1056], BF16) for _ in range(2)]
    lrT_L = [v([128, 128], BF16) for _ in range(2)]
    wg32 = v([128, 2, 256], F32)
    wgp = v([128, 2, 256], BF16)
    nbg = v([128, 2, 2], F32)
    sp_L = [v([128, 2, 2, 128], F32) for _ in range(2)]
    cs_L = [v([128, 2, 2, 128], F32) for _ in range(2)]
    eq_L = [v([128, 2, 2, 128], F32) for _ in range(2)]
    ek_L = [v([128, 2, 2, 128], F32) for _ in range(2)]
    ekd_L = [v([128, 2, 2, 128], F32) for _ in range(2)]
    tot_L = [v([128, 2, 2, 4], F32) for _ in range(2)]
    qtT_L = [v([128, 2, 2, 128], BF16) for _ in range(2)]
    ktT_L = [v([128, 2, 2, 128], BF16) for _ in range(2)]
    kdT_L = [v([128, 2, 2, 128], BF16) for _ in range(2)]
    kdt_L = [v([128, 2, 2, 128], BF16) for _ in range(2)]
    sT_L = [v([128, 2, 4, 128], BF16) for _ in range(2)]
    S32 = v([128, 2, 2, 128], F32)
    Sbf = v([128, 2, 128], BF16)
    Sst = v([128, NTILE, 2, 128], BF16)
    Mf = v([128, 128], F32)
    Mb = v([128, 128], F32)
    ones = v([128, 128], F32)
    gnb = v([128, 128], F32)
    ygt = [v([128, 512], BF16) for _ in range(2)]
    sq = v([128, 128], F32)
    rs = v([128, 8], F32)
    R0 = lambda n: self.res("gla_" + n)
    R = R0
    SLOTTED = ("lrT", "sp", "cs", "eq", "ek", "ekd", "tot", "qtT", "ktT", "kdT", "kdt", "sT")

    def mkR(s):
        return lambda n: R0(n + "_s%d" % s) if n in SLOTTED else R0(n)
    rI = self.res("ident")
    fw.op("gpsimd", lambda e: e.memset(Mf, 1.0), [], [R("Mf")])
    fw.op("gpsimd", lambda e: e.affine_select(out=Mf, in_=Mf, compare_op=ALU.is_ge, fill=0.0, base=0,
                                             pattern=[[1, 128]], channel_multiplier=-1), [R("Mf")], [R("Mf")])
    fw.op("gpsimd", lambda e: e.memset(Mb, 1.0), [], [R("Mb")])
    fw.op("gpsimd", lambda e: e.affine_select(out=Mb, in_=Mb, compare_op=ALU.is_ge, fill=0.0, base=0,
                                             pattern=[[-1, 128]], channel_multiplier=1), [R("Mb")], [R("Mb")])
    fw.op("gpsimd", lambda e: e.memset(ones, 1.0), [], [R("ones")])
    fw.op("vector", lambda e: e.memset(wg32, 0.0), [], [R("wg32")])
    fw.dma("sync", [(wg32[0:16, 0, :], self.w_gate[l, 0, :, :]), (wg32[16:32, 1, :], self.w_gate[l, 1, :, :])], [], [R("wg32")])
    fw.op("vector", lambda e: e.tensor_copy(out=wgp[0:32], in_=wg32[0:32]), [R("wg32")], [R("wgp")])
    fw.dma("sync", [(nbg[:, d, hp:hp + 1], self.b_gate[l, d, hp * 128:(hp + 1) * 128].rearrange("(p o) -> p o", o=1))
                    for d in range(2) for hp in range(2)], [], [R("nbg")])
    fw.op("vector", lambda e: e.tensor_scalar(out=nbg, in0=nbg, scalar1=-1.0, scalar2=None, op0=ALU.mult), [R("nbg")], [R("nbg")])
    fw.dma("sync", [(gnb, self.gnorm[l:l + 1, :].partition_broadcast(128)[:, 0, :])], [], [R("gnb")])
    fw.op("vector", lambda e: e.memset(S32, 0.0), [], [R("S32")])
    fw.op("vector", lambda e: e.memset(Sbf, 0.0), [], [R("Sbf")])

    Plat = self.P[CT:, :].rearrange("(r c) w -> c r w", c=64)
    yglat = self.yg[CT:, :].rearrange("(r c) w -> c r w", c=64)

    def rows(ap, lat, ci, c0, c1):
        if ci < 2:
            return ap[ci * 128:(ci + 1) * 128, c0:c1]
        return lat[ci - 2, :, c0:c1]

    pz, pqk, plr, psc0, psc1, pkd, pdS, po = self.pb
    rpb = [self.res("pb%d" % i) for i in range(8)]

    def load(ci, s):
        fw.dma("sync", [(T[s][:, 0:1024], rows(self.P, Plat, ci, 1024, 2048))], [self.res("P")], [R("T%d" % s)])
        fw.dma("gpsimd", [(T[s][:, 1024:1056], rows(self.P, Plat, ci, 2560, 2592))], [self.res("P")], [R("T%d" % s)],
               sres=R("T%db" % s))

    def gates(s, dirs, need_qk):
        lrT, sp, cs, eq, ek, ekd, tot, qtT, ktT, kdT, kdt, sT = (X[s] for X in (lrT_L, sp_L, cs_L, eq_L, ek_L, ekd_L, tot_L, qtT_L, ktT_L, kdT_L, kdt_L, sT_L))
        R = mkR(s)
        Tt = T[s]
        rT = [R("T%d" % s), R("T%db" % s)]
        plrb = plr[:].bitcast(BF16)
        fw.op("tensor", lambda e: e.transpose(plrb[0:32, 0:128], Tt[:, 1024:1056], self.identB[:]), rT + [rI], [rpb[2]])
        fw.op("scalar", lambda e: e.activation(out=lrT[0:32, :], in_=plrb[0:32, 0:128], func=AF.Copy), [rpb[2]], [R("lrT")])
        pqkb = pqk[:].bitcast(BF16)
        for t4 in range(4):
            fw.op("tensor", lambda e, t4=t4: e.transpose(pqkb[:, t4 * 128:(t4 + 1) * 128], Tt[:, t4 * 128:(t4 + 1) * 128],
                                                        self.identB[:]), rT + [rI], [rpb[1]])
        for d in dirs:
            for hp in range(2):
                fw.op("tensor", lambda e, d=d, hp=hp: e.matmul(pz[:, (d * 2 + hp) * 128:(d * 2 + hp + 1) * 128],
                                                              lhsT=wgp[0:32, d, hp * 128:(hp + 1) * 128], rhs=lrT[0:32, :],
                                                              start=True, stop=True), [R("wgp"), R("lrT")], [rpb[0]])
                fw.op("scalar", lambda e, d=d, hp=hp: e.activation(out=sp[:, d, hp, :], in_=pz[:, (d * 2 + hp) * 128:(d * 2 + hp + 1) * 128],
                                                                  func=AF.Exp, scale=-1.0, bias=nbg[:, d, hp:hp + 1]),
                      [rpb[0], R("nbg")], [R("sp")])
            fw.op("scalar", lambda e, d=d: e.activation(out=sp[:, d], in_=sp[:, d], func=AF.Ln, bias=1.0), [R("sp")], [R("sp")])
            for hp in range(2):
                fw.op("vector", lambda e, d=d, hp=hp: e.tensor_tensor_scan(out=cs[:, d, hp, :], data0=ones, data1=sp[:, d, hp, :],
                                                                          initial=0.0, op0=ALU.mult, op1=ALU.add),
                      [R("sp"), R("ones")], [R("cs")])
                fw.op("vector", lambda e, d=d, hp=hp: e.tensor_copy(out=tot[:, d, hp, 0:1], in_=cs[:, d, hp, 127:128]),
                      [R("cs")], [R("tot")])
                if d == 1:
                    fw.op("vector", lambda e, d=d, hp=hp: e.scalar_tensor_tensor(out=cs[:, d, hp, :], in0=sp[:, d, hp, :],
                                                                                scalar=tot[:, d, hp, 0:1], in1=cs[:, d, hp, :],
                                                                                op0=ALU.add, op1=ALU.subtract),
                          [R("sp"), R("tot"), R("cs")], [R("cs")])
            fw.op("vector", lambda e, d=d: e.tensor_scalar(out=tot[:, d, :, 1:2], in0=tot[:, d, :, 0:1], scalar1=-1.0 / 16, scalar2=None,
                                                          op0=ALU.mult), [R("tot")], [R("tot")])
            fw.op("scalar", lambda e, d=d: e.activation(out=tot[:, d, :, 2:3], in_=tot[:, d, :, 0:1], func=AF.Exp, scale=-1.0 / 16),
                  [R("tot")], [R("tot")])
            for hp in range(2):
                fw.op("scalar", lambda e, d=d, hp=hp: e.activation(out=ekd[:, d, hp, :], in_=cs[:, d, hp, :], func=AF.Exp,
                                                                  scale=1.0 / 16, bias=tot[:, d, hp, 1:2]), [R("cs"), R("tot")], [R("ekd")])
                fw.op("vector", lambda e, d=d, hp=hp: e.tensor_tensor(out=kdT[:, d, hp, :], in0=pqkb[:, (2 + hp) * 128:(3 + hp) * 128],
                                                                     in1=ekd[:, d, hp, :], op=ALU.mult), [rpb[1], R("ekd")], [R("kdT")])
            if need_qk:
                fw.op("scalar", lambda e, d=d: e.activation(out=eq[:, d], in_=cs[:, d], func=AF.Exp, scale=-1.0 / 16), [R("cs")], [R("eq")])
                fw.op("scalar", lambda e, d=d: e.activation(out=ek[:, d], in_=cs[:, d], func=AF.Exp, scale=1.0 / 16), [R("cs")], [R("ek")])
                fw.op("vector", lambda e, d=d: e.tensor_tensor(out=qtT[:, d], in0=pqkb[:, 0:256].rearrange("p (a b) -> p a b", a=2),
                                                              in1=eq[:, d], op=ALU.mult), [rpb[1], R("eq")], [R("qtT")])
                fw.op("gpsimd", lambda e, d=d: e.tensor_copy(out=ktT[:, d], in_=ek[:, d]), [R("ek")], [R("ktT")])
                fw.op("vector", lambda e, d=d: e.tensor_tensor(out=ktT[:, d], in0=pqkb[:, 256:512].rearrange("p (a b) -> p a b", a=2),
                                                              in1=ek[:, d], op=ALU.mult), [rpb[1], R("ek"), R("ktT")], [R("ktT")])
            pkdb = pkd[:].bitcast(BF16)
            for hp in range(2):
                fw.op("tensor", lambda e, d=d, hp=hp: e.transpose(pkdb[:, (d * 2 + hp) * 128:(d * 2 + hp + 1) * 128], kdT[:, d, hp, :],
                                                                 self.identB[:]), [R("kdT"), rI], [rpb[5]])
            fw.op("scalar", lambda e, d=d: e.activation(out=kdt[:, d].rearrange("p a b -> p (a b)"), in_=pkdb[:, d * 256:(d + 1) * 256],
                                                       func=AF.Copy), [rpb[5]], [R("kdt")])

    def dstate(s, d):
        lrT, sp, cs, eq, ek, ekd, tot, qtT, ktT, kdT, kdt, sT = (X[s] for X in (lrT_L, sp_L, cs_L, eq_L, ek_L, ekd_L, tot_L, qtT_L, ktT_L, kdT_L, kdt_L, sT_L))
        R = mkR(s)
        Tt = T[s]
        rT = [R("T%d" % s)]
        for hp in range(2):
            for h2 in range(2):
                h = hp * 2 + h2
                fw.op("tensor", lambda e, hp=hp, h2=h2, h=h: e.matmul(
                    pdS[h2 * 64:(h2 + 1) * 64, (d * 2 + hp) * 128:(d * 2 + hp + 1) * 128],
                    lhsT=kdt[:, d, hp, h2 * 64:(h2 + 1) * 64], rhs=Tt[:, 512 + h * 128:512 + (h + 1) * 128],
                    start=True, stop=True), [R("kdt")] + rT, [rpb[6]])

    def supdate(s, d):
        lrT, sp, cs, eq, ek, ekd, tot, qtT, ktT, kdT, kdt, sT = (X[s] for X in (lrT_L, sp_L, cs_L, eq_L, ek_L, ekd_L, tot_L, qtT_L, ktT_L, kdT_L, kdt_L, sT_L))
        R = mkR(s)
        for hp in range(2):
            fw.op("vector", lambda e, hp=hp: e.scalar_tensor_tensor(out=S32[:, d, hp, :], in0=S32[:, d, hp, :], scalar=tot[:, d, hp, 2:3],
                                                                   in1=pdS[:, (d * 2 + hp) * 128:(d * 2 + hp + 1) * 128],
                                                                   op0=ALU.mult, op1=ALU.add), [R("S32"), R("tot"), rpb[6]], [R("S32")])

    order_b = [1, 0] + list(range(NTILE - 1, 1, -1))
    for n, ci in enumerate(order_b):
        s = n % 2
        load(ci, s)
        gates(s, [1], False)
        fw.op("gpsimd", lambda e, ci=ci: e.tensor_copy(out=Sst[:, ci], in_=S32[:, 1]), [R("S32")], [R("Sst")])
        dstate(s, 1)
        supdate(s, 1)
    def passF(ci):
        s = ci % 2
        lrT, sp, cs, eq, ek, ekd, tot, qtT, ktT, kdT, kdt, sT = (X[s] for X in (lrT_L, sp_L, cs_L, eq_L, ek_L, ekd_L, tot_L, qtT_L, ktT_L, kdT_L, kdt_L, sT_L))
        R = mkR(s)
        load(ci, s)
        gates(s, [0, 1], True)
        Tt = T[s]
        rT = [R("T%d" % s)]
        for d in range(2):
            M = Mf if d == 0 else Mb
            rM = R("Mf") if d == 0 else R("Mb")
            for h in range(4):
                hp, h2 = h // 2, h % 2
                psc = psc0 if d == 0 else psc1
                rps = rpb[3] if d == 0 else rpb[4]
                fw.op("tensor", lambda e, d=d, hp=hp, h2=h2, h=h, psc=psc: e.matmul(
                    psc[:, h * 128:(h + 1) * 128], lhsT=ktT[h2 * 64:(h2 + 1) * 64, d, hp, :], rhs=qtT[h2 * 64:(h2 + 1) * 64, d, hp, :],
                    start=True, stop=True), [R("ktT"), R("qtT")], [rps])
                fw.op("vector" if h % 2 == 0 else "gpsimd" if False else "vector", lambda e, d=d, h=h, psc=psc, M=M: e.tensor_tensor(
                    out=sT[:, d, h, :], in0=psc[:, h * 128:(h + 1) * 128], in1=M, op=ALU.mult), [rps, rM], [R("sT")])
        for h in range(4):
            hp, h2 = h // 2, h % 2
            ops = []
            for d in range(2):
                ops.append((sT[:, d, h, :], Tt[:, 512 + h * 128:512 + (h + 1) * 128], [R("sT")] + rT))
                if d == 0:
                    ops.append((qtT[h2 * 64:(h2 + 1) * 64, 0, hp, :], Sbf[h2 * 64:(h2 + 1) * 64, hp, :], [R("qtT"), R("Sbf")]))
                else:
                    ops.append((qtT[h2 * 64:(h2 + 1) * 64, 1, hp, :], Sst[h2 * 64:(h2 + 1) * 64, ci, hp, :], [R("qtT"), R("Sst")]))
            for n_, (lt, rh, rd) in enumerate(ops):
                fw.op("tensor", lambda e, lt=lt, rh=rh, n_=n_, h=h: e.matmul(po[:, h * 128:(h + 1) * 128], lhsT=lt, rhs=rh,
                                                                            start=(n_ == 0), stop=(n_ == 3)), rd, [rpb[7]])
        dstate(s, 0)
        supdate(s, 0)
        fw.op("gpsimd", lambda e: e.tensor_copy(out=Sbf, in_=S32[:, 0]), [R("S32")], [R("Sbf")])
        yt = ygt[s]
        ry = R("yg%d" % s)
        for h in range(4):
            fw.op("scalar", lambda e, h=h: e.activation(out=sq, in_=po[:, h * 128:(h + 1) * 128], func=AF.Square,
                                                       accum_out=rs[:, h:h + 1]), [rpb[7]], [R("sq"), R("rs")])
        fw.op("vector", lambda e: e.tensor_scalar(out=rs[:, 4:8], in0=rs[:, 0:4], scalar1=1.0 / 128, scalar2=EPS, op0=ALU.mult,
                                                 op1=ALU.add), [R("rs")], [R("rs")])
        fw.op("scalar", lambda e: e.activation(out=rs[:, 4:8], in_=rs[:, 4:8], func=AF.Sqrt), [R("rs")], [R("rs")])
        fw.op("vector", lambda e: e.reciprocal(out=rs[:, 4:8], in_=rs[:, 4:8]), [R("rs")], [R("rs")])
        for h in range(4):
            fw.op("vector", lambda e, h=h, yt=yt: e.scalar_tensor_tensor(out=yt[:, h * 128:(h + 1) * 128], in0=po[:, h * 128:(h + 1) * 128],
                                                                        scalar=rs[:, 4 + h:5 + h], in1=gnb, op0=ALU.mult, op1=ALU.mult),
                  [rpb[7], R("rs"), R("gnb")], [ry])
        fw.dma("gpsimd", [(rows(self.yg, yglat, ci, 0, 512), yt)], [ry], [self.res("yg")])

    for ci_ in range(NTILE):
        passF(ci_)


Prog.gla = _gla


def _s5(self, l):
    fw = self.fw
    v = self.view
    self.aoff = self.phase_base
    TWO_PI = 6.283185307179586
    NG = 8
    X8 = v([128, 9, 8, NG * 16], BF16)
    Ytok = v([128, 9, 8, NG * 16], BF16)
    U8_L = [v([128, 1056], BF16) for _ in range(2)]
    gsc2 = v([128, 1056], F32)
    Xg = v([128, 9, 128], BF16)
    gy8 = v([128, 1056], BF16)
    gsc = v([128, 1056], F32)
    Ere = v([128, 2, NG, 65], F32)
    Eim = v([128, 2, NG, 65], F32)
    ErD = v([128, 2, NG, 65], F32)
    EiD = v([128, 2, NG, 65], F32)
    kvr = v([128, 65], F32)
    kv = v([128, 65], F32)
    kvi = v([128, 65], I32)
    sm = v([128, 24, 2, NG], F32)
    AKr = v([128, 8, 2, NG], F32)
    AKs = v([128, 8, 2, NG], F32)
    sgn = v([128, 2], F32)
    ba = v([128, NG, 16], F32)
    bb = v([128, NG, 16], F32)
    Ca = v([128, NG, 16], F32)
    Cb = v([128, NG, 16], F32)
    Bw = v([128, 2, 4, 16, 16], F32) if False else None
    BA = [[v([128, NG, 16], F32) for _ in range(4)] for _ in range(2)]
    CA = [[v([128, NG, 16], F32) for _ in range(2)] for _ in range(2)]
    Dcol = v([128, NG], F32)
    swapM = v([128, 128], F32)
    maskF = v([128, 8, 16], F32)
    maskB = v([128, 8, 16], F32)
    scr = v([128, 4096], F32)
    M2T = [scr[:, i * 1024:(i + 1) * 1024].rearrange("p (a b) -> p a b", a=64) for i in range(2)]
    M3f = [scr[:, (2 + i) * 1024:(3 + i) * 1024].rearrange("p (a b) -> p a b", a=64) for i in range(2)]
    tA = scr[:, 0:NG * 65].rearrange("p (a b) -> p a b", a=NG)
    tB = scr[:, 1040:1040 + NG * 65].rearrange("p (a b) -> p a b", a=NG)
    tI = scr[:, 2080:2080 + NG * 65].bitcast(I32).rearrange("p (a b) -> p a b", a=NG)
    M2Tp = [v([128, 8, 16], F32) for _ in range(2)]
    M2b = [v([128, 8, 128], BF16) for _ in range(2)]
    M3b_L = [[v([128, 8, 128], BF16) for _ in range(2)] for _ in range(2)]
    M1b_L = [[v([128, 8, 128], BF16) for _ in range(2)] for _ in range(2)]
    Ak = [v([128, 8, 128], F32) for _ in range(2)]
    Pst = [v([128, 132], F32) for _ in range(2)]
    HHb_L = [[v([128, 132], BF16) for _ in range(2)] for _ in range(2)]
    R = lambda n: self.res("s5_" + n)
    R_glob = R
    rI = self.res("ident")
    pb = self.pb
    rpb = [self.res("pb%d" % i) for i in range(8)]
    V, G_ = "vector", "gpsimd"

    def tt(eng, out, a, b, op, reads, writes):
        fw.op(eng, lambda e: e.tensor_tensor(out=out, in0=a, in1=b, op=op), reads, writes)

    def bc(ap, shape):
        return ap.to_broadcast(shape)

    fw.op(G_, lambda e: e.iota(kvi, pattern=[[1, 65]], base=0, channel_multiplier=0), [], [R("kv")])
    fw.op(V, lambda e: e.tensor_copy(out=kv, in_=kvi), [R("kv")], [R("kv")])
    fw.op(V, lambda e: e.tensor_scalar(out=kvr, in0=kv, scalar1=-1.0, scalar2=64.0, op0=ALU.mult, op1=ALU.add), [R("kv")], [R("kv")])
    fw.op(V, lambda e: e.memset(sgn[0:64, 0:1], -1.0), [], [R("sgn")])
    fw.op(V, lambda e: e.memset(sgn[64:128, 0:1], 1.0), [], [R("sgn")])
    fw.op(V, lambda e: e.memset(sgn[0:64, 1:2], 1.0), [], [R("sgn")])
    fw.op(V, lambda e: e.memset(sgn[64:128, 1:2], -1.0), [], [R("sgn")])
    fw.op(V, lambda e: e.tensor_copy(out=swapM[:, 0:64], in_=self.identF[:, 64:128]), [rI], [R("swapM")])
    fw.op(V, lambda e: e.tensor_copy(out=swapM[:, 64:128], in_=self.identF[:, 0:64]), [rI], [R("swapM")])
    fw.op(G_, lambda e: e.memset(maskF, 1.0), [], [R("mask")])
    fw.op(G_, lambda e: e.affine_select(out=maskF, in_=maskF, compare_op=ALU.is_ge, fill=0.0, base=15,
                                       pattern=[[16, 8], [0, 16]], channel_multiplier=-1), [R("mask")], [R("mask")])
    fw.op(G_, lambda e: e.memset(maskB, 1.0), [], [R("mask")])
    fw.op(G_, lambda e: e.affine_select(out=maskB, in_=maskB, compare_op=ALU.is_ge, fill=0.0, base=0,
                                       pattern=[[-16, 8], [0, 16]], channel_multiplier=1), [R("mask")], [R("mask")])

    for gh in range(32 // NG):
        g0 = gh * NG
        fw.barrier(self.R.values())
        fw.dma("sync", [(X8[:, ct], self.P[ct * 1024:(ct + 1) * 1024, g0 * 16:g0 * 16 + NG * 16].rearrange("(c s) w -> c s w", s=8))
                        for ct in range(8)], [self.res("P")], [R("X8")])
        fw.dma("sync", [(X8[0:32, 8], self.P[8192:8448, g0 * 16:g0 * 16 + NG * 16].rearrange("(c s) w -> c s w", s=8))],
               [self.res("P")], [R("X8")])
        rsm = R("sm")
        pairs = []
        for d in range(2):
            for half in range(2):
                ps_ = slice(half * 64, half * 64 + 64)
                pairs.append((sm[ps_, 0, d, :], self.lam_re[l, d, g0:g0 + NG, :].rearrange("g n -> n g")))
                pairs.append((sm[ps_, 1, d, :], self.lam_im[l, d, g0:g0 + NG, :].rearrange("g n -> n g")))
            pairs.append((sm[:, 2, d, :], self.log_dt[l, d:d + 1, g0:g0 + NG].partition_broadcast(128)[:, 0, :]))
        fw.dma("gpsimd", pairs, [], [rsm])
        rb = R("bc")
        fw.dma("sync", [(ba[0:64], self.b_re[l, g0:g0 + NG].rearrange("g n p -> n g p")),
                        (bb[64:128], self.b_re[l, g0:g0 + NG].rearrange("g n p -> n g p")),
                        (ba[64:128], self.b_im[l, g0:g0 + NG].rearrange("g n p -> n g p")),
                        (bb[0:64], self.b_im[l, g0:g0 + NG].rearrange("g n p -> n g p"))], [], [rb])
        for gi in range(NG):
            g = g0 + gi
            fw.dma("sync" if gi % 2 == 0 else "gpsimd",
                   [(Ca[0:64, gi, :], self.c_re[l, g].rearrange("p n -> n p")), (Cb[64:128, gi, :], self.c_re[l, g].rearrange("p n -> n p")),
                    (Ca[64:128, gi, :], self.c_im[l, g].rearrange("p n -> n p")), (Cb[0:64, gi, :], self.c_im[l, g].rearrange("p n -> n p"))],
                   [], [rb], sres=R("bc%d" % (gi % 2)))
        fw.dma("sync", [(Dcol[s_ * 16:(s_ + 1) * 16, :], self.s5_d[l, g0 * 16:g0 * 16 + NG * 16].rearrange("(g p) -> p g", p=16))
                        for s_ in range(8)], [], [R("Dcol")])
        S_ = lambda i: sm[:, i]
        fw.op("scalar", lambda e: e.activation(out=S_(2), in_=S_(2), func=AF.Exp), [rsm], [rsm])
        tt(V, S_(3), S_(0), S_(2), ALU.mult, [rsm], [rsm])
        tt(V, S_(4), S_(1), S_(2), ALU.mult, [rsm], [rsm])
        fw.op(V, lambda e: e.tensor_scalar(out=S_(4), in0=S_(4), scalar1=1.0 / TWO_PI, scalar2=None, op0=ALU.mult), [rsm], [rsm])
        rE = R("E")
        rt = R("tab")
        for d in range(2):
          for (kvx, TRe, TIm) in ((kv, Ere, Eim), (kvr, ErD, EiD)):
            tt(V, tA, bc(sm[:, 3, d, :].unsqueeze(2), [128, NG, 65]), bc(kvx.unsqueeze(1), [128, NG, 65]), ALU.mult, [rsm, R("kv")], [rt])
            fw.op("scalar", lambda e: e.activation(out=tA, in_=tA, func=AF.Exp), [rt], [rt])
            for which in range(2):
                tt(V, tB, bc(sm[:, 4, d, :].unsqueeze(2), [128, NG, 65]), bc(kvx.unsqueeze(1), [128, NG, 65]), ALU.mult, [rsm, R("kv")], [rt])
                if which == 1:
                    fw.op(V, lambda e: e.tensor_scalar(out=tB, in0=tB, scalar1=0.25, scalar2=None, op0=ALU.add), [rt], [rt])
                fw.op(V, lambda e: e.tensor_copy(out=tI, in_=tB), [rt], [rt])
                tt(V, tB, tB, tI, ALU.subtract, [rt], [rt])
                dst = TIm[:, d] if which == 0 else TRe[:, d]
                fw.op(V, lambda e, dst=dst: e.tensor_single_scalar(out=dst, in_=tB, scalar=0.5, op=ALU.is_gt), [rt], [rE])
                tt(V, tB, tB, dst, ALU.subtract, [rt, rE], [rt])
                fw.op(V, lambda e, dst=dst: e.tensor_single_scalar(out=dst, in_=tB, scalar=-0.5, op=ALU.is_lt), [rt], [rE])
                tt(V, tB, tB, dst, ALU.add, [rt, rE], [rt])
                fw.op("scalar", lambda e: e.activation(out=tB, in_=tB, func=AF.Sin, scale=6.283185), [rt], [rt])
                tt(V, dst, tB, tA, ALU.mult, [rt], [rE])
        def coef_(d):
            s = lambda i: sm[:, i, d, :]
            e1r, e1i = Ere[:, d, :, 1], Eim[:, d, :, 1]
            e64r, e64i = Ere[:, d, :, 64], Eim[:, d, :, 64]
            stt = lambda o, i0, c, i1: fw.op(V, lambda e: e.scalar_tensor_tensor(out=o, in0=i0, scalar=c, in1=i1, op0=ALU.add, op1=ALU.mult),
                                             [rsm], [rsm])
            ti_ = tI[:, :, 0]
            fw.op(V, lambda e: e.tensor_copy(out=ti_, in_=s(4)), [rsm], [rt])
            tt(V, s(20), s(4), ti_, ALU.subtract, [rsm, rt], [rsm])
            fw.op(V, lambda e: e.tensor_single_scalar(out=s(21), in_=s(20), scalar=0.5, op=ALU.is_gt), [rsm], [rsm])
            tt(V, s(20), s(20), s(21), ALU.subtract, [rsm], [rsm])
            fw.op(V, lambda e: e.tensor_single_scalar(out=s(21), in_=s(20), scalar=-0.5, op=ALU.is_lt), [rsm], [rsm])
            tt(V, s(20), s(20), s(21), ALU.add, [rsm], [rsm])
            fw.op(V, lambda e: e.tensor_scalar(out=s(20), in0=s(20), scalar1=3.14159265358979, scalar2=None, op0=ALU.mult), [rsm], [rsm])
            tt(V, s(21), s(20), s(20), ALU.mult, [rsm], [rsm])
            fw.op(V, lambda e: e.tensor_scalar(out=s(22), in0=s(21), scalar1=-1.0 / 39916800, scalar2=None, op0=ALU.mult), [rsm], [rsm])
            for c_ in (1.0 / 362880, -1.0 / 5040, 1.0 / 120, -1.0 / 6):
                stt(s(22), s(22), c_, s(21))
            stt(s(22), s(22), 1.0, s(20))
            fw.op(V, lambda e: e.tensor_scalar(out=s(23), in0=s(21), scalar1=1.0 / 479001600, scalar2=None, op0=ALU.mult), [rsm], [rsm])
            for c_ in (-1.0 / 3628800, 1.0 / 40320, -1.0 / 720, 1.0 / 24, -0.5):
                stt(s(23), s(23), c_, s(21))
            fw.op(V, lambda e: e.tensor_scalar(out=s(23), in0=s(23), scalar1=1.0, scalar2=None, op0=ALU.add), [rsm], [rsm])
            fw.op(V, lambda e: e.tensor_scalar(out=s(15), in0=s(3), scalar1=1.0 / 120, scalar2=None, op0=ALU.mult), [rsm], [rsm])
            for c_ in (1.0 / 24, 1.0 / 6, 0.5, 1.0):
                stt(s(15), s(15), c_, s(3))
            tt(V, s(16), s(22), s(23), ALU.mult, [rsm], [rsm])
            fw.op(V, lambda e: e.tensor_scalar(out=s(16), in0=s(16), scalar1=2.0, scalar2=None, op0=ALU.mult), [rsm], [rsm])
            tt(V, s(21), s(22), s(22), ALU.mult, [rsm], [rsm])
            fw.op(V, lambda e: e.tensor_scalar(out=s(21), in0=s(21), scalar1=2.0, scalar2=None, op0=ALU.mult), [rsm], [rsm])
            fw.op(V, lambda e: e.tensor_scalar(out=s(20), in0=s(21), scalar1=-1.0, scalar2=1.0, op0=ALU.mult, op1=ALU.add), [rsm], [rsm])
            tt(V, s(5), s(15), s(20), ALU.mult, [rsm], [rsm])
            tt(V, s(5), s(5), s(21), ALU.subtract, [rsm], [rsm])
            fw.op(V, lambda e: e.tensor_scalar(out=s(15), in0=s(15), scalar1=1.0, scalar2=None, op0=ALU.add), [rsm], [rsm])
            tt(V, s(6), s(15), s(16), ALU.mult, [rsm], [rsm])
            tt(V, s(15), s(0), s(0), ALU.mult, [rsm], [rsm])
            tt(V, s(16), s(1), s(1), ALU.mult, [rsm], [rsm])
            tt(V, s(15), s(15), s(16), ALU.add, [rsm], [rsm])
            fw.op(V, lambda e: e.reciprocal(out=s(7), in_=s(15)), [rsm], [rsm])
            tt(V, s(15), s(5), s(0), ALU.mult, [rsm], [rsm])
            tt(V, s(16), s(6), s(1), ALU.mult, [rsm], [rsm])
            tt(V, s(15), s(15), s(16), ALU.add, [rsm], [rsm])
            tt(V, s(8), s(15), s(7), ALU.mult, [rsm], [rsm])
            tt(V, s(15), s(6), s(0), ALU.mult, [rsm], [rsm])
            tt(V, s(16), s(5), s(1), ALU.mult, [rsm], [rsm])
            tt(V, s(15), s(15), s(16), ALU.subtract, [rsm], [rsm])
            tt(V, s(9), s(15), s(7), ALU.mult, [rsm], [rsm])
            fw.op(V, lambda e: e.tensor_scalar(out=s(10), in0=s(9), scalar1=sgn[:, 0:1], scalar2=None, op0=ALU.mult), [rsm, R("sgn")], [rsm])
            fw.op(V, lambda e: e.tensor_scalar(out=s(11), in0=s(9), scalar1=sgn[:, 1:2], scalar2=None, op0=ALU.mult), [rsm, R("sgn")], [rsm])
            tt(V, s(15), e64r, e64r, ALU.mult, [rE], [rsm])
            tt(V, s(16), e64i, e64i, ALU.mult, [rE], [rsm])
            tt(V, s(15), s(15), s(16), ALU.add, [rsm], [rsm])
            fw.op(V, lambda e: e.reciprocal(out=s(15), in_=s(15)), [rsm], [rsm])
            tt(V, s(12), e64r, s(15), ALU.mult, [rE, rsm], [rsm])
            tt(V, s(16), e64i, s(15), ALU.mult, [rE, rsm], [rsm])
            fw.op(V, lambda e: e.tensor_scalar(out=s(13), in0=s(16), scalar1=sgn[:, 1:2], scalar2=None, op0=ALU.mult), [rsm, R("sgn")], [rsm])
            fw.op(V, lambda e: e.tensor_scalar(out=s(14), in0=s(16), scalar1=sgn[:, 0:1], scalar2=None, op0=ALU.mult), [rsm, R("sgn")], [rsm])
            fw.op(V, lambda e: e.tensor_copy(out=s(17), in_=e1r), [rE], [rsm])
            fw.op(V, lambda e: e.tensor_scalar(out=s(18), in0=e1i, scalar1=sgn[:, 0:1], scalar2=None, op0=ALU.mult), [rE, R("sgn")], [rsm])
            fw.op(V, lambda e: e.tensor_scalar(out=s(19), in0=e1i, scalar1=sgn[:, 1:2], scalar2=None, op0=ALU.mult), [rE, R("sgn")], [rsm])
            B = lambda i: bc(sm[:, i, d, :].unsqueeze(2), [128, NG, 16])
            rB = R("BA")
            Ba, Bbs, Bpa, Bpbs = BA[d]
            tt(V, Ba, ba, B(8), ALU.mult, [rb, rsm], [rB])
            tt(V, Bpa, bb, B(10), ALU.mult, [rb, rsm], [rB])
            tt(V, Ba, Ba, Bpa, ALU.add, [rB], [rB])
            tt(V, Bbs, bb, B(8), ALU.mult, [rb, rsm], [rB])
            tt(V, Bpa, ba, B(11), ALU.mult, [rb, rsm], [rB])
            tt(V, Bbs, Bbs, Bpa, ALU.add, [rB], [rB])
            tt(V, Bpa, Ba, B(12), ALU.mult, [rB, rsm], [rB])
            tt(V, Bpbs, Bbs, B(13), ALU.mult, [rB, rsm], [rB])
            tt(V, Bpa, Bpa, Bpbs, ALU.add, [rB], [rB])
            tt(V, Bpbs, Bbs, B(12), ALU.mult, [rB, rsm], [rB])
            tt(V, gsc[:, 0:NG * 16].rearrange("p (a b) -> p a b", a=NG), Ba, B(14), ALU.mult, [rB, rsm], [R("gsc")])
            tt(V, Bpbs, Bpbs, gsc[:, 0:NG * 16].rearrange("p (a b) -> p a b", a=NG), ALU.add, [rB, R("gsc")], [rB])
            fw.op(V, lambda e: e.tensor_scalar(out=Bbs, in0=Bbs, scalar1=sgn[:, 0:1], scalar2=None, op0=ALU.mult), [rB, R("sgn")], [rB])
            fw.op(V, lambda e: e.tensor_scalar(out=Bpbs, in0=Bpbs, scalar1=sgn[:, 0:1], scalar2=None, op0=ALU.mult), [rB, R("sgn")], [rB])
            Cas, Cbn = CA[d]
            rC = R("CA")
            tmpc = gsc[:, 256:256 + NG * 16].rearrange("p (a b) -> p a b", a=NG)
            tt(V, Cas, Ca, B(17), ALU.mult, [rb, rsm], [rC])
            tt(V, tmpc, Cb, B(18), ALU.mult, [rb, rsm], [R("gsc")])
            tt(V, Cas, Cas, tmpc, ALU.add, [rC, R("gsc")], [rC])
            tt(V, Cbn, Cb, B(17), ALU.mult, [rb, rsm], [rC])
            tt(V, tmpc, Ca, B(19), ALU.mult, [rb, rsm], [R("gsc")])
            tt(V, Cbn, Cbn, tmpc, ALU.add, [rC, R("gsc")], [rC])
            fw.op(V, lambda e: e.tensor_scalar(out=Cas, in0=Cas, scalar1=sgn[:, 1:2], scalar2=None, op0=ALU.mult), [rC, R("sgn")], [rC])
            fw.op(V, lambda e: e.tensor_scalar(out=Cbn, in0=Cbn, scalar1=-1.0, scalar2=None, op0=ALU.mult), [rC], [rC])
        for d_ in range(2):
            coef_(d_)
        rAK = R("AK")
        fw.op(V, lambda e: e.tensor_copy(out=AKr[:, 0], in_=Ere[:, :, :, 64]), [rE], [rAK])
        fw.op(V, lambda e: e.tensor_copy(out=AKs[:, 0], in_=Eim[:, :, :, 64]), [rE], [rAK])
        for k in range(1, 8):
            t15, t16 = sm[:, 15], sm[:, 16]
            tt(V, t15, AKr[:, k - 1], AKr[:, k - 1], ALU.mult, [rAK], [rsm])
            tt(V, t16, AKs[:, k - 1], AKs[:, k - 1], ALU.mult, [rAK], [rsm])
            tt(V, AKr[:, k], t15, t16, ALU.subtract, [rsm], [rAK])
            tt(V, t15, AKr[:, k - 1], AKs[:, k - 1], ALU.mult, [rAK], [rsm])
            fw.op(V, lambda e, k=k: e.tensor_scalar(out=AKs[:, k], in0=t15, scalar1=2.0, scalar2=None, op0=ALU.mult), [rsm], [rAK])
        fw.op(V, lambda e: e.tensor_scalar(out=AKs, in0=AKs, scalar1=sgn[:, 1:2], scalar2=None, op0=ALU.mult), [rAK, R("sgn")], [rAK])

        fw.barrier(self.R.values())
        def grp_(gi):
            sl = (g0 + gi) % 2
            U8, M3b, M1b, HHb = U8_L[sl], M3b_L[sl], M1b_L[sl], HHb_L[sl]
            R1 = R_glob
            SL = ("U8", "M1b0", "M1b1", "M3b0", "M3b1", "HH0", "HH1")
            R = lambda n: R1(n + "_s%d" % sl) if n in SL else R1(n)
            pub = [pb[0][:].bitcast(BF16), pb[1][:].bitcast(BF16)]
            for ct in range(9):
                fw.op(G_ if ct % 2 == 0 else "scalar",
                      (lambda e, ct=ct: e.tensor_copy(out=Xg[:, ct, :].rearrange("p (a b) -> p a b", a=8), in_=X8[:, ct, :, gi * 16:(gi + 1) * 16]))
                      if ct % 2 == 0 else
                      (lambda e, ct=ct: e.activation(out=Xg[:, ct, :].rearrange("p (a b) -> p a b", a=8), in_=X8[:, ct, :, gi * 16:(gi + 1) * 16],
                                                     func=AF.Copy)), [R("X8")], [R("Xg")])
            for ct in range(9):
                npart = 128 if ct < 8 else 32
                bank, off = (0, ct * 128) if ct < 8 else (1, 0)
                fw.op("tensor", lambda e, ct=ct, npart=npart, bank=bank, off=off: e.transpose(
                    pub[bank][:, off:off + npart], Xg[0:npart, ct, :], self.identB[0:npart, 0:npart]),
                    [R("Xg"), rI], [rpb[bank]])
            fw.op("scalar", lambda e: e.activation(out=U8[:, 0:1024], in_=pub[0][:, 0:1024], func=AF.Copy), [rpb[0]], [R("U8")])
            fw.op("scalar", lambda e: e.activation(out=U8[:, 1024:1056], in_=pub[1][:, 0:32], func=AF.Copy), [rpb[1]], [R("U8")])
            U8v = U8.rearrange("p (c j) -> p c j", j=8)
            for d in range(2):
                Ba, Bbs, Bpa, Bpbs = BA[d]
                Cas, Cbn = CA[d]
                rM = R("M%d" % d)
                if d == 0:
                    eM2r, eM2i = ErD[:, d, gi, 1:65], EiD[:, d, gi, 1:65]
                    eM3r, eM3i = Ere[:, d, gi, 0:64], Eim[:, d, gi, 0:64]
                    ePr, ePi = ErD[:, d, gi, 1:9], EiD[:, d, gi, 1:9]
                else:
                    eM2r, eM2i = Ere[:, d, gi, 0:64], Eim[:, d, gi, 0:64]
                    eM3r, eM3i = ErD[:, d, gi, 1:65], EiD[:, d, gi, 1:65]
                    ePr, ePi = Ere[:, d, gi, 56:64], Eim[:, d, gi, 56:64]
                b64 = lambda ap: bc(ap.unsqueeze(2), [128, 64, 16])
                w64 = lambda ap: bc(ap.unsqueeze(1), [128, 64, 16])
                tmp = gsc[:, 0:1024].rearrange("p (a b) -> p a b", a=64)
                rg = R("gsc")
                tt(V, M2T[d], b64(eM2r), w64(Ba[:, gi, :]), ALU.mult, [rE, R("BA")], [rM])
                tt(G_, tmp, b64(eM2i), w64(Bbs[:, gi, :]), ALU.mult, [rE, R("BA")], [rg])
                tt(V, M2T[d], M2T[d], tmp, ALU.add, [rM, rg], [rM])
                tt(V, M3f[d], b64(eM3r), w64(Cas[:, gi, :]), ALU.mult, [rE, R("CA")], [rM])
                tt(G_, tmp, b64(eM3i), w64(Cbn[:, gi, :]), ALU.mult, [rE, R("CA")], [rg])
                tt(V, M3f[d], M3f[d], tmp, ALU.add, [rM, rg], [rM])
                tmp8 = gsc[:, 0:128].rearrange("p (a b) -> p a b", a=8)
                tt(V, M2Tp[d], bc(ePr.unsqueeze(2), [128, 8, 16]), bc(Bpa[:, gi, :].unsqueeze(1), [128, 8, 16]), ALU.mult, [rE, R("BA")], [rM])
                tt(V, tmp8, bc(ePi.unsqueeze(2), [128, 8, 16]), bc(Bpbs[:, gi, :].unsqueeze(1), [128, 8, 16]), ALU.mult, [rE, R("BA")], [rg])
                tt(V, M2Tp[d], M2Tp[d], tmp8, ALU.add, [rM, rg], [rM])
                fw.op("scalar", lambda e, d=d: e.activation(out=M3b[d].rearrange("p a b -> p (a b)"), in_=M3f[d].rearrange("p a b -> p (a b)"),
                                                           func=AF.Copy), [rM], [R("M3b%d" % d)])
                for j in range(8):
                    bank = 2 + j // 4
                    fw.op("tensor", lambda e, d=d, j=j, bank=bank: e.transpose(
                        pb[bank][:, (j % 4) * 128:(j % 4 + 1) * 128], M2T[d][:, j * 8:(j + 1) * 8, :].rearrange("p a b -> p (a b)"),
                        self.identF[:]), [rM, rI], [rpb[bank]])
                for hb_ in range(2):
                    fw.op("scalar" if hb_ == 0 else V, (lambda e, d=d, hb_=hb_: e.activation(
                        out=M2b[d][:, hb_ * 4:(hb_ + 1) * 4, :].rearrange("p a b -> p (a b)"), in_=pb[2 + hb_][:], func=AF.Copy))
                        if hb_ == 0 else (lambda e, d=d, hb_=hb_: e.tensor_copy(
                            out=M2b[d][:, hb_ * 4:(hb_ + 1) * 4, :].rearrange("p a b -> p (a b)"), in_=pb[2 + hb_][:])),
                        [rpb[2 + hb_]], [R("M2b%d" % d)])
                for hb_ in range(2):
                    fw.op("tensor", lambda e, d=d, hb_=hb_: e.matmul(
                        pb[2 + hb_][:], lhsT=M2Tp[d].rearrange("p a b -> p (a b)"),
                        rhs=M3f[d][:, hb_ * 32:(hb_ + 1) * 32, :].rearrange("p a b -> p (a b)"), start=True, stop=True),
                        [rM], [rpb[2 + hb_]])
                if d == 0:
                    blk = pb[2][:, 0:128].rearrange("p (a b) -> p a b", a=8)
                    t8 = gsc[:, 0:128].rearrange("p (a b) -> p a b", a=8)
                    tt(V, t8, blk, maskF, ALU.mult, [rpb[2], R("mask")], [rg])
                    fw.op(V, lambda e: e.scalar_tensor_tensor(out=gsc[:, 0:128], in0=self.identF[:], scalar=Dcol[:, gi:gi + 1],
                                                              in1=gsc[:, 0:128], op0=ALU.mult, op1=ALU.add), [rg, rI, R("Dcol")], [rg])
                    fw.op(V, lambda e, d=d: e.tensor_copy(out=M1b[d][:, 0, :], in_=gsc[:, 0:128]), [rg], [R("M1b%d" % d)])
                    fw.op("scalar", lambda e, d=d: e.activation(out=M1b[d][:, 1:4, :].rearrange("p a b -> p (a b)"), in_=pb[2][:, 128:512],
                                                               func=AF.Copy), [rpb[2]], [R("M1b%d" % d)])
                    fw.op("scalar", lambda e, d=d: e.activation(out=M1b[d][:, 4:8, :].rearrange("p a b -> p (a b)"), in_=pb[3][:],
                                                               func=AF.Copy), [rpb[3]], [R("M1b%d" % d)])
                else:
                    blk = pb[3][:, 384:512].rearrange("p (a b) -> p a b", a=8)
                    tt(V, M1b[d][:, 7, :].rearrange("p (a b) -> p a b", a=8), blk, maskB, ALU.mult, [rpb[3], R("mask")], [R("M1b%d" % d)])
                    fw.op("scalar", lambda e, d=d: e.activation(out=M1b[d][:, 0:4, :].rearrange("p a b -> p (a b)"), in_=pb[2][:],
                                                               func=AF.Copy), [rpb[2]], [R("M1b%d" % d)])
                    fw.op("scalar", lambda e, d=d: e.activation(out=M1b[d][:, 4:7, :].rearrange("p a b -> p (a b)"), in_=pb[3][:, 0:384],
                                                               func=AF.Copy), [rpb[3]], [R("M1b%d" % d)])
                rA = R("Ak%d" % d)
                for k in range(8):
                    fw.op("scalar", lambda e, d=d, k=k: e.activation(out=Ak[d][:, k, :], in_=self.identF[:], func=AF.Copy,
                                                                    scale=AKr[:, k, d, gi:gi + 1]), [rI, rAK], [rA])
                    fw.op(V, lambda e, d=d, k=k: e.scalar_tensor_tensor(out=Ak[d][:, k, :], in0=swapM, scalar=AKs[:, k, d, gi:gi + 1],
                                                                       in1=Ak[d][:, k, :], op0=ALU.mult, op1=ALU.add),
                          [R("swapM"), rAK, rA], [rA])
                ps = pb[4]
                if d == 0:
                    for j in range(8):
                        fw.op("tensor", lambda e, d=d, j=j: e.matmul(ps[:, 0:132], lhsT=M2b[d][:, j, :], rhs=U8v[:, :, j],
                                                                    start=(j == 0), stop=(j == 7)), [R("M2b%d" % d), R("U8")], [rpb[4]])
                else:
                    for j in range(8):
                        fw.op("tensor", lambda e, d=d, j=j: e.matmul(ps[:, 0:128], lhsT=M2b[d][:, j, :], rhs=U8v[:, 4:132, j],
                                                                    start=(j == 0), stop=(j == 7)), [R("M2b%d" % d), R("U8")], [rpb[4]])
                    for j in range(8):
                        fw.op("tensor", lambda e, d=d, j=j: e.matmul(ps[:, 128:132], lhsT=M2b[d][:, j, :], rhs=U8v[:, 0:4, j],
                                                                    start=False, stop=(j == 7), skip_group_check=True),
                              [R("M2b%d" % d), R("U8")], [rpb[4]])
                rP = R("P%d" % d)
                fw.op(V, lambda e, d=d: e.tensor_copy(out=Pst[d], in_=ps[:, 0:132]), [rpb[4]], [rP])
                for k in range(8):
                    sft = 1 << k
                    if d == 0:
                        o_sl, i_sl = slice(sft, 132), slice(0, 132 - sft)
                    else:
                        o_sl, i_sl = slice(0, 132 - sft), slice(sft, 132)
                    fw.op("tensor", lambda e, d=d, k=k, o_sl=o_sl, i_sl=i_sl: e.matmul(ps[:, o_sl], lhsT=Ak[d][:, k, :], rhs=Pst[d][:, i_sl],
                                                                                     start=True, stop=True), [rA, rP], [rpb[4]])
                    fw.op(V, lambda e, d=d, o_sl=o_sl: e.tensor_tensor(out=Pst[d][:, o_sl], in0=Pst[d][:, o_sl], in1=ps[:, o_sl], op=ALU.add),
                          [rP, rpb[4]], [rP])
                rH = R("HH%d" % d)
                fw.op(G_, lambda e, d=d: e.memset(HHb[d], 0.0), [], [rH])
                if d == 0:
                    fw.op(V, lambda e, d=d: e.tensor_copy(out=HHb[d][:, 1:132], in_=Pst[d][:, 0:131]), [rP, rH], [rH])
                else:
                    fw.op(V, lambda e, d=d: e.tensor_copy(out=HHb[d][:, 0:131], in_=Pst[d][:, 1:132]), [rP, rH], [rH])
            for jt in range(8):
                bank = 5 + jt // 3
                yo_ = pb[bank][:, (jt % 3) * 132:(jt % 3 + 1) * 132]
                ops = []
                for js in range(0, jt + 1):
                    ops.append((yo_, M1b[0][:, jt - js, :], U8v[:, :, js], [R("M1b0"), R("U8")]))
                for js in range(jt, 8):
                    ops.append((yo_, M1b[1][:, 7 - (js - jt), :], U8v[:, :, js], [R("M1b1"), R("U8")]))
                ops.append((yo_, M3b[0][:, jt, :], HHb[0][:, 0:132], [R("M3b0"), R("HH0")]))
                ops.append((yo_[:, 4:132], M3b[1][:, jt, :], HHb[1][:, 0:128], [R("M3b1"), R("HH1")]))
                ops.append((yo_[:, 0:4], M3b[1][:, jt, :], HHb[1][:, 128:132], [R("M3b1"), R("HH1")]))
                for n_, (o_, lt, rh, rd) in enumerate(ops):
                    fw.op("tensor", lambda e, o_=o_, lt=lt, rh=rh, n_=n_, last=(n_ == len(ops) - 1): e.matmul(
                        o_, lhsT=lt, rhs=rh, start=(n_ == 0), stop=last, skip_group_check=True), rd, [rpb[bank]])
            gy8v = gy8.rearrange("p (c j) -> p j c", j=8)
            gscv = gsc2[:, 0:1056].rearrange("p (j c) -> p j c", j=8)
            rg = R("gsc2")
            for b3 in range(3):
                njt = 3 if b3 < 2 else 2
                src_ = pb[5 + b3][:, 0:njt * 132].rearrange("p (j c) -> p j c", j=njt)
                tmp_ = gscv[:, b3 * 3:b3 * 3 + njt, :]
                dst_ = gy8v[:, b3 * 3:b3 * 3 + njt, :]
                rp_ = rpb[5 + b3]
                fw.op("scalar", lambda e, src_=src_, tmp_=tmp_: e.activation(out=tmp_, in_=src_, func=AF.Square), [rp_], [rg])
                fw.op(V, lambda e, tmp_=tmp_: e.tensor_scalar(out=tmp_, in0=tmp_, scalar1=0.044715, scalar2=1.0, op0=ALU.mult, op1=ALU.add),
                      [rg], [rg])
                tt(V, tmp_, tmp_, src_, ALU.mult, [rg, rp_], [rg])
                fw.op("scalar", lambda e, tmp_=tmp_: e.activation(out=tmp_, in_=tmp_, func=AF.Sigmoid, scale=1.5957691216), [rg], [rg])
                tt(V, dst_, tmp_, src_, ALU.mult, [rg, rp_], [R("gy8")])
            for ct in range(9):
                npart = 128 if ct < 8 else 32
                bank, off = (0, ct * 128) if ct < 8 else (1, 0)
                fw.op("tensor", lambda e, ct=ct, npart=npart, bank=bank, off=off: e.transpose(
                    pub[bank][0:npart, off:off + 128], gy8[:, ct * 128:ct * 128 + npart], self.identB[:]),
                    [R("gy8"), rI], [rpb[bank]])
                fw.op("scalar" if ct % 2 == 0 else V,
                      (lambda e, ct=ct, npart=npart, bank=bank, off=off: e.activation(
                          out=Ytok[0:npart, ct, :, gi * 16:(gi + 1) * 16], in_=pub[bank][0:npart, off:off + 128].rearrange("p (a b) -> p a b", a=8),
                          func=AF.Copy)) if ct % 2 == 0 else
                      (lambda e, ct=ct, npart=npart, bank=bank, off=off: e.tensor_copy(
                          out=Ytok[0:npart, ct, :, gi * 16:(gi + 1) * 16], in_=pub[bank][0:npart, off:off + 128].rearrange("p (a b) -> p a b", a=8))),
                      [rpb[bank]], [R("Ytok")])
        for gi_ in range(NG):
            grp_(gi_)
        fw.dma("sync", [(self.gy[ct * 1024:(ct + 1) * 1024, g0 * 16:g0 * 16 + NG * 16].rearrange("(c s) w -> c s w", s=8), Ytok[:, ct])
                        for ct in range(8)], [R("Ytok")], [self.res("gy")])
        fw.dma("sync", [(self.gy[8192:8448, g0 * 16:g0 * 16 + NG * 16].rearrange("(c s) w -> c s w", s=8), Ytok[0:32, 8])],
               [R("Ytok")], [self.res("gy")])
    print("s5 arena end", self.aoff)


Prog.s5 = _s5
```

```python
from contextlib import ExitStack
import numpy as np
import concourse.bass as bass
import concourse.mybir as mybir
from concourse.bass_utils import run_bass_kernel_spmd

F32 = mybir.dt.float32
BF16 = mybir.dt.bfloat16
I32 = mybir.dt.int32
AF = mybir.ActivationFunctionType
ALU = mybir.AluOpType
AX = mybir.AxisListType

D = 1024
L = 8192
CT = 256
NT = L + CT
NTILE = NT // 128
INW = 2592
DEPTH = 4
EPS = 1e-6


import heapq


class Res:
    __slots__ = ("name", "w", "r", "dsem", "dcount", "last_dma")

    def __init__(self, name):
        self.name = name
        self.w = None
        self.r = []
        self.dsem = None
        self.dcount = 0
        self.last_dma = None


class _ProbeInst:
    def then_inc(self, *a, **k):
        return self


class _Probe:
    def __init__(self):
        self.name = None
        self.args = None
        self.kw = None

    def __getattr__(self, name):
        def f(*args, **kw):
            self.name, self.args, self.kw = name, args, kw
            return _ProbeInst()
        return f


def _fsize(ap):
    n = 1
    for d_ in ap.shape[1:]:
        n *= d_
    return n


class FW:
    SEM_LIMIT = 24000
    HOP = 1.2

    def __init__(self, nc, stack, schedule=True):
        self.nc = nc
        self.stack = stack
        self.schedule = schedule
        self.nsem = 0
        self.nodes = []
        self.bar = None
        self.bar_start = 0
        self.engnames = ("tensor", "vector", "scalar", "gpsimd", "sync")

    def new_sem(self, name):
        self.nsem += 1
        return self.stack.enter_context(self.nc.semaphore("%s_%d" % (name, self.nsem)))

    def sb(self, name, shape, dt):
        return self.stack.enter_context(self.nc.sbuf_tensor(name, list(shape), dt))

    def ps(self, name, shape, dt):
        return self.stack.enter_context(self.nc.psum_tensor(name, list(shape), dt))

    def _deps(self, reads, writes):
        deps = set()
        for r in reads:
            if r.w is not None:
                deps.add(r.w)
        for w in writes:
            if w.w is not None:
                deps.add(w.w)
            deps.update(w.r)
        if self.bar is not None:
            deps.add(self.bar)
        return deps

    def _cost(self, engname, fn):
        p = _Probe()
        try:
            fn(p)
            nm, kw, args = p.name, p.kw, p.args
            if nm == "matmul":
                rhs = kw["rhs"]
                n = _fsize(rhs)
                passes = 4 if rhs.dtype == F32 else 1
                return max(64, n) * passes / 2400.0 + 0.03
            if nm == "transpose" and engname == "tensor":
                return 128 / 2400.0 + 0.05
            ap = kw.get("out", None)
            if ap is None:
                ap = args[0]
            n = _fsize(ap)
            if engname == "vector":
                return n * 1.3 / 960.0 + 0.12
            if engname == "scalar":
                return n / 1200.0 + 0.25
            return n * 2.0 / 1200.0 + 0.3
        except Exception:
            return 0.5

    def op(self, engname, fn, reads=(), writes=()):
        nid = len(self.nodes)
        deps = self._deps(reads, writes)
        self.nodes.append(dict(id=nid, eng=engname, kind="op", fn=fn, deps=deps, cost=self._cost(engname, fn)))
        for r in reads:
            r.r.append(nid)
        for w in writes:
            w.w = nid
            w.r = []
        return nid

    def dma(self, qname, pairs, reads, writes, sres=None):
        sres = sres or writes[0]
        qt = "sw" if qname == "gpsimd" else "hw"
        if not isinstance(sres.dsem, dict):
            sres.dsem = {}
        st_ = sres.dsem.get(qt)
        if st_ is None or st_[1] >= 16 * 3000:
            st_ = [self.new_sem("d%s_%s" % (qt, sres.name)), 0, None]
            sres.dsem[qt] = st_
        nid = len(self.nodes)
        deps = self._deps(reads, writes)
        if st_[2] is not None:
            deps.add(st_[2])
        nbytes = 0
        for pr in pairs:
            if callable(pr):
                nbytes += 8 << 20
                continue
            (o, i) = pr
            n = 1
            for d_ in o.shape:
                n *= d_
            nbytes += n * (4 if o.dtype in (F32, I32) else 2)
        st_[1] += 16 * len(pairs)
        self.nodes.append(dict(id=nid, eng=qname, kind="dma", pairs=pairs, deps=deps, cost=0.08 * len(pairs), nbytes=nbytes,
                               dsem=st_[0], dval=st_[1]))
        st_[2] = nid
        for r in reads:
            r.r.append(nid)
        for w in writes:
            w.w = nid
            w.r = []
        return nid

    def barrier(self, _unused=None):
        nid = len(self.nodes)
        deps = set(range(self.bar_start, nid))
        self.nodes.append(dict(id=nid, eng=None, kind="bar", deps=deps, cost=0.0))
        self.bar = nid
        self.bar_start = nid

    def _simulate(self):
        nodes = self.nodes
        n = len(nodes)
        fin = [0.0] * n
        start = [0.0] * n
        if not self.schedule:
            order = {e: [] for e in self.engnames}
            for nd in nodes:
                if nd["eng"] is not None:
                    order[nd["eng"]].append(nd["id"])
            return order
        children = [[] for _ in range(n)]
        rem = [0] * n
        for nd in nodes:
            rem[nd["id"]] = len(nd["deps"])
            for d_ in nd["deps"]:
                children[d_].append(nd["id"])
        ready = [0.0] * n
        heap = []
        for nd in nodes:
            if rem[nd["id"]] == 0:
                heapq.heappush(heap, (0.0, nd["id"]))
        efree = {e: 0.0 for e in self.engnames}
        dma_free = 0.0
        order = {e: [] for e in self.engnames}
        done = 0
        while heap:
            rt, nid = heapq.heappop(heap)
            nd = nodes[nid]
            e = nd["eng"]
            if e is None:
                st = rt
                f = rt
            else:
                st = max(rt, efree[e])
                efree[e] = st + nd["cost"]
                order[e].append((st, nid))
                if nd["kind"] == "dma":
                    t0 = max(st + nd["cost"], dma_free)
                    dur = nd["nbytes"] / 150e3
                    dma_free = t0 + dur
                    f = t0 + dur + 2.0
                else:
                    f = st + nd["cost"]
            start[nid] = st
            fin[nid] = f
            done += 1
            for c in children[nid]:
                hop = 0.0 if (nodes[c]["eng"] == e and e == "tensor") else self.HOP
                if nodes[c]["kind"] == "bar" or nd["kind"] == "bar":
                    hop = 0.0
                ready[c] = max(ready[c], f + hop)
                rem[c] -= 1
                if rem[c] == 0:
                    heapq.heappush(heap, (ready[c], c))
        assert done == n, (done, n)
        self.sim_time = max(fin) if fin else 0.0
        out = {}
        for e in self.engnames:
            lst = sorted(order[e])
            out[e] = [nid for (_, nid) in lst]
        return out

    def finish(self, final_res):
        nc = self.nc
        nodes = self.nodes
        fdeps = set(r.w for r in final_res if r.w is not None)
        nid = len(nodes)
        nodes.append(dict(id=nid, eng="sync", kind="waitonly", deps=fdeps | set(range(self.bar_start, nid)), cost=0.0))
        order = self._simulate()
        tok = {}
        for e in self.engnames:
            sem = self.new_sem("prog_" + e)
            cnt = 0
            for nid_ in order[e]:
                nd = nodes[nid_]
                if nd["kind"] == "op":
                    if cnt >= self.SEM_LIMIT:
                        sem = self.new_sem("prog_" + e)
                        cnt = 0
                    cnt += 1
                    tok[nid_] = (sem, cnt, e)
                    nd["sem"] = sem
                elif nd["kind"] == "dma":
                    tok[nid_] = (nd["dsem"], nd["dval"], "dma")
        bartok = {}
        for nd in nodes:
            if nd["kind"] == "bar":
                best = {}
                for d_ in nd["deps"]:
                    if nodes[d_]["kind"] == "bar":
                        for k, v in bartok[d_].items():
                            if k not in best or best[k][1] < v[1]:
                                best[k] = v
                    elif d_ in tok:
                        s_, v_, en = tok[d_]
                        k = id(s_)
                        if k not in best or best[k][1] < v_:
                            best[k] = (s_, v_, "bar")
                bartok[nd["id"]] = best
        selfwait = {"vector": True, "scalar": True, "gpsimd": True, "tensor": False, "sync": False}
        progs = {}
        for e in self.engnames:
            waited = {}
            prog = []
            for nid_ in order[e]:
                nd = nodes[nid_]
                best = {}
                for d_ in nd["deps"]:
                    if nodes[d_]["kind"] == "bar":
                        items = bartok[d_].values()
                    elif d_ in tok:
                        items = [tok[d_]]
                    else:
                        items = []
                    for (s_, v_, en) in items:
                        if en == e and not selfwait[e]:
                            continue
                        k = id(s_)
                        if waited.get(k, 0) >= v_:
                            continue
                        if k not in best or best[k][1] < v_:
                            best[k] = (s_, v_)
                for k, (s_, v_) in best.items():
                    waited[k] = v_
                    prog.append(("wait", s_, v_))
                if nd["kind"] == "op":
                    prog.append(("op", nd["fn"], nd["sem"]))
                elif nd["kind"] == "dma":
                    for pr in nd["pairs"]:
                        if callable(pr):
                            prog.append(("cdma", pr, None, nd["dsem"]))
                        else:
                            prog.append(("dma", pr[0], pr[1], nd["dsem"]))
            progs[e] = prog
        self.progs = progs

        def run(e, prog):
            for it in prog:
                if it[0] == "wait":
                    e.wait_ge(it[1], it[2])
                elif it[0] == "op":
                    it[1](e).then_inc(it[2], 1)
                elif it[0] == "cdma":
                    it[1](e).then_inc(it[3], 16)
                else:
                    e.dma_start(out=it[1], in_=it[2]).then_inc(it[3], 16)

        with nc.allow_non_contiguous_dma(reason="small param layouts"), nc.Block() as block:
            @block.tensor
            def _(e):
                run(e, progs["tensor"])

            @block.vector
            def _(e):
                run(e, progs["vector"])

            @block.scalar
            def _(e):
                run(e, progs["scalar"])

            @block.gpsimd
            def _(e):
                run(e, progs["gpsimd"])

            @block.sync
            def _(e):
                run(e, progs["sync"])


class Prog:
    def __init__(self, depth=DEPTH, stub_s5=False, stub_gla=False, schedule=True):
        self.depth = depth
        self.schedule = schedule
        self.stub_s5 = stub_s5
        self.stub_gla = stub_gla
        nc = self.nc = bass.Bass("TRN2", target_bir_lowering=False)
        di = lambda n, s, dt=F32: nc.dram_tensor(n, list(s), dt, kind="ExternalInput").ap()
        ds = lambda n, s, dt=F32: nc.dram_tensor(n, list(s), dt, kind="Internal").ap()
        self.x = di("x", [L, D])
        self.ctx = di("ctx", [CT, D])
        self.cc = di("cc", [2, D])
        self.norm_g = di("norm_g", [DEPTH, D])
        self.w_mod = di("w_mod", [DEPTH, D, 3 * D])
        self.b_mod = di("b_mod", [DEPTH, 3 * D])
        self.w_in = di("w_in", [DEPTH, D, INW])
        self.lam_re = di("s5_lam_re", [DEPTH, 2, 32, 64])
        self.lam_im = di("s5_lam_im", [DEPTH, 2, 32, 64])
        self.log_dt = di("s5_log_dt", [DEPTH, 2, 32])
        self.b_re = di("s5_b_re", [DEPTH, 32, 64, 16])
        self.b_im = di("s5_b_im", [DEPTH, 32, 64, 16])
        self.c_re = di("s5_c_re", [DEPTH, 32, 16, 64])
        self.c_im = di("s5_c_im", [DEPTH, 32, 16, 64])
        self.s5_d = di("s5_d", [DEPTH, 512])
        self.w_glu = di("s5_w_glu", [DEPTH, 512, 512])
        self.b_glu = di("s5_b_glu", [DEPTH, 512])
        self.w_gate = di("gla_w_gate", [DEPTH, 2, 16, 256])
        self.b_gate = di("gla_b_gate", [DEPTH, 2, 256])
        self.gnorm = di("gla_norm_g", [DEPTH, 128])
        self.w_out = di("w_out", [DEPTH, D, D])
        self.final_norm = di("final_norm", [1, D])
        self.out = nc.dram_tensor("out", [L, D], F32, kind="ExternalOutput").ap()
        self.xs = [ds("xs0", [NT, D]), ds("xs1", [NT, D])]
        self.P = ds("P", [NT, INW], BF16)
        import os
        if os.environ.get("DEBUG_GY"):
            self.gy = nc.dram_tensor("gy", [NT, 512], BF16, kind="ExternalOutput").ap()
        else:
            self.gy = ds("gy", [NT, 512], BF16)
        self.yg = ds("yg", [NT, 512], BF16)
        self.R = {}
        with ExitStack() as st:
            self.fw = FW(nc, st, schedule=self.schedule)
            self.alloc()
            self.consts()
            for l in range(depth):
                self.weights_in(l)
                self.modulation(l)
                self.phase1(l)
                if stub_s5:
                    self.s5_stub(l)
                else:
                    self.fw.barrier(self.R.values())
                    self.s5(l)
                    self.fw.barrier(self.R.values())
                if stub_gla:
                    self.gla_stub(l)
                else:
                    self.fw.barrier(self.R.values())
                    self.gla(l)
                    self.fw.barrier(self.R.values())
                self.weights_out(l)
                self.phase3(l)
            self.fw.finish([self.res("out")])

    def res(self, name):
        if name not in self.R:
            self.R[name] = Res(name)
        return self.R[name]

    def view(self, shape, dt):
        n = 1
        for d_ in shape[1:]:
            n *= d_
        words = n if dt in (F32, I32) else (n + 1) // 2
        ap = self.arena[:, self.aoff:self.aoff + words]
        self.aoff += words
        assert self.aoff <= self.NW, (self.aoff, self.NW)
        if dt != F32:
            ap = ap.bitcast(dt)
        if len(shape) == 3:
            ap = ap.rearrange("p (a b) -> p a b", a=shape[1])
        elif len(shape) == 4:
            ap = ap.rearrange("p (a b c) -> p a b c", a=shape[1], b=shape[2])
        return ap

    def alloc(self):
        fw = self.fw
        self.NW = 52600
        self.arena = fw.sb("arena", [128, self.NW], F32)
        self.aoff = 0
        self.identF = fw.sb("identF", [128, 128], F32)
        self.identB = fw.sb("identB", [128, 128], BF16)
        self.ccT = fw.sb("ccT", [128, 8, 2], F32)
        self.st1 = [fw.sb("st1_%d" % i, [128, 4], F32) for i in range(2)]
        self.pb = [fw.ps("pb%d" % i, [128, 512], F32) for i in range(8)]
        v = self.view
        self.scB = v([128, 8, 2, 128], F32)
        self.modb = v([128, 2, 3 * D], F32)
        self.bglub = v([128, 512], F32)
        self.fnb = v([128, D], F32)
        self.phase_base = self.aoff
        self.winb = v([128, 8, INW], BF16)
        self.woutb = v([128, 8, D], BF16)
        self.wglub = v([128, 4, 512], BF16)
        self.wst = v([128, 6144], F32)
        self.xt = [v([128, D], F32) for i in range(2)]
        self.xn = [v([128, D], F32) for i in range(2)]
        self.yo = [v([128, D], F32) for i in range(2)]
        self.hb = [v([128, D], BF16) for i in range(2)]
        self.mix = self.hb
        self.hT = [v([128, 8, 128], BF16) for i in range(2)]
        self.mixT = self.hT
        self.pj = [v([128, INW], BF16) for i in range(2)]
        self.g3 = [v([128, 2048], BF16) for i in range(2)]
        self.gyT = [v([128, 4, 128], BF16) for i in range(2)]
        self.t3 = [v([128, 512], F32) for i in range(2)]
        self.dense_end = self.aoff
        print("arena dense end", self.dense_end, "phase_base", self.phase_base)

    def consts(self):
        fw = self.fw
        identF, identB = self.identF, self.identB
        rI = self.res("ident")
        fw.op("gpsimd", lambda e: e.memset(identF[:], 0.0), [], [rI])
        fw.op("gpsimd", lambda e: e.affine_select(out=identF[:], in_=identF[:], compare_op=ALU.not_equal, fill=1.0,
                                                 base=0, pattern=[[-1, 128]], channel_multiplier=1), [rI], [rI])
        fw.op("gpsimd", lambda e: e.tensor_copy(out=identB[:], in_=identF[:]), [rI], [rI])
        rc = self.res("cc")
        ccT, scB = self.ccT, self.scB
        fw.dma("sync", [(ccT[:, :, j], self.cc[j, :].rearrange("(k p) -> p k", p=128)) for j in range(2)], [], [rc])
        fw.op("scalar", lambda e: e.activation(out=ccT[:], in_=ccT[:], func=AF.Silu), [rc], [rc])
        for k in range(8):
            for j in range(2):
                fw.op("vector", lambda e, k=k, j=j: e.tensor_copy(out=scB[:, k, j, :],
                                                                   in_=ccT[:, k, j:j + 1].to_broadcast([128, 128])),
                      [rc], [self.res("scB")])
        fw.dma("sync", [(self.fnb, self.final_norm[0:1, :].partition_broadcast(128)[:, 0, :])], [], [self.res("fnb")])

    def _wload(self, src_rows, ncols, dst, rdst, n):
        fw = self.fw
        s_ = n % 2
        rs = self.res("wst_s%d" % s_)
        stg = self.wst[:, s_ * 3072:s_ * 3072 + ncols]
        fw.dma("sync" if n % 2 == 0 else "gpsimd", [(stg, src_rows)], [], [rs])
        fw.op("gpsimd" if n % 2 == 0 else "vector", lambda e: e.tensor_copy(out=dst, in_=stg), [rs], [rdst])

    def weights_in(self, l):
        for k in range(8):
            self._wload(self.w_in[l, k * 128:(k + 1) * 128, :], INW, self.winb[:, k, :], self.res("winb"), k)

    def weights_out(self, l):
        fw = self.fw
        for k in range(8):
            self._wload(self.w_out[l, k * 128:(k + 1) * 128, :], D, self.woutb[:, k, :], self.res("woutb"), k)
        for k in range(4):
            self._wload(self.w_glu[l, k * 128:(k + 1) * 128, :], 512, self.wglub[:, k, :], self.res("wglub"), k)
        fw.dma("sync", [(self.bglub, self.b_glu[l:l + 1, :].partition_broadcast(128)[:, 0, :])], [], [self.res("bglub")])

    def modulation(self, l):
        fw = self.fw
        modb = self.modb
        rs0, rs1 = self.res("wst_s0"), self.res("wst_s1")
        rmod = self.res("modb")
        wv = self.wst.rearrange("p (k n) -> p k n", k=8)
        bt = self.t3[0]
        rbt = self.res("t3_0")
        for q in range(4):
            c0 = q * 768
            fw.dma("sync", [(wv[:, k, :], self.w_mod[l, k * 128:(k + 1) * 128, c0:c0 + 768]) for k in range(4)],
                   [], [rs0, rs1])
            fw.dma("gpsimd", [(wv[:, k, :], self.w_mod[l, k * 128:(k + 1) * 128, c0:c0 + 768]) for k in range(4, 8)],
                   [], [rs0, rs1], sres=self.res("wst_g"))
            for n in range(2):
                col = c0 + n * 384
                fw.dma("sync", [(bt[:, 0:384], self.b_mod[l:l + 1, col:col + 384].partition_broadcast(128)[:, 0, :])], [], [rbt])
                for j in range(2):
                    bi = (n * 2 + j) % 4
                    pb = self.pb[bi]
                    rp = self.res("pb%d" % bi)
                    for k in range(8):
                        fw.op("tensor", lambda e, k=k, j=j, n=n, pb=pb: e.matmul(
                            pb[:, 0:384], lhsT=self.scB[:, k, j, :], rhs=wv[:, k, n * 384:(n + 1) * 384],
                            start=(k == 0), stop=(k == 7)), [rs0, rs1, self.res("scB")], [rp])
                    fw.op("vector", lambda e, j=j, col=col, pb=pb: e.tensor_tensor(
                        out=modb[:, j, col:col + 384], in0=pb[:, 0:384], in1=bt[:, 0:384], op=ALU.add),
                        [rp, rbt], [rmod])
        ngb = self.xn[0]
        rng = self.res("xn0")
        fw.dma("sync", [(ngb, self.norm_g[l:l + 1, :].partition_broadcast(128)[:, 0, :])], [], [rng])
        for j in range(2):
            fw.op("vector", lambda e, j=j: e.scalar_tensor_tensor(
                out=modb[:, j, D:2 * D], in0=modb[:, j, D:2 * D], scalar=1.0, in1=ngb,
                op0=ALU.add, op1=ALU.mult), [rmod, rng], [rmod])

    def xsrc(self, l, i):
        if l == 0:
            if i < 2:
                return self.ctx[i * 128:(i + 1) * 128, :], None
            return self.x[(i - 2) * 128:(i - 1) * 128, :], None
        return self.xs[l % 2][i * 128:(i + 1) * 128, :], self.res("xs%d" % (l % 2))

    def phase1(self, l):
        fw = self.fw
        rmod = self.res("modb")
        rP = self.res("P")
        for i in range(NTILE):
            s = i % 2
            j = 1 if i < 2 else 0
            xt, xn, hb, hT, pj, st1 = self.xt[s], self.xn[s], self.hb[s], self.hT[s], self.pj[s], self.st1[s]
            rxt, rxn, rhb, rhT, rpj, rst = (self.res("%s%d" % (n, s)) for n in ("xt", "xn", "hb", "hT", "pj", "st1"))
            src, rsrc = self.xsrc(l, i)
            fw.dma("sync", [(xt[:], src)], [rsrc] if rsrc else [], [rxt])
            fw.op("scalar", lambda e, xt=xt, xn=xn, st1=st1: e.activation(out=xn[:], in_=xt[:], func=AF.Square,
                                                                        accum_out=st1[:, 0:1]), [rxt], [rxn, rst])
            fw.op("vector", lambda e, st1=st1: e.tensor_scalar(out=st1[:, 1:2], in0=st1[:, 0:1], scalar1=1.0 / D, scalar2=EPS,
                                                              op0=ALU.mult, op1=ALU.add), [rst], [rst])
            fw.op("scalar", lambda e, st1=st1: e.activation(out=st1[:, 2:3], in_=st1[:, 1:2], func=AF.Sqrt), [rst], [rst])
            fw.op("vector", lambda e, st1=st1: e.reciprocal(out=st1[:, 3:4], in_=st1[:, 2:3]), [rst], [rst])
            fw.op("vector", lambda e, xt=xt, xn=xn, st1=st1, j=j: e.scalar_tensor_tensor(
                out=xn[:], in0=xt[:], scalar=st1[:, 3:4], in1=self.modb[:, j, D:2 * D], op0=ALU.mult, op1=ALU.mult),
                [rxt, rst, rmod], [rxn])
            fw.op("gpsimd", lambda e, xn=xn, hb=hb, j=j: e.tensor_tensor(out=hb[:], in0=xn[:], in1=self.modb[:, j, 0:D], op=ALU.add),
                  [rxn, rmod], [rhb])
            ptb = self.pb[0][:].bitcast(BF16)
            rp0 = self.res("pb0")
            for k in range(8):
                fw.op("tensor", lambda e, k=k, hb=hb, ptb=ptb: e.transpose(ptb[:, k * 128:(k + 1) * 128], hb[:, k * 128:(k + 1) * 128],
                                                                         self.identB[:]), [rhb, self.res("ident")], [rp0])
            fw.op("scalar", lambda e, hT=hT, ptb=ptb: e.activation(out=hT[:].rearrange("p k t -> p (k t)"), in_=ptb, func=AF.Copy),
                  [rp0], [rhT])
            chunks = [(0, 512, "copy"), (512, 512, "silu"), (1024, 256, "q"), (1280, 256, "copy"), (1536, 512, "copy"),
                      (2048, 512, "silu"), (2560, 32, "copy")]
            groups = [(0, 512), (512, 512), (1024, 512), (1536, 512), (2048, 512), (2560, 32)]
            for gi, (c0, w) in enumerate(groups):
                b = 1 + (gi % 4)
                pb = self.pb[b]
                rp = self.res("pb%d" % b)
                for k in range(8):
                    fw.op("tensor", lambda e, k=k, c0=c0, w=w, pb=pb, hT=hT: e.matmul(
                        pb[:, 0:w], lhsT=hT[:, k, :], rhs=self.winb[:, k, c0:c0 + w], start=(k == 0), stop=(k == 7)),
                        [rhT, self.res("winb")], [rp])
                for (a0, aw, kind) in chunks:
                    if a0 < c0 or a0 >= c0 + w:
                        continue
                    o = pj[:, a0:a0 + aw]
                    src_ = pb[:, a0 - c0:a0 - c0 + aw]
                    if kind == "silu":
                        fw.op("scalar", lambda e, o=o, src_=src_: e.activation(out=o, in_=src_, func=AF.Silu), [rp], [rpj])
                    elif kind == "q":
                        fw.op("vector", lambda e, o=o, src_=src_: e.tensor_scalar(out=o, in0=src_, scalar1=0.125, scalar2=None,
                                                                                op0=ALU.mult), [rp], [rpj])
                    else:
                        fw.op("vector", lambda e, o=o, src_=src_: e.tensor_copy(out=o, in_=src_), [rp], [rpj])
            fw.dma("gpsimd", [(self.P[i * 128:(i + 1) * 128, :], pj[:])], [rpj], [rP])

    def gelu(self, eng_a, out, in_, tmp, reads, writes, rtmp):
        fw = self.fw
        fw.op("scalar", lambda e: e.activation(out=tmp, in_=in_, func=AF.Square), reads, [rtmp])
        fw.op(eng_a, lambda e: e.tensor_scalar(out=tmp, in0=tmp, scalar1=0.044715, scalar2=1.0, op0=ALU.mult, op1=ALU.add),
              [rtmp], [rtmp])
        fw.op(eng_a, lambda e: e.tensor_tensor(out=tmp, in0=tmp, in1=in_, op=ALU.mult), [rtmp] + list(reads), [rtmp])
        fw.op("scalar", lambda e: e.activation(out=tmp, in_=tmp, func=AF.Sigmoid, scale=1.5957691216), [rtmp], [rtmp])
        fw.op(eng_a, lambda e: e.tensor_tensor(out=out, in0=tmp, in1=in_, op=ALU.mult), [rtmp] + list(reads), writes)

    def s5_stub(self, l):
        fw = self.fw
        for i in range(NTILE):
            s = i % 2
            t = self.g3[s]
            rt = self.res("g3_%d" % s)
            fw.dma("sync", [(t[:, 0:512], self.P[i * 128:(i + 1) * 128, 0:512])], [self.res("P")], [rt])
            self.gelu("vector", t[:, 512:1024], t[:, 0:512], self.t3[s][:], [rt], [rt], self.res("t3_%d" % s))
            fw.dma("sync", [(self.gy[i * 128:(i + 1) * 128, :], t[:, 512:1024])], [rt], [self.res("gy")])

    def gla_stub(self, l):
        fw = self.fw
        for i in range(NTILE):
            s = i % 2
            t = self.g3[s]
            rt = self.res("g3_%d" % s)
            fw.dma("sync", [(t[:, 0:512], self.P[i * 128:(i + 1) * 128, 1536:2048])], [self.res("P")], [rt])
            fw.dma("sync", [(self.yg[i * 128:(i + 1) * 128, :], t[:, 0:512])], [rt], [self.res("yg")])

    def phase3(self, l):
        fw = self.fw
        last = (l == self.depth - 1)
        rmod = self.res("modb")
        rout = self.res("out")
        rxd = self.res("xs%d" % ((l + 1) % 2))
        for i in range(NTILE):
            if last and i < 2:
                continue
            s = i % 2
            j = 1 if i < 2 else 0
            g3, gyT, t3, mix, mixT, yo, xt, st1 = (self.g3[s], self.gyT[s], self.t3[s], self.mix[s], self.mixT[s], self.yo[s],
                                                   self.xt[s], self.st1[s])
            rg3, rgyT, rt3, rmix, rmixT, ryo, rxt, rst = (self.res("%s%d" % (n, s)) for n in
                                                          ("g3_", "gyT", "t3_", "mix", "mixT", "yo", "xt", "st1"))
            rows = slice(i * 128, (i + 1) * 128)
            fw.dma("sync", [(g3[:, 0:512], self.gy[rows, :])], [self.res("gy")], [rg3])
            fw.dma("sync", [(g3[:, 512:1024], self.yg[rows, :])], [self.res("yg")], [rg3])
            fw.dma("sync", [(g3[:, 1024:1536], self.P[rows, 512:1024]), (g3[:, 1536:2048], self.P[rows, 2048:2560])],
                   [self.res("P")], [rg3])
            src, rsrc = self.xsrc(l, i)
            fw.dma("gpsimd", [(xt[:], src)], [rsrc] if rsrc else [], [rxt])
            ptb = self.pb[5][:].bitcast(BF16)
            rp5 = self.res("pb5")
            for k in range(4):
                fw.op("tensor", lambda e, k=k, g3=g3, ptb=ptb: e.transpose(ptb[:, k * 128:(k + 1) * 128], g3[:, k * 128:(k + 1) * 128],
                                                                         self.identB[:]), [rg3, self.res("ident")], [rp5])
            fw.op("scalar", lambda e, gyT=gyT, ptb=ptb: e.activation(out=gyT[:].rearrange("p k t -> p (k t)"), in_=ptb[:, 0:512],
                                                                   func=AF.Copy), [rp5], [rgyT])
            pg = self.pb[6]
            rp6 = self.res("pb6")
            for k in range(4):
                fw.op("tensor", lambda e, k=k, gyT=gyT, pg=pg: e.matmul(pg[:], lhsT=gyT[:, k, :], rhs=self.wglub[:, k, :],
                                                                      start=(k == 0), stop=(k == 3)), [rgyT, self.res("wglub")], [rp6])
            fw.op("vector", lambda e, t3=t3, pg=pg: e.tensor_tensor(out=t3[:], in0=pg[:], in1=self.bglub, op=ALU.add),
                  [rp6, self.res("bglub")], [rt3])
            fw.op("scalar", lambda e, t3=t3: e.activation(out=t3[:], in_=t3[:], func=AF.Sigmoid), [rt3], [rt3])
            fw.op("vector", lambda e, t3=t3, g3=g3: e.tensor_tensor(out=t3[:], in0=t3[:], in1=g3[:, 0:512], op=ALU.mult),
                  [rt3, rg3], [rt3])
            fw.op("vector", lambda e, t3=t3, g3=g3, mix=mix: e.tensor_tensor(out=mix[:, 0:512], in0=t3[:], in1=g3[:, 1024:1536],
                                                                            op=ALU.mult), [rt3, rg3], [rmix])
            fw.op("gpsimd", lambda e, g3=g3, mix=mix: e.tensor_tensor(out=mix[:, 512:1024], in0=g3[:, 512:1024], in1=g3[:, 1536:2048],
                                                                     op=ALU.mult), [rg3], [rmix])
            ptm = self.pb[7][:].bitcast(BF16)
            rp7 = self.res("pb7")
            for k in range(8):
                fw.op("tensor", lambda e, k=k, mix=mix, ptm=ptm: e.transpose(ptm[:, k * 128:(k + 1) * 128], mix[:, k * 128:(k + 1) * 128],
                                                                           self.identB[:]), [rmix, self.res("ident")], [rp7])
            fw.op("scalar", lambda e, mixT=mixT, ptm=ptm: e.activation(out=mixT[:].rearrange("p k t -> p (k t)"), in_=ptm, func=AF.Copy),
                  [rp7], [rmixT])
            for n in range(2):
                b = 1 + n
                pb = self.pb[b]
                rp = self.res("pb%d" % b)
                for k in range(8):
                    fw.op("tensor", lambda e, k=k, n=n, pb=pb, mixT=mixT: e.matmul(
                        pb[:], lhsT=mixT[:, k, :], rhs=self.woutb[:, k, n * 512:(n + 1) * 512], start=(k == 0), stop=(k == 7)),
                        [rmixT, self.res("woutb")], [rp])
                cs = slice(n * 512, (n + 1) * 512)
                fw.op("vector", lambda e, pb=pb, yo=yo, cs=cs, j=j, n=n: e.tensor_tensor(
                    out=yo[:, cs], in0=pb[:], in1=self.modb[:, j, 2 * D + n * 512:2 * D + (n + 1) * 512], op=ALU.mult),
                    [rp, rmod], [ryo])
            fw.op("gpsimd", lambda e, yo=yo, xt=xt: e.tensor_tensor(out=yo[:], in0=yo[:], in1=xt[:], op=ALU.add), [ryo, rxt], [ryo])
            if not last:
                fw.dma("sync", [(self.xs[(l + 1) % 2][rows, :], yo[:])], [ryo], [rxd])
            else:
                xn = self.xn[s]
                rxn = self.res("xn%d" % s)
                fw.op("scalar", lambda e, yo=yo, xn=xn, st1=st1: e.activation(out=xn[:], in_=yo[:], func=AF.Square,
                                                                            accum_out=st1[:, 0:1]), [ryo], [rxn, rst])
                fw.op("vector", lambda e, st1=st1: e.tensor_scalar(out=st1[:, 1:2], in0=st1[:, 0:1], scalar1=1.0 / D, scalar2=EPS,
                                                                  op0=ALU.mult, op1=ALU.add), [rst], [rst])
                fw.op("scalar", lambda e, st1=st1: e.activation(out=st1[:, 2:3], in_=st1[:, 1:2], func=AF.Sqrt), [rst], [rst])
                fw.op("vector", lambda e, st1=st1: e.reciprocal(out=st1[:, 3:4], in_=st1[:, 2:3]), [rst], [rst])
                fw.op("vector", lambda e, yo=yo, xn=xn, st1=st1: e.scalar_tensor_tensor(
                    out=xn[:], in0=yo[:], scalar=st1[:, 3:4], in1=self.fnb, op0=ALU.mult, op1=ALU.mult),
                    [ryo, rst, self.res("fnb")], [rxn])
                fw.dma("sync", [(self.out[(i - 2) * 128:(i - 1) * 128, :], xn[:])], [rxn], [rout])

    def s5(self, l):
        raise NotImplementedError

    def gla(self, l):
        raise NotImplementedError


_CACHE = {}


def make_in_maps(inputs):
    maps = []
    for core in range(8):
        b = core % 4
        m = {
            "x": np.ascontiguousarray(inputs["x"][b]),
            "ctx": np.ascontiguousarray(inputs["ctx"][b]),
            "cc": np.ascontiguousarray(np.stack([inputs["c"][b], inputs["c_ctx"]], axis=0)),
            "final_norm": np.ascontiguousarray(inputs["final_norm"][None, :]),
        }
        for k in ("norm_g", "w_mod", "b_mod", "w_in", "s5_lam_re", "s5_lam_im", "s5_log_dt", "s5_b_re", "s5_b_im", "s5_c_re",
                  "s5_c_im", "s5_d", "s5_w_glu", "s5_b_glu", "gla_w_gate", "gla_b_gate", "gla_norm_g", "w_out"):
            m[k] = np.ascontiguousarray(inputs[k])
        maps.append(m)
    return maps


def kernel(**inputs):
    inputs = {k: np.asarray(v) for k, v in inputs.items()}
    if "prog" not in _CACHE:
        _CACHE["prog"] = Prog()
    prog = _CACHE["prog"]
    res = run_bass_kernel_spmd(prog.nc, make_in_maps(inputs), core_ids=list(range(8)))
    return np.stack([np.asarray(res.results[b]["out"]) for b in range(4)], axis=0).astype(np.float32)


def _gla(self, l):
    fw = self.fw
    v = self.view
    self.aoff = self.phase_base
    NS = 4
    T = [v([128, 1056], BF16) for _ in range(NS)]
    lrT_L = [v([128, 128], BF16) for _ in range(NS)]
    wg32 = v([128, 2, 256], F32)
    wgp = v([128, 2, 256], BF16)
    nbg = v([128, 2, 2], F32)
    sp_L = [v([128, 2, 2, 128], F32) for _ in range(NS)]
    cs_L = [v([128, 2, 2, 128], F32) for _ in range(NS)]
    eq_L = [v([128, 2, 2, 128], F32) for _ in range(NS)]
    ek_L = [v([128, 2, 2, 128], F32) for _ in range(NS)]
    ekd_L = [v([128, 2, 2, 128], F32) for _ in range(NS)]
    tot_L = [v([128, 2, 2, 4], F32) for _ in range(NS)]
    qtT_L = [v([128, 2, 2, 128], BF16) for _ in range(NS)]
    ktT_L = [v([128, 2, 2, 128], BF16) for _ in range(NS)]
    kdT_L = [v([128, 2, 2, 128], BF16) for _ in range(NS)]
    kdt_L = [v([128, 2, 2, 128], BF16) for _ in range(NS)]
    sT_L = [v([128, 2, 4, 128], BF16) for _ in range(NS)]
    S32 = v([128, 2, 2, 128], F32)
    Sbf = v([128, 2, 128], BF16)
    Sst = v([128, NTILE, 2, 128], BF16)
    Mf = v([128, 128], F32)
    Mb = v([128, 128], F32)
    ones = v([128, 128], F32)
    gnb = v([128, 128], F32)
    ygt = [v([128, 512], BF16) for _ in range(NS)]
    sq = v([128, 128], F32)
    rs = v([128, 8], F32)
    R0 = lambda n: self.res("gla_" + n)
    R = R0
    SLOTTED = ("lrT", "sp", "cs", "eq", "ek", "ekd", "tot", "qtT", "ktT", "kdT", "kdt", "sT")

    def mkR(s):
        return lambda n: R0(n + "_s%d" % s) if n in SLOTTED else R0(n)
    rI = self.res("ident")
    fw.op("gpsimd", lambda e: e.memset(Mf, 1.0), [], [R("Mf")])
    fw.op("gpsimd", lambda e: e.affine_select(out=Mf, in_=Mf, compare_op=ALU.is_ge, fill=0.0, base=0,
                                             pattern=[[1, 128]], channel_multiplier=-1), [R("Mf")], [R("Mf")])
    fw.op("gpsimd", lambda e: e.memset(Mb, 1.0), [], [R("Mb")])
    fw.op("gpsimd", lambda e: e.affine_select(out=Mb, in_=Mb, compare_op=ALU.is_ge, fill=0.0, base=0,
                                             pattern=[[-1, 128]], channel_multiplier=1), [R("Mb")], [R("Mb")])
    fw.op("gpsimd", lambda e: e.memset(ones, 1.0), [], [R("ones")])
    fw.op("vector", lambda e: e.memset(wg32, 0.0), [], [R("wg32")])
    fw.dma("sync", [(wg32[0:16, 0, :], self.w_gate[l, 0, :, :]), (wg32[16:32, 1, :], self.w_gate[l, 1, :, :])], [], [R("wg32")])
    fw.op("vector", lambda e: e.tensor_copy(out=wgp[0:32], in_=wg32[0:32]), [R("wg32")], [R("wgp")])
    fw.dma("sync", [(nbg[:, d, hp:hp + 1], self.b_gate[l, d, hp * 128:(hp + 1) * 128].rearrange("(p o) -> p o", o=1))
                    for d in range(2) for hp in range(2)], [], [R("nbg")])
    fw.op("vector", lambda e: e.tensor_scalar(out=nbg, in0=nbg, scalar1=-1.0, scalar2=None, op0=ALU.mult), [R("nbg")], [R("nbg")])
    fw.dma("sync", [(gnb, self.gnorm[l:l + 1, :].partition_broadcast(128)[:, 0, :])], [], [R("gnb")])
    fw.op("vector", lambda e: e.memset(S32, 0.0), [], [R("S32")])
    fw.op("vector", lambda e: e.memset(Sbf, 0.0), [], [R("Sbf")])

    Plat = self.P[CT:, :].rearrange("(r c) w -> c r w", c=64)
    yglat = self.yg[CT:, :].rearrange("(r c) w -> c r w", c=64)

    def rows(ap, lat, ci, c0, c1):
        if ci < 2:
            return ap[ci * 128:(ci + 1) * 128, c0:c1]
        return lat[ci - 2, :, c0:c1]

    pz, pqk, plr, psc0, psc1, pkd, pdS, po = self.pb
    rpb = [self.res("pb%d" % i) for i in range(8)]

    def load(ci, s):
        fw.dma("sync", [(T[s][:, 0:1024], rows(self.P, Plat, ci, 1024, 2048))], [self.res("P")], [R("T%d" % s)])
        fw.dma("gpsimd", [(T[s][:, 1024:1056], rows(self.P, Plat, ci, 2560, 2592))], [self.res("P")], [R("T%d" % s)],
               sres=R("T%db" % s))

    def gates(s, dirs, need_qk):
        lrT, sp, cs, eq, ek, ekd, tot, qtT, ktT, kdT, kdt, sT = (X[s] for X in (lrT_L, sp_L, cs_L, eq_L, ek_L, ekd_L, tot_L, qtT_L, ktT_L, kdT_L, kdt_L, sT_L))
        R = mkR(s)
        Tt = T[s]
        rT = [R("T%d" % s), R("T%db" % s)]
        plrb = plr[:].bitcast(BF16)
        fw.op("tensor", lambda e: e.transpose(plrb[0:32, 0:128], Tt[:, 1024:1056], self.identB[:]), rT + [rI], [rpb[2]])
        fw.op("scalar", lambda e: e.activation(out=lrT[0:32, :], in_=plrb[0:32, 0:128], func=AF.Copy), [rpb[2]], [R("lrT")])
        pqkb = pqk[:].bitcast(BF16)
        for t4 in range(4):
            fw.op("tensor", lambda e, t4=t4: e.transpose(pqkb[:, t4 * 128:(t4 + 1) * 128], Tt[:, t4 * 128:(t4 + 1) * 128],
                                                        self.identB[:]), rT + [rI], [rpb[1]])
        for d in dirs:
            for hp in range(2):
                fw.op("tensor", lambda e, d=d, hp=hp: e.matmul(pz[:, (d * 2 + hp) * 128:(d * 2 + hp + 1) * 128],
                                                              lhsT=wgp[0:32, d, hp * 128:(hp + 1) * 128], rhs=lrT[0:32, :],
                                                              start=True, stop=True), [R("wgp"), R("lrT")], [rpb[0]])
                fw.op("scalar", lambda e, d=d, hp=hp: e.activation(out=sp[:, d, hp, :], in_=pz[:, (d * 2 + hp) * 128:(d * 2 + hp + 1) * 128],
                                                                  func=AF.Exp, scale=-1.0, bias=nbg[:, d, hp:hp + 1]),
                      [rpb[0], R("nbg")], [R("sp")])
            fw.op("scalar", lambda e, d=d: e.activation(out=sp[:, d], in_=sp[:, d], func=AF.Ln, bias=1.0), [R("sp")], [R("sp")])
            for hp in range(2):
                fw.op("vector", lambda e, d=d, hp=hp: e.tensor_tensor_scan(out=cs[:, d, hp, :], data0=ones, data1=sp[:, d, hp, :],
                                                                          initial=0.0, op0=ALU.mult, op1=ALU.add),
                      [R("sp"), R("ones")], [R("cs")])
                fw.op("vector", lambda e, d=d, hp=hp: e.tensor_copy(out=tot[:, d, hp, 0:1], in_=cs[:, d, hp, 127:128]),
                      [R("cs")], [R("tot")])
                if d == 1:
                    fw.op("vector", lambda e, d=d, hp=hp: e.scalar_tensor_tensor(out=cs[:, d, hp, :], in0=sp[:, d, hp, :],
                                                                                scalar=tot[:, d, hp, 0:1], in1=cs[:, d, hp, :],
                                                                                op0=ALU.add, op1=ALU.subtract),
                          [R("sp"), R("tot"), R("cs")], [R("cs")])
            fw.op("vector", lambda e, d=d: e.tensor_scalar(out=tot[:, d, :, 1:2], in0=tot[:, d, :, 0:1], scalar1=-1.0 / 16, scalar2=None,
                                                          op0=ALU.mult), [R("tot")], [R("tot")])
            fw.op("scalar", lambda e, d=d: e.activation(out=tot[:, d, :, 2:3], in_=tot[:, d, :, 0:1], func=AF.Exp, scale=-1.0 / 16),
                  [R("tot")], [R("tot")])
            for hp in range(2):
                fw.op("scalar", lambda e, d=d, hp=hp: e.activation(out=ekd[:, d, hp, :], in_=cs[:, d, hp, :], func=AF.Exp,
                                                                  scale=1.0 / 16, bias=tot[:, d, hp, 1:2]), [R("cs"), R("tot")], [R("ekd")])
                fw.op("vector", lambda e, d=d, hp=hp: e.tensor_tensor(out=kdT[:, d, hp, :], in0=pqkb[:, (2 + hp) * 128:(3 + hp) * 128],
                                                                     in1=ekd[:, d, hp, :], op=ALU.mult), [rpb[1], R("ekd")], [R("kdT")])
            if need_qk:
                fw.op("scalar", lambda e, d=d: e.activation(out=eq[:, d], in_=cs[:, d], func=AF.Exp, scale=-1.0 / 16), [R("cs")], [R("eq")])
                fw.op("scalar", lambda e, d=d: e.activation(out=ek[:, d], in_=cs[:, d], func=AF.Exp, scale=1.0 / 16), [R("cs")], [R("ek")])
                fw.op("vector", lambda e, d=d: e.tensor_tensor(out=qtT[:, d], in0=pqkb[:, 0:256].rearrange("p (a b) -> p a b", a=2),
                                                              in1=eq[:, d], op=ALU.mult), [rpb[1], R("eq")], [R("qtT")])
                fw.op("gpsimd", lambda e, d=d: e.tensor_copy(out=ktT[:, d], in_=ek[:, d]), [R("ek")], [R("ktT")])
                fw.op("vector", lambda e, d=d: e.tensor_tensor(out=ktT[:, d], in0=pqkb[:, 256:512].rearrange("p (a b) -> p a b", a=2),
                                                              in1=ek[:, d], op=ALU.mult), [rpb[1], R("ek"), R("ktT")], [R("ktT")])
            pkdb = pkd[:].bitcast(BF16)
            for hp in range(2):
                fw.op("tensor", lambda e, d=d, hp=hp: e.transpose(pkdb[:, (d * 2 + hp) * 128:(d * 2 + hp + 1) * 128], kdT[:, d, hp, :],
                                                                 self.identB[:]), [R("kdT"), rI], [rpb[5]])
            fw.op("scalar", lambda e, d=d: e.activation(out=kdt[:, d].rearrange("p a b -> p (a b)"), in_=pkdb[:, d * 256:(d + 1) * 256],
                                                       func=AF.Copy), [rpb[5]], [R("kdt")])

    def dstate(s, d):
        lrT, sp, cs, eq, ek, ekd, tot, qtT, ktT, kdT, kdt, sT = (X[s] for X in (lrT_L, sp_L, cs_L, eq_L, ek_L, ekd_L, tot_L, qtT_L, ktT_L, kdT_L, kdt_L, sT_L))
        R = mkR(s)
        Tt = T[s]
        rT = [R("T%d" % s)]
        for hp in range(2):
            for h2 in range(2):
                h = hp * 2 + h2
                fw.op("tensor", lambda e, hp=hp, h2=h2, h=h: e.matmul(
                    pdS[h2 * 64:(h2 + 1) * 64, (d * 2 + hp) * 128:(d * 2 + hp + 1) * 128],
                    lhsT=kdt[:, d, hp, h2 * 64:(h2 + 1) * 64], rhs=Tt[:, 512 + h * 128:512 + (h + 1) * 128],
                    start=True, stop=True), [R("kdt")] + rT, [rpb[6]])

    def supdate(s, d):
        lrT, sp, cs, eq, ek, ekd, tot, qtT, ktT, kdT, kdt, sT = (X[s] for X in (lrT_L, sp_L, cs_L, eq_L, ek_L, ekd_L, tot_L, qtT_L, ktT_L, kdT_L, kdt_L, sT_L))
        R = mkR(s)
        for hp in range(2):
            fw.op("vector", lambda e, hp=hp: e.scalar_tensor_tensor(out=S32[:, d, hp, :], in0=S32[:, d, hp, :], scalar=tot[:, d, hp, 2:3],
                                                                   in1=pdS[:, (d * 2 + hp) * 128:(d * 2 + hp + 1) * 128],
                                                                   op0=ALU.mult, op1=ALU.add), [R("S32"), R("tot"), rpb[6]], [R("S32")])

    order_b = [1, 0] + list(range(NTILE - 1, 1, -1))
    for n, ci in enumerate(order_b):
        s = n % NS
        load(ci, s)
        gates(s, [1], False)
        fw.op("gpsimd", lambda e, ci=ci: e.tensor_copy(out=Sst[:, ci], in_=S32[:, 1]), [R("S32")], [R("Sst")])
        dstate(s, 1)
        supdate(s, 1)
    def passF(ci):
        s = ci % NS
        lrT, sp, cs, eq, ek, ekd, tot, qtT, ktT, kdT, kdt, sT = (X[s] for X in (lrT_L, sp_L, cs_L, eq_L, ek_L, ekd_L, tot_L, qtT_L, ktT_L, kdT_L, kdt_L, sT_L))
        R = mkR(s)
        load(ci, s)
        gates(s, [0, 1], True)
        Tt = T[s]
        rT = [R("T%d" % s)]
        for d in range(2):
            M = Mf if d == 0 else Mb
            rM = R("Mf") if d == 0 else R("Mb")
            for h in range(4):
                hp, h2 = h // 2, h % 2
                psc = psc0 if d == 0 else psc1
                rps = rpb[3] if d == 0 else rpb[4]
                fw.op("tensor", lambda e, d=d, hp=hp, h2=h2, h=h, psc=psc: e.matmul(
                    psc[:, h * 128:(h + 1) * 128], lhsT=ktT[h2 * 64:(h2 + 1) * 64, d, hp, :], rhs=qtT[h2 * 64:(h2 + 1) * 64, d, hp, :],
                    start=True, stop=True), [R("ktT"), R("qtT")], [rps])
                fw.op("vector" if h % 2 == 0 else "gpsimd" if False else "vector", lambda e, d=d, h=h, psc=psc, M=M: e.tensor_tensor(
                    out=sT[:, d, h, :], in0=psc[:, h * 128:(h + 1) * 128], in1=M, op=ALU.mult), [rps, rM], [R("sT")])
        for h in range(4):
            hp, h2 = h // 2, h % 2
            ops = []
            for d in range(2):
                ops.append((sT[:, d, h, :], Tt[:, 512 + h * 128:512 + (h + 1) * 128], [R("sT")] + rT))
                if d == 0:
                    ops.append((qtT[h2 * 64:(h2 + 1) * 64, 0, hp, :], Sbf[h2 * 64:(h2 + 1) * 64, hp, :], [R("qtT"), R("Sbf")]))
                else:
                    ops.append((qtT[h2 * 64:(h2 + 1) * 64, 1, hp, :], Sst[h2 * 64:(h2 + 1) * 64, ci, hp, :], [R("qtT"), R("Sst")]))
            for n_, (lt, rh, rd) in enumerate(ops):
                fw.op("tensor", lambda e, lt=lt, rh=rh, n_=n_, h=h: e.matmul(po[:, h * 128:(h + 1) * 128], lhsT=lt, rhs=rh,
                                                                            start=(n_ == 0), stop=(n_ == 3)), rd, [rpb[7]])
        dstate(s, 0)
        supdate(s, 0)
        fw.op("gpsimd", lambda e: e.tensor_copy(out=Sbf, in_=S32[:, 0]), [R("S32")], [R("Sbf")])
        yt = ygt[s]
        ry = R("yg%d" % s)
        for h in range(4):
            fw.op("scalar", lambda e, h=h: e.activation(out=sq, in_=po[:, h * 128:(h + 1) * 128], func=AF.Square,
                                                       accum_out=rs[:, h:h + 1]), [rpb[7]], [R("sq"), R("rs")])
        fw.op("vector", lambda e: e.tensor_scalar(out=rs[:, 4:8], in0=rs[:, 0:4], scalar1=1.0 / 128, scalar2=EPS, op0=ALU.mult,
                                                 op1=ALU.add), [R("rs")], [R("rs")])
        fw.op("scalar", lambda e: e.activation(out=rs[:, 4:8], in_=rs[:, 4:8], func=AF.Sqrt), [R("rs")], [R("rs")])
        fw.op("vector", lambda e: e.reciprocal(out=rs[:, 4:8], in_=rs[:, 4:8]), [R("rs")], [R("rs")])
        for h in range(4):
            fw.op("vector", lambda e, h=h, yt=yt: e.scalar_tensor_tensor(out=yt[:, h * 128:(h + 1) * 128], in0=po[:, h * 128:(h + 1) * 128],
                                                                        scalar=rs[:, 4 + h:5 + h], in1=gnb, op0=ALU.mult, op1=ALU.mult),
                  [rpb[7], R("rs"), R("gnb")], [ry])
        fw.dma("gpsimd", [(rows(self.yg, yglat, ci, 0, 512), yt)], [ry], [self.res("yg")])

    for ci_ in range(NTILE):
        passF(ci_)


Prog.gla = _gla


def _s5(self, l):
    fw = self.fw
    v = self.view
    self.aoff = self.phase_base
    TWO_PI = 6.283185307179586
    NG = 8
    X8 = v([128, 9, 8, NG * 16], BF16)
    Ytok = v([128, 9, 8, NG * 16], BF16)
    U8_L = [v([128, 1056], BF16) for _ in range(2)]
    gsc2 = v([128, 1056], F32)
    Xg = v([128, 9, 128], BF16)
    gy8 = v([128, 1056], BF16)
    gsc = v([128, 1056], F32)
    Ere = v([128, 2, NG, 65], F32)
    Eim = v([128, 2, NG, 65], F32)
    ErD = v([128, 2, NG, 65], F32)
    EiD = v([128, 2, NG, 65], F32)
    kvr = v([128, 65], F32)
    kv = v([128, 65], F32)
    kvi = v([128, 65], I32)
    sm = v([128, 24, 2, NG], F32)
    AKr = v([128, 8, 2, NG], F32)
    AKs = v([128, 8, 2, NG], F32)
    sgn = v([128, 2], F32)
    ba = v([128, NG, 16], F32)
    bb = v([128, NG, 16], F32)
    Ca = v([128, NG, 16], F32)
    Cb = v([128, NG, 16], F32)
    Bw = v([128, 2, 4, 16, 16], F32) if False else None
    BA = [[v([128, NG, 16], F32) for _ in range(4)] for _ in range(2)]
    CA = [[v([128, NG, 16], F32) for _ in range(2)] for _ in range(2)]
    Dcol = v([128, NG], F32)
    swapM = v([128, 128], F32)
    FOLD = v([128, 64], F32)
    colL = v([128, 8, 2, NG], F32)
    colR = v([128, 8, 2, NG], F32)
    maskF = v([128, 8, 16], F32)
    maskB = v([128, 8, 16], F32)
    scr_L = [v([128, 4096], F32) for _ in range(2)]
    scr = scr_L[0]
    M2T_L = [[sc_[:, i * 1024:(i + 1) * 1024].rearrange("p (a b) -> p a b", a=64) for i in range(2)] for sc_ in scr_L]
    M3f_L = [[sc_[:, (2 + i) * 1024:(3 + i) * 1024].rearrange("p (a b) -> p a b", a=64) for i in range(2)] for sc_ in scr_L]
    tA = scr[:, 0:NG * 65].rearrange("p (a b) -> p a b", a=NG)
    tB = scr[:, 1040:1040 + NG * 65].rearrange("p (a b) -> p a b", a=NG)
    tI = scr[:, 2080:2080 + NG * 65].bitcast(I32).rearrange("p (a b) -> p a b", a=NG)
    M2Tp_L = [[v([128, 8, 16], F32) for _ in range(2)] for _ in range(2)]
    M2b_L = [[v([128, 8, 128], BF16) for _ in range(2)] for _ in range(2)]
    M3b_L = [[v([128, 8, 128], BF16) for _ in range(2)] for _ in range(2)]
    M1b_L = [[v([128, 8, 128], BF16) for _ in range(2)] for _ in range(2)]
    Ak_L = [[v([128, 8, 128], F32) for _ in range(2)] for _ in range(2)]
    Pst_L = [[v([128, 132], F32) for _ in range(2)] for _ in range(2)]
    HHb_L = [[v([128, 132], BF16) for _ in range(2)] for _ in range(2)]
    R = lambda n: self.res("s5_" + n)
    R_glob = R
    rI = self.res("ident")
    pb = self.pb
    rpb = [self.res("pb%d" % i) for i in range(8)]
    V, G_ = "vector", "gpsimd"

    def tt(eng, out, a, b, op, reads, writes):
        fw.op(eng, lambda e: e.tensor_tensor(out=out, in0=a, in1=b, op=op), reads, writes)

    def bc(ap, shape):
        return ap.to_broadcast(shape)

    fw.op(G_, lambda e: e.iota(kvi, pattern=[[1, 65]], base=0, channel_multiplier=0), [], [R("kv")])
    fw.op(V, lambda e: e.tensor_copy(out=kv, in_=kvi), [R("kv")], [R("kv")])
    fw.op(V, lambda e: e.tensor_scalar(out=kvr, in0=kv, scalar1=-1.0, scalar2=64.0, op0=ALU.mult, op1=ALU.add), [R("kv")], [R("kv")])
    fw.op(V, lambda e: e.memset(sgn[0:64, 0:1], -1.0), [], [R("sgn")])
    fw.op(V, lambda e: e.memset(sgn[64:128, 0:1], 1.0), [], [R("sgn")])
    fw.op(V, lambda e: e.memset(sgn[0:64, 1:2], 1.0), [], [R("sgn")])
    fw.op(V, lambda e: e.memset(sgn[64:128, 1:2], -1.0), [], [R("sgn")])
    fw.op(V, lambda e: e.tensor_copy(out=swapM[:, 0:64], in_=self.identF[:, 64:128]), [rI], [R("swapM")])
    fw.op(V, lambda e: e.tensor_copy(out=swapM[:, 64:128], in_=self.identF[:, 0:64]), [rI], [R("swapM")])
    tt(V, FOLD, self.identF[:, 0:64], self.identF[:, 64:128], ALU.add, [rI], [R("FOLD")])
    fw.op(G_, lambda e: e.memset(maskF, 1.0), [], [R("mask")])
    fw.op(G_, lambda e: e.affine_select(out=maskF, in_=maskF, compare_op=ALU.is_ge, fill=0.0, base=15,
                                       pattern=[[16, 8], [0, 16]], channel_multiplier=-1), [R("mask")], [R("mask")])
    fw.op(G_, lambda e: e.memset(maskB, 1.0), [], [R("mask")])
    fw.op(G_, lambda e: e.affine_select(out=maskB, in_=maskB, compare_op=ALU.is_ge, fill=0.0, base=0,
                                       pattern=[[-16, 8], [0, 16]], channel_multiplier=1), [R("mask")], [R("mask")])

    for gh in range(32 // NG):
        g0 = gh * NG
        fw.dma("sync", [(X8[:, ct], self.P[ct * 1024:(ct + 1) * 1024, g0 * 16:g0 * 16 + NG * 16].rearrange("(c s) w -> c s w", s=8))
                        for ct in range(8)], [self.res("P")], [R("X8")])
        fw.dma("sync", [(X8[0:32, 8], self.P[8192:8448, g0 * 16:g0 * 16 + NG * 16].rearrange("(c s) w -> c s w", s=8))],
               [self.res("P")], [R("X8")])
        rsm = R("sm")
        pairs = []
        for d in range(2):
            for half in range(2):
                ps_ = slice(half * 64, half * 64 + 64)
                pairs.append((sm[ps_, 0, d, :], self.lam_re[l, d, g0:g0 + NG, :].rearrange("g n -> n g")))
                pairs.append((sm[ps_, 1, d, :], self.lam_im[l, d, g0:g0 + NG, :].rearrange("g n -> n g")))
            pairs.append((sm[:, 2, d, :], self.log_dt[l, d:d + 1, g0:g0 + NG].partition_broadcast(128)[:, 0, :]))
        fw.dma("gpsimd", pairs, [], [rsm])
        rb = R("bc")
        fw.dma("sync", [(ba[0:64], self.b_re[l, g0:g0 + NG].rearrange("g n p -> n g p")),
                        (bb[64:128], self.b_re[l, g0:g0 + NG].rearrange("g n p -> n g p")),
                        (ba[64:128], self.b_im[l, g0:g0 + NG].rearrange("g n p -> n g p")),
                        (bb[0:64], self.b_im[l, g0:g0 + NG].rearrange("g n p -> n g p"))], [], [rb])
        for gi in range(NG):
            g = g0 + gi
            fw.dma("sync" if gi % 2 == 0 else "gpsimd",
                   [(Ca[0:64, gi, :], self.c_re[l, g].rearrange("p n -> n p")), (Cb[64:128, gi, :], self.c_re[l, g].rearrange("p n -> n p")),
                    (Ca[64:128, gi, :], self.c_im[l, g].rearrange("p n -> n p")), (Cb[0:64, gi, :], self.c_im[l, g].rearrange("p n -> n p"))],
                   [], [rb], sres=R("bc%d" % (gi % 2)))
        fw.dma("sync", [(Dcol[s_ * 16:(s_ + 1) * 16, :], self.s5_d[l, g0 * 16:g0 * 16 + NG * 16].rearrange("(g p) -> p g", p=16))
                        for s_ in range(8)], [], [R("Dcol")])
        S_ = lambda i: sm[:, i]
        fw.op("scalar", lambda e: e.activation(out=S_(2), in_=S_(2), func=AF.Exp), [rsm], [rsm])
        tt(V, S_(3), S_(0), S_(2), ALU.mult, [rsm], [rsm])
        tt(V, S_(4), S_(1), S_(2), ALU.mult, [rsm], [rsm])
        fw.op(V, lambda e: e.tensor_scalar(out=S_(4), in0=S_(4), scalar1=1.0 / TWO_PI, scalar2=None, op0=ALU.mult), [rsm], [rsm])
        rE = R("E")
        rt = R("tab")
        rtw = [rt, R("M0_s0"), R("M1_s0")]
        for d in range(2):
          for (kvx, TRe, TIm) in ((kv, Ere, Eim), (kvr, ErD, EiD)):
            tt(V, tA, bc(sm[:, 3, d, :].unsqueeze(2), [128, NG, 65]), bc(kvx.unsqueeze(1), [128, NG, 65]), ALU.mult, [rsm, R("kv")], rtw)
            fw.op("scalar", lambda e: e.activation(out=tA, in_=tA, func=AF.Exp), [rt], rtw)
            for which in range(2):
                tt(V, tB, bc(sm[:, 4, d, :].unsqueeze(2), [128, NG, 65]), bc(kvx.unsqueeze(1), [128, NG, 65]), ALU.mult, [rsm, R("kv")], rtw)
                if which == 1:
                    fw.op(V, lambda e: e.tensor_scalar(out=tB, in0=tB, scalar1=0.25, scalar2=None, op0=ALU.add), [rt], rtw)
                fw.op(V, lambda e: e.tensor_copy(out=tI, in_=tB), [rt], rtw)
                tt(V, tB, tB, tI, ALU.subtract, [rt], rtw)
                dst = TIm[:, d] if which == 0 else TRe[:, d]
                fw.op(V, lambda e, dst=dst: e.tensor_single_scalar(out=dst, in_=tB, scalar=0.5, op=ALU.is_gt), [rt], [rE])
                tt(V, tB, tB, dst, ALU.subtract, [rt, rE], rtw)
                fw.op(V, lambda e, dst=dst: e.tensor_single_scalar(out=dst, in_=tB, scalar=-0.5, op=ALU.is_lt), [rt], [rE])
                tt(V, tB, tB, dst, ALU.add, [rt, rE], rtw)
                fw.op("scalar", lambda e: e.activation(out=tB, in_=tB, func=AF.Sin, scale=6.283185), [rt], rtw)
                tt(V, dst, tB, tA, ALU.mult, [rt], [rE])
        def coef_(d):
            s = lambda i: sm[:, i, d, :]
            e1r, e1i = Ere[:, d, :, 1], Eim[:, d, :, 1]
            e64r, e64i = Ere[:, d, :, 64], Eim[:, d, :, 64]
            stt = lambda o, i0, c, i1: fw.op(V, lambda e: e.scalar_tensor_tensor(out=o, in0=i0, scalar=c, in1=i1, op0=ALU.add, op1=ALU.mult),
                                             [rsm], [rsm])
            ti_ = tI[:, :, 0]
            fw.op(V, lambda e: e.tensor_copy(out=ti_, in_=s(4)), [rsm], rtw)
            tt(V, s(20), s(4), ti_, ALU.subtract, [rsm, rt], [rsm])
            fw.op(V, lambda e: e.tensor_single_scalar(out=s(21), in_=s(20), scalar=0.5, op=ALU.is_gt), [rsm], [rsm])
            tt(V, s(20), s(20), s(21), ALU.subtract, [rsm], [rsm])
            fw.op(V, lambda e: e.tensor_single_scalar(out=s(21), in_=s(20), scalar=-0.5, op=ALU.is_lt), [rsm], [rsm])
            tt(V, s(20), s(20), s(21), ALU.add, [rsm], [rsm])
            fw.op(V, lambda e: e.tensor_scalar(out=s(20), in0=s(20), scalar1=3.14159265358979, scalar2=None, op0=ALU.mult), [rsm], [rsm])
            tt(V, s(21), s(20), s(20), ALU.mult, [rsm], [rsm])
            fw.op(V, lambda e: e.tensor_scalar(out=s(22), in0=s(21), scalar1=-1.0 / 39916800, scalar2=None, op0=ALU.mult), [rsm], [rsm])
            for c_ in (1.0 / 362880, -1.0 / 5040, 1.0 / 120, -1.0 / 6):
                stt(s(22), s(22), c_, s(21))
            stt(s(22), s(22), 1.0, s(20))
            fw.op(V, lambda e: e.tensor_scalar(out=s(23), in0=s(21), scalar1=1.0 / 479001600, scalar2=None, op0=ALU.mult), [rsm], [rsm])
            for c_ in (-1.0 / 3628800, 1.0 / 40320, -1.0 / 720, 1.0 / 24, -0.5):
                stt(s(23), s(23), c_, s(21))
            fw.op(V, lambda e: e.tensor_scalar(out=s(23), in0=s(23), scalar1=1.0, scalar2=None, op0=ALU.add), [rsm], [rsm])
            fw.op(V, lambda e: e.tensor_scalar(out=s(15), in0=s(3), scalar1=1.0 / 120, scalar2=None, op0=ALU.mult), [rsm], [rsm])
            for c_ in (1.0 / 24, 1.0 / 6, 0.5, 1.0):
                stt(s(15), s(15), c_, s(3))
            tt(V, s(16), s(22), s(23), ALU.mult, [rsm], [rsm])
            fw.op(V, lambda e: e.tensor_scalar(out=s(16), in0=s(16), scalar1=2.0, scalar2=None, op0=ALU.mult), [rsm], [rsm])
            tt(V, s(21), s(22), s(22), ALU.mult, [rsm], [rsm])
            fw.op(V, lambda e: e.tensor_scalar(out=s(21), in0=s(21), scalar1=2.0, scalar2=None, op0=ALU.mult), [rsm], [rsm])
            fw.op(V, lambda e: e.tensor_scalar(out=s(20), in0=s(21), scalar1=-1.0, scalar2=1.0, op0=ALU.mult, op1=ALU.add), [rsm], [rsm])
            tt(V, s(5), s(15), s(20), ALU.mult, [rsm], [rsm])
            tt(V, s(5), s(5), s(21), ALU.subtract, [rsm], [rsm])
            fw.op(V, lambda e: e.tensor_scalar(out=s(15), in0=s(15), scalar1=1.0, scalar2=None, op0=ALU.add), [rsm], [rsm])
            tt(V, s(6), s(15), s(16), ALU.mult, [rsm], [rsm])
            tt(V, s(15), s(0), s(0), ALU.mult, [rsm], [rsm])
            tt(V, s(16), s(1), s(1), ALU.mult, [rsm], [rsm])
            tt(V, s(15), s(15), s(16), ALU.add, [rsm], [rsm])
            fw.op(V, lambda e: e.reciprocal(out=s(7), in_=s(15)), [rsm], [rsm])
            tt(V, s(15), s(5), s(0), ALU.mult, [rsm], [rsm])
            tt(V, s(16), s(6), s(1), ALU.mult, [rsm], [rsm])
            tt(V, s(15), s(15), s(16), ALU.add, [rsm], [rsm])
            tt(V, s(8), s(15), s(7), ALU.mult, [rsm], [rsm])
            tt(V, s(15), s(6), s(0), ALU.mult, [rsm], [rsm])
            tt(V, s(16), s(5), s(1), ALU.mult, [rsm], [rsm])
            tt(V, s(15), s(15), s(16), ALU.subtract, [rsm], [rsm])
            tt(V, s(9), s(15), s(7), ALU.mult, [rsm], [rsm])
            fw.op(V, lambda e: e.tensor_scalar(out=s(10), in0=s(9), scalar1=sgn[:, 0:1], scalar2=None, op0=ALU.mult), [rsm, R("sgn")], [rsm])
            fw.op(V, lambda e: e.tensor_scalar(out=s(11), in0=s(9), scalar1=sgn[:, 1:2], scalar2=None, op0=ALU.mult), [rsm, R("sgn")], [rsm])
            tt(V, s(15), e64r, e64r, ALU.mult, [rE], [rsm])
            tt(V, s(16), e64i, e64i, ALU.mult, [rE], [rsm])
            tt(V, s(15), s(15), s(16), ALU.add, [rsm], [rsm])
            fw.op(V, lambda e: e.reciprocal(out=s(15), in_=s(15)), [rsm], [rsm])
            tt(V, s(12), e64r, s(15), ALU.mult, [rE, rsm], [rsm])
            tt(V, s(16), e64i, s(15), ALU.mult, [rE, rsm], [rsm])
            fw.op(V, lambda e: e.tensor_scalar(out=s(13), in0=s(16), scalar1=sgn[:, 1:2], scalar2=None, op0=ALU.mult), [rsm, R("sgn")], [rsm])
            fw.op(V, lambda e: e.tensor_scalar(out=s(14), in0=s(16), scalar1=sgn[:, 0:1], scalar2=None, op0=ALU.mult), [rsm, R("sgn")], [rsm])
            fw.op(V, lambda e: e.tensor_copy(out=s(17), in_=e1r), [rE], [rsm])
            fw.op(V, lambda e: e.tensor_scalar(out=s(18), in0=e1i, scalar1=sgn[:, 0:1], scalar2=None, op0=ALU.mult), [rE, R("sgn")], [rsm])
            fw.op(V, lambda e: e.tensor_scalar(out=s(19), in0=e1i, scalar1=sgn[:, 1:2], scalar2=None, op0=ALU.mult), [rE, R("sgn")], [rsm])
            B = lambda i: bc(sm[:, i, d, :].unsqueeze(2), [128, NG, 16])
            rB = R("BA")
            Ba, Bbs, Bpa, Bpbs = BA[d]
            tt(V, Ba, ba, B(8), ALU.mult, [rb, rsm], [rB])
            tt(V, Bpa, bb, B(10), ALU.mult, [rb, rsm], [rB])
            tt(V, Ba, Ba, Bpa, ALU.add, [rB], [rB])
            tt(V, Bbs, bb, B(8), ALU.mult, [rb, rsm], [rB])
            tt(V, Bpa, ba, B(11), ALU.mult, [rb, rsm], [rB])
            tt(V, Bbs, Bbs, Bpa, ALU.add, [rB], [rB])
            tt(V, Bpa, Ba, B(12), ALU.mult, [rB, rsm], [rB])
            tt(V, Bpbs, Bbs, B(13), ALU.mult, [rB, rsm], [rB])
            tt(V, Bpa, Bpa, Bpbs, ALU.add, [rB], [rB])
            tt(V, Bpbs, Bbs, B(12), ALU.mult, [rB, rsm], [rB])
            tt(V, gsc[:, 0:NG * 16].rearrange("p (a b) -> p a b", a=NG), Ba, B(14), ALU.mult, [rB, rsm], [R("gsc")])
            tt(V, Bpbs, Bpbs, gsc[:, 0:NG * 16].rearrange("p (a b) -> p a b", a=NG), ALU.add, [rB, R("gsc")], [rB])
            fw.op(V, lambda e: e.tensor_scalar(out=Bbs, in0=Bbs, scalar1=sgn[:, 0:1], scalar2=None, op0=ALU.mult), [rB, R("sgn")], [rB])
            fw.op(V, lambda e: e.tensor_scalar(out=Bpbs, in0=Bpbs, scalar1=sgn[:, 0:1], scalar2=None, op0=ALU.mult), [rB, R("sgn")], [rB])
            Cas, Cbn = CA[d]
            rC = R("CA")
            tmpc = gsc[:, 256:256 + NG * 16].rearrange("p (a b) -> p a b", a=NG)
            tt(V, Cas, Ca, B(17), ALU.mult, [rb, rsm], [rC])
            tt(V, tmpc, Cb, B(18), ALU.mult, [rb, rsm], [R("gsc")])
            tt(V, Cas, Cas, tmpc, ALU.add, [rC, R("gsc")], [rC])
            tt(V, Cbn, Cb, B(17), ALU.mult, [rb, rsm], [rC])
            tt(V, tmpc, Ca, B(19), ALU.mult, [rb, rsm], [R("gsc")])
            tt(V, Cbn, Cbn, tmpc, ALU.add, [rC, R("gsc")], [rC])
            fw.op(V, lambda e: e.tensor_scalar(out=Cas, in0=Cas, scalar1=sgn[:, 1:2], scalar2=None, op0=ALU.mult), [rC, R("sgn")], [rC])
            fw.op(V, lambda e: e.tensor_scalar(out=Cbn, in0=Cbn, scalar1=-1.0, scalar2=None, op0=ALU.mult), [rC], [rC])
        for d_ in range(2):
            coef_(d_)
        rAK = R("AK")
        fw.op(V, lambda e: e.tensor_copy(out=AKr[:, 0], in_=Ere[:, :, :, 64]), [rE], [rAK])
        fw.op(V, lambda e: e.tensor_copy(out=AKs[:, 0], in_=Eim[:, :, :, 64]), [rE], [rAK])
        for k in range(1, 8):
            t15, t16 = sm[:, 15], sm[:, 16]
            tt(V, t15, AKr[:, k - 1], AKr[:, k - 1], ALU.mult, [rAK], [rsm])
            tt(V, t16, AKs[:, k - 1], AKs[:, k - 1], ALU.mult, [rAK], [rsm])
            tt(V, AKr[:, k], t15, t16, ALU.subtract, [rsm], [rAK])
            tt(V, t15, AKr[:, k - 1], AKs[:, k - 1], ALU.mult, [rAK], [rsm])
            fw.op(V, lambda e, k=k: e.tensor_scalar(out=AKs[:, k], in0=t15, scalar1=2.0, scalar2=None, op0=ALU.mult), [rsm], [rAK])
        fw.op(V, lambda e: e.tensor_scalar(out=AKs, in0=AKs, scalar1=sgn[:, 1:2], scalar2=None, op0=ALU.mult), [rAK, R("sgn")], [rAK])
        fw.op(V, lambda e: e.tensor_copy(out=colL, in_=AKr), [rAK], [R("col")])
        fw.op(V, lambda e: e.tensor_copy(out=colL[64:128], in_=AKs[64:128]), [rAK, R("col")], [R("col")])
        fw.op(V, lambda e: e.tensor_copy(out=colR, in_=AKs), [rAK], [R("col")])
        fw.op(V, lambda e: e.tensor_copy(out=colR[64:128], in_=AKr[64:128]), [rAK, R("col")], [R("col")])

        def grp_(gi):
            sl = (g0 + gi) % 2
            U8, M3b, M1b, HHb = U8_L[sl], M3b_L[sl], M1b_L[sl], HHb_L[sl]
            M2T, M3f, M2Tp, M2b, Ak, Pst = M2T_L[sl], M3f_L[sl], M2Tp_L[sl], M2b_L[sl], Ak_L[sl], Pst_L[sl]
            R1 = R_glob
            SL = ("U8", "M1b0", "M1b1", "M3b0", "M3b1", "HH0", "HH1", "M0", "M1", "M2b0", "M2b1", "Ak0", "Ak1", "P0", "P1")
            R = lambda n: R1(n + "_s%d" % sl) if n in SL else R1(n)
            pub = [pb[0][:].bitcast(BF16), pb[1][:].bitcast(BF16)]
            for ct in range(9):
                fw.op(G_ if ct % 2 == 0 else "scalar",
                      (lambda e, ct=ct: e.tensor_copy(out=Xg[:, ct, :].rearrange("p (a b) -> p a b", a=8), in_=X8[:, ct, :, gi * 16:(gi + 1) * 16]))
                      if ct % 2 == 0 else
                      (lambda e, ct=ct: e.activation(out=Xg[:, ct, :].rearrange("p (a b) -> p a b", a=8), in_=X8[:, ct, :, gi * 16:(gi + 1) * 16],
                                                     func=AF.Copy)), [R("X8")], [R("Xg")])
            for ct in range(9):
                npart = 128 if ct < 8 else 32
                bank, off = (0, ct * 128) if ct < 8 else (1, 0)
                fw.op("tensor", lambda e, ct=ct, npart=npart, bank=bank, off=off: e.transpose(
                    pub[bank][:, off:off + npart], Xg[0:npart, ct, :], self.identB[0:npart, 0:npart]),
                    [R("Xg"), rI], [rpb[bank]])
            fw.op("scalar", lambda e: e.activation(out=U8[:, 0:1024], in_=pub[0][:, 0:1024], func=AF.Copy), [rpb[0]], [R("U8")])
            fw.op("scalar", lambda e: e.activation(out=U8[:, 1024:1056], in_=pub[1][:, 0:32], func=AF.Copy), [rpb[1]], [R("U8")])
            U8v = U8.rearrange("p (c j) -> p c j", j=8)
            for d in range(2):
                Ba, Bbs, Bpa, Bpbs = BA[d]
                Cas, Cbn = CA[d]
                rM = R("M%d" % d)
                if d == 0:
                    eM2r, eM2i = ErD[:, d, gi, 1:65], EiD[:, d, gi, 1:65]
                    eM3r, eM3i = Ere[:, d, gi, 0:64], Eim[:, d, gi, 0:64]
                    ePr, ePi = ErD[:, d, gi, 1:9], EiD[:, d, gi, 1:9]
                else:
                    eM2r, eM2i = Ere[:, d, gi, 0:64], Eim[:, d, gi, 0:64]
                    eM3r, eM3i = ErD[:, d, gi, 1:65], EiD[:, d, gi, 1:65]
                    ePr, ePi = Ere[:, d, gi, 56:64], Eim[:, d, gi, 56:64]
                b64 = lambda ap: bc(ap.unsqueeze(2), [128, 64, 16])
                w64 = lambda ap: bc(ap.unsqueeze(1), [128, 64, 16])
                tmp = gsc[:, 0:1024].rearrange("p (a b) -> p a b", a=64)
                rg = R("gsc")
                tt(V, M2T[d], b64(eM2r), w64(Ba[:, gi, :]), ALU.mult, [rE, R("BA")], [rM])
                tt(G_, tmp, b64(eM2i), w64(Bbs[:, gi, :]), ALU.mult, [rE, R("BA")], [rg])
                tt(G_, M2T[d], M2T[d], tmp, ALU.add, [rM, rg], [rM])
                tt(V, M3f[d], b64(eM3r), w64(Cas[:, gi, :]), ALU.mult, [rE, R("CA")], [rM])
                tt(G_, tmp, b64(eM3i), w64(Cbn[:, gi, :]), ALU.mult, [rE, R("CA")], [rg])
                tt(G_, M3f[d], M3f[d], tmp, ALU.add, [rM, rg], [rM])
                tmp8 = gsc[:, 0:128].rearrange("p (a b) -> p a b", a=8)
                tt(V, M2Tp[d], bc(ePr.unsqueeze(2), [128, 8, 16]), bc(Bpa[:, gi, :].unsqueeze(1), [128, 8, 16]), ALU.mult, [rE, R("BA")], [rM])
                tt(V, tmp8, bc(ePi.unsqueeze(2), [128, 8, 16]), bc(Bpbs[:, gi, :].unsqueeze(1), [128, 8, 16]), ALU.mult, [rE, R("BA")], [rg])
                tt(V, M2Tp[d], M2Tp[d], tmp8, ALU.add, [rM, rg], [rM])
                fw.op("scalar", lambda e, d=d: e.activation(out=M3b[d].rearrange("p a b -> p (a b)"), in_=M3f[d].rearrange("p a b -> p (a b)"),
                                                           func=AF.Copy), [rM], [R("M3b%d" % d)])
                for j in range(8):
                    bank = 2 + j // 4
                    fw.op("tensor", lambda e, d=d, j=j, bank=bank: e.transpose(
                        pb[bank][:, (j % 4) * 128:(j % 4 + 1) * 128], M2T[d][:, j * 8:(j + 1) * 8, :].rearrange("p a b -> p (a b)"),
                        self.identF[:]), [rM, rI], [rpb[bank]])
                for hb_ in range(2):
                    fw.op("scalar" if hb_ == 0 else V, (lambda e, d=d, hb_=hb_: e.activation(
                        out=M2b[d][:, hb_ * 4:(hb_ + 1) * 4, :].rearrange("p a b -> p (a b)"), in_=pb[2 + hb_][:], func=AF.Copy))
                        if hb_ == 0 else (lambda e, d=d, hb_=hb_: e.tensor_copy(
                            out=M2b[d][:, hb_ * 4:(hb_ + 1) * 4, :].rearrange("p a b -> p (a b)"), in_=pb[2 + hb_][:])),
                        [rpb[2 + hb_]], [R("M2b%d" % d)])
                for hb_ in range(2):
                    fw.op("tensor", lambda e, d=d, hb_=hb_: e.matmul(
                        pb[2 + hb_][:], lhsT=M2Tp[d].rearrange("p a b -> p (a b)"),
                        rhs=M3f[d][:, hb_ * 32:(hb_ + 1) * 32, :].rearrange("p a b -> p (a b)"), start=True, stop=True),
                        [rM], [rpb[2 + hb_]])
                if d == 0:
                    blk = pb[2][:, 0:128].rearrange("p (a b) -> p a b", a=8)
                    t8 = gsc[:, 0:128].rearrange("p (a b) -> p a b", a=8)
                    tt(V, t8, blk, maskF, ALU.mult, [rpb[2], R("mask")], [rg])
                    fw.op(V, lambda e: e.scalar_tensor_tensor(out=gsc[:, 0:128], in0=self.identF[:], scalar=Dcol[:, gi:gi + 1],
                                                              in1=gsc[:, 0:128], op0=ALU.mult, op1=ALU.add), [rg, rI, R("Dcol")], [rg])
                    fw.op(V, lambda e, d=d: e.tensor_copy(out=M1b[d][:, 0, :], in_=gsc[:, 0:128]), [rg], [R("M1b%d" % d)])
                    fw.op("scalar", lambda e, d=d: e.activation(out=M1b[d][:, 1:4, :].rearrange("p a b -> p (a b)"), in_=pb[2][:, 128:512],
                                                               func=AF.Copy), [rpb[2]], [R("M1b%d" % d)])
                    fw.op("scalar", lambda e, d=d: e.activation(out=M1b[d][:, 4:8, :].rearrange("p a b -> p (a b)"), in_=pb[3][:],
                                                               func=AF.Copy), [rpb[3]], [R("M1b%d" % d)])
                else:
                    blk = pb[3][:, 384:512].rearrange("p (a b) -> p a b", a=8)
                    tt(V, M1b[d][:, 7, :].rearrange("p (a b) -> p a b", a=8), blk, maskB, ALU.mult, [rpb[3], R("mask")], [R("M1b%d" % d)])
                    fw.op("scalar", lambda e, d=d: e.activation(out=M1b[d][:, 0:4, :].rearrange("p a b -> p (a b)"), in_=pb[2][:],
                                                               func=AF.Copy), [rpb[2]], [R("M1b%d" % d)])
                    fw.op("scalar", lambda e, d=d: e.activation(out=M1b[d][:, 4:7, :].rearrange("p a b -> p (a b)"), in_=pb[3][:, 0:384],
                                                               func=AF.Copy), [rpb[3]], [R("M1b%d" % d)])
                rA = R("Ak%d" % d)
                for k in range(8):
                    fw.op("scalar", lambda e, d=d, k=k: e.activation(out=Ak[d][:, k, 0:64], in_=FOLD, func=AF.Copy,
                                                                    scale=colL[:, k, d, gi:gi + 1]), [R("FOLD"), R("col")], [rA])
                    fw.op("scalar", lambda e, d=d, k=k: e.activation(out=Ak[d][:, k, 64:128], in_=FOLD, func=AF.Copy,
                                                                    scale=colR[:, k, d, gi:gi + 1]), [R("FOLD"), R("col")], [rA])
                ps = pb[4]
                if d == 0:
                    for j in range(8):
                        fw.op("tensor", lambda e, d=d, j=j: e.matmul(ps[:, 0:132], lhsT=M2b[d][:, j, :], rhs=U8v[:, :, j],
                                                                    start=(j == 0), stop=(j == 7)), [R("M2b%d" % d), R("U8")], [rpb[4]])
                else:
                    for j in range(8):
                        fw.op("tensor", lambda e, d=d, j=j: e.matmul(ps[:, 0:128], lhsT=M2b[d][:, j, :], rhs=U8v[:, 4:132, j],
                                                                    start=(j == 0), stop=(j == 7)), [R("M2b%d" % d), R("U8")], [rpb[4]])
                    for j in range(8):
                        fw.op("tensor", lambda e, d=d, j=j: e.matmul(ps[:, 128:132], lhsT=M2b[d][:, j, :], rhs=U8v[:, 0:4, j],
                                                                    start=False, stop=(j == 7), skip_group_check=True),
                              [R("M2b%d" % d), R("U8")], [rpb[4]])
                rP = R("P%d" % d)
                fw.op(V, lambda e, d=d: e.tensor_copy(out=Pst[d], in_=ps[:, 0:132]), [rpb[4]], [rP])
                for k in range(8):
                    sft = 1 << k
                    if d == 0:
                        o_sl, i_sl = slice(sft, 132), slice(0, 132 - sft)
                    else:
                        o_sl, i_sl = slice(0, 132 - sft), slice(sft, 132)
                    fw.op("tensor", lambda e, d=d, k=k, o_sl=o_sl, i_sl=i_sl: e.matmul(ps[:, o_sl], lhsT=Ak[d][:, k, :], rhs=Pst[d][:, i_sl],
                                                                                     start=True, stop=True), [rA, rP], [rpb[4]])
                    fw.op(V, lambda e, d=d, o_sl=o_sl: e.tensor_tensor(out=Pst[d][:, o_sl], in0=Pst[d][:, o_sl], in1=ps[:, o_sl], op=ALU.add),
                          [rP, rpb[4]], [rP])
                rH = R("HH%d" % d)
                fw.op(G_, lambda e, d=d: e.memset(HHb[d], 0.0), [], [rH])
                if d == 0:
                    fw.op(V, lambda e, d=d: e.tensor_copy(out=HHb[d][:, 1:132], in_=Pst[d][:, 0:131]), [rP, rH], [rH])
                else:
                    fw.op(V, lambda e, d=d: e.tensor_copy(out=HHb[d][:, 0:131], in_=Pst[d][:, 1:132]), [rP, rH], [rH])
            for jt in range(8):
                bank = 5 + jt // 3
                yo_ = pb[bank][:, (jt % 3) * 132:(jt % 3 + 1) * 132]
                ops = []
                for js in range(0, jt + 1):
                    ops.append((yo_, M1b[0][:, jt - js, :], U8v[:, :, js], [R("M1b0"), R("U8")]))
                for js in range(jt, 8):
                    ops.append((yo_, M1b[1][:, 7 - (js - jt), :], U8v[:, :, js], [R("M1b1"), R("U8")]))
                ops.append((yo_, M3b[0][:, jt, :], HHb[0][:, 0:132], [R("M3b0"), R("HH0")]))
                ops.append((yo_[:, 4:132], M3b[1][:, jt, :], HHb[1][:, 0:128], [R("M3b1"), R("HH1")]))
                ops.append((yo_[:, 0:4], M3b[1][:, jt, :], HHb[1][:, 128:132], [R("M3b1"), R("HH1")]))
                for n_, (o_, lt, rh, rd) in enumerate(ops):
                    fw.op("tensor", lambda e, o_=o_, lt=lt, rh=rh, n_=n_, last=(n_ == len(ops) - 1): e.matmul(
                        o_, lhsT=lt, rhs=rh, start=(n_ == 0), stop=last, skip_group_check=True), rd, [rpb[bank]])
            gy8v = gy8.rearrange("p (c j) -> p j c", j=8)
            gscv = gsc2[:, 0:1056].rearrange("p (j c) -> p j c", j=8)
            rg = R("gsc2")
            for b3 in range(3):
                njt = 3 if b3 < 2 else 2
                src_ = pb[5 + b3][:, 0:njt * 132].rearrange("p (j c) -> p j c", j=njt)
                dst_ = gy8v[:, b3 * 3:b3 * 3 + njt, :]
                fw.op("scalar", lambda e, src_=src_, dst_=dst_: e.activation(out=dst_, in_=src_, func=AF.Gelu_apprx_tanh),
                      [rpb[5 + b3]], [R("gy8")])
            for ct in range(9):
                npart = 128 if ct < 8 else 32
                bank, off = (0, ct * 128) if ct < 8 else (1, 0)
                fw.op("tensor", lambda e, ct=ct, npart=npart, bank=bank, off=off: e.transpose(
                    pub[bank][0:npart, off:off + 128], gy8[:, ct * 128:ct * 128 + npart], self.identB[:]),
                    [R("gy8"), rI], [rpb[bank]])
                fw.op("scalar" if ct % 2 == 0 else V,
                      (lambda e, ct=ct, npart=npart, bank=bank, off=off: e.activation(
                          out=Ytok[0:npart, ct, :, gi * 16:(gi + 1) * 16], in_=pub[bank][0:npart, off:off + 128].rearrange("p (a b) -> p a b", a=8),
                          func=AF.Copy)) if ct % 2 == 0 else
                      (lambda e, ct=ct, npart=npart, bank=bank, off=off: e.tensor_copy(
                          out=Ytok[0:npart, ct, :, gi * 16:(gi + 1) * 16], in_=pub[bank][0:npart, off:off + 128].rearrange("p (a b) -> p a b", a=8))),
                      [rpb[bank]], [R("Ytok")])
        for gi_ in range(NG):
            grp_(gi_)
        fw.dma("sync", [(self.gy[ct * 1024:(ct + 1) * 1024, g0 * 16:g0 * 16 + NG * 16].rearrange("(c s) w -> c s w", s=8), Ytok[:, ct])
                        for ct in range(8)], [R("Ytok")], [self.res("gy")])
        fw.dma("sync", [(self.gy[8192:8448, g0 * 16:g0 * 16 + NG * 16].rearrange("(c s) w -> c s w", s=8), Ytok[0:32, 8])],
               [R("Ytok")], [self.res("gy")])
    print("s5 arena end", self.aoff)


Prog.s5 = _s5
```

```python
from contextlib import ExitStack
import numpy as np
import concourse.bass as bass
import concourse.mybir as mybir
from concourse.bass_utils import run_bass_kernel_spmd

F32 = mybir.dt.float32
BF16 = mybir.dt.bfloat16
I32 = mybir.dt.int32
AF = mybir.ActivationFunctionType
ALU = mybir.AluOpType
AX = mybir.AxisListType

D = 1024
L = 8192
CT = 256
NT = L + CT
NTILE = NT // 128
INW = 2592
DEPTH = 4
EPS = 1e-6


import heapq


class Res:
    __slots__ = ("name", "w", "r", "dsem", "dcount", "last_dma")

    def __init__(self, name):
        self.name = name
        self.w = None
        self.r = []
        self.dsem = None
        self.dcount = 0
        self.last_dma = None


class _ProbeInst:
    def then_inc(self, *a, **k):
        return self


class _Probe:
    def __init__(self):
        self.name = None
        self.args = None
        self.kw = None

    def __getattr__(self, name):
        def f(*args, **kw):
            self.name, self.args, self.kw = name, args, kw
            return _ProbeInst()
        return f


def _fsize(ap):
    n = 1
    for d_ in ap.shape[1:]:
        n *= d_
    return n


class FW:
    SEM_LIMIT = 24000
    HOP = 1.2

    def __init__(self, nc, stack, schedule=True):
        self.nc = nc
        self.stack = stack
        self.schedule = schedule
        self.nsem = 0
        self.nodes = []
        self.bar = None
        self.bar_start = 0
        self.engnames = ("tensor", "vector", "scalar", "gpsimd", "sync")

    def new_sem(self, name):
        self.nsem += 1
        return self.stack.enter_context(self.nc.semaphore("%s_%d" % (name, self.nsem)))

    def sb(self, name, shape, dt):
        return self.stack.enter_context(self.nc.sbuf_tensor(name, list(shape), dt))

    def ps(self, name, shape, dt):
        return self.stack.enter_context(self.nc.psum_tensor(name, list(shape), dt))

    def _deps(self, reads, writes):
        deps = set()
        for r in reads:
            if r.w is not None:
                deps.add(r.w)
        for w in writes:
            if w.w is not None:
                deps.add(w.w)
            deps.update(w.r)
        if self.bar is not None:
            deps.add(self.bar)
        return deps

    def _cost(self, engname, fn):
        p = _Probe()
        try:
            fn(p)
            nm, kw, args = p.name, p.kw, p.args
            if nm == "matmul":
                rhs = kw["rhs"]
                n = _fsize(rhs)
                passes = 4 if rhs.dtype == F32 else 1
                return max(64, n) * passes / 2400.0 + 0.03
            if nm == "transpose" and engname == "tensor":
                return 128 / 2400.0 + 0.05
            ap = kw.get("out", None)
            if ap is None:
                ap = args[0]
            n = _fsize(ap)
            if engname == "vector":
                return n * 1.3 / 960.0 + 0.12
            if engname == "scalar":
                return n / 1200.0 + 0.25
            return n * 2.0 / 1200.0 + 0.3
        except Exception:
            return 0.5

    def op(self, engname, fn, reads=(), writes=()):
        nid = len(self.nodes)
        deps = self._deps(reads, writes)
        self.nodes.append(dict(id=nid, eng=engname, kind="op", fn=fn, deps=deps, cost=self._cost(engname, fn)))
        for r in reads:
            r.r.append(nid)
        for w in writes:
            w.w = nid
            w.r = []
        return nid

    def dma(self, qname, pairs, reads, writes, sres=None):
        sres = sres or writes[0]
        qt = "sw" if qname == "gpsimd" else "hw"
        if not isinstance(sres.dsem, dict):
            sres.dsem = {}
        st_ = sres.dsem.get(qt)
        if st_ is None or st_[1] >= 16 * 3000:
            st_ = [self.new_sem("d%s_%s" % (qt, sres.name)), 0, None]
            sres.dsem[qt] = st_
        nid = len(self.nodes)
        deps = self._deps(reads, writes)
        if st_[2] is not None:
            deps.add(st_[2])
        nbytes = 0
        for pr in pairs:
            if callable(pr):
                nbytes += 8 << 20
                continue
            (o, i) = pr
            n = 1
            for d_ in o.shape:
                n *= d_
            nbytes += n * (4 if o.dtype in (F32, I32) else 2)
        st_[1] += 16 * len(pairs)
        self.nodes.append(dict(id=nid, eng=qname, kind="dma", pairs=pairs, deps=deps, cost=0.08 * len(pairs), nbytes=nbytes,
                               dsem=st_[0], dval=st_[1]))
        st_[2] = nid
        for r in reads:
            r.r.append(nid)
        for w in writes:
            w.w = nid
            w.r = []
        return nid

    def barrier(self, _unused=None):
        nid = len(self.nodes)
        deps = set(range(self.bar_start, nid))
        self.nodes.append(dict(id=nid, eng=None, kind="bar", deps=deps, cost=0.0))
        self.bar = nid
        self.bar_start = nid

    def _simulate(self):
        nodes = self.nodes
        n = len(nodes)
        fin = [0.0] * n
        start = [0.0] * n
        if not self.schedule:
            order = {e: [] for e in self.engnames}
            for nd in nodes:
                if nd["eng"] is not None:
                    order[nd["eng"]].append(nd["id"])
            return order
        children = [[] for _ in range(n)]
        rem = [0] * n
        for nd in nodes:
            rem[nd["id"]] = len(nd["deps"])
            for d_ in nd["deps"]:
                children[d_].append(nd["id"])
        ready = [0.0] * n
        heap = []
        for nd in nodes:
            if rem[nd["id"]] == 0:
                heapq.heappush(heap, (0.0, nd["id"]))
        efree = {e: 0.0 for e in self.engnames}
        dma_free = 0.0
        order = {e: [] for e in self.engnames}
        done = 0
        while heap:
            rt, nid = heapq.heappop(heap)
            nd = nodes[nid]
            e = nd["eng"]
            if e is None:
                st = rt
                f = rt
            else:
                st = max(rt, efree[e])
                efree[e] = st + nd["cost"]
                order[e].append((st, nid))
                if nd["kind"] == "dma":
                    t0 = max(st + nd["cost"], dma_free)
                    dur = nd["nbytes"] / 150e3
                    dma_free = t0 + dur
                    f = t0 + dur + 2.0
                else:
                    f = st + nd["cost"]
            start[nid] = st
            fin[nid] = f
            done += 1
            for c in children[nid]:
                hop = 0.0 if (nodes[c]["eng"] == e and e == "tensor") else self.HOP
                if nodes[c]["kind"] == "bar" or nd["kind"] == "bar":
                    hop = 0.0
                ready[c] = max(ready[c], f + hop)
                rem[c] -= 1
                if rem[c] == 0:
                    heapq.heappush(heap, (ready[c], c))
        assert done == n, (done, n)
        self.sim_time = max(fin) if fin else 0.0
        out = {}
        for e in self.engnames:
            lst = sorted(order[e])
            out[e] = [nid for (_, nid) in lst]
        return out

    def finish(self, final_res):
        nc = self.nc
        nodes = self.nodes
        fdeps = set(r.w for r in final_res if r.w is not None)
        nid = len(nodes)
        nodes.append(dict(id=nid, eng="sync", kind="waitonly", deps=fdeps | set(range(self.bar_start, nid)), cost=0.0))
        order = self._simulate()
        tok = {}
        for e in self.engnames:
            sem = self.new_sem("prog_" + e)
            cnt = 0
            for nid_ in order[e]:
                nd = nodes[nid_]
                if nd["kind"] == "op":
                    if cnt >= self.SEM_LIMIT:
                        sem = self.new_sem("prog_" + e)
                        cnt = 0
                    cnt += 1
                    tok[nid_] = (sem, cnt, e)
                    nd["sem"] = sem
                elif nd["kind"] == "dma":
                    tok[nid_] = (nd["dsem"], nd["dval"], "dma")
        bartok = {}
        for nd in nodes:
            if nd["kind"] == "bar":
                best = {}
                for d_ in nd["deps"]:
                    if nodes[d_]["kind"] == "bar":
                        for k, v in bartok[d_].items():
                            if k not in best or best[k][1] < v[1]:
                                best[k] = v
                    elif d_ in tok:
                        s_, v_, en = tok[d_]
                        k = id(s_)
                        if k not in best or best[k][1] < v_:
                            best[k] = (s_, v_, "bar")
                bartok[nd["id"]] = best
        selfwait = {"vector": True, "scalar": True, "gpsimd": True, "tensor": False, "sync": False}
        progs = {}
        for e in self.engnames:
            waited = {}
            prog = []
            for nid_ in order[e]:
                nd = nodes[nid_]
                best = {}
                for d_ in nd["deps"]:
                    if nodes[d_]["kind"] == "bar":
                        items = bartok[d_].values()
                    elif d_ in tok:
                        items = [tok[d_]]
                    else:
                        items = []
                    for (s_, v_, en) in items:
                        if en == e and not selfwait[e]:
                            continue
                        k = id(s_)
                        if waited.get(k, 0) >= v_:
                            continue
                        if k not in best or best[k][1] < v_:
                            best[k] = (s_, v_)
                for k, (s_, v_) in best.items():
                    waited[k] = v_
                    prog.append(("wait", s_, v_))
                if nd["kind"] == "op":
                    prog.append(("op", nd["fn"], nd["sem"]))
                elif nd["kind"] == "dma":
                    for pr in nd["pairs"]:
                        if callable(pr):
                            prog.append(("cdma", pr, None, nd["dsem"]))
                        else:
                            prog.append(("dma", pr[0], pr[1], nd["dsem"]))
            progs[e] = prog
        self.progs = progs

        def run(e, prog):
            for it in prog:
                if it[0] == "wait":
                    e.wait_ge(it[1], it[2])
                elif it[0] == "op":
                    it[1](e).then_inc(it[2], 1)
                elif it[0] == "cdma":
                    it[1](e).then_inc(it[3], 16)
                else:
                    e.dma_start(out=it[1], in_=it[2]).then_inc(it[3], 16)

        with nc.allow_non_contiguous_dma(reason="small param layouts"), nc.Block() as block:
            @block.tensor
            def _(e):
                run(e, progs["tensor"])

            @block.vector
            def _(e):
                run(e, progs["vector"])

            @block.scalar
            def _(e):
                run(e, progs["scalar"])

            @block.gpsimd
            def _(e):
                run(e, progs["gpsimd"])

            @block.sync
            def _(e):
                run(e, progs["sync"])


class Prog:
    def __init__(self, depth=DEPTH, stub_s5=False, stub_gla=False, schedule=True):
        self.depth = depth
        self.schedule = schedule
        self.stub_s5 = stub_s5
        self.stub_gla = stub_gla
        nc = self.nc = bass.Bass("TRN2", target_bir_lowering=False)
        di = lambda n, s, dt=F32: nc.dram_tensor(n, list(s), dt, kind="ExternalInput").ap()
        ds = lambda n, s, dt=F32: nc.dram_tensor(n, list(s), dt, kind="Internal").ap()
        self.x = di("x", [L, D])
        self.ctx = di("ctx", [CT, D])
        self.cc = di("cc", [2, D])
        self.norm_g = di("norm_g", [DEPTH, D])
        self.w_mod = di("w_mod", [DEPTH, D, 3 * D])
        self.b_mod = di("b_mod", [DEPTH, 3 * D])
        self.w_in = di("w_in", [DEPTH, D, INW])
        self.lam_re = di("s5_lam_re", [DEPTH, 2, 32, 64])
        self.lam_im = di("s5_lam_im", [DEPTH, 2, 32, 64])
        self.log_dt = di("s5_log_dt", [DEPTH, 2, 32])
        self.b_re = di("s5_b_re", [DEPTH, 32, 64, 16])
        self.b_im = di("s5_b_im", [DEPTH, 32, 64, 16])
        self.c_re = di("s5_c_re", [DEPTH, 32, 16, 64])
        self.c_im = di("s5_c_im", [DEPTH, 32, 16, 64])
        self.s5_d = di("s5_d", [DEPTH, 512])
        self.w_glu = di("s5_w_glu", [DEPTH, 512, 512])
        self.b_glu = di("s5_b_glu", [DEPTH, 512])
        self.w_gate = di("gla_w_gate", [DEPTH, 2, 16, 256])
        self.b_gate = di("gla_b_gate", [DEPTH, 2, 256])
        self.gnorm = di("gla_norm_g", [DEPTH, 128])
        self.w_out = di("w_out", [DEPTH, D, D])
        self.final_norm = di("final_norm", [1, D])
        self.out = nc.dram_tensor("out", [L, D], F32, kind="ExternalOutput").ap()
        self.xs = [ds("xs0", [NT, D]), ds("xs1", [NT, D])]
        self.P = ds("P", [NT, INW], BF16)
        import os
        if os.environ.get("DEBUG_GY"):
            self.gy = nc.dram_tensor("gy", [NT, 512], BF16, kind="ExternalOutput").ap()
        else:
            self.gy = ds("gy", [NT, 512], BF16)
        self.yg = ds("yg", [NT, 512], BF16)
        self.R = {}
        with ExitStack() as st:
            self.fw = FW(nc, st, schedule=self.schedule)
            self.alloc()
            self.consts()
            for l in range(depth):
                self.weights_in(l)
                self.modulation(l)
                self.phase1(l)
                if stub_s5:
                    self.s5_stub(l)
                else:
                    self.fw.barrier(self.R.values())
                    self.s5(l)
                    self.fw.barrier(self.R.values())
                if stub_gla:
                    self.gla_stub(l)
                else:
                    self.fw.barrier(self.R.values())
                    self.gla(l)
                    self.fw.barrier(self.R.values())
                self.weights_out(l)
                self.phase3(l)
            self.fw.finish([self.res("out")])

    def res(self, name):
        if name not in self.R:
            self.R[name] = Res(name)
        return self.R[name]

    def view(self, shape, dt):
        n = 1
        for d_ in shape[1:]:
            n *= d_
        words = n if dt in (F32, I32) else (n + 1) // 2
        ap = self.arena[:, self.aoff:self.aoff + words]
        self.aoff += words
        assert self.aoff <= self.NW, (self.aoff, self.NW)
        if dt != F32:
            ap = ap.bitcast(dt)
        if len(shape) == 3:
            ap = ap.rearrange("p (a b) -> p a b", a=shape[1])
        elif len(shape) == 4:
            ap = ap.rearrange("p (a b c) -> p a b c", a=shape[1], b=shape[2])
        return ap

    def alloc(self):
        fw = self.fw
        self.NW = 52600
        self.arena = fw.sb("arena", [128, self.NW], F32)
        self.aoff = 0
        self.identF = fw.sb("identF", [128, 128], F32)
        self.identB = fw.sb("identB", [128, 128], BF16)
        self.ccT = fw.sb("ccT", [128, 8, 2], F32)
        self.st1 = [fw.sb("st1_%d" % i, [128, 4], F32) for i in range(2)]
        self.pb = [fw.ps("pb%d" % i, [128, 512], F32) for i in range(8)]
        v = self.view
        self.scB = v([128, 8, 2, 128], F32)
        self.modb = v([128, 2, 3 * D], F32)
        self.bglub = v([128, 512], F32)
        self.fnb = v([128, D], F32)
        self.phase_base = self.aoff
        self.winb = v([128, 8, INW], BF16)
        self.woutb = v([128, 8, D], BF16)
        self.wglub = v([128, 4, 512], BF16)
        self.wst = v([128, 6144], F32)
        self.xt = [v([128, D], F32) for i in range(2)]
        self.xn = [v([128, D], F32) for i in range(2)]
        self.yo = [v([128, D], F32) for i in range(2)]
        self.hb = [v([128, D], BF16) for i in range(2)]
        self.mix = self.hb
        self.hT = [v([128, 8, 128], BF16) for i in range(2)]
        self.mixT = self.hT
        self.pj = [v([128, INW], BF16) for i in range(2)]
        self.g3 = [v([128, 2048], BF16) for i in range(2)]
        self.gyT = [v([128, 4, 128], BF16) for i in range(2)]
        self.t3 = [v([128, 512], F32) for i in range(2)]
        self.dense_end = self.aoff
        print("arena dense end", self.dense_end, "phase_base", self.phase_base)

    def consts(self):
        fw = self.fw
        identF, identB = self.identF, self.identB
        rI = self.res("ident")
        fw.op("gpsimd", lambda e: e.memset(identF[:], 0.0), [], [rI])
        fw.op("gpsimd", lambda e: e.affine_select(out=identF[:], in_=identF[:], compare_op=ALU.not_equal, fill=1.0,
                                                 base=0, pattern=[[-1, 128]], channel_multiplier=1), [rI], [rI])
        fw.op("gpsimd", lambda e: e.tensor_copy(out=identB[:], in_=identF[:]), [rI], [rI])
        rc = self.res("cc")
        ccT, scB = self.ccT, self.scB
        fw.dma("sync", [(ccT[:, :, j], self.cc[j, :].rearrange("(k p) -> p k", p=128)) for j in range(2)], [], [rc])
        fw.op("scalar", lambda e: e.activation(out=ccT[:], in_=ccT[:], func=AF.Silu), [rc], [rc])
        for k in range(8):
            for j in range(2):
                fw.op("vector", lambda e, k=k, j=j: e.tensor_copy(out=scB[:, k, j, :],
                                                                   in_=ccT[:, k, j:j + 1].to_broadcast([128, 128])),
                      [rc], [self.res("scB")])
        fw.dma("sync", [(self.fnb, self.final_norm[0:1, :].partition_broadcast(128)[:, 0, :])], [], [self.res("fnb")])

    def _wload(self, src_rows, ncols, dst, rdst, n):
        fw = self.fw
        s_ = n % 2
        rs = self.res("wst_s%d" % s_)
        stg = self.wst[:, s_ * 3072:s_ * 3072 + ncols]
        fw.dma("sync" if n % 2 == 0 else "gpsimd", [(stg, src_rows)], [], [rs])
        fw.op("gpsimd" if n % 2 == 0 else "vector", lambda e: e.tensor_copy(out=dst, in_=stg), [rs], [rdst])

    def weights_in(self, l):
        for k in range(8):
            self._wload(self.w_in[l, k * 128:(k + 1) * 128, :], INW, self.winb[:, k, :], self.res("winb"), k)

    def weights_out(self, l):
        fw = self.fw
        for k in range(8):
            self._wload(self.w_out[l, k * 128:(k + 1) * 128, :], D, self.woutb[:, k, :], self.res("woutb"), k)
        for k in range(4):
            self._wload(self.w_glu[l, k * 128:(k + 1) * 128, :], 512, self.wglub[:, k, :], self.res("wglub"), k)
        fw.dma("sync", [(self.bglub, self.b_glu[l:l + 1, :].partition_broadcast(128)[:, 0, :])], [], [self.res("bglub")])

    def modulation(self, l):
        fw = self.fw
        modb = self.modb
        rs0, rs1 = self.res("wst_s0"), self.res("wst_s1")
        rmod = self.res("modb")
        wv = self.wst.rearrange("p (k n) -> p k n", k=8)
        bt = self.t3[0]
        rbt = self.res("t3_0")
        for q in range(4):
            c0 = q * 768
            fw.dma("sync", [(wv[:, k, :], self.w_mod[l, k * 128:(k + 1) * 128, c0:c0 + 768]) for k in range(4)],
                   [], [rs0, rs1])
            fw.dma("gpsimd", [(wv[:, k, :], self.w_mod[l, k * 128:(k + 1) * 128, c0:c0 + 768]) for k in range(4, 8)],
                   [], [rs0, rs1], sres=self.res("wst_g"))
            for n in range(2):
                col = c0 + n * 384
                fw.dma("sync", [(bt[:, 0:384], self.b_mod[l:l + 1, col:col + 384].partition_broadcast(128)[:, 0, :])], [], [rbt])
                for j in range(2):
                    bi = (n * 2 + j) % 4
                    pb = self.pb[bi]
                    rp = self.res("pb%d" % bi)
                    for k in range(8):
                        fw.op("tensor", lambda e, k=k, j=j, n=n, pb=pb: e.matmul(
                            pb[:, 0:384], lhsT=self.scB[:, k, j, :], rhs=wv[:, k, n * 384:(n + 1) * 384],
                            start=(k == 0), stop=(k == 7)), [rs0, rs1, self.res("scB")], [rp])
                    fw.op("vector", lambda e, j=j, col=col, pb=pb: e.tensor_tensor(
                        out=modb[:, j, col:col + 384], in0=pb[:, 0:384], in1=bt[:, 0:384], op=ALU.add),
                        [rp, rbt], [rmod])
        ngb = self.xn[0]
        rng = self.res("xn0")
        fw.dma("sync", [(ngb, self.norm_g[l:l + 1, :].partition_broadcast(128)[:, 0, :])], [], [rng])
        for j in range(2):
            fw.op("vector", lambda e, j=j: e.scalar_tensor_tensor(
                out=modb[:, j, D:2 * D], in0=modb[:, j, D:2 * D], scalar=1.0, in1=ngb,
                op0=ALU.add, op1=ALU.mult), [rmod, rng], [rmod])

    def xsrc(self, l, i):
        if l == 0:
            if i < 2:
                return self.ctx[i * 128:(i + 1) * 128, :], None
            return self.x[(i - 2) * 128:(i - 1) * 128, :], None
        return self.xs[l % 2][i * 128:(i + 1) * 128, :], self.res("xs%d" % (l % 2))

    def phase1(self, l):
        fw = self.fw
        rmod = self.res("modb")
        rP = self.res("P")
        for i in range(NTILE):
            s = i % 2
            j = 1 if i < 2 else 0
            xt, xn, hb, hT, pj, st1 = self.xt[s], self.xn[s], self.hb[s], self.hT[s], self.pj[s], self.st1[s]
            rxt, rxn, rhb, rhT, rpj, rst = (self.res("%s%d" % (n, s)) for n in ("xt", "xn", "hb", "hT", "pj", "st1"))
            src, rsrc = self.xsrc(l, i)
            fw.dma("sync", [(xt[:], src)], [rsrc] if rsrc else [], [rxt])
            fw.op("scalar", lambda e, xt=xt, xn=xn, st1=st1: e.activation(out=xn[:], in_=xt[:], func=AF.Square,
                                                                        accum_out=st1[:, 0:1]), [rxt], [rxn, rst])
            fw.op("vector", lambda e, st1=st1: e.tensor_scalar(out=st1[:, 1:2], in0=st1[:, 0:1], scalar1=1.0 / D, scalar2=EPS,
                                                              op0=ALU.mult, op1=ALU.add), [rst], [rst])
            fw.op("scalar", lambda e, st1=st1: e.activation(out=st1[:, 2:3], in_=st1[:, 1:2], func=AF.Sqrt), [rst], [rst])
            fw.op("vector", lambda e, st1=st1: e.reciprocal(out=st1[:, 3:4], in_=st1[:, 2:3]), [rst], [rst])
            fw.op("vector", lambda e, xt=xt, xn=xn, st1=st1, j=j: e.scalar_tensor_tensor(
                out=xn[:], in0=xt[:], scalar=st1[:, 3:4], in1=self.modb[:, j, D:2 * D], op0=ALU.mult, op1=ALU.mult),
                [rxt, rst, rmod], [rxn])
            fw.op("gpsimd", lambda e, xn=xn, hb=hb, j=j: e.tensor_tensor(out=hb[:], in0=xn[:], in1=self.modb[:, j, 0:D], op=ALU.add),
                  [rxn, rmod], [rhb])
            tb_ = 0 if s == 0 else 5
            ptb = self.pb[tb_][:].bitcast(BF16)
            rp0 = self.res("pb%d" % tb_)
            for k in range(8):
                fw.op("tensor", lambda e, k=k, hb=hb, ptb=ptb: e.transpose(ptb[:, k * 128:(k + 1) * 128], hb[:, k * 128:(k + 1) * 128],
                                                                         self.identB[:]), [rhb, self.res("ident")], [rp0])
            fw.op("scalar", lambda e, hT=hT, ptb=ptb: e.activation(out=hT[:].rearrange("p k t -> p (k t)"), in_=ptb, func=AF.Copy),
                  [rp0], [rhT])
            chunks = [(0, 512, "copy"), (512, 512, "silu"), (1024, 256, "q"), (1280, 256, "copy"), (1536, 512, "copy"),
                      (2048, 512, "silu"), (2560, 32, "copy")]
            groups = [(0, 512), (512, 512), (1024, 512), (1536, 512), (2048, 512), (2560, 32)]
            for gi, (c0, w) in enumerate(groups):
                b = (1, 2, 3, 4, 6, 7)[gi % 6]
                pb = self.pb[b]
                rp = self.res("pb%d" % b)
                for k in range(8):
                    fw.op("tensor", lambda e, k=k, c0=c0, w=w, pb=pb, hT=hT: e.matmul(
                        pb[:, 0:w], lhsT=hT[:, k, :], rhs=self.winb[:, k, c0:c0 + w], start=(k == 0), stop=(k == 7)),
                        [rhT, self.res("winb")], [rp])
                for (a0, aw, kind) in chunks:
                    if a0 < c0 or a0 >= c0 + w:
                        continue
                    o = pj[:, a0:a0 + aw]
                    src_ = pb[:, a0 - c0:a0 - c0 + aw]
                    if kind == "silu":
                        fw.op("scalar", lambda e, o=o, src_=src_: e.activation(out=o, in_=src_, func=AF.Silu), [rp], [rpj])
                    elif kind == "q":
                        fw.op("vector", lambda e, o=o, src_=src_: e.tensor_scalar(out=o, in0=src_, scalar1=0.125, scalar2=None,
                                                                                op0=ALU.mult), [rp], [rpj])
                    else:
                        fw.op("vector", lambda e, o=o, src_=src_: e.tensor_copy(out=o, in_=src_), [rp], [rpj])
            fw.dma("gpsimd", [(self.P[i * 128:(i + 1) * 128, :], pj[:])], [rpj], [rP])

    def gelu(self, eng_a, out, in_, tmp, reads, writes, rtmp):
        fw = self.fw
        fw.op("scalar", lambda e: e.activation(out=tmp, in_=in_, func=AF.Square), reads, [rtmp])
        fw.op(eng_a, lambda e: e.tensor_scalar(out=tmp, in0=tmp, scalar1=0.044715, scalar2=1.0, op0=ALU.mult, op1=ALU.add),
              [rtmp], [rtmp])
        fw.op(eng_a, lambda e: e.tensor_tensor(out=tmp, in0=tmp, in1=in_, op=ALU.mult), [rtmp] + list(reads), [rtmp])
        fw.op("scalar", lambda e: e.activation(out=tmp, in_=tmp, func=AF.Sigmoid, scale=1.5957691216), [rtmp], [rtmp])
        fw.op(eng_a, lambda e: e.tensor_tensor(out=out, in0=tmp, in1=in_, op=ALU.mult), [rtmp] + list(reads), writes)

    def s5_stub(self, l):
        fw = self.fw
        for i in range(NTILE):
            s = i % 2
            t = self.g3[s]
            rt = self.res("g3_%d" % s)
            fw.dma("sync", [(t[:, 0:512], self.P[i * 128:(i + 1) * 128, 0:512])], [self.res("P")], [rt])
            self.gelu("vector", t[:, 512:1024], t[:, 0:512], self.t3[s][:], [rt], [rt], self.res("t3_%d" % s))
            fw.dma("sync", [(self.gy[i * 128:(i + 1) * 128, :], t[:, 512:1024])], [rt], [self.res("gy")])

    def gla_stub(self, l):
        fw = self.fw
        for i in range(NTILE):
            s = i % 2
            t = self.g3[s]
            rt = self.res("g3_%d" % s)
            fw.dma("sync", [(t[:, 0:512], self.P[i * 128:(i + 1) * 128, 1536:2048])], [self.res("P")], [rt])
            fw.dma("sync", [(self.yg[i * 128:(i + 1) * 128, :], t[:, 0:512])], [rt], [self.res("yg")])

    def phase3(self, l):
        fw = self.fw
        last = (l == self.depth - 1)
        rmod = self.res("modb")
        rout = self.res("out")
        rxd = self.res("xs%d" % ((l + 1) % 2))
        for i in range(NTILE):
            if last and i < 2:
                continue
            s = i % 2
            j = 1 if i < 2 else 0
            g3, gyT, t3, mix, mixT, yo, xt, st1 = (self.g3[s], self.gyT[s], self.t3[s], self.mix[s], self.mixT[s], self.yo[s],
                                                   self.xt[s], self.st1[s])
            rg3, rgyT, rt3, rmix, rmixT, ryo, rxt, rst = (self.res("%s%d" % (n, s)) for n in
                                                          ("g3_", "gyT", "t3_", "mix", "mixT", "yo", "xt", "st1"))
            rows = slice(i * 128, (i + 1) * 128)
            fw.dma("sync", [(g3[:, 0:512], self.gy[rows, :])], [self.res("gy")], [rg3])
            fw.dma("sync", [(g3[:, 512:1024], self.yg[rows, :])], [self.res("yg")], [rg3])
            fw.dma("sync", [(g3[:, 1024:1536], self.P[rows, 512:1024]), (g3[:, 1536:2048], self.P[rows, 2048:2560])],
                   [self.res("P")], [rg3])
            src, rsrc = self.xsrc(l, i)
            fw.dma("gpsimd", [(xt[:], src)], [rsrc] if rsrc else [], [rxt])
            b5_ = 5 if s == 0 else 0
            ptb = self.pb[b5_][:].bitcast(BF16)
            rp5 = self.res("pb%d" % b5_)
            for k in range(4):
                fw.op("tensor", lambda e, k=k, g3=g3, ptb=ptb: e.transpose(ptb[:, k * 128:(k + 1) * 128], g3[:, k * 128:(k + 1) * 128],
                                                                         self.identB[:]), [rg3, self.res("ident")], [rp5])
            fw.op("scalar", lambda e, gyT=gyT, ptb=ptb: e.activation(out=gyT[:].rearrange("p k t -> p (k t)"), in_=ptb[:, 0:512],
                                                                   func=AF.Copy), [rp5], [rgyT])
            b6_ = 6 if s == 0 else 3
            pg = self.pb[b6_]
            rp6 = self.res("pb%d" % b6_)
            for k in range(4):
                fw.op("tensor", lambda e, k=k, gyT=gyT, pg=pg: e.matmul(pg[:], lhsT=gyT[:, k, :], rhs=self.wglub[:, k, :],
                                                                      start=(k == 0), stop=(k == 3)), [rgyT, self.res("wglub")], [rp6])
            fw.op("vector", lambda e, t3=t3, pg=pg: e.tensor_tensor(out=t3[:], in0=pg[:], in1=self.bglub, op=ALU.add),
                  [rp6, self.res("bglub")], [rt3])
            fw.op("scalar", lambda e, t3=t3: e.activation(out=t3[:], in_=t3[:], func=AF.Sigmoid), [rt3], [rt3])
            fw.op("vector", lambda e, t3=t3, g3=g3: e.tensor_tensor(out=t3[:], in0=t3[:], in1=g3[:, 0:512], op=ALU.mult),
                  [rt3, rg3], [rt3])
            fw.op("vector", lambda e, t3=t3, g3=g3, mix=mix: e.tensor_tensor(out=mix[:, 0:512], in0=t3[:], in1=g3[:, 1024:1536],
                                                                            op=ALU.mult), [rt3, rg3], [rmix])
            fw.op("gpsimd", lambda e, g3=g3, mix=mix: e.tensor_tensor(out=mix[:, 512:1024], in0=g3[:, 512:1024], in1=g3[:, 1536:2048],
                                                                     op=ALU.mult), [rg3], [rmix])
            b7_ = 7 if s == 0 else 4
            ptm = self.pb[b7_][:].bitcast(BF16)
            rp7 = self.res("pb%d" % b7_)
            for k in range(8):
                fw.op("tensor", lambda e, k=k, mix=mix, ptm=ptm: e.transpose(ptm[:, k * 128:(k + 1) * 128], mix[:, k * 128:(k + 1) * 128],
                                                                           self.identB[:]), [rmix, self.res("ident")], [rp7])
            fw.op("scalar", lambda e, mixT=mixT, ptm=ptm: e.activation(out=mixT[:].rearrange("p k t -> p (k t)"), in_=ptm, func=AF.Copy),
                  [rp7], [rmixT])
            for n in range(2):
                b = 1 + n
                pb = self.pb[b]
                rp = self.res("pb%d" % b)
                for k in range(8):
                    fw.op("tensor", lambda e, k=k, n=n, pb=pb, mixT=mixT: e.matmul(
                        pb[:], lhsT=mixT[:, k, :], rhs=self.woutb[:, k, n * 512:(n + 1) * 512], start=(k == 0), stop=(k == 7)),
                        [rmixT, self.res("woutb")], [rp])
                cs = slice(n * 512, (n + 1) * 512)
                fw.op("vector", lambda e, pb=pb, yo=yo, cs=cs, j=j, n=n: e.tensor_tensor(
                    out=yo[:, cs], in0=pb[:], in1=self.modb[:, j, 2 * D + n * 512:2 * D + (n + 1) * 512], op=ALU.mult),
                    [rp, rmod], [ryo])
            fw.op("gpsimd", lambda e, yo=yo, xt=xt: e.tensor_tensor(out=yo[:], in0=yo[:], in1=xt[:], op=ALU.add), [ryo, rxt], [ryo])
            if not last:
                fw.dma("sync", [(self.xs[(l + 1) % 2][rows, :], yo[:])], [ryo], [rxd])
            else:
                xn = self.xn[s]
                rxn = self.res("xn%d" % s)
                fw.op("scalar", lambda e, yo=yo, xn=xn, st1=st1: e.activation(out=xn[:], in_=yo[:], func=AF.Square,
                                                                            accum_out=st1[:, 0:1]), [ryo], [rxn, rst])
                fw.op("vector", lambda e, st1=st1: e.tensor_scalar(out=st1[:, 1:2], in0=st1[:, 0:1], scalar1=1.0 / D, scalar2=EPS,
                                                                  op0=ALU.mult, op1=ALU.add), [rst], [rst])
                fw.op("scalar", lambda e, st1=st1: e.activation(out=st1[:, 2:3], in_=st1[:, 1:2], func=AF.Sqrt), [rst], [rst])
                fw.op("vector", lambda e, st1=st1: e.reciprocal(out=st1[:, 3:4], in_=st1[:, 2:3]), [rst], [rst])
                fw.op("vector", lambda e, yo=yo, xn=xn, st1=st1: e.scalar_tensor_tensor(
                    out=xn[:], in0=yo[:], scalar=st1[:, 3:4], in1=self.fnb, op0=ALU.mult, op1=ALU.mult),
                    [ryo, rst, self.res("fnb")], [rxn])
                fw.dma("sync", [(self.out[(i - 2) * 128:(i - 1) * 128, :], xn[:])], [rxn], [rout])

    def s5(self, l):
        raise NotImplementedError

    def gla(self, l):
        raise NotImplementedError


_CACHE = {}


def make_in_maps(inputs):
    maps = []
    for core in range(8):
        b = core % 4
        m = {
            "x": np.ascontiguousarray(inputs["x"][b]),
            "ctx": np.ascontiguousarray(inputs["ctx"][b]),
            "cc": np.ascontiguousarray(np.stack([inputs["c"][b], inputs["c_ctx"]], axis=0)),
            "final_norm": np.ascontiguousarray(inputs["final_norm"][None, :]),
        }
        for k in ("norm_g", "w_mod", "b_mod", "w_in", "s5_lam_re", "s5_lam_im", "s5_log_dt", "s5_b_re", "s5_b_im", "s5_c_re",
                  "s5_c_im", "s5_d", "s5_w_glu", "s5_b_glu", "gla_w_gate", "gla_b_gate", "gla_norm_g", "w_out"):
            m[k] = np.ascontiguousarray(inputs[k])
        maps.append(m)
    return maps


def kernel(**inputs):
    inputs = {k: np.asarray(v) for k, v in inputs.items()}
    if "prog" not in _CACHE:
        _CACHE["prog"] = Prog()
    prog = _CACHE["prog"]
    res = run_bass_kernel_spmd(prog.nc, make_in_maps(inputs), core_ids=list(range(8)))
    return np.stack([np.asarray(res.results[b]["out"]) for b in range(4)], axis=0).astype(np.float32)


def _gla(self, l):
    fw = self.fw
    v = self.view
    self.aoff = self.phase_base
    NS = 4
    T = [v([128, 1056], BF16) for _ in range(NS)]
    lrT_L = [v([128, 128], BF16) for _ in range(NS)]
    wg32 = v([128, 2, 256], F32)
    wgp = v([128, 2, 256], BF16)
    nbg = v([128, 2, 2], F32)
    sp_L = [v([128, 2, 2, 128], F32) for _ in range(NS)]
    cs_L = [v([128, 2, 2, 128], F32) for _ in range(NS)]
    eq_L = [v([128, 2, 2, 128], F32) for _ in range(NS)]
    ek_L = [v([128, 2, 2, 128], F32) for _ in range(NS)]
    ekd_L = [v([128, 2, 2, 128], F32) for _ in range(NS)]
    tot_L = [v([128, 2, 2, 4], F32) for _ in range(NS)]
    qtT_L = [v([128, 2, 2, 128], BF16) for _ in range(NS)]
    ktT_L = [v([128, 2, 2, 128], BF16) for _ in range(NS)]
    kdT_L = [v([128, 2, 2, 128], BF16) for _ in range(NS)]
    kdt_L = [v([128, 2, 2, 128], BF16) for _ in range(NS)]
    sT_L = [v([128, 2, 4, 128], BF16) for _ in range(NS)]
    S32 = v([128, 2, 2, 128], F32)
    Sbf = v([128, 2, 128], BF16)
    Sst = v([128, NTILE, 2, 128], BF16)
    Mf = v([128, 128], F32)
    Mb = v([128, 128], F32)
    ones = v([128, 128], F32)
    gnb = v([128, 128], F32)
    ygt = [v([128, 512], BF16) for _ in range(NS)]
    sq = v([128, 128], F32)
    rs = v([128, 8], F32)
    R0 = lambda n: self.res("gla_" + n)
    R = R0
    SLOTTED = ("lrT", "sp", "cs", "eq", "ek", "ekd", "tot", "qtT", "ktT", "kdT", "kdt", "sT")

    def mkR(s):
        return lambda n: R0(n + "_s%d" % s) if n in SLOTTED else R0(n)
    rI = self.res("ident")
    fw.op("gpsimd", lambda e: e.memset(Mf, 1.0), [], [R("Mf")])
    fw.op("gpsimd", lambda e: e.affine_select(out=Mf, in_=Mf, compare_op=ALU.is_ge, fill=0.0, base=0,
                                             pattern=[[1, 128]], channel_multiplier=-1), [R("Mf")], [R("Mf")])
    fw.op("gpsimd", lambda e: e.memset(Mb, 1.0), [], [R("Mb")])
    fw.op("gpsimd", lambda e: e.affine_select(out=Mb, in_=Mb, compare_op=ALU.is_ge, fill=0.0, base=0,
                                             pattern=[[-1, 128]], channel_multiplier=1), [R("Mb")], [R("Mb")])
    fw.op("gpsimd", lambda e: e.memset(ones, 1.0), [], [R("ones")])
    fw.op("vector", lambda e: e.memset(wg32, 0.0), [], [R("wg32")])
    fw.dma("sync", [(wg32[0:16, 0, :], self.w_gate[l, 0, :, :]), (wg32[16:32, 1, :], self.w_gate[l, 1, :, :])], [], [R("wg32")])
    fw.op("vector", lambda e: e.tensor_copy(out=wgp[0:32], in_=wg32[0:32]), [R("wg32")], [R("wgp")])
    fw.dma("sync", [(nbg[:, d, hp:hp + 1], self.b_gate[l, d, hp * 128:(hp + 1) * 128].rearrange("(p o) -> p o", o=1))
                    for d in range(2) for hp in range(2)], [], [R("nbg")])
    fw.op("vector", lambda e: e.tensor_scalar(out=nbg, in0=nbg, scalar1=-1.0, scalar2=None, op0=ALU.mult), [R("nbg")], [R("nbg")])
    fw.dma("sync", [(gnb, self.gnorm[l:l + 1, :].partition_broadcast(128)[:, 0, :])], [], [R("gnb")])
    fw.op("vector", lambda e: e.memset(S32, 0.0), [], [R("S32")])
    fw.op("vector", lambda e: e.memset(Sbf, 0.0), [], [R("Sbf")])

    Plat = self.P[CT:, :].rearrange("(r c) w -> c r w", c=64)
    yglat = self.yg[CT:, :].rearrange("(r c) w -> c r w", c=64)

    def rows(ap, lat, ci, c0, c1):
        if ci < 2:
            return ap[ci * 128:(ci + 1) * 128, c0:c1]
        return lat[ci - 2, :, c0:c1]

    pz, pqk, plr, psc0, psc1, pkd, pdS, po = self.pb
    rpb = [self.res("pb%d" % i) for i in range(8)]

    def load(ci, s):
        fw.dma("sync", [(T[s][:, 0:1024], rows(self.P, Plat, ci, 1024, 2048))], [self.res("P")], [R("T%d" % s)])
        fw.dma("gpsimd", [(T[s][:, 1024:1056], rows(self.P, Plat, ci, 2560, 2592))], [self.res("P")], [R("T%d" % s)],
               sres=R("T%db" % s))

    def gates(s, dirs, need_qk):
        lrT, sp, cs, eq, ek, ekd, tot, qtT, ktT, kdT, kdt, sT = (X[s] for X in (lrT_L, sp_L, cs_L, eq_L, ek_L, ekd_L, tot_L, qtT_L, ktT_L, kdT_L, kdt_L, sT_L))
        R = mkR(s)
        Tt = T[s]
        rT = [R("T%d" % s), R("T%db" % s)]
        plrb = plr[:].bitcast(BF16)
        fw.op("tensor", lambda e: e.transpose(plrb[0:32, 0:128], Tt[:, 1024:1056], self.identB[:]), rT + [rI], [rpb[2]])
        fw.op("scalar", lambda e: e.activation(out=lrT[0:32, :], in_=plrb[0:32, 0:128], func=AF.Copy), [rpb[2]], [R("lrT")])
        pqkb = pqk[:].bitcast(BF16)
        for t4 in range(4):
            fw.op("tensor", lambda e, t4=t4: e.transpose(pqkb[:, t4 * 128:(t4 + 1) * 128], Tt[:, t4 * 128:(t4 + 1) * 128],
                                                        self.identB[:]), rT + [rI], [rpb[1]])
        for d in dirs:
            for hp in range(2):
                fw.op("tensor", lambda e, d=d, hp=hp: e.matmul(pz[:, (d * 2 + hp) * 128:(d * 2 + hp + 1) * 128],
                                                              lhsT=wgp[0:32, d, hp * 128:(hp + 1) * 128], rhs=lrT[0:32, :],
                                                              start=True, stop=True), [R("wgp"), R("lrT")], [rpb[0]])
                fw.op("scalar", lambda e, d=d, hp=hp: e.activation(out=sp[:, d, hp, :], in_=pz[:, (d * 2 + hp) * 128:(d * 2 + hp + 1) * 128],
                                                                  func=AF.Exp, scale=-1.0, bias=nbg[:, d, hp:hp + 1]),
                      [rpb[0], R("nbg")], [R("sp")])
            fw.op("scalar", lambda e, d=d: e.activation(out=sp[:, d], in_=sp[:, d], func=AF.Ln, bias=1.0), [R("sp")], [R("sp")])
            for hp in range(2):
                fw.op("vector", lambda e, d=d, hp=hp: e.tensor_tensor_scan(out=cs[:, d, hp, :], data0=ones, data1=sp[:, d, hp, :],
                                                                          initial=0.0, op0=ALU.mult, op1=ALU.add),
                      [R("sp"), R("ones")], [R("cs")])
                fw.op("vector", lambda e, d=d, hp=hp: e.tensor_copy(out=tot[:, d, hp, 0:1], in_=cs[:, d, hp, 127:128]),
                      [R("cs")], [R("tot")])
                if d == 1:
                    fw.op("vector", lambda e, d=d, hp=hp: e.scalar_tensor_tensor(out=cs[:, d, hp, :], in0=sp[:, d, hp, :],
                                                                                scalar=tot[:, d, hp, 0:1], in1=cs[:, d, hp, :],
                                                                                op0=ALU.add, op1=ALU.subtract),
                          [R("sp"), R("tot"), R("cs")], [R("cs")])
            fw.op("vector", lambda e, d=d: e.tensor_scalar(out=tot[:, d, :, 1:2], in0=tot[:, d, :, 0:1], scalar1=-1.0 / 16, scalar2=None,
                                                          op0=ALU.mult), [R("tot")], [R("tot")])
            fw.op("scalar", lambda e, d=d: e.activation(out=tot[:, d, :, 2:3], in_=tot[:, d, :, 0:1], func=AF.Exp, scale=-1.0 / 16),
                  [R("tot")], [R("tot")])
            for hp in range(2):
                fw.op("scalar", lambda e, d=d, hp=hp: e.activation(out=ekd[:, d, hp, :], in_=cs[:, d, hp, :], func=AF.Exp,
                                                                  scale=1.0 / 16, bias=tot[:, d, hp, 1:2]), [R("cs"), R("tot")], [R("ekd")])
                fw.op("vector", lambda e, d=d, hp=hp: e.tensor_tensor(out=kdT[:, d, hp, :], in0=pqkb[:, (2 + hp) * 128:(3 + hp) * 128],
                                                                     in1=ekd[:, d, hp, :], op=ALU.mult), [rpb[1], R("ekd")], [R("kdT")])
            if need_qk:
                fw.op("scalar", lambda e, d=d: e.activation(out=eq[:, d], in_=cs[:, d], func=AF.Exp, scale=-1.0 / 16), [R("cs")], [R("eq")])
                fw.op("scalar", lambda e, d=d: e.activation(out=ek[:, d], in_=cs[:, d], func=AF.Exp, scale=1.0 / 16), [R("cs")], [R("ek")])
                fw.op("vector", lambda e, d=d: e.tensor_tensor(out=qtT[:, d], in0=pqkb[:, 0:256].rearrange("p (a b) -> p a b", a=2),
                                                              in1=eq[:, d], op=ALU.mult), [rpb[1], R("eq")], [R("qtT")])
                fw.op("gpsimd", lambda e, d=d: e.tensor_copy(out=ktT[:, d], in_=ek[:, d]), [R("ek")], [R("ktT")])
                fw.op("vector", lambda e, d=d: e.tensor_tensor(out=ktT[:, d], in0=pqkb[:, 256:512].rearrange("p (a b) -> p a b", a=2),
                                                              in1=ek[:, d], op=ALU.mult), [rpb[1], R("ek"), R("ktT")], [R("ktT")])
            pkdb = pkd[:].bitcast(BF16)
            for hp in range(2):
                fw.op("tensor", lambda e, d=d, hp=hp: e.transpose(pkdb[:, (d * 2 + hp) * 128:(d * 2 + hp + 1) * 128], kdT[:, d, hp, :],
                                                                 self.identB[:]), [R("kdT"), rI], [rpb[5]])
            fw.op("scalar", lambda e, d=d: e.activation(out=kdt[:, d].rearrange("p a b -> p (a b)"), in_=pkdb[:, d * 256:(d + 1) * 256],
                                                       func=AF.Copy), [rpb[5]], [R("kdt")])

    def dstate(s, d):
        lrT, sp, cs, eq, ek, ekd, tot, qtT, ktT, kdT, kdt, sT = (X[s] for X in (lrT_L, sp_L, cs_L, eq_L, ek_L, ekd_L, tot_L, qtT_L, ktT_L, kdT_L, kdt_L, sT_L))
        R = mkR(s)
        Tt = T[s]
        rT = [R("T%d" % s)]
        for hp in range(2):
            for h2 in range(2):
                h = hp * 2 + h2
                fw.op("tensor", lambda e, hp=hp, h2=h2, h=h: e.matmul(
                    pdS[h2 * 64:(h2 + 1) * 64, (d * 2 + hp) * 128:(d * 2 + hp + 1) * 128],
                    lhsT=kdt[:, d, hp, h2 * 64:(h2 + 1) * 64], rhs=Tt[:, 512 + h * 128:512 + (h + 1) * 128],
                    start=True, stop=True), [R("kdt")] + rT, [rpb[6]])

    def supdate(s, d):
        lrT, sp, cs, eq, ek, ekd, tot, qtT, ktT, kdT, kdt, sT = (X[s] for X in (lrT_L, sp_L, cs_L, eq_L, ek_L, ekd_L, tot_L, qtT_L, ktT_L, kdT_L, kdt_L, sT_L))
        R = mkR(s)
        for hp in range(2):
            fw.op("vector", lambda e, hp=hp: e.scalar_tensor_tensor(out=S32[:, d, hp, :], in0=S32[:, d, hp, :], scalar=tot[:, d, hp, 2:3],
                                                                   in1=pdS[:, (d * 2 + hp) * 128:(d * 2 + hp + 1) * 128],
                                                                   op0=ALU.mult, op1=ALU.add), [R("S32"), R("tot"), rpb[6]], [R("S32")])

    order_b = [1, 0] + list(range(NTILE - 1, 1, -1))
    for n, ci in enumerate(order_b):
        s = n % NS
        load(ci, s)
        gates(s, [1], False)
        fw.op("gpsimd", lambda e, ci=ci: e.tensor_copy(out=Sst[:, ci], in_=S32[:, 1]), [R("S32")], [R("Sst")])
        dstate(s, 1)
        supdate(s, 1)
    def passF(ci):
        s = ci % NS
        lrT, sp, cs, eq, ek, ekd, tot, qtT, ktT, kdT, kdt, sT = (X[s] for X in (lrT_L, sp_L, cs_L, eq_L, ek_L, ekd_L, tot_L, qtT_L, ktT_L, kdT_L, kdt_L, sT_L))
        R = mkR(s)
        load(ci, s)
        gates(s, [0, 1], True)
        Tt = T[s]
        rT = [R("T%d" % s)]
        for d in range(2):
            M = Mf if d == 0 else Mb
            rM = R("Mf") if d == 0 else R("Mb")
            for h in range(4):
                hp, h2 = h // 2, h % 2
                psc = psc0 if d == 0 else psc1
                rps = rpb[3] if d == 0 else rpb[4]
                fw.op("tensor", lambda e, d=d, hp=hp, h2=h2, h=h, psc=psc: e.matmul(
                    psc[:, h * 128:(h + 1) * 128], lhsT=ktT[h2 * 64:(h2 + 1) * 64, d, hp, :], rhs=qtT[h2 * 64:(h2 + 1) * 64, d, hp, :],
                    start=True, stop=True), [R("ktT"), R("qtT")], [rps])
                fw.op("vector" if h % 2 == 0 else "gpsimd" if False else "vector", lambda e, d=d, h=h, psc=psc, M=M: e.tensor_tensor(
                    out=sT[:, d, h, :], in0=psc[:, h * 128:(h + 1) * 128], in1=M, op=ALU.mult), [rps, rM], [R("sT")])
        for h in range(4):
            hp, h2 = h // 2, h % 2
            ops = []
            for d in range(2):
                ops.append((sT[:, d, h, :], Tt[:, 512 + h * 128:512 + (h + 1) * 128], [R("sT")] + rT))
                if d == 0:
                    ops.append((qtT[h2 * 64:(h2 + 1) * 64, 0, hp, :], Sbf[h2 * 64:(h2 + 1) * 64, hp, :], [R("qtT"), R("Sbf")]))
                else:
                    ops.append((qtT[h2 * 64:(h2 + 1) * 64, 1, hp, :], Sst[h2 * 64:(h2 + 1) * 64, ci, hp, :], [R("qtT"), R("Sst")]))
            for n_, (lt, rh, rd) in enumerate(ops):
                fw.op("tensor", lambda e, lt=lt, rh=rh, n_=n_, h=h: e.matmul(po[:, h * 128:(h + 1) * 128], lhsT=lt, rhs=rh,
                                                                            start=(n_ == 0), stop=(n_ == 3)), rd, [rpb[7]])
        dstate(s, 0)
        supdate(s, 0)
        fw.op("gpsimd", lambda e: e.tensor_copy(out=Sbf, in_=S32[:, 0]), [R("S32")], [R("Sbf")])
        yt = ygt[s]
        ry = R("yg%d" % s)
        for h in range(4):
            fw.op("scalar", lambda e, h=h: e.activation(out=sq, in_=po[:, h * 128:(h + 1) * 128], func=AF.Square,
                                                       accum_out=rs[:, h:h + 1]), [rpb[7]], [R("sq"), R("rs")])
        fw.op("vector", lambda e: e.tensor_scalar(out=rs[:, 4:8], in0=rs[:, 0:4], scalar1=1.0 / 128, scalar2=EPS, op0=ALU.mult,
                                                 op1=ALU.add), [R("rs")], [R("rs")])
        fw.op("scalar", lambda e: e.activation(out=rs[:, 4:8], in_=rs[:, 4:8], func=AF.Sqrt), [R("rs")], [R("rs")])
        fw.op("vector", lambda e: e.reciprocal(out=rs[:, 4:8], in_=rs[:, 4:8]), [R("rs")], [R("rs")])
        for h in range(4):
            fw.op("vector", lambda e, h=h, yt=yt: e.scalar_tensor_tensor(out=yt[:, h * 128:(h + 1) * 128], in0=po[:, h * 128:(h + 1) * 128],
                                                                        scalar=rs[:, 4 + h:5 + h], in1=gnb, op0=ALU.mult, op1=ALU.mult),
                  [rpb[7], R("rs"), R("gnb")], [ry])
        fw.dma("gpsimd", [(rows(self.yg, yglat, ci, 0, 512), yt)], [ry], [self.res("yg")])

    for ci_ in range(NTILE):
        passF(ci_)


Prog.gla = _gla


def _s5(self, l):
    fw = self.fw
    v = self.view
    self.aoff = self.phase_base
    TWO_PI = 6.283185307179586
    NG = 8
    X8 = v([128, 9, 8, NG * 16], BF16)
    Ytok = v([128, 9, 8, NG * 16], BF16)
    U8_L = [v([128, 1056], BF16) for _ in range(2)]
    gsc2 = v([128, 1056], F32)
    Xg = v([128, 9, 128], BF16)
    gy8 = v([128, 1056], BF16)
    gsc = v([128, 1056], F32)
    Ere = v([128, 2, NG, 65], F32)
    Eim = v([128, 2, NG, 65], F32)
    ErD = v([128, 2, NG, 65], F32)
    EiD = v([128, 2, NG, 65], F32)
    kvr = v([128, 65], F32)
    kv = v([128, 65], F32)
    kvi = v([128, 65], I32)
    sm = v([128, 24, 2, NG], F32)
    AKr = v([128, 8, 2, NG], F32)
    AKs = v([128, 8, 2, NG], F32)
    sgn = v([128, 2], F32)
    ba = v([128, NG, 16], F32)
    bb = v([128, NG, 16], F32)
    Ca = v([128, NG, 16], F32)
    Cb = v([128, NG, 16], F32)
    Bw = v([128, 2, 4, 16, 16], F32) if False else None
    BA = [[v([128, NG, 16], F32) for _ in range(4)] for _ in range(2)]
    CA = [[v([128, NG, 16], F32) for _ in range(2)] for _ in range(2)]
    Dcol = v([128, NG], F32)
    swapM = v([128, 128], F32)
    FOLD = v([128, 64], F32)
    colL = v([128, 8, 2, NG], F32)
    colR = v([128, 8, 2, NG], F32)
    maskF = v([128, 8, 16], F32)
    maskB = v([128, 8, 16], F32)
    scr_L = [v([128, 4096], F32) for _ in range(2)]
    scr = scr_L[0]
    M2T_L = [[sc_[:, i * 1024:(i + 1) * 1024].rearrange("p (a b) -> p a b", a=64) for i in range(2)] for sc_ in scr_L]
    M3f_L = [[sc_[:, (2 + i) * 1024:(3 + i) * 1024].rearrange("p (a b) -> p a b", a=64) for i in range(2)] for sc_ in scr_L]
    tA = scr[:, 0:NG * 65].rearrange("p (a b) -> p a b", a=NG)
    tB = scr[:, 1040:1040 + NG * 65].rearrange("p (a b) -> p a b", a=NG)
    tI = scr[:, 2080:2080 + NG * 65].bitcast(I32).rearrange("p (a b) -> p a b", a=NG)
    M2Tp_L = [[v([128, 8, 16], F32) for _ in range(2)] for _ in range(2)]
    M2b_L = [[v([128, 8, 128], BF16) for _ in range(2)] for _ in range(2)]
    M3b_L = [[v([128, 8, 128], BF16) for _ in range(2)] for _ in range(2)]
    M1b_L = [[v([128, 8, 128], BF16) for _ in range(2)] for _ in range(2)]
    Ak_L = [[v([128, 8, 128], F32) for _ in range(2)] for _ in range(2)]
    Pst_L = [[v([128, 132], F32) for _ in range(2)] for _ in range(2)]
    HHb_L = [[v([128, 132], BF16) for _ in range(2)] for _ in range(2)]
    R = lambda n: self.res("s5_" + n)
    R_glob = R
    rI = self.res("ident")
    pb = self.pb
    rpb = [self.res("pb%d" % i) for i in range(8)]
    V, G_ = "vector", "gpsimd"

    def tt(eng, out, a, b, op, reads, writes):
        fw.op(eng, lambda e: e.tensor_tensor(out=out, in0=a, in1=b, op=op), reads, writes)

    def bc(ap, shape):
        return ap.to_broadcast(shape)

    fw.op(G_, lambda e: e.iota(kvi, pattern=[[1, 65]], base=0, channel_multiplier=0), [], [R("kv")])
    fw.op(V, lambda e: e.tensor_copy(out=kv, in_=kvi), [R("kv")], [R("kv")])
    fw.op(V, lambda e: e.tensor_scalar(out=kvr, in0=kv, scalar1=-1.0, scalar2=64.0, op0=ALU.mult, op1=ALU.add), [R("kv")], [R("kv")])
    fw.op(V, lambda e: e.memset(sgn[0:64, 0:1], -1.0), [], [R("sgn")])
    fw.op(V, lambda e: e.memset(sgn[64:128, 0:1], 1.0), [], [R("sgn")])
    fw.op(V, lambda e: e.memset(sgn[0:64, 1:2], 1.0), [], [R("sgn")])
    fw.op(V, lambda e: e.memset(sgn[64:128, 1:2], -1.0), [], [R("sgn")])
    fw.op(V, lambda e: e.tensor_copy(out=swapM[:, 0:64], in_=self.identF[:, 64:128]), [rI], [R("swapM")])
    fw.op(V, lambda e: e.tensor_copy(out=swapM[:, 64:128], in_=self.identF[:, 0:64]), [rI], [R("swapM")])
    tt(V, FOLD, self.identF[:, 0:64], self.identF[:, 64:128], ALU.add, [rI], [R("FOLD")])
    fw.op(G_, lambda e: e.memset(maskF, 1.0), [], [R("mask")])
    fw.op(G_, lambda e: e.affine_select(out=maskF, in_=maskF, compare_op=ALU.is_ge, fill=0.0, base=15,
                                       pattern=[[16, 8], [0, 16]], channel_multiplier=-1), [R("mask")], [R("mask")])
    fw.op(G_, lambda e: e.memset(maskB, 1.0), [], [R("mask")])
    fw.op(G_, lambda e: e.affine_select(out=maskB, in_=maskB, compare_op=ALU.is_ge, fill=0.0, base=0,
                                       pattern=[[-16, 8], [0, 16]], channel_multiplier=1), [R("mask")], [R("mask")])

    for gh in range(32 // NG):
        g0 = gh * NG
        fw.dma("sync", [(X8[:, ct], self.P[ct * 1024:(ct + 1) * 1024, g0 * 16:g0 * 16 + NG * 16].rearrange("(c s) w -> c s w", s=8))
                        for ct in range(8)], [self.res("P")], [R("X8")])
        fw.dma("sync", [(X8[0:32, 8], self.P[8192:8448, g0 * 16:g0 * 16 + NG * 16].rearrange("(c s) w -> c s w", s=8))],
               [self.res("P")], [R("X8")])
        rsm = R("sm")
        pairs = []
        for d in range(2):
            for half in range(2):
                ps_ = slice(half * 64, half * 64 + 64)
                pairs.append((sm[ps_, 0, d, :], self.lam_re[l, d, g0:g0 + NG, :].rearrange("g n -> n g")))
                pairs.append((sm[ps_, 1, d, :], self.lam_im[l, d, g0:g0 + NG, :].rearrange("g n -> n g")))
            pairs.append((sm[:, 2, d, :], self.log_dt[l, d:d + 1, g0:g0 + NG].partition_broadcast(128)[:, 0, :]))
        fw.dma("gpsimd", pairs, [], [rsm])
        rb = R("bc")
        fw.dma("sync", [(ba[0:64], self.b_re[l, g0:g0 + NG].rearrange("g n p -> n g p")),
                        (bb[64:128], self.b_re[l, g0:g0 + NG].rearrange("g n p -> n g p")),
                        (ba[64:128], self.b_im[l, g0:g0 + NG].rearrange("g n p -> n g p")),
                        (bb[0:64], self.b_im[l, g0:g0 + NG].rearrange("g n p -> n g p"))], [], [rb])
        for gi in range(NG):
            g = g0 + gi
            fw.dma("sync" if gi % 2 == 0 else "gpsimd",
                   [(Ca[0:64, gi, :], self.c_re[l, g].rearrange("p n -> n p")), (Cb[64:128, gi, :], self.c_re[l, g].rearrange("p n -> n p")),
                    (Ca[64:128, gi, :], self.c_im[l, g].rearrange("p n -> n p")), (Cb[0:64, gi, :], self.c_im[l, g].rearrange("p n -> n p"))],
                   [], [rb], sres=R("bc%d" % (gi % 2)))
        fw.dma("sync", [(Dcol[s_ * 16:(s_ + 1) * 16, :], self.s5_d[l, g0 * 16:g0 * 16 + NG * 16].rearrange("(g p) -> p g", p=16))
                        for s_ in range(8)], [], [R("Dcol")])
        S_ = lambda i: sm[:, i]
        fw.op("scalar", lambda e: e.activation(out=S_(2), in_=S_(2), func=AF.Exp), [rsm], [rsm])
        tt(V, S_(3), S_(0), S_(2), ALU.mult, [rsm], [rsm])
        tt(V, S_(4), S_(1), S_(2), ALU.mult, [rsm], [rsm])
        fw.op(V, lambda e: e.tensor_scalar(out=S_(4), in0=S_(4), scalar1=1.0 / TWO_PI, scalar2=None, op0=ALU.mult), [rsm], [rsm])
        rE = R("E")
        rt = R("tab")
        rtw = [rt, R("M0_s0"), R("M1_s0")]
        for d in range(2):
          for (kvx, TRe, TIm) in ((kv, Ere, Eim), (kvr, ErD, EiD)):
            tt(V, tA, bc(sm[:, 3, d, :].unsqueeze(2), [128, NG, 65]), bc(kvx.unsqueeze(1), [128, NG, 65]), ALU.mult, [rsm, R("kv")], rtw)
            fw.op("scalar", lambda e: e.activation(out=tA, in_=tA, func=AF.Exp), [rt], rtw)
            for which in range(2):
                tt(V, tB, bc(sm[:, 4, d, :].unsqueeze(2), [128, NG, 65]), bc(kvx.unsqueeze(1), [128, NG, 65]), ALU.mult, [rsm, R("kv")], rtw)
                if which == 1:
                    fw.op(V, lambda e: e.tensor_scalar(out=tB, in0=tB, scalar1=0.25, scalar2=None, op0=ALU.add), [rt], rtw)
                fw.op(V, lambda e: e.tensor_copy(out=tI, in_=tB), [rt], rtw)
                tt(V, tB, tB, tI, ALU.subtract, [rt], rtw)
                dst = TIm[:, d] if which == 0 else TRe[:, d]
                fw.op(V, lambda e, dst=dst: e.tensor_single_scalar(out=dst, in_=tB, scalar=0.5, op=ALU.is_gt), [rt], [rE])
                tt(V, tB, tB, dst, ALU.subtract, [rt, rE], rtw)
                fw.op(V, lambda e, dst=dst: e.tensor_single_scalar(out=dst, in_=tB, scalar=-0.5, op=ALU.is_lt), [rt], [rE])
                tt(V, tB, tB, dst, ALU.add, [rt, rE], rtw)
                fw.op("scalar", lambda e: e.activation(out=tB, in_=tB, func=AF.Sin, scale=6.283185), [rt], rtw)
                tt(V, dst, tB, tA, ALU.mult, [rt], [rE])
        def coef_(d):
            s = lambda i: sm[:, i, d, :]
            e1r, e1i = Ere[:, d, :, 1], Eim[:, d, :, 1]
            e64r, e64i = Ere[:, d, :, 64], Eim[:, d, :, 64]
            stt = lambda o, i0, c, i1: fw.op(V, lambda e: e.scalar_tensor_tensor(out=o, in0=i0, scalar=c, in1=i1, op0=ALU.add, op1=ALU.mult),
                                             [rsm], [rsm])
            ti_ = tI[:, :, 0]
            fw.op(V, lambda e: e.tensor_copy(out=ti_, in_=s(4)), [rsm], rtw)
            tt(V, s(20), s(4), ti_, ALU.subtract, [rsm, rt], [rsm])
            fw.op(V, lambda e: e.tensor_single_scalar(out=s(21), in_=s(20), scalar=0.5, op=ALU.is_gt), [rsm], [rsm])
            tt(V, s(20), s(20), s(21), ALU.subtract, [rsm], [rsm])
            fw.op(V, lambda e: e.tensor_single_scalar(out=s(21), in_=s(20), scalar=-0.5, op=ALU.is_lt), [rsm], [rsm])
            tt(V, s(20), s(20), s(21), ALU.add, [rsm], [rsm])
            fw.op(V, lambda e: e.tensor_scalar(out=s(20), in0=s(20), scalar1=3.14159265358979, scalar2=None, op0=ALU.mult), [rsm], [rsm])
            tt(V, s(21), s(20), s(20), ALU.mult, [rsm], [rsm])
            fw.op(V, lambda e: e.tensor_scalar(out=s(22), in0=s(21), scalar1=-1.0 / 39916800, scalar2=None, op0=ALU.mult), [rsm], [rsm])
            for c_ in (1.0 / 362880, -1.0 / 5040, 1.0 / 120, -1.0 / 6):
                stt(s(22), s(22), c_, s(21))
            stt(s(22), s(22), 1.0, s(20))
            fw.op(V, lambda e: e.tensor_scalar(out=s(23), in0=s(21), scalar1=1.0 / 479001600, scalar2=None, op0=ALU.mult), [rsm], [rsm])
            for c_ in (-1.0 / 3628800, 1.0 / 40320, -1.0 / 720, 1.0 / 24, -0.5):
                stt(s(23), s(23), c_, s(21))
            fw.op(V, lambda e: e.tensor_scalar(out=s(23), in0=s(23), scalar1=1.0, scalar2=None, op0=ALU.add), [rsm], [rsm])
            fw.op(V, lambda e: e.tensor_scalar(out=s(15), in0=s(3), scalar1=1.0 / 120, scalar2=None, op0=ALU.mult), [rsm], [rsm])
            for c_ in (1.0 / 24, 1.0 / 6, 0.5, 1.0):
                stt(s(15), s(15), c_, s(3))
            tt(V, s(16), s(22), s(23), ALU.mult, [rsm], [rsm])
            fw.op(V, lambda e: e.tensor_scalar(out=s(16), in0=s(16), scalar1=2.0, scalar2=None, op0=ALU.mult), [rsm], [rsm])
            tt(V, s(21), s(22), s(22), ALU.mult, [rsm], [rsm])
            fw.op(V, lambda e: e.tensor_scalar(out=s(21), in0=s(21), scalar1=2.0, scalar2=None, op0=ALU.mult), [rsm], [rsm])
            fw.op(V, lambda e: e.tensor_scalar(out=s(20), in0=s(21), scalar1=-1.0, scalar2=1.0, op0=ALU.mult, op1=ALU.add), [rsm], [rsm])
            tt(V, s(5), s(15), s(20), ALU.mult, [rsm], [rsm])
            tt(V, s(5), s(5), s(21), ALU.subtract, [rsm], [rsm])
            fw.op(V, lambda e: e.tensor_scalar(out=s(15), in0=s(15), scalar1=1.0, scalar2=None, op0=ALU.add), [rsm], [rsm])
            tt(V, s(6), s(15), s(16), ALU.mult, [rsm], [rsm])
            tt(V, s(15), s(0), s(0), ALU.mult, [rsm], [rsm])
            tt(V, s(16), s(1), s(1), ALU.mult, [rsm], [rsm])
            tt(V, s(15), s(15), s(16), ALU.add, [rsm], [rsm])
            fw.op(V, lambda e: e.reciprocal(out=s(7), in_=s(15)), [rsm], [rsm])
            tt(V, s(15), s(5), s(0), ALU.mult, [rsm], [rsm])
            tt(V, s(16), s(6), s(1), ALU.mult, [rsm], [rsm])
            tt(V, s(15), s(15), s(16), ALU.add, [rsm], [rsm])
            tt(V, s(8), s(15), s(7), ALU.mult, [rsm], [rsm])
            tt(V, s(15), s(6), s(0), ALU.mult, [rsm], [rsm])
            tt(V, s(16), s(5), s(1), ALU.mult, [rsm], [rsm])
            tt(V, s(15), s(15), s(16), ALU.subtract, [rsm], [rsm])
            tt(V, s(9), s(15), s(7), ALU.mult, [rsm], [rsm])
            fw.op(V, lambda e: e.tensor_scalar(out=s(10), in0=s(9), scalar1=sgn[:, 0:1], scalar2=None, op0=ALU.mult), [rsm, R("sgn")], [rsm])
            fw.op(V, lambda e: e.tensor_scalar(out=s(11), in0=s(9), scalar1=sgn[:, 1:2], scalar2=None, op0=ALU.mult), [rsm, R("sgn")], [rsm])
            tt(V, s(15), e64r, e64r, ALU.mult, [rE], [rsm])
            tt(V, s(16), e64i, e64i, ALU.mult, [rE], [rsm])
            tt(V, s(15), s(15), s(16), ALU.add, [rsm], [rsm])
            fw.op(V, lambda e: e.reciprocal(out=s(15), in_=s(15)), [rsm], [rsm])
            tt(V, s(12), e64r, s(15), ALU.mult, [rE, rsm], [rsm])
            tt(V, s(16), e64i, s(15), ALU.mult, [rE, rsm], [rsm])
            fw.op(V, lambda e: e.tensor_scalar(out=s(13), in0=s(16), scalar1=sgn[:, 1:2], scalar2=None, op0=ALU.mult), [rsm, R("sgn")], [rsm])
            fw.op(V, lambda e: e.tensor_scalar(out=s(14), in0=s(16), scalar1=sgn[:, 0:1], scalar2=None, op0=ALU.mult), [rsm, R("sgn")], [rsm])
            fw.op(V, lambda e: e.tensor_copy(out=s(17), in_=e1r), [rE], [rsm])
            fw.op(V, lambda e: e.tensor_scalar(out=s(18), in0=e1i, scalar1=sgn[:, 0:1], scalar2=None, op0=ALU.mult), [rE, R("sgn")], [rsm])
            fw.op(V, lambda e: e.tensor_scalar(out=s(19), in0=e1i, scalar1=sgn[:, 1:2], scalar2=None, op0=ALU.mult), [rE, R("sgn")], [rsm])
            B = lambda i: bc(sm[:, i, d, :].unsqueeze(2), [128, NG, 16])
            rB = R("BA")
            Ba, Bbs, Bpa, Bpbs = BA[d]
            tt(V, Ba, ba, B(8), ALU.mult, [rb, rsm], [rB])
            tt(V, Bpa, bb, B(10), ALU.mult, [rb, rsm], [rB])
            tt(V, Ba, Ba, Bpa, ALU.add, [rB], [rB])
            tt(V, Bbs, bb, B(8), ALU.mult, [rb, rsm], [rB])
            tt(V, Bpa, ba, B(11), ALU.mult, [rb, rsm], [rB])
            tt(V, Bbs, Bbs, Bpa, ALU.add, [rB], [rB])
            tt(V, Bpa, Ba, B(12), ALU.mult, [rB, rsm], [rB])
            tt(V, Bpbs, Bbs, B(13), ALU.mult, [rB, rsm], [rB])
            tt(V, Bpa, Bpa, Bpbs, ALU.add, [rB], [rB])
            tt(V, Bpbs, Bbs, B(12), ALU.mult, [rB, rsm], [rB])
            tt(V, gsc[:, 0:NG * 16].rearrange("p (a b) -> p a b", a=NG), Ba, B(14), ALU.mult, [rB, rsm], [R("gsc")])
            tt(V, Bpbs, Bpbs, gsc[:, 0:NG * 16].rearrange("p (a b) -> p a b", a=NG), ALU.add, [rB, R("gsc")], [rB])
            fw.op(V, lambda e: e.tensor_scalar(out=Bbs, in0=Bbs, scalar1=sgn[:, 0:1], scalar2=None, op0=ALU.mult), [rB, R("sgn")], [rB])
            fw.op(V, lambda e: e.tensor_scalar(out=Bpbs, in0=Bpbs, scalar1=sgn[:, 0:1], scalar2=None, op0=ALU.mult), [rB, R("sgn")], [rB])
            Cas, Cbn = CA[d]
            rC = R("CA")
            tmpc = gsc[:, 256:256 + NG * 16].rearrange("p (a b) -> p a b", a=NG)
            tt(V, Cas, Ca, B(17), ALU.mult, [rb, rsm], [rC])
            tt(V, tmpc, Cb, B(18), ALU.mult, [rb, rsm], [R("gsc")])
            tt(V, Cas, Cas, tmpc, ALU.add, [rC, R("gsc")], [rC])
            tt(V, Cbn, Cb, B(17), ALU.mult, [rb, rsm], [rC])
            tt(V, tmpc, Ca, B(19), ALU.mult, [rb, rsm], [R("gsc")])
            tt(V, Cbn, Cbn, tmpc, ALU.add, [rC, R("gsc")], [rC])
            fw.op(V, lambda e: e.tensor_scalar(out=Cas, in0=Cas, scalar1=sgn[:, 1:2], scalar2=None, op0=ALU.mult), [rC, R("sgn")], [rC])
            fw.op(V, lambda e: e.tensor_scalar(out=Cbn, in0=Cbn, scalar1=-1.0, scalar2=None, op0=ALU.mult), [rC], [rC])
        for d_ in range(2):
            coef_(d_)
        rAK = R("AK")
        fw.op(V, lambda e: e.tensor_copy(out=AKr[:, 0], in_=Ere[:, :, :, 64]), [rE], [rAK])
        fw.op(V, lambda e: e.tensor_copy(out=AKs[:, 0], in_=Eim[:, :, :, 64]), [rE], [rAK])
        for k in range(1, 8):
            t15, t16 = sm[:, 15], sm[:, 16]
            tt(V, t15, AKr[:, k - 1], AKr[:, k - 1], ALU.mult, [rAK], [rsm])
            tt(V, t16, AKs[:, k - 1], AKs[:, k - 1], ALU.mult, [rAK], [rsm])
            tt(V, AKr[:, k], t15, t16, ALU.subtract, [rsm], [rAK])
            tt(V, t15, AKr[:, k - 1], AKs[:, k - 1], ALU.mult, [rAK], [rsm])
            fw.op(V, lambda e, k=k: e.tensor_scalar(out=AKs[:, k], in0=t15, scalar1=2.0, scalar2=None, op0=ALU.mult), [rsm], [rAK])
        fw.op(V, lambda e: e.tensor_scalar(out=AKs, in0=AKs, scalar1=sgn[:, 1:2], scalar2=None, op0=ALU.mult), [rAK, R("sgn")], [rAK])
        fw.op(V, lambda e: e.tensor_copy(out=colL, in_=AKr), [rAK], [R("col")])
        fw.op(V, lambda e: e.tensor_copy(out=colL[64:128], in_=AKs[64:128]), [rAK, R("col")], [R("col")])
        fw.op(V, lambda e: e.tensor_copy(out=colR, in_=AKs), [rAK], [R("col")])
        fw.op(V, lambda e: e.tensor_copy(out=colR[64:128], in_=AKr[64:128]), [rAK, R("col")], [R("col")])

        def grp_(gi):
            sl = (g0 + gi) % 2
            U8, M3b, M1b, HHb = U8_L[sl], M3b_L[sl], M1b_L[sl], HHb_L[sl]
            M2T, M3f, M2Tp, M2b, Ak, Pst = M2T_L[sl], M3f_L[sl], M2Tp_L[sl], M2b_L[sl], Ak_L[sl], Pst_L[sl]
            R1 = R_glob
            SL = ("U8", "M1b0", "M1b1", "M3b0", "M3b1", "HH0", "HH1", "M0", "M1", "M2b0", "M2b1", "Ak0", "Ak1", "P0", "P1")
            R = lambda n: R1(n + "_s%d" % sl) if n in SL else R1(n)
            pub = [pb[0][:].bitcast(BF16), pb[1][:].bitcast(BF16)]
            for ct in range(9):
                fw.op(G_ if ct % 2 == 0 else "scalar",
                      (lambda e, ct=ct: e.tensor_copy(out=Xg[:, ct, :].rearrange("p (a b) -> p a b", a=8), in_=X8[:, ct, :, gi * 16:(gi + 1) * 16]))
                      if ct % 2 == 0 else
                      (lambda e, ct=ct: e.activation(out=Xg[:, ct, :].rearrange("p (a b) -> p a b", a=8), in_=X8[:, ct, :, gi * 16:(gi + 1) * 16],
                                                     func=AF.Copy)), [R("X8")], [R("Xg")])
            for ct in range(9):
                npart = 128 if ct < 8 else 32
                bank, off = (0, ct * 128) if ct < 8 else (1, 0)
                fw.op("tensor", lambda e, ct=ct, npart=npart, bank=bank, off=off: e.transpose(
                    pub[bank][:, off:off + npart], Xg[0:npart, ct, :], self.identB[0:npart, 0:npart]),
                    [R("Xg"), rI], [rpb[bank]])
            fw.op("scalar", lambda e: e.activation(out=U8[:, 0:1024], in_=pub[0][:, 0:1024], func=AF.Copy), [rpb[0]], [R("U8")])
            fw.op("scalar", lambda e: e.activation(out=U8[:, 1024:1056], in_=pub[1][:, 0:32], func=AF.Copy), [rpb[1]], [R("U8")])
            U8v = U8.rearrange("p (c j) -> p c j", j=8)
            for d in range(2):
                Ba, Bbs, Bpa, Bpbs = BA[d]
                Cas, Cbn = CA[d]
                rM = R("M%d" % d)
                if d == 0:
                    eM2r, eM2i = ErD[:, d, gi, 1:65], EiD[:, d, gi, 1:65]
                    eM3r, eM3i = Ere[:, d, gi, 0:64], Eim[:, d, gi, 0:64]
                    ePr, ePi = ErD[:, d, gi, 1:9], EiD[:, d, gi, 1:9]
                else:
                    eM2r, eM2i = Ere[:, d, gi, 0:64], Eim[:, d, gi, 0:64]
                    eM3r, eM3i = ErD[:, d, gi, 1:65], EiD[:, d, gi, 1:65]
                    ePr, ePi = Ere[:, d, gi, 56:64], Eim[:, d, gi, 56:64]
                b64 = lambda ap: bc(ap.unsqueeze(2), [128, 64, 16])
                w64 = lambda ap: bc(ap.unsqueeze(1), [128, 64, 16])
                tmp = gsc[:, 0:1024].rearrange("p (a b) -> p a b", a=64)
                rg = R("gsc")
                tt(V, M2T[d], b64(eM2r), w64(Ba[:, gi, :]), ALU.mult, [rE, R("BA")], [rM])
                tt(G_, tmp, b64(eM2i), w64(Bbs[:, gi, :]), ALU.mult, [rE, R("BA")], [rg])
                tt(G_, M2T[d], M2T[d], tmp, ALU.add, [rM, rg], [rM])
                tt(V, M3f[d], b64(eM3r), w64(Cas[:, gi, :]), ALU.mult, [rE, R("CA")], [rM])
                tt(G_, tmp, b64(eM3i), w64(Cbn[:, gi, :]), ALU.mult, [rE, R("CA")], [rg])
                tt(G_, M3f[d], M3f[d], tmp, ALU.add, [rM, rg], [rM])
                tmp8 = gsc[:, 0:128].rearrange("p (a b) -> p a b", a=8)
                tt(V, M2Tp[d], bc(ePr.unsqueeze(2), [128, 8, 16]), bc(Bpa[:, gi, :].unsqueeze(1), [128, 8, 16]), ALU.mult, [rE, R("BA")], [rM])
                tt(V, tmp8, bc(ePi.unsqueeze(2), [128, 8, 16]), bc(Bpbs[:, gi, :].unsqueeze(1), [128, 8, 16]), ALU.mult, [rE, R("BA")], [rg])
                tt(V, M2Tp[d], M2Tp[d], tmp8, ALU.add, [rM, rg], [rM])
                fw.op("scalar", lambda e, d=d: e.activation(out=M3b[d].rearrange("p a b -> p (a b)"), in_=M3f[d].rearrange("p a b -> p (a b)"),
                                                           func=AF.Copy), [rM], [R("M3b%d" % d)])
                for j in range(8):
                    bank = 2 + j // 4
                    fw.op("tensor", lambda e, d=d, j=j, bank=bank: e.transpose(
                        pb[bank][:, (j % 4) * 128:(j % 4 + 1) * 128], M2T[d][:, j * 8:(j + 1) * 8, :].rearrange("p a b -> p (a b)"),
                        self.identF[:]), [rM, rI], [rpb[bank]])
                for hb_ in range(2):
                    fw.op("scalar" if hb_ == 0 else V, (lambda e, d=d, hb_=hb_: e.activation(
                        out=M2b[d][:, hb_ * 4:(hb_ + 1) * 4, :].rearrange("p a b -> p (a b)"), in_=pb[2 + hb_][:], func=AF.Copy))
                        if hb_ == 0 else (lambda e, d=d, hb_=hb_: e.tensor_copy(
                            out=M2b[d][:, hb_ * 4:(hb_ + 1) * 4, :].rearrange("p a b -> p (a b)"), in_=pb[2 + hb_][:])),
                        [rpb[2 + hb_]], [R("M2b%d" % d)])
                for hb_ in range(2):
                    fw.op("tensor", lambda e, d=d, hb_=hb_: e.matmul(
                        pb[2 + hb_][:], lhsT=M2Tp[d].rearrange("p a b -> p (a b)"),
                        rhs=M3f[d][:, hb_ * 32:(hb_ + 1) * 32, :].rearrange("p a b -> p (a b)"), start=True, stop=True),
                        [rM], [rpb[2 + hb_]])
                if d == 0:
                    blk = pb[2][:, 0:128].rearrange("p (a b) -> p a b", a=8)
                    t8 = gsc[:, 0:128].rearrange("p (a b) -> p a b", a=8)
                    tt(V, t8, blk, maskF, ALU.mult, [rpb[2], R("mask")], [rg])
                    fw.op(V, lambda e: e.scalar_tensor_tensor(out=gsc[:, 0:128], in0=self.identF[:], scalar=Dcol[:, gi:gi + 1],
                                                              in1=gsc[:, 0:128], op0=ALU.mult, op1=ALU.add), [rg, rI, R("Dcol")], [rg])
                    fw.op(V, lambda e, d=d: e.tensor_copy(out=M1b[d][:, 0, :], in_=gsc[:, 0:128]), [rg], [R("M1b%d" % d)])
                    fw.op("scalar", lambda e, d=d: e.activation(out=M1b[d][:, 1:4, :].rearrange("p a b -> p (a b)"), in_=pb[2][:, 128:512],
                                                               func=AF.Copy), [rpb[2]], [R("M1b%d" % d)])
                    fw.op("scalar", lambda e, d=d: e.activation(out=M1b[d][:, 4:8, :].rearrange("p a b -> p (a b)"), in_=pb[3][:],
                                                               func=AF.Copy), [rpb[3]], [R("M1b%d" % d)])
                else:
                    blk = pb[3][:, 384:512].rearrange("p (a b) -> p a b", a=8)
                    tt(V, M1b[d][:, 7, :].rearrange("p (a b) -> p a b", a=8), blk, maskB, ALU.mult, [rpb[3], R("mask")], [R("M1b%d" % d)])
                    fw.op("scalar", lambda e, d=d: e.activation(out=M1b[d][:, 0:4, :].rearrange("p a b -> p (a b)"), in_=pb[2][:],
                                                               func=AF.Copy), [rpb[2]], [R("M1b%d" % d)])
                    fw.op("scalar", lambda e, d=d: e.activation(out=M1b[d][:, 4:7, :].rearrange("p a b -> p (a b)"), in_=pb[3][:, 0:384],
                                                               func=AF.Copy), [rpb[3]], [R("M1b%d" % d)])
                rA = R("Ak%d" % d)
                for k in range(8):
                    fw.op("scalar", lambda e, d=d, k=k: e.activation(out=Ak[d][:, k, 0:64], in_=FOLD, func=AF.Copy,
                                                                    scale=colL[:, k, d, gi:gi + 1]), [R("FOLD"), R("col")], [rA])
                    fw.op("scalar", lambda e, d=d, k=k: e.activation(out=Ak[d][:, k, 64:128], in_=FOLD, func=AF.Copy,
                                                                    scale=colR[:, k, d, gi:gi + 1]), [R("FOLD"), R("col")], [rA])
                ps = pb[4]
                if d == 0:
                    for j in range(8):
                        fw.op("tensor", lambda e, d=d, j=j: e.matmul(ps[:, 0:132], lhsT=M2b[d][:, j, :], rhs=U8v[:, :, j],
                                                                    start=(j == 0), stop=(j == 7)), [R("M2b%d" % d), R("U8")], [rpb[4]])
                else:
                    for j in range(8):
                        fw.op("tensor", lambda e, d=d, j=j: e.matmul(ps[:, 0:128], lhsT=M2b[d][:, j, :], rhs=U8v[:, 4:132, j],
                                                                    start=(j == 0), stop=(j == 7)), [R("M2b%d" % d), R("U8")], [rpb[4]])
                    for j in range(8):
                        fw.op("tensor", lambda e, d=d, j=j: e.matmul(ps[:, 128:132], lhsT=M2b[d][:, j, :], rhs=U8v[:, 0:4, j],
                                                                    start=False, stop=(j == 7), skip_group_check=True),
                              [R("M2b%d" % d), R("U8")], [rpb[4]])
                rP = R("P%d" % d)
                fw.op(V, lambda e, d=d: e.tensor_copy(out=Pst[d], in_=ps[:, 0:132]), [rpb[4]], [rP])
                for k in range(8):
                    sft = 1 << k
                    if d == 0:
                        o_sl, i_sl = slice(sft, 132), slice(0, 132 - sft)
                    else:
                        o_sl, i_sl = slice(0, 132 - sft), slice(sft, 132)
                    fw.op("tensor", lambda e, d=d, k=k, o_sl=o_sl, i_sl=i_sl: e.matmul(ps[:, o_sl], lhsT=Ak[d][:, k, :], rhs=Pst[d][:, i_sl],
                                                                                     start=True, stop=True), [rA, rP], [rpb[4]])
                    fw.op(V, lambda e, d=d, o_sl=o_sl: e.tensor_tensor(out=Pst[d][:, o_sl], in0=Pst[d][:, o_sl], in1=ps[:, o_sl], op=ALU.add),
                          [rP, rpb[4]], [rP])
                rH = R("HH%d" % d)
                fw.op(G_, lambda e, d=d: e.memset(HHb[d], 0.0), [], [rH])
                if d == 0:
                    fw.op(V, lambda e, d=d: e.tensor_copy(out=HHb[d][:, 1:132], in_=Pst[d][:, 0:131]), [rP, rH], [rH])
                else:
                    fw.op(V, lambda e, d=d: e.tensor_copy(out=HHb[d][:, 0:131], in_=Pst[d][:, 1:132]), [rP, rH], [rH])
            for jt in range(8):
                bank = 5 + jt // 3
                yo_ = pb[bank][:, (jt % 3) * 132:(jt % 3 + 1) * 132]
                ops = []
                for js in range(0, jt + 1):
                    ops.append((yo_, M1b[0][:, jt - js, :], U8v[:, :, js], [R("M1b0"), R("U8")]))
                for js in range(jt, 8):
                    ops.append((yo_, M1b[1][:, 7 - (js - jt), :], U8v[:, :, js], [R("M1b1"), R("U8")]))
                ops.append((yo_, M3b[0][:, jt, :], HHb[0][:, 0:132], [R("M3b0"), R("HH0")]))
                ops.append((yo_[:, 4:132], M3b[1][:, jt, :], HHb[1][:, 0:128], [R("M3b1"), R("HH1")]))
                ops.append((yo_[:, 0:4], M3b[1][:, jt, :], HHb[1][:, 128:132], [R("M3b1"), R("HH1")]))
                for n_, (o_, lt, rh, rd) in enumerate(ops):
                    fw.op("tensor", lambda e, o_=o_, lt=lt, rh=rh, n_=n_, last=(n_ == len(ops) - 1): e.matmul(
                        o_, lhsT=lt, rhs=rh, start=(n_ == 0), stop=last, skip_group_check=True), rd, [rpb[bank]])
            gy8v = gy8.rearrange("p (c j) -> p j c", j=8)
            gscv = gsc2[:, 0:1056].rearrange("p (j c) -> p j c", j=8)
            rg = R("gsc2")
            for b3 in range(3):
                njt = 3 if b3 < 2 else 2
                src_ = pb[5 + b3][:, 0:njt * 132].rearrange("p (j c) -> p j c", j=njt)
                dst_ = gy8v[:, b3 * 3:b3 * 3 + njt, :]
                fw.op("scalar", lambda e, src_=src_, dst_=dst_: e.activation(out=dst_, in_=src_, func=AF.Gelu_apprx_tanh),
                      [rpb[5 + b3]], [R("gy8")])
            for ct in range(9):
                npart = 128 if ct < 8 else 32
                bank, off = (0, ct * 128) if ct < 8 else (1, 0)
                fw.op("tensor", lambda e, ct=ct, npart=npart, bank=bank, off=off: e.transpose(
                    pub[bank][0:npart, off:off + 128], gy8[:, ct * 128:ct * 128 + npart], self.identB[:]),
                    [R("gy8"), rI], [rpb[bank]])
                fw.op("scalar" if ct % 2 == 0 else V,
                      (lambda e, ct=ct, npart=npart, bank=bank, off=off: e.activation(
                          out=Ytok[0:npart, ct, :, gi * 16:(gi + 1) * 16], in_=pub[bank][0:npart, off:off + 128].rearrange("p (a b) -> p a b", a=8),
                          func=AF.Copy)) if ct % 2 == 0 else
                      (lambda e, ct=ct, npart=npart, bank=bank, off=off: e.tensor_copy(
                          out=Ytok[0:npart, ct, :, gi * 16:(gi + 1) * 16], in_=pub[bank][0:npart, off:off + 128].rearrange("p (a b) -> p a b", a=8))),
                      [rpb[bank]], [R("Ytok")])
        for gi_ in range(NG):
            grp_(gi_)
        fw.dma("sync", [(self.gy[ct * 1024:(ct + 1) * 1024, g0 * 16:g0 * 16 + NG * 16].rearrange("(c s) w -> c s w", s=8), Ytok[:, ct])
                        for ct in range(8)], [R("Ytok")], [self.res("gy")])
        fw.dma("sync", [(self.gy[8192:8448, g0 * 16:g0 * 16 + NG * 16].rearrange("(c s) w -> c s w", s=8), Ytok[0:32, 8])],
               [R("Ytok")], [self.res("gy")])
    print("s5 arena end", self.aoff)


Prog.s5 = _s5
```
